# Optimizing a Trainium2 kernel written in Bass

```python
import math, functools
import jax, jax.numpy as jnp
from jax import lax
import numpy as np

D_MODEL = 1024
BATCH = 4
SEQ = 4096
DEPTH = 1
DEC_BATCH = 128
DEC_SEQ = 8
PAST_LEN = 2048
PAGE_SIZE = 128

H_A = 8
HD_A = 64
W_A = H_A * HD_A
DECAY_LORA = 64
AAA_LORA = 64
GATE_LORA = 128
RWKV_SPLITS = [W_A, 2 * W_A, 3 * W_A, 3 * W_A + DECAY_LORA, 3 * W_A + DECAY_LORA + AAA_LORA]
RWKV_COLS = 3 * W_A + DECAY_LORA + AAA_LORA + GATE_LORA
H_B = 4
HD_B = 64
W_B = H_B * 2 * HD_B
DIFF_COLS = 3 * W_B
GATE_COLS = 2 * D_MODEL
N_COLS = RWKV_COLS + DIFF_COLS + GATE_COLS
D_FF = 4 * D_MODEL
Q_BLOCK = 128
NORM_EPS = 1e-6
GN_EPS = 64e-5
SUBLN_EPS = 1e-5
NEG = -1e30

kernel_name = "rwkv7_diffattn_gated_hybrid_step"


def rmsnorm(x, g, eps):
    xf = x.astype(jnp.float32)
    y = xf * lax.rsqrt(jnp.mean(xf * xf, axis=-1, keepdims=True) + eps)
    return (y * g.astype(jnp.float32)).astype(x.dtype)


def alibi_slopes():
    return 2.0 ** (-8.0 * jnp.arange(1, H_B + 1, dtype=jnp.float32) / H_B)


def rwkv7_scan(S0, r, w, k, v, a, b):
    xs = tuple(jnp.moveaxis(t.astype(jnp.float32), 1, 0) for t in (r, w, k, v, a, b))

    def step(S, inp):
        r_t, w_t, k_t, v_t, a_t, b_t = inp
        sa = jnp.einsum('bhij,bhj->bhi', S, a_t)
        S = S * w_t[:, :, None, :] + sa[..., None] * b_t[:, :, None, :] + v_t[..., None] * k_t[:, :, None, :]
        return S, jnp.einsum('bhij,bhj->bhi', S, r_t)

    S, ys = lax.scan(step, S0.astype(jnp.float32), xs)
    return S, jnp.moveaxis(ys, 0, 1)


def rwkv7_branch(Pr, S0, lp):
    B, T = Pr.shape[:2]
    f32 = jnp.float32
    r, k, v, wd, ad, gd = jnp.split(Pr, RWKV_SPLITS, axis=-1)
    w = -jax.nn.softplus(-(lp['w0'] + jnp.tanh(wd) @ lp['w2']).astype(f32)) - 0.5
    decay = jnp.exp(-jnp.exp(w))
    a = jax.nn.sigmoid((lp['a0'] + ad @ lp['a2']).astype(f32))
    g = jax.nn.sigmoid(gd) @ lp['g2']
    hs = lambda t: t.reshape(B, T, H_A, HD_A)
    kk = hs((k * lp['k_k']).astype(f32))
    kk = kk / jnp.maximum(jnp.linalg.norm(kk, axis=-1, keepdims=True), 1e-12)
    k = k.astype(f32) * (1.0 + (a - 1.0) * lp['k_a'].astype(f32))
    S, y = rwkv7_scan(S0, hs(r), hs(decay), hs(k), hs(v), -kk, kk * hs(a))
    mean = jnp.mean(y, axis=-1, keepdims=True)
    var = jnp.mean(jnp.square(y - mean), axis=-1, keepdims=True)
    y = (y - mean) * lax.rsqrt(var + GN_EPS) * lp['ln_x_w'].astype(f32).reshape(H_A, HD_A) \
        + lp['ln_x_b'].astype(f32).reshape(H_A, HD_A)
    bonus = jnp.sum(hs(r).astype(f32) * hs(k) * lp['r_k'].astype(f32), axis=-1, keepdims=True) * hs(v).astype(f32)
    o = (y + bonus).reshape(B, T, W_A).astype(Pr.dtype) * g
    return o, S


def diff_attend(q, q_pos, segs, lam, slopes):
    scale = HD_B ** -0.5
    scores = []
    for k, _, k_pos in segs:
        s = jnp.einsum('bqhcd,bkhcd->bhcqk', q, k).astype(jnp.float32) * scale
        dist = q_pos[:, None] - k_pos[None, :]
        bias = -slopes[:, None, None] * dist.astype(jnp.float32)
        scores.append(jnp.where(dist >= 0, s + bias[None, :, None], NEG))
    p = jax.nn.softmax(jnp.concatenate(scores, axis=-1), axis=-1)
    attn = p[:, :, 0] - lam * p[:, :, 1]
    out = None
    start = 0
    for _, v, _ in segs:
        n = v.shape[1]
        o = jnp.einsum('bhqk,bkhe->bqhe', attn[..., start:start + n].astype(v.dtype), v)
        out = o if out is None else out + o
        start += n
    return out


def prompt_attend(q, k, v, lam, slopes):
    B, S = q.shape[:2]
    nb = S // Q_BLOCK
    qb = jnp.moveaxis(q.reshape(B, nb, Q_BLOCK, H_B, 2, HD_B), 1, 0)
    k_pos = jnp.arange(S)

    def blk(args):
        qi, i = args
        q_pos = i * Q_BLOCK + jnp.arange(Q_BLOCK)
        return diff_attend(qi, q_pos, ((k, v, k_pos),), lam, slopes)

    out = lax.map(blk, (qb, jnp.arange(nb)))
    return jnp.moveaxis(out, 0, 1).reshape(B, S, H_B, 2 * HD_B)


def sample_attend(q, k, v, lam, k_cache, v_cache, page_table, slopes):
    Bd, T = q.shape[:2]
    past = page_table.shape[1] * PAGE_SIZE
    kp = k_cache[page_table].reshape(Bd, past, H_B, 2, HD_B)
    vp = v_cache[page_table].reshape(Bd, past, H_B, 2 * HD_B)
    q_pos = past + jnp.arange(T)
    segs = ((kp, vp, jnp.arange(past)), (k, v, q_pos))
    return diff_attend(q, q_pos, segs, lam, slopes)


def layer_forward(x, prev_row, S0, attend, lam_init, lp):
    B, T, _ = x.shape
    f32 = jnp.float32
    xn = rmsnorm(x, lp['g_mix'], NORM_EPS)
    P = xn @ lp['w_in']
    Pr = P[..., :RWKV_COLS]
    first_prev = (prev_row.astype(x.dtype) @ lp['w_in'][:, :RWKV_COLS])[:, None]
    Pprev = jnp.concatenate([first_prev, Pr[:, :-1]], axis=1)
    Pr = Pr + (Pprev - Pr) * lp['mu_shift']
    o_a, S = rwkv7_branch(Pr, S0, lp)
    off = RWKV_COLS
    q = P[..., off:off + W_B].reshape(B, T, H_B, 2, HD_B)
    k = P[..., off + W_B:off + 2 * W_B].reshape(B, T, H_B, 2, HD_B)
    v = P[..., off + 2 * W_B:off + 3 * W_B].reshape(B, T, H_B, 2 * HD_B)
    lam = (jnp.exp(jnp.sum(lp['lam_q1'].astype(f32) * lp['lam_k1'].astype(f32)))
           - jnp.exp(jnp.sum(lp['lam_q2'].astype(f32) * lp['lam_k2'].astype(f32))) + lam_init)
    att = attend(q, k, v, lam)
    o_b = (rmsnorm(att, lp['subln_g'], SUBLN_EPS) * (1.0 - lam_init)).reshape(B, T, W_B)
    off2 = RWKV_COLS + DIFF_COLS
    gate_a = jax.nn.sigmoid(P[..., off2:off2 + D_MODEL])
    gate_b = jax.nn.sigmoid(P[..., off2 + D_MODEL:off2 + 2 * D_MODEL])
    m = gate_a * (o_a @ lp['w_br_a']) + gate_b * (o_b @ lp['w_br_b'])
    x = x + m @ lp['w_out']
    hn = rmsnorm(x, lp['g_ffn'], NORM_EPS)
    x = x + jnp.square(jax.nn.relu(hn @ lp['w_up'])) @ lp['w_down']
    return x, S, k.reshape(B, T, H_B, 2 * HD_B), v, xn[:, -1]


def setup_inputs(seed: int = 0) -> dict:
    key = jax.random.key(seed)
    ks = iter(jax.random.split(key, 40))
    f32 = jnp.float32
    L = DEPTH
    n_pages = PAST_LEN // PAGE_SIZE
    n_used = DEC_BATCH * n_pages
    n_pool = n_used + n_used // 4

    def nrm(shape, scale):
        return scale * jax.random.normal(next(ks), shape, f32)

    x_prompt = nrm((BATCH, SEQ, D_MODEL), 1.0)
    x_sample = nrm((DEC_BATCH, DEC_SEQ, D_MODEL), 1.0)
    cache_k = nrm((L, n_pool, PAGE_SIZE, H_B, 2 * HD_B), 1.0)
    cache_v = nrm((L, n_pool, PAGE_SIZE, H_B, 2 * HD_B), 1.0)
    state_rwkv = nrm((L, DEC_BATCH, H_A, HD_A, HD_A), 0.5)
    state_shift = nrm((L, DEC_BATCH, D_MODEL), 1.0)
    page_table = jax.random.permutation(next(ks), n_pool)[:n_used].reshape(DEC_BATCH, n_pages).astype(jnp.int32)
    w_in = nrm((L, D_MODEL, N_COLS), D_MODEL ** -0.5)
    mu_shift = jax.random.uniform(next(ks), (L, RWKV_COLS), f32, 0.0, 1.0)
    w0 = jax.random.uniform(next(ks), (L, W_A), f32, -6.5, -1.5)
    w2 = nrm((L, DECAY_LORA, W_A), 0.1)
    a0 = nrm((L, W_A), 0.1)
    a2 = nrm((L, AAA_LORA, W_A), AAA_LORA ** -0.5)
    g2 = nrm((L, GATE_LORA, W_A), GATE_LORA ** -0.5)
    k_k = 0.85 + nrm((L, W_A), 0.02)
    k_a = 1.0 + nrm((L, W_A), 0.02)
    r_k = nrm((L, H_A, HD_A), 0.1)
    ln_x_w = 1.0 + nrm((L, W_A), 0.05)
    ln_x_b = nrm((L, W_A), 0.02)
    lam_q1 = nrm((L, HD_B), 0.1)
    lam_k1 = nrm((L, HD_B), 0.1)
    lam_q2 = nrm((L, HD_B), 0.1)
    lam_k2 = nrm((L, HD_B), 0.1)
    subln_g = 1.0 + nrm((L, 2 * HD_B), 0.05)
    w_br_a = nrm((L, W_A, D_MODEL), W_A ** -0.5)
    w_br_b = nrm((L, W_B, D_MODEL), W_B ** -0.5)
    w_out = nrm((L, D_MODEL, D_MODEL), D_MODEL ** -0.5)
    g_mix = 1.0 + nrm((L, D_MODEL), 0.05)
    g_ffn = 1.0 + nrm((L, D_MODEL), 0.05)
    w_up = nrm((L, D_MODEL, D_FF), D_MODEL ** -0.5)
    w_down = nrm((L, D_FF, D_MODEL), D_FF ** -0.5)
    g_final = 1.0 + nrm((D_MODEL,), 0.05)
    return {'x_prompt': x_prompt, 'x_sample': x_sample, 'cache_k': cache_k, 'cache_v': cache_v,
            'state_rwkv': state_rwkv, 'state_shift': state_shift, 'page_table': page_table,
            'w_in': w_in, 'mu_shift': mu_shift, 'w0': w0, 'w2': w2, 'a0': a0, 'a2': a2, 'g2': g2,
            'k_k': k_k, 'k_a': k_a, 'r_k': r_k, 'ln_x_w': ln_x_w, 'ln_x_b': ln_x_b,
            'lam_q1': lam_q1, 'lam_k1': lam_k1, 'lam_q2': lam_q2, 'lam_k2': lam_k2, 'subln_g': subln_g,
            'w_br_a': w_br_a, 'w_br_b': w_br_b, 'w_out': w_out, 'g_mix': g_mix, 'g_ffn': g_ffn,
            'w_up': w_up, 'w_down': w_down, 'g_final': g_final}


def reference(x_prompt, x_sample, cache_k, cache_v, state_rwkv, state_shift, page_table,
              w_in, mu_shift, w0, w2, a0, a2, g2, k_k, k_a, r_k, ln_x_w, ln_x_b,
              lam_q1, lam_k1, lam_q2, lam_k2, subln_g, w_br_a, w_br_b, w_out,
              g_mix, g_ffn, w_up, w_down, g_final):
    weights = dict(w_in=w_in, mu_shift=mu_shift, w0=w0, w2=w2, a0=a0, a2=a2, g2=g2,
                   k_k=k_k, k_a=k_a, r_k=r_k, ln_x_w=ln_x_w, ln_x_b=ln_x_b,
                   lam_q1=lam_q1, lam_k1=lam_k1, lam_q2=lam_q2, lam_k2=lam_k2, subln_g=subln_g,
                   w_br_a=w_br_a, w_br_b=w_br_b, w_out=w_out, g_mix=g_mix, g_ffn=g_ffn,
                   w_up=w_up, w_down=w_down)
    slopes = alibi_slopes()
    B = x_prompt.shape[0]
    hp, hs = x_prompt, x_sample
    kps, vps, kss, vss, sps, sss, rps, rss = [], [], [], [], [], [], [], []
    for l in range(DEPTH):
        lp = {name: arr[l] for name, arr in weights.items()}
        lam_init = 0.8 - 0.6 * math.exp(-0.3 * l)
        p_attn = functools.partial(prompt_attend, slopes=slopes)
        s_attn = functools.partial(sample_attend, k_cache=cache_k[l], v_cache=cache_v[l],
                                   page_table=page_table, slopes=slopes)
        hp, Sp, kp, vp, rp = layer_forward(hp, jnp.zeros((B, D_MODEL), x_prompt.dtype),
                                           jnp.zeros((B, H_A, HD_A, HD_A), jnp.float32),
                                           p_attn, lam_init, lp)
        hs, Ss, ksm, vsm, rsm = layer_forward(hs, state_shift[l], state_rwkv[l], s_attn, lam_init, lp)
        kps.append(kp); vps.append(vp); kss.append(ksm); vss.append(vsm)
        sps.append(Sp); sss.append(Ss); rps.append(rp); rss.append(rsm)
    y_prompt = rmsnorm(hp, g_final, NORM_EPS)
    y_sample = rmsnorm(hs, g_final, NORM_EPS)
    return (y_prompt, y_sample, jnp.stack(kps), jnp.stack(vps), jnp.stack(kss), jnp.stack(vss),
            jnp.stack(sps), jnp.stack(sss), jnp.stack(rps), jnp.stack(rss))
```

```python
import numpy as np
import contextlib
import concourse.bass as bass
import concourse.mybir as mybir
from concourse.bass_utils import run_bass_kernel_spmd

F32 = mybir.dt.float32
BF16 = mybir.dt.bfloat16
I32 = mybir.dt.int32
AF = mybir.ActivationFunctionType
ALU = mybir.AluOpType
AX = mybir.AxisListType

NDMA = 12
SAME_ENGINE_SYNC = True

D = 1024
NCOL = 5376
NT_PRIOR = 16
NT_OWN = 16
NTILE = 33
NTOK = NTILE * 128
NQT = 17
NQ = NQT * 128
NPOOL = 2560
EPS = 1e-6


class Buf:
    __slots__ = ("name", "w", "r")

    def __init__(self, name):
        self.name = name
        self.w = {}
        self.r = {}


def _bl(xs):
    return [x.b if hasattr(x, "b") else x for x in xs]


class Tl:
    __slots__ = ("t", "b")

    def __init__(self, t, b):
        self.t = t
        self.b = b


class Prog:
    def __init__(self, nc, stack):
        self.nc = nc
        self.stack = stack
        self.eng = {"pe": nc.tensor, "act": nc.scalar, "dve": nc.vector, "pool": nc.gpsimd, "sp": nc.sync}
        self.sem = {}
        for e in self.eng:
            self.sem[e] = stack.enter_context(nc.semaphore("s_" + e))
        for i in range(NDMA):
            self.sem[("d", i)] = stack.enter_context(nc.semaphore("d%d" % i))
        self.cnt = {e: 0 for e in self.eng}
        self.dcnt = [0] * NDMA
        self.drr = 0
        self.lists = {e: [] for e in self.eng}
        self.waited = {e: {} for e in self.eng}
        self.nbuf = 0

    def buf(self, name=None):
        self.nbuf += 1
        return Buf(name or "b%d" % self.nbuf)

    def _deps(self, e, reads, writes, disjoint=(), xr=()):
        deps = {}
        reads, writes, disjoint, xr = _bl(reads), _bl(writes), _bl(disjoint), _bl(xr)

        def need(k, v):
            if deps.get(k, 0) < v:
                deps[k] = v

        for b in reads:
            for k, v in b.w.items():
                need(k, v)
        for b in writes:
            for k, v in b.w.items():
                need(k, v)
            for k, v in b.r.items():
                need(k, v)
        for b in disjoint:
            for k, v in b.r.items():
                need(k, v)
        for b in xr:
            for k, v in b.w.items():
                need(k, v)
            for k, v in b.r.items():
                need(k, v)
        waits = []
        for k, v in deps.items():
            if k == e and (e == "pe" or not SAME_ENGINE_SYNC):
                continue
            if self.waited[e].get(k, 0) >= v:
                continue
            self.waited[e][k] = v
            waits.append((k, v))
        return waits

    def op(self, e, fn, reads=(), writes=(), signal=True, disjoint=(), xr=()):
        waits = self._deps(e, reads, writes, disjoint, xr)
        reads, writes, disjoint = _bl(reads) + _bl(xr), _bl(writes), _bl(disjoint)
        if signal:
            self.cnt[e] += 1
            tok = (e, self.cnt[e])
            inc = (e, 1)
        else:
            tok = (e, self.cnt[e] + 1)
            inc = None
        for b in reads:
            b.r[tok[0]] = max(b.r.get(tok[0], 0), tok[1])
        for b in writes:
            b.w = {tok[0]: tok[1]}
            b.r = {}
        for b in disjoint:
            b.w[tok[0]] = max(b.w.get(tok[0], 0), tok[1])
        self.lists[e].append((waits, fn, inc))
        return tok

    def dma(self, q, out, in_, reads=(), writes=(), disjoint=(), **kw):
        i = self.drr
        self.drr = (i + 1) % NDMA
        key = ("d", i)
        waits = self._deps(q, reads, writes, disjoint)
        reads, writes, disjoint = _bl(reads), _bl(writes), _bl(disjoint)
        prev = self.dcnt[i]
        if prev > 0 and self.waited[q].get(key, 0) < prev:
            self.waited[q][key] = prev
            waits.append((key, prev))
        self.dcnt[i] += 16
        tok = (key, self.dcnt[i])
        for b in reads:
            b.r[key] = max(b.r.get(key, 0), tok[1])
        for b in writes:
            b.w = {tok[0]: tok[1]}
            b.r = {}
        for b in disjoint:
            b.w[tok[0]] = max(b.w.get(tok[0], 0), tok[1])
        self.lists[q].append((waits, lambda eng: eng.dma_start(out=out, in_=in_, **kw), (key, 16)))
        return tok

    def dma_ind(self, out, in_, idx_ap, reads=(), writes=()):
        q = "pool"
        i = self.drr
        self.drr = (i + 1) % NDMA
        key = ("d", i)
        waits = self._deps(q, reads, writes)
        reads, writes = _bl(reads), _bl(writes)
        prev = self.dcnt[i]
        if prev > 0 and self.waited[q].get(key, 0) < prev:
            self.waited[q][key] = prev
            waits.append((key, prev))
        self.dcnt[i] += 16
        tok = (key, self.dcnt[i])
        for b in reads:
            b.r[key] = max(b.r.get(key, 0), tok[1])
        for b in writes:
            b.w = {tok[0]: tok[1]}
            b.r = {}
        self.lists[q].append((waits, lambda eng: eng.indirect_dma_start(
            out=out, out_offset=None, in_=in_, in_offset=bass.IndirectOffsetOnAxis(ap=idx_ap, axis=0)), (key, 16)))
        return tok

    def final_wait(self, e):
        waits = []
        for i in range(NDMA):
            k, v = ("d", i), self.dcnt[i]
            if v > 0 and self.waited[e].get(k, 0) < v:
                self.waited[e][k] = v
                waits.append((k, v))
        self.lists[e].append((waits, None, None))

    def replay(self, e, eng):
        for waits, fn, inc in self.lists[e]:
            for k, v in waits:
                eng.wait_ge(self.sem[k], v)
            if fn is None:
                continue
            ins = fn(eng)
            if inc is not None:
                ins.then_inc(self.sem[inc[0]], inc[1])
        self.lists[e] = []

    def emit(self, name=None):
        with self.nc.Block(name) as block:
            @block.sync
            def _(eng):
                self.replay("sp", eng)

            @block.scalar
            def _(eng):
                self.replay("act", eng)

            @block.vector
            def _(eng):
                self.replay("dve", eng)

            @block.gpsimd
            def _(eng):
                self.replay("pool", eng)

            @block.tensor
            def _(eng):
                self.replay("pe", eng)


class Rot:
    def __init__(self, P, tiles):
        self.items = [(t, P.buf()) for t in tiles]
        self.i = 0

    def next(self):
        it = self.items[self.i % len(self.items)]
        self.i += 1
        return it


CH_R, CH_K, CH_V, CH_L1, CH_GD, CH_Q, CH_AK, CH_AV, CH_G = 0, 4, 8, 12, 13, 14, 18, 22, 26


def tile_kind(t):
    return "prior" if t < NT_PRIOR else ("own" if t < NT_PRIOR + NT_OWN else "sample")


def phase_A(P, nc, T):
    with contextlib.ExitStack() as st:
        sb = lambda n, s, d: st.enter_context(nc.sbuf_tensor(n, s, d))
        ps = lambda n: st.enter_context(nc.psum_tensor(n, [128, 512], F32))
        win = sb("win", [128, 8, NCOL], BF16)
        b_win = P.buf()
        for kc in range(8):
            for j in range(3):
                P.dma("pool", win[:, kc, j * 1792:(j + 1) * 1792],
                      T["w_in"][kc * 128:(kc + 1) * 128, j * 1792:(j + 1) * 1792], disjoint=[b_win])
        gmix = sb("gmix", [128, D], F32)
        b_gmix = P.buf()
        P.dma("sp", gmix[:], T["g_mix"].broadcast_to([128, D]), writes=[b_gmix])
        mu = sb("mu", [128, 14], F32)
        b_mu = P.buf()
        P.dma("sp", mu[:], T["mu_fm"], writes=[b_mu])
        identb = sb("identb", [128, 128], BF16)
        b_id = P.buf()
        P.op("pool", lambda e: e.memset(identb[:], 0.0), writes=[b_id])
        P.op("pool", lambda e: e.affine_select(out=identb[:], in_=identb[:], pattern=[[-1, 128]],
                                                compare_op=ALU.not_equal, fill=1.0, base=0,
                                                channel_multiplier=1), writes=[b_id])
        carry = sb("carry", [128, 14], F32)
        b_carry = P.buf()
        P.op("pool", lambda e: e.memset(carry[:], 0.0), writes=[b_carry])

        xt_r = Rot(P, [sb("xt%d" % i, [128, D], F32) for i in range(3)])
        xn_r = Rot(P, [sb("xn%d" % i, [128, D], F32) for i in range(2)])
        xnb_r = Rot(P, [sb("xnb%d" % i, [128, D], BF16) for i in range(2)])
        junk = sb("junk", [128, D], BF16)
        b_junk = P.buf()
        ss_r = Rot(P, [sb("ss%d" % i, [128, 4], F32) for i in range(4)])
        xnT_r = Rot(P, [sb("xnT%d" % i, [128, 8, 512], BF16) for i in range(2)])
        pb_r = Rot(P, [sb("pb%d" % i, [128, 520], F32) for i in range(3)])
        prs_r = Rot(P, [sb("prs%d" % i, [128, 512], F32) for i in range(3)])
        dd_r = Rot(P, [sb("dd%d" % i, [128, 512], F32) for i in range(2)])
        stg_r = Rot(P, [sb("stg%d" % i, [128, 512], BF16) for i in range(4)])
        tmf_r = Rot(P, [sb("tmf%d" % i, [128, 512], F32) for i in range(3)])
        tmb_r = Rot(P, [sb("tmb%d" % i, [128, 512], BF16) for i in range(2)])
        ps_fm = Rot(P, [ps("psfm%d" % i) for i in range(4)])
        ps_tm = Rot(P, [ps("pstm%d" % i) for i in range(2)])
        ps_tr = Rot(P, [ps("pstr%d" % i) for i in range(2)])

        ssin = sb("ssin", [16, D], F32)
        ssb = sb("ssb", [16, D], BF16)
        ssT = sb("ssT", [128, 8, 16], BF16)
        b_ssin, b_ssb, b_ssT = P.buf(), P.buf(), P.buf()
        P.dma("sp", ssin[:], T["st_shift"], writes=[b_ssin])
        P.op("dve", lambda e: e.tensor_copy(out=ssb[:], in_=ssin[:]), reads=[b_ssin], writes=[b_ssb])
        pt, b_pt = ps_tr.next()
        ptb = pt[:].bitcast(BF16)
        for kc in range(8):
            P.op("pe", lambda e, kc=kc, ptb=ptb: e.transpose(out=ptb[:, kc * 16:(kc + 1) * 16], in_=ssb[:, kc * 128:(kc + 1) * 128],
                                                    identity=identb[0:16, 0:16]),
                 reads=[b_ssb, b_id], writes=[b_pt] if kc == 0 else (), disjoint=[b_pt] if kc else ())
        P.op("act", lambda e, ptb=ptb: e.copy(out=ssT[:].rearrange("p k b -> p (k b)"), in_=ptb[:, 0:128]), reads=[b_pt], writes=[b_ssT])
        prevb = sb("prevb", [128, 128], F32)
        b_prevb = P.buf()
        pss = sb("pss", [128, 16], F32)
        b_pss = P.buf()

        groups = [(g * 4, 4) for g in range(8)] + [(32, 1)]
        if DEBUG_GROUPS is not None:
            groups = [groups[i] for i in DEBUG_GROUPS]
        for (t0, nt) in groups:
            kind = tile_kind(t0)
            G = nt * 128
            tok0 = t0 * 128
            xnT, b_xnT = xnT_r.next()
            for ti in range(nt):
                t = t0 + ti
                xt, b_xt = xt_r.next()
                P.dma("sp", xt[:], T["x_all"][t * 128:(t + 1) * 128, :], writes=[b_xt])
                ss, b_ss = ss_r.next()
                P.op("act", lambda e, xt=xt, ss=ss: e.activation(out=junk[:], in_=xt[:], func=AF.Square, accum_out=ss[:, 0:1]),
                     reads=[b_xt], writes=[b_junk, b_ss])
                P.op("act", lambda e, ss=ss: e.activation(out=ss[:, 1:2], in_=ss[:, 0:1], func=AF.Sqrt, scale=1.0 / D, bias=T["cst"].t[:, 0:1]),
                     reads=[b_ss, T["cst"]], writes=[b_ss])
                P.op("dve", lambda e, ss=ss: e.reciprocal(out=ss[:, 2:3], in_=ss[:, 1:2]), reads=[b_ss], writes=[b_ss])
                xn, b_xn = xn_r.next()
                P.op("dve", lambda e, xn=xn, xt=xt, ss=ss: e.scalar_tensor_tensor(out=xn[:], in0=xt[:], scalar=ss[:, 2:3], in1=gmix[:],
                                                                                    op0=ALU.mult, op1=ALU.mult),
                     reads=[b_xt, b_ss, b_gmix], writes=[b_xn])
                if t == NT_PRIOR + NT_OWN - 1:
                    P.dma("sp", T["shift_out"][0:1, :], xn[127:128, :], reads=[b_xn])
                if kind == "sample":
                    for b in range(16):
                        P.dma("sp", T["shift_out"][1 + b:2 + b, :], xn[b * 8 + 7:b * 8 + 8, :], reads=[b_xn])
                xnb, b_xnb = xnb_r.next()
                P.op("act", lambda e, xnb=xnb, xn=xn: e.copy(out=xnb[:], in_=xn[:]), reads=[b_xn], writes=[b_xnb])
                pt, b_pt = ps_tr.next()
                ptb = pt[:].bitcast(BF16)
                for kc in range(8):
                    P.op("pe", lambda e, kc=kc, ptb=ptb, xnb=xnb: e.transpose(out=ptb[:, kc * 128:(kc + 1) * 128],
                                                                                in_=xnb[:, kc * 128:(kc + 1) * 128], identity=identb[:]),
                         reads=[b_xnb, b_id], writes=[b_pt] if kc == 0 else (), disjoint=[b_pt] if kc else ())
                P.op("dve", lambda e, ptb=ptb, xnT=xnT, ti=ti: e.tensor_copy(out=xnT[:, :, ti * 128:(ti + 1) * 128],
                                                                               in_=ptb.rearrange("p (k t) -> p k t", k=8)),
                     reads=[b_pt], disjoint=[b_xnT], writes=())
            if kind == "prior" and t0 + nt == NT_PRIOR:
                chunks = list(range(0, 14)) + list(range(CH_AK, CH_AK + 4))
            elif kind == "prior":
                chunks = list(range(CH_K, CH_K + 4)) + list(range(CH_V, CH_V + 4)) + [CH_L1] + list(range(CH_AK, CH_AK + 4))
            else:
                chunks = list(range(0, 14)) + list(range(CH_Q, CH_Q + 8)) + list(range(CH_G, CH_G + 16))
                if DEBUG_CHUNKS is not None:
                    chunks = DEBUG_CHUNKS
            for c in chunks:
                pf, b_pf = ps_fm.next()
                for kc in range(8):
                    P.op("pe", lambda e, pf=pf, kc=kc, c=c, xnT=xnT, G=G: e.matmul(pf[:, 0:G], lhsT=win[:, kc, c * 128:(c + 1) * 128],
                                                                                  rhs=xnT[:, kc, 0:G], start=(kc == 0), stop=(kc == 7)),
                         reads=[b_win, b_xnT], writes=[b_pf] if kc == 0 else (), disjoint=[b_pf] if kc else (), signal=(kc == 7))
                if c < 14:
                    if kind != "sample":
                        pb, b_pb = pb_r.next()
                        P.op("act", lambda e, pb=pb, pf=pf, G=G: e.copy(out=pb[:, 1:1 + G], in_=pf[:, 0:G]), reads=[b_pf], writes=[b_pb])
                        P.op("pool", lambda e, pb=pb, c=c: e.tensor_copy(out=pb[:, 0:1], in_=carry[:, c:c + 1]), reads=[b_carry], disjoint=[b_pb])
                        P.op("pool", lambda e, pb=pb, c=c, G=G: e.tensor_copy(out=carry[:, c:c + 1], in_=pb[:, G:G + 1]), reads=[b_pb], disjoint=[b_carry])
                        cur = pb[:, 1:1 + G]
                        prev = pb[:, 0:G]
                        rd = [b_pb]
                    else:
                        pb, b_pb = pb_r.next()
                        P.op("act", lambda e, pb=pb, pf=pf, G=G: e.copy(out=pb[:, 0:G], in_=pf[:, 0:G]), reads=[b_pf], writes=[b_pb])
                        pf2, b_pf2 = ps_fm.next()
                        for kc in range(8):
                            P.op("pe", lambda e, pf2=pf2, kc=kc, c=c: e.matmul(pf2[:, 0:16], lhsT=win[:, kc, c * 128:(c + 1) * 128],
                                                                                rhs=ssT[:, kc, :], start=(kc == 0), stop=(kc == 7)),
                                 reads=[b_win, b_ssT], writes=[b_pf2] if kc == 0 else (), disjoint=[b_pf2] if kc else (), signal=(kc == 7))
                        P.op("dve", lambda e, pb=pb: e.tensor_copy(out=prevb[:].rearrange("p (b t) -> p b t", t=8)[:, :, 1:8],
                                                                    in_=pb[:, 0:128].rearrange("p (b t) -> p b t", t=8)[:, :, 0:7]),
                             reads=[b_pb], writes=[b_prevb])
                        P.op("dve", lambda e, pf2=pf2: e.tensor_copy(out=prevb[:].rearrange("p (b t) -> p b t", t=8)[:, :, 0:1],
                                                                      in_=pf2[:, 0:16].rearrange("p (b o) -> p b o", o=1)),
                             reads=[b_pf2], disjoint=[b_prevb])
                        cur = pb[:, 0:G]
                        prev = prevb[:, 0:G]
                        rd = [b_pb, b_prevb]
                    dd, b_dd = dd_r.next()
                    P.op("dve", lambda e, dd=dd, prev=prev, cur=cur, G=G: e.tensor_tensor(out=dd[:, 0:G], in0=prev, in1=cur, op=ALU.subtract),
                         reads=rd, writes=[b_dd])
                    prs, b_prs = prs_r.next()
                    P.op("dve", lambda e, prs=prs, dd=dd, cur=cur, c=c, G=G: e.scalar_tensor_tensor(out=prs[:, 0:G], in0=dd[:, 0:G], scalar=mu[:, c:c + 1],
                                                                                                    in1=cur, op0=ALU.mult, op1=ALU.add),
                         reads=rd + [b_dd, b_mu], writes=[b_prs])
                    P.dma("sp", T["prT_d"][c][:, tok0:tok0 + G], prs[:, 0:G], reads=[b_prs], disjoint=[T["b_prT"]])
                elif c < CH_G:
                    stg, b_stg = stg_r.next()
                    P.op("act", lambda e, stg=stg, pf=pf, G=G: e.copy(out=stg[:, 0:G], in_=pf[:, 0:G]), reads=[b_pf], writes=[b_stg])
                    if c < CH_AK:
                        q0 = tok0 - NT_PRIOR * 128
                        P.dma("sp", T["qT_d"][c - CH_Q][:, q0:q0 + G], stg[:, 0:G], reads=[b_stg], disjoint=[T["b_qT"]])
                    else:
                        P.dma("sp", T["kT_d"][c - CH_AK][:, tok0:tok0 + G], stg[:, 0:G], reads=[b_stg], disjoint=[T["b_kT"]])
                else:
                    stg, b_stg = stg_r.next()
                    P.op("act", lambda e, stg=stg, pf=pf, G=G: e.activation(out=stg[:, 0:G], in_=pf[:, 0:G], func=AF.Sigmoid),
                         reads=[b_pf], writes=[b_stg])
                    q0 = tok0 - NT_PRIOR * 128
                    P.dma("sp", T["gates_d"][c - CH_G][:, q0:q0 + G], stg[:, 0:G], reads=[b_stg], disjoint=[T["b_gates"]])
            for ti in range(nt if not DEBUG_NOTM else 0):
                t = t0 + ti
                halves = [("v", CH_AV * 128)] + ([] if kind == "prior" else [("k", CH_AK * 128)])
                if DEBUG_HALF is not None:
                    halves = [h for h in halves if h[0] in DEBUG_HALF]
                for (nm, c0) in halves:
                    pm, b_pm = ps_tm.next()
                    for kc in range(8):
                        P.op("pe", lambda e, pm=pm, kc=kc, c0=c0, xnT=xnT, ti=ti: e.matmul(pm[:, :], lhsT=xnT[:, kc, ti * 128:(ti + 1) * 128],
                                                                                         rhs=win[:, kc, c0:c0 + 512], start=(kc == 0), stop=(kc == 7)),
                             reads=[b_win, b_xnT], writes=[b_pm] if kc == 0 else (), disjoint=[b_pm] if kc else (), signal=(kc == 7))
                    tmf, b_tmf = tmf_r.next()
                    P.op("act", lambda e, tmf=tmf, pm=pm: e.copy(out=tmf[:], in_=pm[:]), reads=[b_pm], writes=[b_tmf])
                    if kind != "prior":
                        q0 = (t - NT_PRIOR) * 128
                        if not DEBUG_NOOUT:
                            P.dma("sp", T[nm + "out"][q0:q0 + 128, :], tmf[:], reads=[b_tmf])
                    if nm == "v":
                        tmb, b_tmb = tmb_r.next()
                        P.op("pool", lambda e, tmb=tmb, tmf=tmf: e.tensor_copy(out=tmb[:], in_=tmf[:]), reads=[b_tmf], writes=[b_tmb])
                        P.dma("sp", T["v_d"][t * 128:(t + 1) * 128, :], tmb[:], reads=[b_tmb], disjoint=[T["b_v"]])
        P.emit("phaseA")


def phase_B1(P, nc, T):
    V, A, PO, PE = "dve", "act", "pool", "pe"
    with contextlib.ExitStack() as st:
        def mk(n, s, d):
            return Tl(st.enter_context(nc.sbuf_tensor("b1_" + n, s, d)), P.buf(n))

        def mkps(n, cols=512):
            return Tl(st.enter_context(nc.psum_tensor("b1_" + n, [128, cols], F32)), P.buf(n))

        def op(e, fn, r=(), w=(), **kw):
            return P.op(e, fn, reads=r, writes=w, **kw)

        cst = T["cst"]
        identb = mk("identb", [128, 128], BF16)
        identf = mk("identf", [128, 128], F32)
        P.dma("pool", identb.t[:], T["ident"], writes=[identb])
        P.dma("sp", identf.t[:], T["ident"], writes=[identf])
        MTc = {"p": mk("MTc_p", [128, 256], BF16), "s": mk("MTc_s", [128, 256], BF16)}
        Ms = {"p": mk("Ms_p", [128, 128], BF16), "s": mk("Ms_s", [128, 128], BF16)}
        RM = {"p": mk("RM_p", [128, 512], F32), "s": mk("RM_s", [128, 512], F32)}
        for k in "ps":
            P.dma("pool", MTc[k].t[:], T["mtc_" + k], writes=[MTc[k]])
            P.dma("pool", Ms[k].t[:], T["ms_" + k], writes=[Ms[k]])
            P.dma("sp", RM[k].t[:], T["rm_" + k], writes=[RM[k]])
        bones = mk("bones", [128, 128], BF16)
        P.dma("pool", bones.t[:], T["bones"], writes=[bones])
        sel8 = mk("sel8", [128, 4, 8], BF16)
        P.dma("pool", sel8.t[:].rearrange("p a b -> p (a b)"), T["sel8"], writes=[sel8])
        segsel = mk("segsel", [128, 16], F32)
        P.dma("sp", segsel.t[:], T["segsel"], writes=[segsel])
        LW = mk("LW", [128, 512], BF16)
        P.dma("pool", LW.t[0:64, :], T["w2"], disjoint=[LW])
        P.dma("pool", LW.t[64:128, :], T["a2"], disjoint=[LW])
        G2 = mk("G2", [128, 512], BF16)
        P.dma("pool", G2.t[:], T["g2"], writes=[G2])
        fm = {}
        for n in ["w0", "a0", "kk", "ka", "rk"]:
            fm[n] = mk("fm_" + n, [128, 4], F32)
            P.dma("sp", fm[n].t[:], T[n + "_fm"], writes=[fm[n]])
        lnw = mk("lnw", [128, 512], F32)
        lnb = mk("lnb", [128, 512], F32)
        P.dma("sp", lnw.t[:], T["ln_w"].broadcast_to([128, 512]), writes=[lnw])
        P.dma("sp", lnb.t[:], T["ln_b"].broadcast_to([128, 512]), writes=[lnb])

        def bc4(x):
            return x.t[:, :].unsqueeze(2).broadcast_to([128, 4, 128])

        PRr = Rot(P, [st.enter_context(nc.sbuf_tensor("b1_PR%d" % i, [128, 14, 128], F32)) for i in range(2)])
        LB = mk("LB", [128, 128], BF16)
        GS = mk("GS", [128, 128], BF16)
        f4 = lambda n: mk(n, [128, 4, 128], F32)
        b4 = lambda n: mk(n, [128, 4, 128], BF16)
        XW, LD, ASIG, CUM, CUMp, EW, EWi, EWp = [f4(n) for n in ["XW", "LD", "ASIG", "CUM", "CUMp", "EW", "EWi", "EWp"]]
        KK, SQ, RNr, KKN, T1, KM, TMPB, XA = [f4(n) for n in ["KK", "SQ", "RNr", "KKN", "T1", "KM", "TMPB", "XA"]]
        KK2, BT, KT, VB, PRD = [b4(n) for n in ["KK2", "BT", "KT", "VB", "PRD"]]
        ART = mk("ART", [128, 4, 2, 128], BF16)
        Atm, Btm, Ktm, Vtm = [mk(n, [128, 512], BF16) for n in ["Atm", "Btm", "Ktm", "Vtm"]]
        COEF = mk("COEF", [128, 8], F32)
        Gtm = mk("Gtm", [128, 512], F32)
        NAXr = Rot(P, [st.enter_context(nc.sbuf_tensor("b1_NAX%d" % i, [128, 256], BF16)) for i in range(2)])
        MAr = Rot(P, [st.enter_context(nc.sbuf_tensor("b1_MA%d" % i, [128, 256], BF16)) for i in range(2)])
        PXr = Rot(P, [st.enter_context(nc.sbuf_tensor("b1_PX%d" % i, [128, 256], BF16)) for i in range(3)])
        Pr_ = Rot(P, [st.enter_context(nc.sbuf_tensor("b1_Pk%d" % i, [128, 128], BF16)) for i in range(3)])
        TTr = Rot(P, [st.enter_context(nc.sbuf_tensor("b1_TT%d" % i, [128, 128], BF16)) for i in range(2)])
        MVr = Rot(P, [st.enter_context(nc.sbuf_tensor("b1_MV%d" % i, [128, 64], BF16)) for i in range(2)])
        UVr = Rot(P, [st.enter_context(nc.sbuf_tensor("b1_UV%d" % i, [128, 64], F32)) for i in range(2)])
        Usbr = Rot(P, [st.enter_context(nc.sbuf_tensor("b1_Usb%d" % i, [128, 64], BF16)) for i in range(2)])
        AhT = st.enter_context(nc.sbuf_tensor("b1_AhT", [128, 4, 128], BF16))
        b_AhT = [P.buf() for _ in range(8)]
        S0 = st.enter_context(nc.sbuf_tensor("b1_S0", [128, 4, 64], F32))
        S0b = st.enter_context(nc.sbuf_tensor("b1_S0b", [128, 4, 64], BF16))
        S0W = st.enter_context(nc.sbuf_tensor("b1_S0W", [128, 4, 64], F32))
        b_S = [P.buf() for _ in range(8)]
        b_Sb = [P.buf() for _ in range(8)]
        b_SW = P.buf()
        op(PO, lambda e: e.memset(S0[:], 0.0), w=b_S)
        op(PO, lambda e: e.memset(S0b[:], 0.0), w=b_Sb)
        Ytm = mk("Ytm", [128, 512], F32)
        O1, O2, O3 = [mk(n, [128, 512], F32) for n in ["O1", "O2", "O3"]]
        ST8 = mk("ST8", [128, 32], F32)
        OAb = mk("OAb", [128, 512], BF16)
        OAT = mk("OAT", [128, 4, 128], BF16)
        S0s = st.enter_context(nc.sbuf_tensor("b1_S0s", [128, 4, 16, 64], F32))
        S0sb = st.enter_context(nc.sbuf_tensor("b1_S0sb", [128, 4, 16, 64], BF16))
        b_S0s = [P.buf() for _ in range(4)]
        b_S0sb = [P.buf() for _ in range(4)]
        Zin = mk("Zin", [64, 16, 128], F32)
        TMPs = mk("TMPs", [128, 16, 64], F32)
        Y1 = mk("Y1", [128, 64], F32)
        U1 = mk("U1", [128, 64], F32)
        UBLK = mk("UBLK", [128, 16, 64], BF16)
        VBLK = mk("VBLK", [128, 16, 64], BF16)
        SOUT = mk("SOUT", [64, 16, 128], F32)
        psr = Rot(P, [st.enter_context(nc.psum_tensor("b1_psr%d" % i, [128, 512], F32)) for i in range(4)])
        ps2 = Rot(P, [st.enter_context(nc.psum_tensor("b1_ps2_%d" % i, [128, 1024], F32)) for i in range(1)])
        psY = mkps("psY")
        psG = mkps("psG")

        def nextps():
            t, b = psr.next()
            return Tl(t, b)

        for hp in range(4):
            for h2 in range(2):
                P.dma("sp", Zin.t[:, :, h2 * 64:(h2 + 1) * 64],
                      T["st_rwkv"][:, 2 * hp + h2, :, :].rearrange("b i j -> i b j"), writes=[Zin] if h2 == 0 else (), disjoint=[Zin] if h2 else ())
            for g in range(2):
                pt = nextps()
                for k in range(8):
                    b = g * 8 + k
                    op(PE, lambda e, pt=pt, k=k, b=b: e.transpose(out=pt.t[:, k * 64:(k + 1) * 64], in_=Zin.t[:, b, :], identity=identf.t[0:64, 0:64]),
                       r=[Zin, identf], w=[pt] if k == 0 else (), disjoint=[pt] if k else ())
                op(A, lambda e, pt=pt, hp=hp, g=g: e.copy(out=S0s[:, hp, g * 8:(g + 1) * 8, :], in_=pt.t[:, 0:512].rearrange("p (b i) -> p b i", i=64)),
                   xr=[pt], disjoint=[b_S0s[hp]])
            op(PO, lambda e, hp=hp: e.tensor_copy(out=S0sb[:, hp, :, :], in_=S0s[:, hp, :, :]), r=[b_S0s[hp]], w=[b_S0sb[hp]])

        def rwkv_tile(t):
            kind = tile_kind(t)
            mk_ = "s" if kind == "sample" else "p"
            full = kind != "prior"
            tok0 = t * 128
            q0 = tok0 - NT_PRIOR * 128
            PRt, b_PR = PRr.next()
            PR = Tl(PRt, b_PR)
            c0 = 0 if full else 4
            c1 = 14 if full else 13
            P.dma("sp", PRt[:, c0:c1, :], T["prT_d"].rearrange("c p t -> p c t")[:, c0:c1, tok0:tok0 + 128],
                  reads=[T["b_prT"]], writes=[PR])
            Rv, Kv, Vv = PRt[:, 0:4, :], PRt[:, 4:8, :], PRt[:, 8:12, :]
            op(A, lambda e: e.activation(out=LB.t[0:64, :], in_=PRt[0:64, 12, :], func=AF.Tanh), r=[PR], w=[LB])
            op(A, lambda e: e.copy(out=LB.t[64:128, :], in_=PRt[64:128, 12, :]), r=[PR], disjoint=[LB])
            pW = nextps()
            pA = nextps()
            for hp in range(4):
                op(PE, lambda e, hp=hp: e.matmul(pW.t[:, hp * 128:(hp + 1) * 128], lhsT=LW.t[0:64, hp * 128:(hp + 1) * 128], rhs=LB.t[0:64, :], start=True, stop=True),
                   r=[LW, LB], w=[pW] if hp == 0 else (), disjoint=[pW] if hp else ())
            for hp in range(4):
                op(PE, lambda e, hp=hp: e.matmul(pA.t[:, hp * 128:(hp + 1) * 128], lhsT=LW.t[64:128, hp * 128:(hp + 1) * 128], rhs=LB.t[64:128, :], start=True, stop=True),
                   r=[LW, LB], w=[pA] if hp == 0 else (), disjoint=[pA] if hp else ())
            v4 = lambda x: x.t[:, 0:512].rearrange("p (a b) -> p a b", a=4) if x.t.shape[-1] == 512 else x.t[:]
            op(V, lambda e: e.tensor_tensor(out=XW.t[:], in0=v4(pW), in1=bc4(fm["w0"]), op=ALU.add), xr=[pW], r=[fm["w0"]], w=[XW])
            op(A, lambda e: e.activation(out=XW.t[:], in_=XW.t[:], func=AF.Sigmoid), w=[XW])
            op(PO, lambda e: e.tensor_scalar(out=LD.t[:], in0=XW.t[:], scalar1=-0.6065306597126334, scalar2=None, op0=ALU.mult), r=[XW], w=[LD])
            op(V, lambda e: e.tensor_tensor(out=XA.t[:], in0=v4(pA), in1=bc4(fm["a0"]), op=ALU.add), xr=[pA], r=[fm["a0"]], w=[XA])
            op(A, lambda e: e.activation(out=ASIG.t[:], in_=XA.t[:], func=AF.Sigmoid), r=[XA], w=[ASIG])
            fl = lambda x: x.t[:].rearrange("p a b -> p (a b)")
            op(V, lambda e: e.tensor_tensor_scan(out=fl(CUM), data0=RM[mk_].t[:], data1=fl(LD), initial=0.0, op0=ALU.mult, op1=ALU.add),
               r=[RM[mk_], LD], w=[CUM])
            op(PO, lambda e: e.tensor_tensor(out=CUMp.t[:], in0=CUM.t[:], in1=LD.t[:], op=ALU.subtract), r=[CUM, LD], w=[CUMp])
            op(A, lambda e: e.activation(out=EW.t[:], in_=CUM.t[:], func=AF.Exp), r=[CUM], w=[EW])
            op(A, lambda e: e.activation(out=EWi.t[:], in_=CUM.t[:], func=AF.Exp, scale=-1.0), r=[CUM], w=[EWi])
            op(A, lambda e: e.activation(out=EWp.t[:], in_=CUMp.t[:], func=AF.Exp), r=[CUMp], w=[EWp])
            op(PO, lambda e: e.tensor_tensor(out=KK.t[:], in0=Kv, in1=bc4(fm["kk"]), op=ALU.mult), r=[PR, fm["kk"]], w=[KK])
            op(PO, lambda e: e.tensor_tensor(out=KK2.t[:], in0=KK.t[:], in1=KK.t[:], op=ALU.mult), r=[KK], w=[KK2])
            pN = nextps()
            op(PE, lambda e: e.matmul(pN.t[:, 0:512], lhsT=bones.t[:], rhs=fl(KK2), start=True, stop=True), r=[bones, KK2], w=[pN])
            op(A, lambda e: e.activation(out=SQ.t[:], in_=v4(pN), func=AF.Sqrt, bias=cst.t[:, 1:2]), xr=[pN], r=[cst], w=[SQ])
            op(V, lambda e: e.reciprocal(out=RNr.t[:], in_=SQ.t[:]), r=[SQ], w=[RNr])
            op(PO, lambda e: e.tensor_tensor(out=KKN.t[:], in0=KK.t[:], in1=RNr.t[:], op=ALU.mult), r=[KK, RNr], w=[KKN])
            op(V, lambda e: e.scalar_tensor_tensor(out=T1.t[:], in0=ASIG.t[:], scalar=-1.0, in1=bc4(fm["ka"]), op0=ALU.add, op1=ALU.mult),
               r=[ASIG, fm["ka"]], w=[T1])
            op(V, lambda e: e.scalar_tensor_tensor(out=KM.t[:], in0=T1.t[:], scalar=1.0, in1=Kv, op0=ALU.add, op1=ALU.mult), r=[T1, PR], w=[KM])
            op(V, lambda e: e.scalar_tensor_tensor(out=ART.t[:, :, 1, :], in0=KKN.t[:], scalar=-1.0, in1=EWp.t[:], op0=ALU.mult, op1=ALU.mult),
               r=[KKN, EWp], disjoint=[ART])
            if full:
                op(PO, lambda e: e.tensor_tensor(out=ART.t[:, :, 0, :], in0=Rv, in1=EW.t[:], op=ALU.mult), r=[PR, EW], disjoint=[ART])
            op(PO, lambda e: e.tensor_tensor(out=TMPB.t[:], in0=KKN.t[:], in1=ASIG.t[:], op=ALU.mult), r=[KKN, ASIG], w=[TMPB])
            op(PO, lambda e: e.tensor_tensor(out=BT.t[:], in0=TMPB.t[:], in1=EWi.t[:], op=ALU.mult), r=[TMPB, EWi], w=[BT])
            op(V, lambda e: e.tensor_tensor(out=KT.t[:], in0=KM.t[:], in1=EWi.t[:], op=ALU.mult), r=[KM, EWi], w=[KT])
            op(A, lambda e: e.copy(out=VB.t[:], in_=Vv), r=[PR], w=[VB])
            for (src_fn, dst, rd) in [(lambda hp: ART.t[:, hp, 1, :], Atm, ART), (lambda hp: BT.t[:, hp, :], Btm, BT),
                                      (lambda hp: KT.t[:, hp, :], Ktm, KT), (lambda hp: VB.t[:, hp, :], Vtm, VB)]:
                pt = nextps()
                ptb = pt.t[:].bitcast(BF16)
                for hp in range(4):
                    op(PE, lambda e, hp=hp, ptb=ptb, src_fn=src_fn: e.transpose(out=ptb[:, hp * 128:(hp + 1) * 128], in_=src_fn(hp), identity=identb.t[:]),
                       r=[rd, identb], w=[pt] if hp == 0 else (), disjoint=[pt] if hp else ())
                op(A, lambda e, ptb=ptb, dst=dst: e.copy(out=dst.t[:], in_=ptb[:, 0:512]), xr=[pt], w=[dst])
            if full:
                op(A, lambda e: e.activation(out=GS.t[:], in_=PRt[:, 13, :], func=AF.Sigmoid), r=[PR], w=[GS])
                op(PE, lambda e: e.matmul(psG.t[:, 0:512], lhsT=GS.t[:], rhs=G2.t[:], start=True, stop=True), r=[GS, G2], w=[psG])
                op(A, lambda e: e.copy(out=Gtm.t[:], in_=psG.t[:, 0:512]), xr=[psG], w=[Gtm])
                op(PO, lambda e: e.tensor_tensor(out=TMPB.t[:], in0=Rv, in1=bc4(fm["rk"]), op=ALU.mult), r=[PR, fm["rk"]], w=[TMPB])
                op(PO, lambda e: e.tensor_tensor(out=PRD.t[:], in0=TMPB.t[:], in1=KM.t[:], op=ALU.mult), r=[TMPB, KM], w=[PRD])
                pC = nextps()
                for hp in range(4):
                    op(PE, lambda e, hp=hp: e.matmul(pC.t[:, 0:8], lhsT=PRD.t[:, hp, :], rhs=sel8.t[:, hp, :], start=(hp == 0), stop=(hp == 3)),
                       r=[PRD, sel8], w=[pC] if hp == 0 else (), disjoint=[pC] if hp else (), signal=(hp == 3))
                op(A, lambda e: e.copy(out=COEF.t[:], in_=pC.t[:, 0:8]), xr=[pC], w=[COEF])
            if kind != "sample":
                op(PO, lambda e: e.tensor_tensor(out=S0W[:], in0=S0[:], in1=EW.t[:, :, 127:128].broadcast_to([128, 4, 64]), op=ALU.mult),
                   r=b_S + [EW], w=[b_SW])
            nsq = 2 if kind == "sample" else 6
            for h in range(8):
                hp, h2 = divmod(h, 2)
                base = 64 * h2
                hc = slice(h * 64, (h + 1) * 64)
                NAXt, b_NAX = NAXr.next()
                MAt, b_MA = MAr.next()
                p1 = nextps()
                op(PE, lambda e, p1=p1, hp=hp, base=base: e.matmul(p1.t[:, 0:256], lhsT=BT.t[base:base + 64, hp, :],
                                                                  rhs=ART.t[base:base + 64, hp, :, :].rearrange("p a b -> p (a b)"), start=True, stop=True),
                   r=[BT, ART], w=[p1])
                op(V, lambda e, p1=p1, NAXt=NAXt: e.tensor_tensor(out=NAXt[:], in0=p1.t[:, 0:256], in1=MTc[mk_].t[:], op=ALU.mult),
                   xr=[p1], r=[MTc[mk_]], w=[b_NAX])
                p2 = nextps()
                op(PE, lambda e, p2=p2, hp=hp, base=base: e.matmul(p2.t[:, 0:256], lhsT=KT.t[base:base + 64, hp, :],
                                                                  rhs=ART.t[base:base + 64, hp, :, :].rearrange("p a b -> p (a b)"), start=True, stop=True),
                   r=[KT, ART], w=[p2])
                op(V, lambda e, p2=p2, MAt=MAt: e.tensor_tensor(out=MAt[:], in0=p2.t[:, 0:256], in1=MTc[mk_].t[:], op=ALU.mult),
                   xr=[p2], r=[MTc[mk_]], w=[b_MA])
                p3 = nextps()
                op(PE, lambda e, p3=p3, hp=hp, base=base: e.matmul(p3.t[:, 0:128], lhsT=ART.t[base:base + 64, hp, 1, :], rhs=BT.t[base:base + 64, hp, :], start=True, stop=True),
                   r=[ART, BT], w=[p3])
                Pk, b_Pk = Pr_.next()
                op(V, lambda e, p3=p3, Pk=Pk: e.tensor_tensor(out=Pk[:], in0=p3.t[:, 0:128], in1=Ms[mk_].t[:], op=ALU.mult), xr=[p3], r=[Ms[mk_]], w=[b_Pk])
                PX, b_PX = PXr.next()
                op(PO, lambda e, PX=PX, NAXt=NAXt: e.tensor_copy(out=PX[:, 0:128], in_=NAXt[:, 128:256]), r=[b_NAX], w=[b_PX])
                op(PO, lambda e, PX=PX, NAXt=NAXt: e.tensor_tensor(out=PX[:, 128:256], in0=NAXt[:, 128:256], in1=identb.t[:], op=ALU.add),
                   r=[b_NAX, identb], disjoint=[b_PX])
                for k in range(nsq + 1):
                    last = (k == nsq)
                    if not last:
                        PXn, b_PXn = PXr.next()
                        Pkn, b_Pkn = Pr_.next()
                        pd1 = nextps()
                        op(PE, lambda e, pd1=pd1, Pk=Pk, PX=PX: e.matmul(pd1.t[:, 0:128], lhsT=Pk[:], rhs=PX[:, 0:128], start=True, stop=True),
                           r=[b_Pk, b_PX], w=[pd1])
                        op(A, lambda e, pd1=pd1, PXn=PXn: e.copy(out=PXn[:, 0:128], in_=pd1.t[:, 0:128]), xr=[pd1], w=[b_PXn])
                        pe1 = nextps()
                        op(PE, lambda e, pe1=pe1, Pk=Pk, PX=PX: e.matmul(pe1.t[:, 0:128], lhsT=PX[:, 0:128], rhs=Pk[:], start=True, stop=True),
                           r=[b_Pk, b_PX], w=[pe1])
                        op(A, lambda e, pe1=pe1, Pkn=Pkn: e.copy(out=Pkn[:], in_=pe1.t[:, 0:128]), xr=[pe1], w=[b_Pkn])
                    if k >= 1:
                        pd2 = nextps()
                        op(PE, lambda e, pd2=pd2, Pk=Pk, PX=PX: e.matmul(pd2.t[:, 0:128], lhsT=Pk[:], rhs=PX[:, 128:256], start=True, stop=True),
                           r=[b_Pk, b_PX], w=[pd2])
                        if last:
                            TTt, b_TT = TTr.next()
                            op(V, lambda e, pd2=pd2, PX=PX, TTt=TTt: e.tensor_tensor(out=TTt[:], in0=pd2.t[:, 0:128], in1=PX[:, 128:256], op=ALU.add),
                               xr=[pd2], r=[b_PX], w=[b_TT])
                        else:
                            op(V, lambda e, pd2=pd2, PX=PX, PXn=PXn: e.tensor_tensor(out=PXn[:, 128:256], in0=pd2.t[:, 0:128], in1=PX[:, 128:256], op=ALU.add),
                               xr=[pd2], r=[b_PX], disjoint=[b_PXn])
                    elif not last:
                        op(PO, lambda e, PX=PX, PXn=PXn: e.tensor_copy(out=PXn[:, 128:256], in_=PX[:, 128:256]), r=[b_PX], disjoint=[b_PXn])
                    if not last:
                        PX, b_PX, Pk, b_Pk = PXn, b_PXn, Pkn, b_Pkn
                MVt, b_MV = MVr.next()
                UVt, b_UV = UVr.next()
                pm = nextps()
                op(PE, lambda e, pm=pm, MAt=MAt, hc=hc: e.matmul(pm.t[:, 0:64], lhsT=MAt[:, 128:256], rhs=Vtm.t[:, hc], start=True, stop=True),
                   r=[b_MA, Vtm], w=[pm])
                op(A, lambda e, pm=pm, MVt=MVt: e.copy(out=MVt[:], in_=pm.t[:, 0:64]), xr=[pm], w=[b_MV])
                pu = nextps()
                op(PE, lambda e, pu=pu, TTt=TTt, MVt=MVt: e.matmul(pu.t[:, 0:64], lhsT=TTt[:], rhs=MVt[:], start=True, stop=True), r=[b_TT, b_MV], w=[pu])
                op(A, lambda e, pu=pu, UVt=UVt: e.copy(out=UVt[:], in_=pu.t[:, 0:64]), xr=[pu], w=[b_UV])
                pa = nextps()
                op(PE, lambda e, pa=pa, TTt=TTt, hc=hc, base=base: e.matmul(pa.t[base:base + 64, 0:128], lhsT=Atm.t[:, hc], rhs=TTt[:], start=True, stop=True),
                   r=[Atm, b_TT], w=[pa])
                op(A, lambda e, pa=pa, hp=hp, base=base: e.copy(out=AhT[base:base + 64, hp, :], in_=pa.t[base:base + 64, 0:128]), xr=[pa], w=[b_AhT[h]])
                Usb, b_Usb = Usbr.next()
                if kind != "sample":
                    pU = nextps()
                    op(PE, lambda e, pU=pU, hp=hp, base=base: e.matmul(pU.t[:, 0:64], lhsT=AhT[base:base + 64, hp, :], rhs=S0b[base:base + 64, hp, :], start=True, stop=True),
                       r=[b_AhT[h], b_Sb[h]], w=[pU])
                    op(V, lambda e, pU=pU, Usb=Usb, UVt=UVt: e.tensor_tensor(out=Usb[:], in0=pU.t[:, 0:64], in1=UVt[:], op=ALU.add), xr=[pU], r=[b_UV], w=[b_Usb])
                    if full:
                        first = (h == 0)
                        op(PE, lambda e, hp=hp, base=base, hc=hc: e.matmul(psY.t[:, hc], lhsT=ART.t[base:base + 64, hp, 0, :], rhs=S0b[base:base + 64, hp, :], start=True, stop=False),
                           r=[ART, b_Sb[h]], w=[psY] if first else (), disjoint=() if first else [psY], signal=False)
                        op(PE, lambda e, hc=hc, NAXt=NAXt, Usb=Usb: e.matmul(psY.t[:, hc], lhsT=NAXt[:, 0:128], rhs=Usb[:], start=False, stop=False),
                           r=[b_NAX, b_Usb], disjoint=[psY], signal=False)
                        op(PE, lambda e, hc=hc, MAt=MAt: e.matmul(psY.t[:, hc], lhsT=MAt[:, 0:128], rhs=Vtm.t[:, hc], start=False, stop=True),
                           r=[b_MA, Vtm], disjoint=[psY])
                    pS = nextps()
                    op(PE, lambda e, pS=pS, hc=hc, base=base, Usb=Usb: e.matmul(pS.t[base:base + 64, 0:64], lhsT=Btm.t[:, hc], rhs=Usb[:], start=True, stop=False),
                       r=[Btm, b_Usb], w=[pS], signal=False)
                    op(PE, lambda e, pS=pS, hc=hc, base=base: e.matmul(pS.t[base:base + 64, 0:64], lhsT=Ktm.t[:, hc], rhs=Vtm.t[:, hc], start=False, stop=True),
                       r=[Ktm, Vtm], disjoint=[pS])
                    op(V, lambda e, pS=pS, hp=hp, base=base: e.scalar_tensor_tensor(out=S0[base:base + 64, hp, :], in0=pS.t[base:base + 64, 0:64],
                                                                                     scalar=EW.t[base:base + 64, hp, 127:128], in1=S0W[base:base + 64, hp, :],
                                                                                     op0=ALU.mult, op1=ALU.add),
                       xr=[pS], r=[EW, b_SW], w=[b_S[h]])
                    op(A, lambda e, hp=hp, base=base: e.copy(out=S0b[base:base + 64, hp, :], in_=S0[base:base + 64, hp, :]), r=[b_S[h]], w=[b_Sb[h]])
                else:
                    pq, b_pq = ps2.next()
                    for half in range(2):
                        op(PE, lambda e, half=half, hp=hp, base=base, pq=pq: e.matmul(pq[:, half * 512:(half + 1) * 512], lhsT=AhT[base:base + 64, hp, :],
                                                                                        rhs=S0sb[base:base + 64, hp, half * 8:(half + 1) * 8, :].rearrange("p b i -> p (b i)"),
                                                                                        start=True, stop=True),
                           r=[b_AhT[h], b_S0sb[hp]], w=[b_pq] if half == 0 else (), disjoint=[b_pq] if half else ())
                    for half in range(2):
                        op(V, lambda e, half=half, pq=pq: e.tensor_tensor(out=TMPs.t[:, half * 8:(half + 1) * 8, :],
                                                                            in0=pq[:, half * 512:(half + 1) * 512].rearrange("p (b i) -> p b i", i=64),
                                                                            in1=segsel.t[:, half * 8:(half + 1) * 8].unsqueeze(2).broadcast_to([128, 8, 64]), op=ALU.mult),
                           xr=[b_pq], r=[segsel], w=[TMPs] if half == 0 else (), disjoint=[TMPs] if half else ())
                    op(V, lambda e: e.tensor_reduce(out=U1.t[:], in_=TMPs.t[:].rearrange("p b i -> p i b"), axis=AX.X, op=ALU.add), r=[TMPs], w=[U1])
                    op(V, lambda e, Usb=Usb, UVt=UVt: e.tensor_tensor(out=Usb[:], in0=U1.t[:], in1=UVt[:], op=ALU.add), r=[U1, b_UV], w=[b_Usb])
                    pq, b_pq = ps2.next()
                    for half in range(2):
                        op(PE, lambda e, half=half, hp=hp, base=base, pq=pq: e.matmul(pq[:, half * 512:(half + 1) * 512], lhsT=ART.t[base:base + 64, hp, 0, :],
                                                                                        rhs=S0sb[base:base + 64, hp, half * 8:(half + 1) * 8, :].rearrange("p b i -> p (b i)"),
                                                                                        start=True, stop=True),
                           r=[ART, b_S0sb[hp]], w=[b_pq] if half == 0 else (), disjoint=[b_pq] if half else ())
                    for half in range(2):
                        op(V, lambda e, half=half, pq=pq: e.tensor_tensor(out=TMPs.t[:, half * 8:(half + 1) * 8, :],
                                                                            in0=pq[:, half * 512:(half + 1) * 512].rearrange("p (b i) -> p b i", i=64),
                                                                            in1=segsel.t[:, half * 8:(half + 1) * 8].unsqueeze(2).broadcast_to([128, 8, 64]), op=ALU.mult),
                           xr=[b_pq], r=[segsel], w=[TMPs] if half == 0 else (), disjoint=[TMPs] if half else ())
                    op(V, lambda e: e.tensor_reduce(out=Y1.t[:], in_=TMPs.t[:].rearrange("p b i -> p i b"), axis=AX.X, op=ALU.add), r=[TMPs], w=[Y1])
                    py = nextps()
                    op(PE, lambda e, py=py, NAXt=NAXt, Usb=Usb: e.matmul(py.t[:, 0:64], lhsT=NAXt[:, 0:128], rhs=Usb[:], start=True, stop=False),
                       r=[b_NAX, b_Usb], w=[py], signal=False)
                    op(PE, lambda e, py=py, MAt=MAt, hc=hc: e.matmul(py.t[:, 0:64], lhsT=MAt[:, 0:128], rhs=Vtm.t[:, hc], start=False, stop=True),
                       r=[b_MA, Vtm], disjoint=[py])
                    op(V, lambda e, py=py, hc=hc: e.tensor_tensor(out=Ytm.t[:, hc], in0=py.t[:, 0:64], in1=Y1.t[:], op=ALU.add), xr=[py], r=[Y1], disjoint=[Ytm])
                    op(PO, lambda e, Usb=Usb: e.tensor_tensor(out=UBLK.t[:], in0=Usb[:, :].unsqueeze(1).broadcast_to([128, 16, 64]),
                                                               in1=segsel.t[:, :].unsqueeze(2).broadcast_to([128, 16, 64]), op=ALU.mult),
                       r=[b_Usb, segsel], w=[UBLK])
                    op(PO, lambda e, hc=hc: e.tensor_tensor(out=VBLK.t[:], in0=Vtm.t[:, hc].unsqueeze(1).broadcast_to([128, 16, 64]),
                                                             in1=segsel.t[:, :].unsqueeze(2).broadcast_to([128, 16, 64]), op=ALU.mult),
                       r=[Vtm, segsel], w=[VBLK])
                    pq, b_pq = ps2.next()
                    for half in range(2):
                        op(PE, lambda e, half=half, hc=hc, base=base, pq=pq: e.matmul(pq[base:base + 64, half * 512:(half + 1) * 512], lhsT=Btm.t[:, hc],
                                                                                        rhs=UBLK.t[:, half * 8:(half + 1) * 8, :].rearrange("p b i -> p (b i)"), start=True, stop=False),
                           r=[Btm, UBLK], w=[b_pq] if half == 0 else (), disjoint=[b_pq] if half else (), signal=False)
                        op(PE, lambda e, half=half, hc=hc, base=base, pq=pq: e.matmul(pq[base:base + 64, half * 512:(half + 1) * 512], lhsT=Ktm.t[:, hc],
                                                                                        rhs=VBLK.t[:, half * 8:(half + 1) * 8, :].rearrange("p b i -> p (b i)"), start=False, stop=True),
                           r=[Ktm, VBLK], disjoint=[b_pq])
                    op(V, lambda e, hp=hp, base=base, pq=pq: e.tensor_tensor(out=S0s[base:base + 64, hp, :, :], in0=pq[base:base + 64, :].rearrange("p (b i) -> p b i", i=64),
                                                                               in1=S0s[base:base + 64, hp, :, :], op=ALU.add),
                       xr=[b_pq], w=[P.buf()], r=[b_S0s[hp]])
                    op(PO, lambda e, hp=hp, base=base: e.tensor_tensor(out=S0s[base:base + 64, hp, :, :], in0=S0s[base:base + 64, hp, :, :],
                                                                        in1=EW.t[base:base + 64, hp, :].rearrange("p (b t) -> p b t", t=8)[:, :, 7:8].broadcast_to([64, 16, 64]),
                                                                        op=ALU.mult),
                       r=[EW], w=[b_S0s[hp]] if h2 == 1 else (), disjoint=[b_S0s[hp]] if h2 == 0 else ())
            if full:
                if kind != "sample":
                    op(A, lambda e: e.copy(out=Ytm.t[:], in_=psY.t[:, 0:512]), xr=[psY], w=[Ytm])
                y3 = lambda x: x.t[:, 0:512].rearrange("p (h i) -> p h i", i=64)
                b8 = lambda ap: ap.unsqueeze(2).broadcast_to([128, 8, 64])
                op(V, lambda e: e.tensor_reduce(out=ST8.t[:, 0:8], in_=y3(Ytm), axis=AX.X, op=ALU.add), r=[Ytm], w=[ST8])
                op(PO, lambda e: e.tensor_scalar(out=ST8.t[:, 8:16], in0=ST8.t[:, 0:8], scalar1=-1.0 / 64, scalar2=None, op0=ALU.mult), w=[ST8])
                op(PO, lambda e: e.tensor_tensor(out=y3(O1), in0=y3(Ytm), in1=b8(ST8.t[:, 8:16]), op=ALU.add), r=[Ytm, ST8], w=[O1])
                op(PO, lambda e: e.tensor_tensor(out=O2.t[:], in0=O1.t[:], in1=O1.t[:], op=ALU.mult), r=[O1], w=[O2])
                op(V, lambda e: e.tensor_reduce(out=ST8.t[:, 16:24], in_=y3(O2), axis=AX.X, op=ALU.add), r=[O2], w=[ST8])
                op(A, lambda e: e.activation(out=ST8.t[:, 24:32], in_=ST8.t[:, 16:24], func=AF.Sqrt, scale=1.0 / 64, bias=cst.t[:, 2:3]), r=[cst], w=[ST8])
                op(V, lambda e: e.reciprocal(out=ST8.t[:, 16:24], in_=ST8.t[:, 24:32]), w=[ST8])
                op(PO, lambda e: e.tensor_tensor(out=y3(O2), in0=y3(O1), in1=b8(ST8.t[:, 16:24]), op=ALU.mult), r=[O1, ST8], w=[O2])
                op(PO, lambda e: e.tensor_tensor(out=O1.t[:], in0=O2.t[:], in1=lnw.t[:], op=ALU.mult), r=[O2, lnw], w=[O1])
                op(PO, lambda e: e.tensor_tensor(out=O2.t[:], in0=O1.t[:], in1=lnb.t[:], op=ALU.add), r=[O1, lnb], w=[O2])
                op(V, lambda e: e.tensor_tensor(out=y3(O3), in0=Vtm.t[:, 0:512].rearrange("p (h i) -> p h i", i=64), in1=b8(COEF.t[:, 0:8]), op=ALU.mult),
                   r=[Vtm, COEF], w=[O3])
                op(V, lambda e: e.tensor_tensor(out=O1.t[:], in0=O2.t[:], in1=O3.t[:], op=ALU.add), r=[O2, O3], w=[O1])
                op(V, lambda e: e.tensor_tensor(out=OAb.t[:], in0=O1.t[:], in1=Gtm.t[:], op=ALU.mult), r=[O1, Gtm], w=[OAb])
                pt = nextps()
                ptb = pt.t[:].bitcast(BF16)
                for hp in range(4):
                    op(PE, lambda e, hp=hp, ptb=ptb: e.transpose(out=ptb[:, hp * 128:(hp + 1) * 128], in_=OAb.t[:, hp * 128:(hp + 1) * 128], identity=identb.t[:]),
                       r=[OAb, identb], w=[pt] if hp == 0 else (), disjoint=[pt] if hp else ())
                op(A, lambda e, ptb=ptb: e.copy(out=OAT.t[:].rearrange("p a b -> p (a b)"), in_=ptb[:, 0:512]), xr=[pt], w=[OAT])
                P.dma("sp", T["oaT_d"].rearrange("c p t -> p c t")[:, :, q0:q0 + 128], OAT.t[:], reads=[OAT], disjoint=[T["b_oaT"]])

        tiles = list(range(NTILE))
        if DEBUG_TILES is not None:
            tiles = DEBUG_TILES
        for t in tiles:
            rwkv_tile(t)
        for hp in range(4):
            pt = nextps()
            op(PE, lambda e, pt=pt, hp=hp: e.transpose(out=pt.t[0:64, 0:128], in_=S0[:, hp, :], identity=identf.t[:]),
               r=[b_S[2 * hp], b_S[2 * hp + 1], identf], w=[pt])
            op(A, lambda e, pt=pt, hp=hp: e.copy(out=SOUT.t[:, hp, :], in_=pt.t[0:64, 0:128]), xr=[pt], disjoint=[SOUT])
        for h2 in range(2):
            P.dma("sp", T["rwkv_p_out"].rearrange("(a h) i j -> h i a j", h=2)[h2], SOUT.t[:, 0:4, h2 * 64:(h2 + 1) * 64], reads=[SOUT])
        for hp in range(4):
            for g in range(4):
                pt = nextps()
                for k in range(4):
                    b = g * 4 + k
                    op(PE, lambda e, pt=pt, k=k, b=b, hp=hp: e.transpose(out=pt.t[0:64, k * 128:(k + 1) * 128], in_=S0s[:, hp, b, :], identity=identf.t[:]),
                       r=[b_S0s[hp], identf], w=[pt] if k == 0 else (), disjoint=[pt] if k else ())
                op(A, lambda e, pt=pt, g=g: e.copy(out=SOUT.t[:, g * 4:(g + 1) * 4, :], in_=pt.t[0:64, 0:512].rearrange("p (b c) -> p b c", c=128)),
                   xr=[pt], w=[SOUT] if g == 0 else (), disjoint=[SOUT] if g else ())
            for h2 in range(2):
                P.dma("sp", T["rwkv_s_out"][:, 2 * hp + h2, :, :].rearrange("b i j -> i b j"),
                      SOUT.t[:, :, h2 * 64:(h2 + 1) * 64], reads=[SOUT])
        P.emit("phaseB1")


def phase_B2(P, nc, T):
    V, A, PO, PE = "dve", "act", "pool", "pe"
    with contextlib.ExitStack() as st:
        def mk(n, s, d):
            return Tl(st.enter_context(nc.sbuf_tensor("b2_" + n, s, d)), P.buf(n))

        def mkps(n, cols=512):
            return Tl(st.enter_context(nc.psum_tensor("b2_" + n, [128, cols], F32)), P.buf(n))

        def op(e, fn, r=(), w=(), **kw):
            return P.op(e, fn, reads=r, writes=w, **kw)

        cst = T["cst"]
        identb = mk("identb", [128, 128], BF16)
        identf = mk("identf", [128, 128], F32)
        P.dma("pool", identb.t[:], T["ident"], writes=[identb])
        P.dma("sp", identf.t[:], T["ident"], writes=[identf])
        cm_p = mk("cm_p", [128, 128], BF16)
        cm_s = mk("cm_s", [128, 128], BF16)
        P.dma("pool", cm_p.t[:], T["mtc_p"][:, 0:128], writes=[cm_p])
        P.dma("pool", cm_s.t[:], T["mtc_s"][:, 0:128], writes=[cm_s])
        KT = mk("KT", [128, 4, NTOK], BF16)
        QT = mk("QT", [128, 4, NQ], BF16)
        Vx = mk("Vx", [128, NTILE, 4, 130], BF16)
        for h in range(4):
            P.dma("sp", KT.t[:, h, :], T["kT_d"][h], reads=[T["b_kT"]], disjoint=[KT])
            P.dma("sp", QT.t[:, h, :], T["qT_d"][h], reads=[T["b_qT"]], disjoint=[QT])
            for t0 in range(0, NTILE, 8):
                t1 = min(NTILE, t0 + 8)
                P.dma("sp", Vx.t[:, t0:t1, h, 0:128], T["v_d"].rearrange("(t p) (h e) -> h p t e", p=128, h=4)[h][:, t0:t1, :], reads=[T["b_v"]], disjoint=[Vx])
        op(PO, lambda e: e.memset(Vx.t[:, :, :, 128:130], 1.0), disjoint=[Vx])
        btab = mk("btab", [128, 4, 32], F32)
        btabp = mk("btabp", [128, 4, 32], F32)
        pmask = mk("pmask", [128, 1], F32)
        P.dma("sp", btab.t[:].rearrange("p a b -> p (a b)"), T["btab"], writes=[btab])
        P.dma("sp", pmask.t[:], T["pmask"], writes=[pmask])
        op(V, lambda e: e.tensor_scalar(out=btabp.t[:], in0=btab.t[:], scalar1=pmask.t[:, 0:1], scalar2=None, op0=ALU.add), r=[btab, pmask], w=[btabp])
        bown = mk("bown", [128, 4], F32)
        P.dma("sp", bown.t[:], T["bown"], writes=[bown])
        bcache = mk("bcache", [128, 16, 32], F32)
        P.dma("sp", bcache.t[:].rearrange("p a b -> p (a b)"), T["bcache"], writes=[bcache])
        lv = mk("lv", [128, 4, 64], F32)
        for i, n in enumerate(["lam_q1", "lam_k1", "lam_q2", "lam_k2"]):
            P.dma("sp", lv.t[:, i, :], T[n].broadcast_to([128, 64]), disjoint=[lv])
        lt = mk("lt", [128, 2, 64], F32)
        ls = mk("ls", [128, 8], F32)
        op(V, lambda e: e.tensor_tensor(out=lt.t[:, 0, :], in0=lv.t[:, 0, :], in1=lv.t[:, 1, :], op=ALU.mult), r=[lv], disjoint=[lt])
        op(V, lambda e: e.tensor_tensor(out=lt.t[:, 1, :], in0=lv.t[:, 2, :], in1=lv.t[:, 3, :], op=ALU.mult), r=[lv], disjoint=[lt])
        op(V, lambda e: e.tensor_reduce(out=ls.t[:, 0:2], in_=lt.t[:], axis=AX.X, op=ALU.add), r=[lt], w=[ls])
        op(A, lambda e: e.activation(out=ls.t[:, 2:4], in_=ls.t[:, 0:2], func=AF.Exp), w=[ls])
        op(V, lambda e: e.tensor_tensor(out=ls.t[:, 4:5], in0=ls.t[:, 2:3], in1=ls.t[:, 3:4], op=ALU.subtract), w=[ls])
        op(V, lambda e: e.tensor_scalar(out=ls.t[:, 5:6], in0=ls.t[:, 4:5], scalar1=0.2, scalar2=-1.0, op0=ALU.add, op1=ALU.mult), w=[ls])
        sg = mk("sg", [128, 128], F32)
        P.dma("sp", sg.t[:], T["subln_g"].broadcast_to([128, 128]), writes=[sg])
        op(V, lambda e: e.tensor_scalar(out=sg.t[:], in0=sg.t[:], scalar1=0.8, scalar2=None, op0=ALU.mult), w=[sg])

        psr = Rot(P, [st.enter_context(nc.psum_tensor("b2_psr%d" % i, [128, 1024], F32)) for i in range(2)])
        psO = [mkps("psO%d" % i) for i in range(3)]
        pstr = mkps("pstr")

        def nextps():
            t, b = psr.next()
            return Tl(t, b)

        ETr = Rot(P, [st.enter_context(nc.sbuf_tensor("b2_ET%d" % i, [128, 256], BF16)) for i in range(3)])
        T2 = mk("T2", [128, 128], F32)
        ATT = mk("ATT", [128, 128], F32)
        junk = mk("junk", [128, 128], BF16)
        sc = mk("sc", [128, 8], F32)
        OBb = mk("OBb", [128, 512], BF16)
        OBT = mk("OBT", [128, 4, 128], BF16)

        def combine(h, o0, o0c, o1, o1c, xr0, xr1):
            op(V, lambda e: e.reciprocal(out=sc.t[:, 0:1], in_=o0[:, 128:129]), xr=[xr0], w=[sc])
            op(V, lambda e: e.reciprocal(out=sc.t[:, 1:2], in_=o1[:, 128:129]), xr=[xr1], w=[sc])
            op(V, lambda e: e.tensor_tensor(out=sc.t[:, 2:3], in0=sc.t[:, 1:2], in1=ls.t[:, 5:6], op=ALU.mult), r=[ls], w=[sc])
            op(V, lambda e: e.tensor_scalar(out=T2.t[:], in0=o1[:, 0:128], scalar1=sc.t[:, 2:3], scalar2=None, op0=ALU.mult), xr=[xr1], r=[sc], w=[T2])
            op(V, lambda e: e.scalar_tensor_tensor(out=ATT.t[:], in0=o0[:, 0:128], scalar=sc.t[:, 0:1], in1=T2.t[:], op0=ALU.mult, op1=ALU.add),
               xr=[xr0], r=[sc, T2], w=[ATT])
            op(A, lambda e: e.activation(out=junk.t[:], in_=ATT.t[:], func=AF.Square, accum_out=sc.t[:, 3:4]), r=[ATT], w=[junk, sc])
            op(A, lambda e: e.activation(out=sc.t[:, 4:5], in_=sc.t[:, 3:4], func=AF.Sqrt, scale=1.0 / 128, bias=cst.t[:, 3:4]), r=[cst], w=[sc])
            op(V, lambda e: e.reciprocal(out=sc.t[:, 5:6], in_=sc.t[:, 4:5]), w=[sc])
            op(V, lambda e: e.scalar_tensor_tensor(out=OBb.t[:, h * 128:(h + 1) * 128], in0=ATT.t[:], scalar=sc.t[:, 5:6], in1=sg.t[:], op0=ALU.mult, op1=ALU.mult),
               r=[ATT, sc, sg], disjoint=[OBb])

        def finish_tile(qi):
            ptb = pstr.t[:].bitcast(BF16)
            for h in range(4):
                op(PE, lambda e, h=h: e.transpose(out=ptb[:, h * 128:(h + 1) * 128], in_=OBb.t[:, h * 128:(h + 1) * 128], identity=identb.t[:]),
                   r=[OBb, identb], w=[pstr] if h == 0 else (), disjoint=[pstr] if h else ())
            op(A, lambda e: e.copy(out=OBT.t[:].rearrange("p a b -> p (a b)"), in_=ptb[:, 0:512]), xr=[pstr], w=[OBT])
            P.dma("sp", T["obT_d"].rearrange("c p t -> p c t")[:, :, qi * 128:(qi + 1) * 128], OBT.t[:], reads=[OBT], disjoint=[T["b_obT"]])

        qtiles = list(range(NT_OWN)) if DEBUG_QT is None else DEBUG_QT
        if DEBUG_B2 < 1:
            qtiles = []
        for qi in qtiles:
            qg = NT_PRIOR + qi
            for h in range(4):
                o0, o1 = psO[0], psO[1]
                kts = list(range(0, qg + 1))
                for ii, kt in enumerate(kts):
                    pS = nextps()
                    for c in range(2):
                        op(PE, lambda e, c=c, kt=kt, pS=pS, h=h, qi=qi: e.matmul(pS.t[:, c * 512:c * 512 + 128], lhsT=KT.t[c * 64:(c + 1) * 64, h, kt * 128:(kt + 1) * 128],
                                                                     rhs=QT.t[c * 64:(c + 1) * 64, h, qi * 128:(qi + 1) * 128], start=True, stop=True),
                           r=[KT, QT], w=[pS] if c == 0 else (), disjoint=[pS] if c else ())
                    if DEBUG_B2 < 2:
                        continue
                    ET, b_ET = ETr.next()
                    o = qg - kt
                    bt = btabp if kt < NT_PRIOR else btab
                    op(A, lambda e, pS=pS, ET=ET, bt=bt, o=o, h=h: e.activation(out=ET[:].rearrange("p (c q) -> p c q", c=2), in_=pS.t[:, :].rearrange("p (c x) -> p c x", c=2)[:, :, 0:128],
                                                                           func=AF.Exp, scale=0.125, bias=bt.t[:, h, o:o + 1]),
                       xr=[pS], r=[bt], w=[b_ET])
                    if kt == qg:
                        op(V, lambda e, ET=ET: e.tensor_tensor(out=ET[:].rearrange("p (c q) -> p c q", c=2), in0=ET[:].rearrange("p (c q) -> p c q", c=2),
                                                                in1=cm_p.t[:, :].unsqueeze(1).broadcast_to([128, 2, 128]), op=ALU.mult), r=[cm_p], w=[b_ET])
                    first, last = (ii == 0), (ii == len(kts) - 1)
                    if DEBUG_B2 < 3:
                        continue
                    op(PE, lambda e, ET=ET, kt=kt, first=first, last=last, h=h, o0=o0: e.matmul(o0.t[:, 0:129], lhsT=ET[:, 0:128], rhs=Vx.t[:, kt, h, 0:129], start=first, stop=last),
                       r=[b_ET, Vx], w=[o0] if first else (), disjoint=() if first else [o0], signal=last)
                    op(PE, lambda e, ET=ET, kt=kt, first=first, last=last, h=h, o1=o1: e.matmul(o1.t[:, 0:129], lhsT=ET[:, 128:256], rhs=Vx.t[:, kt, h, 0:129], start=first, stop=last),
                       r=[b_ET, Vx], w=[o1] if first else (), disjoint=() if first else [o1], signal=last)
                if DEBUG_B2 >= 4:
                    combine(h, o0.t, None, o1.t, None, o0, o1)
            if DEBUG_B2 >= 5:
                finish_tile(qi)

        if DEBUG_SAMPLE_ATT:
            zb = mk("zb", [128, 512], BF16)
            op(PO, lambda e: e.memset(zb.t[:], 0.0), w=[zb])
            slot = {}
            for h in range(4):
                for c in range(2):
                    i = h * 2 + c
                    slot[(h, c)] = (i // 3, (i % 3) * 130)
            for bnk in range(3):
                op(PE, lambda e, bnk=bnk: e.matmul(psO[bnk].t[:, 0:512], lhsT=zb.t[:, 0:128], rhs=zb.t[:, 0:512], start=True, stop=False), r=[zb], w=[psO[bnk]])
            ptb_i = mk("ptab", [128, 256], I32)
            P.dma("sp", ptb_i.t[:], T["ptab"].broadcast_to([128, 256]), writes=[ptb_i])
            pcol = mk("pcol", [128, 1], I32)
            op(PO, lambda e: e.iota(pcol.t[:], pattern=[[0, 1]], base=0, channel_multiplier=1), w=[pcol])
            pcolf = mk("pcolf", [128, 1], F32)
            op(V, lambda e: e.tensor_copy(out=pcolf.t[:], in_=pcol.t[:]), r=[pcol], w=[pcolf])
            idx = mk("idx", [128, 256], I32)
            op(V, lambda e: e.tensor_scalar(out=idx.t[:], in0=ptb_i.t[:], scalar1=128.0, scalar2=pcolf.t[:, 0:1], op0=ALU.mult, op1=ALU.add), r=[ptb_i, pcolf], w=[idx])
            Kpg = Rot(P, [st.enter_context(nc.sbuf_tensor("b2_Kpg%d" % i, [128, 512], F32)) for i in range(3)])
            Vpg = Rot(P, [st.enter_context(nc.sbuf_tensor("b2_Vpg%d" % i, [128, 512], F32)) for i in range(3)])
            KcTr = Rot(P, [st.enter_context(nc.sbuf_tensor("b2_KcT%d" % i, [128, 4, 128], BF16)) for i in range(2)])
            Vpxr = Rot(P, [st.enter_context(nc.sbuf_tensor("b2_Vpx%d" % i, [128, 4, 130], BF16)) for i in range(2)])
            for (vt, vb) in Vpxr.items:
                op(PO, lambda e, vt=vt: e.memset(vt[:], 1.0), w=[vb])
            XB = mk("XB", [128, 2, 32], F32)
            ETz = [mk("ETz%d" % i, [128, 4, 2, 128], BF16) for i in range(3)]
            ETz_b = [None] * 3
            for z in ETz:
                op(PO, lambda e, z=z: e.memset(z.t[:], 0.0), w=[z])
            ez = 0
            sq0 = NT_OWN * 128
            for b in range(16):
                for pg in range(16):
                    col = b * 16 + pg
                    kp, b_kp = Kpg.next()
                    vp, b_vp = Vpg.next()
                    P.dma_ind(kp[:, :], T["cache_k"], idx.t[:, col:col + 1], reads=[idx], writes=[b_kp])
                    P.dma_ind(vp[:, :], T["cache_v"], idx.t[:, col:col + 1], reads=[idx], writes=[b_vp])
                    pT = nextps()
                    for h in range(4):
                        op(PE, lambda e, h=h, kp=kp, pT=pT: e.transpose(out=pT.t[:, h * 128:(h + 1) * 128], in_=kp[:, h * 128:(h + 1) * 128], identity=identf.t[:]),
                           r=[b_kp, identf], w=[pT] if h == 0 else (), disjoint=[pT] if h else ())
                    kc, b_kc = KcTr.next()
                    op(A, lambda e, kc=kc, pT=pT: e.copy(out=kc[:].rearrange("p a b -> p (a b)"), in_=pT.t[:, 0:512]), xr=[pT], w=[b_kc])
                    vx, b_vx = Vpxr.next()
                    op(V, lambda e, vx=vx, vp=vp: e.tensor_copy(out=vx[:, :, 0:128], in_=vp[:, :].rearrange("p (h e) -> p h e", h=4)), r=[b_vp], w=[b_vx])
                    pS = nextps()
                    for h in range(4):
                        for c in range(2):
                            first = (h == 0 and c == 0)
                            op(PE, lambda e, h=h, c=c, kc=kc, pS=pS, b=b: e.matmul(pS.t[:, c * 512 + h * 8:c * 512 + h * 8 + 8], lhsT=kc[c * 64:(c + 1) * 64, h, :],
                                                                               rhs=QT.t[c * 64:(c + 1) * 64, h, sq0 + b * 8:sq0 + b * 8 + 8], start=True, stop=True),
                               r=[b_kc, QT], w=[pS] if first else (), disjoint=() if first else [pS])
                    op(V, lambda e, pS=pS, pg=pg: e.scalar_tensor_tensor(out=XB.t[:], in0=pS.t[:, :].rearrange("p (c x) -> p c x", c=2)[:, :, 0:32], scalar=0.125,
                                                                           in1=bcache.t[:, pg, :].unsqueeze(1).broadcast_to([128, 2, 32]), op0=ALU.mult, op1=ALU.add),
                       xr=[pS], r=[bcache], w=[XB])
                    z = ETz[ez % 3]
                    zprev = ETz_b[ez % 3]
                    if zprev is not None and zprev != b:
                        op(PO, lambda e, z=z, zprev=zprev: e.memset(z.t[:, :, :, zprev * 8:zprev * 8 + 8], 0.0), w=[z])
                    ETz_b[ez % 3] = b
                    ez += 1
                    op(A, lambda e, z=z, b=b: e.activation(out=z.t[:, :, :, b * 8:b * 8 + 8], in_=XB.t[:].rearrange("p c (h q) -> p h c q", h=4), func=AF.Exp),
                       r=[XB], w=[z])
                    for h in range(4):
                        for c in range(2):
                            bnk, c0 = slot[(h, c)]
                            op(PE, lambda e, h=h, c=c, z=z, vx=vx, bnk=bnk, c0=c0: e.matmul(psO[bnk].t[:, c0:c0 + 129], lhsT=z.t[:, h, c, :], rhs=vx[:, h, 0:129], start=False, stop=False),
                               r=[z, b_vx], disjoint=[psO[bnk]], signal=(h == 3 and c == 1))
            kt = NTILE - 1
            for h in range(4):
                pS = nextps()
                for c in range(2):
                    op(PE, lambda e, c=c, h=h, pS=pS: e.matmul(pS.t[:, c * 512:c * 512 + 128], lhsT=KT.t[c * 64:(c + 1) * 64, h, kt * 128:(kt + 1) * 128],
                                                                 rhs=QT.t[c * 64:(c + 1) * 64, h, sq0:sq0 + 128], start=True, stop=True),
                       r=[KT, QT], w=[pS] if c == 0 else (), disjoint=[pS] if c else ())
                ET, b_ET = ETr.next()
                op(A, lambda e, pS=pS, ET=ET, h=h: e.activation(out=ET[:].rearrange("p (c q) -> p c q", c=2), in_=pS.t[:, :].rearrange("p (c x) -> p c x", c=2)[:, :, 0:128],
                                                                func=AF.Exp, scale=0.125, bias=bown.t[:, h:h + 1]), xr=[pS], r=[bown], w=[b_ET])
                op(V, lambda e, ET=ET: e.tensor_tensor(out=ET[:].rearrange("p (c q) -> p c q", c=2), in0=ET[:].rearrange("p (c q) -> p c q", c=2),
                                                        in1=cm_s.t[:, :].unsqueeze(1).broadcast_to([128, 2, 128]), op=ALU.mult), r=[cm_s], w=[b_ET])
                for c in range(2):
                    bnk, c0 = slot[(h, c)]
                    op(PE, lambda e, c=c, h=h, ET=ET, bnk=bnk, c0=c0: e.matmul(psO[bnk].t[:, c0:c0 + 129], lhsT=ET[:, c * 128:(c + 1) * 128], rhs=Vx.t[:, kt, h, 0:129], start=False, stop=True),
                       r=[b_ET, Vx], disjoint=[psO[bnk]])
            for h in range(4):
                b0, c0 = slot[(h, 0)]
                b1, c1 = slot[(h, 1)]
                combine(h, psO[b0].t[:, c0:c0 + 129], None, psO[b1].t[:, c1:c1 + 129], None, psO[b0], psO[b1])
            finish_tile(NT_OWN)
        P.emit("phaseB2")


def phase_C1(P, nc, T):
    V, A, PO, PE = "dve", "act", "pool", "pe"
    with contextlib.ExitStack() as st:
        def mk(n, s, d):
            return Tl(st.enter_context(nc.sbuf_tensor("c1_" + n, s, d)), P.buf(n))

        def op(e, fn, r=(), w=(), **kw):
            return P.op(e, fn, reads=r, writes=w, **kw)

        cst = T["cst"]
        identb = mk("identb", [128, 128], BF16)
        P.dma("pool", identb.t[:], T["ident"], writes=[identb])
        wa = mk("wa", [128, 4, D], BF16)
        wb = mk("wb", [128, 4, D], BF16)
        wo = mk("wo", [128, 8, D], BF16)
        for kc in range(4):
            P.dma("pool", wa.t[:, kc, :], T["w_br_a"][kc * 128:(kc + 1) * 128, :], disjoint=[wa])
            P.dma("pool", wb.t[:, kc, :], T["w_br_b"][kc * 128:(kc + 1) * 128, :], disjoint=[wb])
        for kc in range(8):
            P.dma("pool", wo.t[:, kc, :], T["w_out"][kc * 128:(kc + 1) * 128, :], disjoint=[wo])
        gffn = mk("gffn", [128, D], F32)
        P.dma("sp", gffn.t[:], T["g_ffn"].broadcast_to([128, D]), writes=[gffn])
        psr = Rot(P, [st.enter_context(nc.psum_tensor("c1_psr%d" % i, [128, 512], F32)) for i in range(8)])

        def nextps():
            t, b = psr.next()
            return Tl(t, b)

        oaT = [mk("oaT%d" % i, [128, 4, 512], BF16) for i in range(2)]
        obT = [mk("obT%d" % i, [128, 4, 512], BF16) for i in range(2)]
        gts = [mk("gts%d" % i, [128, 16, 512], BF16) for i in range(2)]
        mT = [mk("mT%d" % i, [128, 8, 512], BF16) for i in range(2)]
        t1 = [mk("t1_%d" % i, [128, 512], F32) for i in range(2)]
        t2 = [mk("t2_%d" % i, [128, 512], F32) for i in range(2)]
        xt = [mk("xt%d" % i, [128, D], F32) for i in range(2)]
        xn = [mk("xn%d" % i, [128, D], F32) for i in range(2)]
        hb = [mk("hb%d" % i, [128, D], BF16) for i in range(2)]
        hT = [mk("hT%d" % i, [128, 8, 128], BF16) for i in range(2)]
        junk = mk("junk", [128, D], BF16)
        ss = [mk("ss%d" % i, [128, 4], F32) for i in range(2)]
        groups = [(g * 4, 4) for g in range(4)] + [(16, 1)]
        tcount = 0
        for gi, (qt0, nt) in enumerate(groups):
            G = nt * 128
            q0 = qt0 * 128
            oa, ob, gt, m = oaT[gi % 2], obT[gi % 2], gts[gi % 2], mT[gi % 2]
            P.dma("sp", oa.t[:, :, 0:G], T["oaT_d"].rearrange("c p t -> p c t")[:, :, q0:q0 + G], reads=[T["b_oaT"]], writes=[oa])
            P.dma("sp", ob.t[:, :, 0:G], T["obT_d"].rearrange("c p t -> p c t")[:, :, q0:q0 + G], reads=[T["b_obT"]], writes=[ob])
            P.dma("sp", gt.t[:, :, 0:G], T["gates_d"].rearrange("c p t -> p c t")[:, :, q0:q0 + G], reads=[T["b_gates"]], writes=[gt])
            for nch in range(8):
                pa = nextps()
                pb = nextps()
                for kc in range(4):
                    op(PE, lambda e, pa=pa, kc=kc, nch=nch, oa=oa, G=G: e.matmul(pa.t[:, 0:G], lhsT=wa.t[:, kc, nch * 128:(nch + 1) * 128], rhs=oa.t[:, kc, 0:G], start=(kc == 0), stop=(kc == 3)),
                       r=[wa, oa], w=[pa] if kc == 0 else (), disjoint=[pa] if kc else (), signal=(kc == 3))
                for kc in range(4):
                    op(PE, lambda e, pb=pb, kc=kc, nch=nch, ob=ob, G=G: e.matmul(pb.t[:, 0:G], lhsT=wb.t[:, kc, nch * 128:(nch + 1) * 128], rhs=ob.t[:, kc, 0:G], start=(kc == 0), stop=(kc == 3)),
                       r=[wb, ob], w=[pb] if kc == 0 else (), disjoint=[pb] if kc else (), signal=(kc == 3))
                ta, tb = t1[nch % 2], t2[nch % 2]
                op(V, lambda e, pa=pa, ta=ta, gt=gt, nch=nch, G=G: e.tensor_tensor(out=ta.t[:, 0:G], in0=pa.t[:, 0:G], in1=gt.t[:, nch, 0:G], op=ALU.mult), xr=[pa], r=[gt], w=[ta])
                op(V, lambda e, pb=pb, tb=tb, gt=gt, nch=nch, G=G: e.tensor_tensor(out=tb.t[:, 0:G], in0=pb.t[:, 0:G], in1=gt.t[:, 8 + nch, 0:G], op=ALU.mult), xr=[pb], r=[gt], w=[tb])
                op(PO, lambda e, ta=ta, tb=tb, m=m, nch=nch, G=G: e.tensor_tensor(out=m.t[:, nch, 0:G], in0=ta.t[:, 0:G], in1=tb.t[:, 0:G], op=ALU.add), r=[ta, tb], disjoint=[m])
            for ti in range(nt):
                qt = qt0 + ti
                tg = NT_PRIOR + qt
                x, xo, h_, hTt, s_ = xt[tcount % 2], xn[tcount % 2], hb[tcount % 2], hT[tcount % 2], ss[tcount % 2]
                tcount += 1
                P.dma("sp", x.t[:], T["x_all"][tg * 128:(tg + 1) * 128, :], writes=[x])
                for half in range(2):
                    px = nextps()
                    for kc in range(8):
                        op(PE, lambda e, px=px, kc=kc, half=half, m=m, ti=ti: e.matmul(px.t[:, 0:512], lhsT=m.t[:, kc, ti * 128:(ti + 1) * 128], rhs=wo.t[:, kc, half * 512:(half + 1) * 512],
                                                                                    start=(kc == 0), stop=(kc == 7)),
                           r=[m, wo], w=[px] if kc == 0 else (), disjoint=[px] if kc else (), signal=(kc == 7))
                    op(V, lambda e, px=px, half=half, x=x, xo=xo: e.tensor_tensor(out=xo.t[:, half * 512:(half + 1) * 512], in0=px.t[:, 0:512], in1=x.t[:, half * 512:(half + 1) * 512], op=ALU.add),
                       xr=[px], r=[x], w=[xo] if half == 0 else (), disjoint=[xo] if half else ())
                P.dma("sp", T["xnew_d"][qt * 128:(qt + 1) * 128, :], xo.t[:], reads=[xo], disjoint=[T["b_xnew"]])
                op(A, lambda e, xo=xo, s_=s_: e.activation(out=junk.t[:], in_=xo.t[:], func=AF.Square, accum_out=s_.t[:, 0:1]), r=[xo], w=[junk, s_])
                op(A, lambda e, s_=s_: e.activation(out=s_.t[:, 1:2], in_=s_.t[:, 0:1], func=AF.Sqrt, scale=1.0 / D, bias=cst.t[:, 0:1]), r=[cst], w=[s_])
                op(V, lambda e, s_=s_: e.reciprocal(out=s_.t[:, 2:3], in_=s_.t[:, 1:2]), w=[s_])
                op(V, lambda e, xo=xo, s_=s_, h_=h_: e.scalar_tensor_tensor(out=h_.t[:], in0=xo.t[:], scalar=s_.t[:, 2:3], in1=gffn.t[:], op0=ALU.mult, op1=ALU.mult),
                   r=[xo, s_, gffn], w=[h_])
                pt = nextps()
                ptb = pt.t[:].bitcast(BF16)
                for kc in range(8):
                    op(PE, lambda e, kc=kc, ptb=ptb, h_=h_: e.transpose(out=ptb[:, kc * 128:(kc + 1) * 128], in_=h_.t[:, kc * 128:(kc + 1) * 128], identity=identb.t[:]),
                       r=[h_, identb], w=[pt] if kc == 0 else (), disjoint=[pt] if kc else ())
                op(A, lambda e, ptb=ptb, hTt=hTt: e.copy(out=hTt.t[:].rearrange("p a b -> p (a b)"), in_=ptb[:, 0:1024]), xr=[pt], w=[hTt])
                P.dma("sp", T["hnT_d"].rearrange("c p t -> p c t")[:, :, qt * 128:(qt + 1) * 128], hTt.t[:], reads=[hTt], disjoint=[T["b_hnT"]])
        P.emit("phaseC1")


def phase_C2(P, nc, T):
    V, A, PO, PE = "dve", "act", "pool", "pe"
    with contextlib.ExitStack() as st:
        def mk(n, s, d):
            return Tl(st.enter_context(nc.sbuf_tensor("c2_" + n, s, d)), P.buf(n))

        def op(e, fn, r=(), w=(), **kw):
            return P.op(e, fn, reads=r, writes=w, **kw)

        cst = T["cst"]
        wu = mk("wu", [128, 8, 4096], BF16)
        wd = mk("wd", [128, 32, D], BF16)
        for kc in range(8):
            for j in range(2):
                P.dma("pool", wu.t[:, kc, j * 2048:(j + 1) * 2048], T["w_up"][kc * 128:(kc + 1) * 128, j * 2048:(j + 1) * 2048], disjoint=[wu])
        for fc in range(32):
            P.dma("pool", wd.t[:, fc, :], T["w_down"][fc * 128:(fc + 1) * 128, :], disjoint=[wd])
        gfin = mk("gfin", [128, D], F32)
        P.dma("sp", gfin.t[:], T["g_final"].broadcast_to([128, D]), writes=[gfin])
        psr = Rot(P, [st.enter_context(nc.psum_tensor("c2_psr%d" % i, [128, 512], F32)) for i in range(8)])

        def nextps():
            t, b = psr.next()
            return Tl(t, b)

        hT = [mk("hT%d" % i, [128, 8, 256], BF16) for i in range(2)]
        aT = [mk("aT%d" % i, [128, 32, 256], BF16) for i in range(1)]
        rl = [mk("rl%d" % i, [128, 256], F32) for i in range(3)]
        xw = [mk("xw%d" % i, [128, D], F32) for i in range(2)]
        yo = [mk("yo%d" % i, [128, D], F32) for i in range(2)]
        yf = [mk("yf%d" % i, [128, D], F32) for i in range(2)]
        junk = mk("junk", [128, D], BF16)
        ss = [mk("ss%d" % i, [128, 4], F32) for i in range(2)]
        groups = [(g * 2, 2) for g in range(8)] + [(16, 1)]
        tcount = 0
        for gi, (qt0, nt) in enumerate(groups):
            G = nt * 128
            q0 = qt0 * 128
            h_ = hT[gi % 2]
            a_ = aT[0]
            P.dma("sp", h_.t[:, :, 0:G], T["hnT_d"].rearrange("c p t -> p c t")[:, :, q0:q0 + G], reads=[T["b_hnT"]], writes=[h_])
            for fc in range(32):
                pu = nextps()
                for kc in range(8):
                    op(PE, lambda e, pu=pu, kc=kc, fc=fc, h_=h_, G=G: e.matmul(pu.t[:, 0:G], lhsT=wu.t[:, kc, fc * 128:(fc + 1) * 128], rhs=h_.t[:, kc, 0:G], start=(kc == 0), stop=(kc == 7)),
                       r=[wu, h_], w=[pu] if kc == 0 else (), disjoint=[pu] if kc else (), signal=(kc == 7))
                r_ = rl[fc % 3]
                op(A, lambda e, pu=pu, r_=r_, G=G: e.activation(out=r_.t[:, 0:G], in_=pu.t[:, 0:G], func=AF.Relu), xr=[pu], w=[r_])
                op(PO, lambda e, r_=r_, a_=a_, fc=fc, G=G: e.tensor_tensor(out=a_.t[:, fc, 0:G], in0=r_.t[:, 0:G], in1=r_.t[:, 0:G], op=ALU.mult), r=[r_], disjoint=[a_])
            for ti in range(nt):
                qt = qt0 + ti
                x, y, yfin, s_ = xw[tcount % 2], yo[tcount % 2], yf[tcount % 2], ss[tcount % 2]
                tcount += 1
                P.dma("sp", x.t[:], T["xnew_d"][qt * 128:(qt + 1) * 128, :], reads=[T["b_xnew"]], writes=[x])
                for half in range(2):
                    pd = nextps()
                    for fc in range(32):
                        op(PE, lambda e, pd=pd, fc=fc, half=half, a_=a_, ti=ti: e.matmul(pd.t[:, 0:512], lhsT=a_.t[:, fc, ti * 128:(ti + 1) * 128], rhs=wd.t[:, fc, half * 512:(half + 1) * 512],
                                                                                      start=(fc == 0), stop=(fc == 31)),
                           r=[a_, wd], w=[pd] if fc == 0 else (), disjoint=[pd] if fc else (), signal=(fc == 31))
                    op(V, lambda e, pd=pd, half=half, x=x, y=y: e.tensor_tensor(out=y.t[:, half * 512:(half + 1) * 512], in0=pd.t[:, 0:512], in1=x.t[:, half * 512:(half + 1) * 512], op=ALU.add),
                       xr=[pd], r=[x], w=[y] if half == 0 else (), disjoint=[y] if half else ())
                op(A, lambda e, y=y, s_=s_: e.activation(out=junk.t[:], in_=y.t[:], func=AF.Square, accum_out=s_.t[:, 0:1]), r=[y], w=[junk, s_])
                op(A, lambda e, s_=s_: e.activation(out=s_.t[:, 1:2], in_=s_.t[:, 0:1], func=AF.Sqrt, scale=1.0 / D, bias=cst.t[:, 0:1]), r=[cst], w=[s_])
                op(V, lambda e, s_=s_: e.reciprocal(out=s_.t[:, 2:3], in_=s_.t[:, 1:2]), w=[s_])
                op(V, lambda e, y=y, s_=s_, yfin=yfin: e.scalar_tensor_tensor(out=yfin.t[:], in0=y.t[:], scalar=s_.t[:, 2:3], in1=gfin.t[:], op0=ALU.mult, op1=ALU.mult),
                   r=[y, s_, gfin], w=[yfin])
                P.dma("sp", T["yout"][qt * 128:(qt + 1) * 128, :], yfin.t[:], reads=[yfin])
        P.emit("phaseC2")


EPS_T = [None, None]
DEBUG_QT = None
DEBUG_DUMP = False
DEBUG_B2 = 9
DEBUG_SAMPLE_ATT = True
DEBUG_TILES = None
DEBUG_GROUPS = None
DEBUG_CHUNKS = None
DEBUG_NOTM = False
DEBUG_HALF = None
DEBUG_NOOUT = False


def build_program(stage=99):
    nc = bass.Bass("TRN2", target_bir_lowering=False)
    din = lambda n, s, d=F32: nc.dram_tensor(n, s, d, kind="ExternalInput").ap()
    dout = lambda n, s, d=F32: nc.dram_tensor(n, s, d, kind="ExternalOutput").ap()
    dscr = (lambda n, s, d: nc.dram_tensor(n, s, d, kind="ExternalOutput").ap()) if DEBUG_DUMP else (lambda n, s, d: nc.dram_tensor(n, s, d).ap())
    T = {}
    T["x_all"] = din("x_all", [NTOK, D])
    T["st_shift"] = din("st_shift", [16, D])
    T["w_in"] = din("w_in", [D, NCOL])
    T["g_mix"] = din("g_mix", [1, D])
    T["mu_fm"] = din("mu_fm", [128, 14])
    for n, shp in [("ident", [128, 128]), ("mtc_p", [128, 256]), ("mtc_s", [128, 256]), ("ms_p", [128, 128]), ("ms_s", [128, 128]),
                   ("rm_p", [128, 512]), ("rm_s", [128, 512]), ("bones", [128, 128]), ("sel8", [128, 32]), ("segsel", [128, 16]),
                   ("w2", [64, 512]), ("a2", [64, 512]), ("g2", [128, 512]), ("w0_fm", [128, 4]), ("a0_fm", [128, 4]),
                   ("kk_fm", [128, 4]), ("ka_fm", [128, 4]), ("rk_fm", [128, 4]), ("ln_w", [1, 512]), ("ln_b", [1, 512]),
                   ("st_rwkv", [16, 8, 64, 64]), ("pmask", [128, 1]), ("btab", [128, 128]), ("bown", [128, 4]), ("bcache", [128, 512]),
                   ("lam_q1", [1, 64]), ("lam_k1", [1, 64]), ("lam_q2", [1, 64]), ("lam_k2", [1, 64]), ("subln_g", [1, 128]),
                   ] + ([("cache_k", [NPOOL * 128, 512]), ("cache_v", [NPOOL * 128, 512])] if DEBUG_SAMPLE_ATT else []) + [
                   ("w_br_a", [512, D]), ("w_br_b", [512, D]), ("w_out", [D, D]), ("g_ffn", [1, D]), ("w_up", [D, 4096]), ("w_down", [4096, D]),
                   ("g_final", [1, D])]:
        T[n] = din(n, shp)
    T["ptab"] = din("ptab", [1, 256], I32)
    T["yout"] = dout("yout", [NQ, D])
    T["obT_d"] = dscr("obT_d", [4, 128, NQ], BF16)
    T["xnew_d"] = dscr("xnew_d", [NQ, D], F32)
    T["hnT_d"] = dscr("hnT_d", [8, 128, NQ], BF16)
    T["kout"] = dout("kout", [NQ, 512])
    T["vout"] = dout("vout", [NQ, 512])
    T["shift_out"] = dout("shift_out", [17, D])
    T["rwkv_p_out"] = dout("rwkv_p_out", [8, 64, 64])
    T["rwkv_s_out"] = dout("rwkv_s_out", [16, 8, 64, 64])
    T["prT_d"] = dscr("prT_d", [14, 128, NTOK], F32)
    T["qT_d"] = dscr("qT_d", [4, 128, NQ], BF16)
    T["kT_d"] = dscr("kT_d", [4, 128, NTOK], BF16)
    T["v_d"] = dscr("v_d", [NTOK, 512], BF16)
    T["gates_d"] = dscr("gates_d", [16, 128, NQ], BF16)
    T["oaT_d"] = dscr("oaT_d", [4, 128, NQ], BF16)
    with contextlib.ExitStack() as st:
        P = Prog(nc, st)
        for n in ["prT", "qT", "kT", "v", "gates", "oaT", "obT", "xnew", "hnT"]:
            T["b_" + n] = P.buf()
        cst = Tl(st.enter_context(nc.sbuf_tensor("cst", [128, 8], F32)), P.buf())
        T["cst"] = cst
        for i, val in enumerate([EPS, 1e-24, 64e-5, 1e-5]):
            P.op("pool", lambda e, i=i, val=val: e.memset(cst.t[:, i:i + 1], val), disjoint=[cst])
        phase_A(P, nc, T)
        if stage >= 2:
            phase_B1(P, nc, T)
        if stage >= 3:
            phase_B2(P, nc, T)
        if stage >= 4:
            phase_C1(P, nc, T)
            phase_C2(P, nc, T)
        P.final_wait("sp")
        P.emit("final")
    return nc


def _consts():
    s = np.arange(128)
    seg = s // 8
    le = (s[:, None] <= s[None, :]).astype(np.float32)
    lt = (s[:, None] < s[None, :]).astype(np.float32)
    same = (seg[:, None] == seg[None, :]).astype(np.float32)
    c = {}
    c["ident"] = np.eye(128, dtype=np.float32)
    c["mtc_p"] = np.concatenate([le, lt], axis=1)
    c["mtc_s"] = np.concatenate([le * same, lt * same], axis=1)
    c["ms_p"] = lt.T.copy()
    c["ms_s"] = (lt * same).T.copy()
    rm = np.ones((128, 4, 128), np.float32)
    rm[:, :, 0] = 0
    c["rm_p"] = rm.reshape(128, 512)
    rm = np.ones((128, 4, 128), np.float32)
    rm[:, :, 0::8] = 0
    c["rm_s"] = rm.reshape(128, 512)
    bo = np.zeros((128, 128), np.float32)
    bo[:64, :64] = 1
    bo[64:, 64:] = 1
    c["bones"] = bo
    sel = np.zeros((128, 4, 8), np.float32)
    for p in range(128):
        for hp in range(4):
            sel[p, hp, 2 * hp + p // 64] = 1
    c["sel8"] = sel.reshape(128, 32)
    c["segsel"] = (seg[:, None] == np.arange(16)[None, :]).astype(np.float32)
    slopes = (2.0 ** (-8.0 * np.arange(1, 5) / 4)).astype(np.float32)
    pp = np.arange(128, dtype=np.float32)
    bt = np.zeros((128, 4, 32), np.float32)
    for h in range(4):
        for o in range(32):
            bt[:, h, o] = slopes[h] * (pp - 128.0 * o)
    c["btab"] = bt.reshape(128, 128)
    c["bown"] = (slopes[None, :] * (pp[:, None] % 8)).astype(np.float32)
    bc = np.zeros((128, 16, 4, 8), np.float32)
    for pg in range(16):
        for h in range(4):
            bc[:, pg, h, :] = (slopes[h] * (pg * 128.0 + pp - 2048.0))[:, None]
    c["bcache"] = bc.reshape(128, 512)
    return c


def make_in_maps(inp):
    f = lambda k: np.asarray(inp[k], np.float32)
    xp = f("x_prompt")
    xs = f("x_sample")
    maps = []
    fm4 = lambda v: np.ascontiguousarray(np.asarray(v, np.float32).reshape(4, 128).T)
    shared = {
        "w_in": np.ascontiguousarray(f("w_in")[0]),
        "g_mix": f("g_mix")[0].reshape(1, D).copy(),
        "mu_fm": np.ascontiguousarray(f("mu_shift").reshape(14, 128).T),
        "w2": np.ascontiguousarray(f("w2")[0]), "a2": np.ascontiguousarray(f("a2")[0]), "g2": np.ascontiguousarray(f("g2")[0]),
        "w0_fm": fm4(f("w0")[0]), "a0_fm": fm4(f("a0")[0]), "kk_fm": fm4(f("k_k")[0]), "ka_fm": fm4(f("k_a")[0]),
        "rk_fm": fm4(f("r_k")[0].reshape(512)),
        "ln_w": f("ln_x_w")[0].reshape(1, 512).copy(), "ln_b": f("ln_x_b")[0].reshape(1, 512).copy(),
        "lam_q1": f("lam_q1")[0].reshape(1, 64).copy(), "lam_k1": f("lam_k1")[0].reshape(1, 64).copy(),
        "lam_q2": f("lam_q2")[0].reshape(1, 64).copy(), "lam_k2": f("lam_k2")[0].reshape(1, 64).copy(),
        "subln_g": f("subln_g")[0].reshape(1, 128).copy(),
        "cache_k": np.asarray(inp["cache_k"], np.float32).reshape(NPOOL * 128, 512),
        "cache_v": np.asarray(inp["cache_v"], np.float32).reshape(NPOOL * 128, 512),
        "w_br_a": np.ascontiguousarray(f("w_br_a")[0]), "w_br_b": np.ascontiguousarray(f("w_br_b")[0]),
        "w_out": np.ascontiguousarray(f("w_out")[0]), "g_ffn": f("g_ffn")[0].reshape(1, D).copy(),
        "w_up": np.ascontiguousarray(f("w_up")[0]), "w_down": np.ascontiguousarray(f("w_down")[0]),
        "g_final": f("g_final").reshape(1, D).copy(),
    }
    shared.update(_consts())
    for c in range(8):
        b, p = divmod(c, 2)
        xa = np.zeros((NTOK, D), np.float32)
        if p == 1:
            xa[0:2048] = xp[b, 0:2048]
        xa[2048:4096] = xp[b, 2048 * p:2048 * p + 2048]
        xa[4096:] = xs[16 * c:16 * c + 16].reshape(128, D)
        m = dict(shared)
        m["x_all"] = xa
        m["st_shift"] = np.ascontiguousarray(f("state_shift")[0, 16 * c:16 * c + 16])
        m["st_rwkv"] = np.ascontiguousarray(f("state_rwkv")[0, 16 * c:16 * c + 16])
        m["ptab"] = np.ascontiguousarray(np.asarray(inp["page_table"], np.int32)[16 * c:16 * c + 16]).reshape(1, 256)
        m["pmask"] = np.full((128, 1), 0.0 if p == 1 else -30000.0, np.float32)
        maps.append(m)
    return maps


_NC_CACHE = {}


def kernel(**inp):
    if "nc" not in _NC_CACHE:
        _NC_CACHE["nc"] = build_program()
    nc = _NC_CACHE["nc"]
    in_maps = make_in_maps(inp)
    res = run_bass_kernel_spmd(nc, in_maps, core_ids=list(range(8)))
    R = res.results
    kp = np.zeros((1, 4, 4096, 4, 128), np.float32)
    vp = np.zeros((1, 4, 4096, 4, 128), np.float32)
    ks = np.zeros((1, 128, 8, 4, 128), np.float32)
    vs = np.zeros((1, 128, 8, 4, 128), np.float32)
    shp = np.zeros((1, 4, D), np.float32)
    shs = np.zeros((1, 128, D), np.float32)
    yp = np.zeros((4, 4096, D), np.float32)
    ys = np.zeros((128, 8, D), np.float32)
    rp = np.zeros((1, 4, 8, 64, 64), np.float32)
    rs = np.zeros((1, 128, 8, 64, 64), np.float32)
    for c in range(8):
        b, p = divmod(c, 2)
        r = R[c]
        kp[0, b, 2048 * p:2048 * p + 2048] = r["kout"][0:2048].reshape(2048, 4, 128)
        vp[0, b, 2048 * p:2048 * p + 2048] = r["vout"][0:2048].reshape(2048, 4, 128)
        ks[0, 16 * c:16 * c + 16] = r["kout"][2048:].reshape(16, 8, 4, 128)
        vs[0, 16 * c:16 * c + 16] = r["vout"][2048:].reshape(16, 8, 4, 128)
        if p == 1:
            shp[0, b] = r["shift_out"][0]
            rp[0, b] = r["rwkv_p_out"]
        shs[0, 16 * c:16 * c + 16] = r["shift_out"][1:17]
        rs[0, 16 * c:16 * c + 16] = r["rwkv_s_out"]
        yp[b, 2048 * p:2048 * p + 2048] = r["yout"][0:2048]
        ys[16 * c:16 * c + 16] = r["yout"][2048:].reshape(16, 8, D)
    return (yp, ys, kp, vp, ks, vs, rp, rs, shp, shs)
```

```python
import numpy as np
import contextlib
import concourse.bass as bass
import concourse.mybir as mybir
from concourse.bass_utils import run_bass_kernel_spmd

F32 = mybir.dt.float32
BF16 = mybir.dt.bfloat16
I32 = mybir.dt.int32
AF = mybir.ActivationFunctionType
ALU = mybir.AluOpType
AX = mybir.AxisListType

NDMA = 12
SAME_ENGINE_SYNC = True

D = 1024
NCOL = 5376
NT_PRIOR = 16
NT_OWN = 16
NTILE = 33
NTOK = NTILE * 128
NQT = 17
NQ = NQT * 128
NPOOL = 2560
EPS = 1e-6


class Buf:
    __slots__ = ("name", "w", "r")

    def __init__(self, name):
        self.name = name
        self.w = {}
        self.r = {}


def _bl(xs):
    return [x.b if hasattr(x, "b") else x for x in xs]


class Tl:
    __slots__ = ("t", "b")

    def __init__(self, t, b):
        self.t = t
        self.b = b


class Prog:
    def __init__(self, nc, stack):
        self.nc = nc
        self.stack = stack
        self.eng = {"pe": nc.tensor, "act": nc.scalar, "dve": nc.vector, "pool": nc.gpsimd, "sp": nc.sync}
        self.sem = {}
        for e in self.eng:
            self.sem[e] = stack.enter_context(nc.semaphore("s_" + e))
        for i in range(NDMA):
            self.sem[("d", i)] = stack.enter_context(nc.semaphore("d%d" % i))
        self.cnt = {e: 0 for e in self.eng}
        self.dcnt = [0] * NDMA
        self.drr = 0
        self.lists = {e: [] for e in self.eng}
        self.waited = {e: {} for e in self.eng}
        self.nbuf = 0

    def buf(self, name=None):
        self.nbuf += 1
        return Buf(name or "b%d" % self.nbuf)

    def _deps(self, e, reads, writes, disjoint=(), xr=()):
        deps = {}
        reads, writes, disjoint, xr = _bl(reads), _bl(writes), _bl(disjoint), _bl(xr)

        def need(k, v):
            if deps.get(k, 0) < v:
                deps[k] = v

        for b in reads:
            for k, v in b.w.items():
                need(k, v)
        for b in writes:
            for k, v in b.w.items():
                need(k, v)
            for k, v in b.r.items():
                need(k, v)
        for b in disjoint:
            for k, v in b.r.items():
                need(k, v)
        for b in xr:
            for k, v in b.w.items():
                need(k, v)
            for k, v in b.r.items():
                need(k, v)
        waits = []
        for k, v in deps.items():
            if k == e and (e == "pe" or not SAME_ENGINE_SYNC):
                continue
            if self.waited[e].get(k, 0) >= v:
                continue
            self.waited[e][k] = v
            waits.append((k, v))
        return waits

    def op(self, e, fn, reads=(), writes=(), signal=True, disjoint=(), xr=()):
        waits = self._deps(e, reads, writes, disjoint, xr)
        reads, writes, disjoint = _bl(reads) + _bl(xr), _bl(writes), _bl(disjoint)
        if signal:
            self.cnt[e] += 1
            tok = (e, self.cnt[e])
            inc = (e, 1)
        else:
            tok = (e, self.cnt[e] + 1)
            inc = None
        for b in reads:
            b.r[tok[0]] = max(b.r.get(tok[0], 0), tok[1])
        for b in writes:
            b.w = {tok[0]: tok[1]}
            b.r = {}
        for b in disjoint:
            b.w[tok[0]] = max(b.w.get(tok[0], 0), tok[1])
        self.lists[e].append((waits, fn, inc))
        return tok

    def dma(self, q, out, in_, reads=(), writes=(), disjoint=(), **kw):
        i = self.drr
        self.drr = (i + 1) % NDMA
        key = ("d", i)
        waits = self._deps(q, reads, writes, disjoint)
        reads, writes, disjoint = _bl(reads), _bl(writes), _bl(disjoint)
        prev = self.dcnt[i]
        if prev > 0 and self.waited[q].get(key, 0) < prev:
            self.waited[q][key] = prev
            waits.append((key, prev))
        self.dcnt[i] += 16
        tok = (key, self.dcnt[i])
        for b in reads:
            b.r[key] = max(b.r.get(key, 0), tok[1])
        for b in writes:
            b.w = {tok[0]: tok[1]}
            b.r = {}
        for b in disjoint:
            b.w[tok[0]] = max(b.w.get(tok[0], 0), tok[1])
        self.lists[q].append((waits, lambda eng: eng.dma_start(out=out, in_=in_, **kw), (key, 16)))
        return tok

    def dma_ind(self, out, in_, idx_ap, reads=(), writes=()):
        q = "pool"
        i = self.drr
        self.drr = (i + 1) % NDMA
        key = ("d", i)
        waits = self._deps(q, reads, writes)
        reads, writes = _bl(reads), _bl(writes)
        prev = self.dcnt[i]
        if prev > 0 and self.waited[q].get(key, 0) < prev:
            self.waited[q][key] = prev
            waits.append((key, prev))
        self.dcnt[i] += 16
        tok = (key, self.dcnt[i])
        for b in reads:
            b.r[key] = max(b.r.get(key, 0), tok[1])
        for b in writes:
            b.w = {tok[0]: tok[1]}
            b.r = {}
        self.lists[q].append((waits, lambda eng: eng.indirect_dma_start(
            out=out, out_offset=None, in_=in_, in_offset=bass.IndirectOffsetOnAxis(ap=idx_ap, axis=0)), (key, 16)))
        return tok

    def final_wait(self, e):
        waits = []
        for i in range(NDMA):
            k, v = ("d", i), self.dcnt[i]
            if v > 0 and self.waited[e].get(k, 0) < v:
                self.waited[e][k] = v
                waits.append((k, v))
        self.lists[e].append((waits, None, None))

    def replay(self, e, eng):
        for waits, fn, inc in self.lists[e]:
            for k, v in waits:
                eng.wait_ge(self.sem[k], v)
            if fn is None:
                continue
            ins = fn(eng)
            if inc is not None:
                ins.then_inc(self.sem[inc[0]], inc[1])
        self.lists[e] = []

    def emit(self, name=None):
        with self.nc.Block(name) as block:
            @block.sync
            def _(eng):
                self.replay("sp", eng)

            @block.scalar
            def _(eng):
                self.replay("act", eng)

            @block.vector
            def _(eng):
                self.replay("dve", eng)

            @block.gpsimd
            def _(eng):
                self.replay("pool", eng)

            @block.tensor
            def _(eng):
                self.replay("pe", eng)


class Rot:
    def __init__(self, P, tiles):
        self.items = [(t, P.buf()) for t in tiles]
        self.i = 0

    def next(self):
        it = self.items[self.i % len(self.items)]
        self.i += 1
        return it


CH_R, CH_K, CH_V, CH_L1, CH_GD, CH_Q, CH_AK, CH_AV, CH_G = 0, 4, 8, 12, 13, 14, 18, 22, 26


def tile_kind(t):
    return "prior" if t < NT_PRIOR else ("own" if t < NT_PRIOR + NT_OWN else "sample")


def phase_A(P, nc, T):
    with contextlib.ExitStack() as st:
        sb = lambda n, s, d: st.enter_context(nc.sbuf_tensor(n, s, d))
        ps = lambda n: st.enter_context(nc.psum_tensor(n, [128, 512], F32))
        win = sb("win", [128, 8, NCOL], BF16)
        b_win = P.buf()
        for kc in range(8):
            for j in range(3):
                P.dma("pool", win[:, kc, j * 1792:(j + 1) * 1792],
                      T["w_in"][kc * 128:(kc + 1) * 128, j * 1792:(j + 1) * 1792], disjoint=[b_win])
        gmix = sb("gmix", [128, D], F32)
        b_gmix = P.buf()
        P.dma("sp", gmix[:], T["g_mix"].broadcast_to([128, D]), writes=[b_gmix])
        mu = sb("mu", [128, 14], F32)
        b_mu = P.buf()
        P.dma("sp", mu[:], T["mu_fm"], writes=[b_mu])
        identb = sb("identb", [128, 128], BF16)
        b_id = P.buf()
        P.op("pool", lambda e: e.memset(identb[:], 0.0), writes=[b_id])
        P.op("pool", lambda e: e.affine_select(out=identb[:], in_=identb[:], pattern=[[-1, 128]],
                                                compare_op=ALU.not_equal, fill=1.0, base=0,
                                                channel_multiplier=1), writes=[b_id])
        carry = sb("carry", [128, 14], F32)
        b_carry = P.buf()
        P.op("pool", lambda e: e.memset(carry[:], 0.0), writes=[b_carry])

        xt_r = Rot(P, [sb("xt%d" % i, [128, D], F32) for i in range(3)])
        xn_r = Rot(P, [sb("xn%d" % i, [128, D], F32) for i in range(2)])
        xnb_r = Rot(P, [sb("xnb%d" % i, [128, D], BF16) for i in range(2)])
        junk = sb("junk", [128, D], BF16)
        b_junk = P.buf()
        ss_r = Rot(P, [sb("ss%d" % i, [128, 4], F32) for i in range(4)])
        xnT_r = Rot(P, [sb("xnT%d" % i, [128, 8, 512], BF16) for i in range(2)])
        pb_r = Rot(P, [sb("pb%d" % i, [128, 520], F32) for i in range(3)])
        prs_r = Rot(P, [sb("prs%d" % i, [128, 512], F32) for i in range(3)])
        dd_r = Rot(P, [sb("dd%d" % i, [128, 512], F32) for i in range(2)])
        stg_r = Rot(P, [sb("stg%d" % i, [128, 512], BF16) for i in range(4)])
        tmf_r = Rot(P, [sb("tmf%d" % i, [128, 512], F32) for i in range(3)])
        tmb_r = Rot(P, [sb("tmb%d" % i, [128, 512], BF16) for i in range(2)])
        ps_fm = Rot(P, [ps("psfm%d" % i) for i in range(4)])
        ps_tm = Rot(P, [ps("pstm%d" % i) for i in range(2)])
        ps_tr = Rot(P, [ps("pstr%d" % i) for i in range(2)])

        ssin = sb("ssin", [16, D], F32)
        ssb = sb("ssb", [16, D], BF16)
        ssT = sb("ssT", [128, 8, 16], BF16)
        b_ssin, b_ssb, b_ssT = P.buf(), P.buf(), P.buf()
        P.dma("sp", ssin[:], T["st_shift"], writes=[b_ssin])
        P.op("dve", lambda e: e.tensor_copy(out=ssb[:], in_=ssin[:]), reads=[b_ssin], writes=[b_ssb])
        pt, b_pt = ps_tr.next()
        ptb = pt[:].bitcast(BF16)
        for kc in range(8):
            P.op("pe", lambda e, kc=kc, ptb=ptb: e.transpose(out=ptb[:, kc * 16:(kc + 1) * 16], in_=ssb[:, kc * 128:(kc + 1) * 128],
                                                    identity=identb[0:16, 0:16]),
                 reads=[b_ssb, b_id], writes=[b_pt] if kc == 0 else (), disjoint=[b_pt] if kc else ())
        P.op("act", lambda e, ptb=ptb: e.copy(out=ssT[:].rearrange("p k b -> p (k b)"), in_=ptb[:, 0:128]), reads=[b_pt], writes=[b_ssT])
        prevb = sb("prevb", [128, 128], F32)
        b_prevb = P.buf()
        pss = sb("pss", [128, 16], F32)
        b_pss = P.buf()

        groups = [(g * 4, 4) for g in range(8)] + [(32, 1)]
        if DEBUG_GROUPS is not None:
            groups = [groups[i] for i in DEBUG_GROUPS]
        for (t0, nt) in groups:
            kind = tile_kind(t0)
            G = nt * 128
            tok0 = t0 * 128
            xnT, b_xnT = xnT_r.next()
            for ti in range(nt):
                t = t0 + ti
                xt, b_xt = xt_r.next()
                P.dma("sp", xt[:], T["x_all"][t * 128:(t + 1) * 128, :], writes=[b_xt])
                ss, b_ss = ss_r.next()
                P.op("act", lambda e, xt=xt, ss=ss: e.activation(out=junk[:], in_=xt[:], func=AF.Square, accum_out=ss[:, 0:1]),
                     reads=[b_xt], writes=[b_junk, b_ss])
                P.op("act", lambda e, ss=ss: e.activation(out=ss[:, 1:2], in_=ss[:, 0:1], func=AF.Sqrt, scale=1.0 / D, bias=T["cst"].t[:, 0:1]),
                     reads=[b_ss, T["cst"]], writes=[b_ss])
                P.op("dve", lambda e, ss=ss: e.reciprocal(out=ss[:, 2:3], in_=ss[:, 1:2]), reads=[b_ss], writes=[b_ss])
                xn, b_xn = xn_r.next()
                P.op("dve", lambda e, xn=xn, xt=xt, ss=ss: e.scalar_tensor_tensor(out=xn[:], in0=xt[:], scalar=ss[:, 2:3], in1=gmix[:],
                                                                                    op0=ALU.mult, op1=ALU.mult),
                     reads=[b_xt, b_ss, b_gmix], writes=[b_xn])
                if t == NT_PRIOR + NT_OWN - 1:
                    P.dma("sp", T["shift_out"][0:1, :], xn[127:128, :], reads=[b_xn])
                if kind == "sample":
                    for b in range(16):
                        P.dma("sp", T["shift_out"][1 + b:2 + b, :], xn[b * 8 + 7:b * 8 + 8, :], reads=[b_xn])
                xnb, b_xnb = xnb_r.next()
                P.op("act", lambda e, xnb=xnb, xn=xn: e.copy(out=xnb[:], in_=xn[:]), reads=[b_xn], writes=[b_xnb])
                pt, b_pt = ps_tr.next()
                ptb = pt[:].bitcast(BF16)
                for kc in range(8):
                    P.op("pe", lambda e, kc=kc, ptb=ptb, xnb=xnb: e.transpose(out=ptb[:, kc * 128:(kc + 1) * 128],
                                                                                in_=xnb[:, kc * 128:(kc + 1) * 128], identity=identb[:]),
                         reads=[b_xnb, b_id], writes=[b_pt] if kc == 0 else (), disjoint=[b_pt] if kc else ())
                P.op("dve", lambda e, ptb=ptb, xnT=xnT, ti=ti: e.tensor_copy(out=xnT[:, :, ti * 128:(ti + 1) * 128],
                                                                               in_=ptb.rearrange("p (k t) -> p k t", k=8)),
                     reads=[b_pt], disjoint=[b_xnT], writes=())
            if kind == "prior" and t0 + nt == NT_PRIOR:
                chunks = list(range(0, 14)) + list(range(CH_AK, CH_AK + 4))
            elif kind == "prior":
                chunks = list(range(CH_K, CH_K + 4)) + list(range(CH_V, CH_V + 4)) + [CH_L1] + list(range(CH_AK, CH_AK + 4))
            else:
                chunks = list(range(0, 14)) + list(range(CH_Q, CH_Q + 8)) + list(range(CH_G, CH_G + 16))
                if DEBUG_CHUNKS is not None:
                    chunks = DEBUG_CHUNKS
            for c in chunks:
                pf, b_pf = ps_fm.next()
                for kc in range(8):
                    P.op("pe", lambda e, pf=pf, kc=kc, c=c, xnT=xnT, G=G: e.matmul(pf[:, 0:G], lhsT=win[:, kc, c * 128:(c + 1) * 128],
                                                                                  rhs=xnT[:, kc, 0:G], start=(kc == 0), stop=(kc == 7)),
                         reads=[b_win, b_xnT], writes=[b_pf] if kc == 0 else (), disjoint=[b_pf] if kc else (), signal=(kc == 7))
                if c < 14:
                    if kind != "sample":
                        pb, b_pb = pb_r.next()
                        P.op("act", lambda e, pb=pb, pf=pf, G=G: e.copy(out=pb[:, 1:1 + G], in_=pf[:, 0:G]), reads=[b_pf], writes=[b_pb])
                        P.op("pool", lambda e, pb=pb, c=c: e.tensor_copy(out=pb[:, 0:1], in_=carry[:, c:c + 1]), reads=[b_carry], disjoint=[b_pb])
                        P.op("pool", lambda e, pb=pb, c=c, G=G: e.tensor_copy(out=carry[:, c:c + 1], in_=pb[:, G:G + 1]), reads=[b_pb], disjoint=[b_carry])
                        cur = pb[:, 1:1 + G]
                        prev = pb[:, 0:G]
                        rd = [b_pb]
                    else:
                        pb, b_pb = pb_r.next()
                        P.op("act", lambda e, pb=pb, pf=pf, G=G: e.copy(out=pb[:, 0:G], in_=pf[:, 0:G]), reads=[b_pf], writes=[b_pb])
                        pf2, b_pf2 = ps_fm.next()
                        for kc in range(8):
                            P.op("pe", lambda e, pf2=pf2, kc=kc, c=c: e.matmul(pf2[:, 0:16], lhsT=win[:, kc, c * 128:(c + 1) * 128],
                                                                                rhs=ssT[:, kc, :], start=(kc == 0), stop=(kc == 7)),
                                 reads=[b_win, b_ssT], writes=[b_pf2] if kc == 0 else (), disjoint=[b_pf2] if kc else (), signal=(kc == 7))
                        P.op("dve", lambda e, pb=pb: e.tensor_copy(out=prevb[:].rearrange("p (b t) -> p b t", t=8)[:, :, 1:8],
                                                                    in_=pb[:, 0:128].rearrange("p (b t) -> p b t", t=8)[:, :, 0:7]),
                             reads=[b_pb], writes=[b_prevb])
                        P.op("dve", lambda e, pf2=pf2: e.tensor_copy(out=prevb[:].rearrange("p (b t) -> p b t", t=8)[:, :, 0:1],
                                                                      in_=pf2[:, 0:16].rearrange("p (b o) -> p b o", o=1)),
                             reads=[b_pf2], disjoint=[b_prevb])
                        cur = pb[:, 0:G]
                        prev = prevb[:, 0:G]
                        rd = [b_pb, b_prevb]
                    dd, b_dd = dd_r.next()
                    P.op("dve", lambda e, dd=dd, prev=prev, cur=cur, G=G: e.tensor_tensor(out=dd[:, 0:G], in0=prev, in1=cur, op=ALU.subtract),
                         reads=rd, writes=[b_dd])
                    prs, b_prs = prs_r.next()
                    P.op("dve", lambda e, prs=prs, dd=dd, cur=cur, c=c, G=G: e.scalar_tensor_tensor(out=prs[:, 0:G], in0=dd[:, 0:G], scalar=mu[:, c:c + 1],
                                                                                                    in1=cur, op0=ALU.mult, op1=ALU.add),
                         reads=rd + [b_dd, b_mu], writes=[b_prs])
                    P.dma("sp", T["prT_d"][c][:, tok0:tok0 + G], prs[:, 0:G], reads=[b_prs], disjoint=[T["b_prT"]])
                elif c < CH_G:
                    stg, b_stg = stg_r.next()
                    P.op("act", lambda e, stg=stg, pf=pf, G=G: e.copy(out=stg[:, 0:G], in_=pf[:, 0:G]), reads=[b_pf], writes=[b_stg])
                    if c < CH_AK:
                        q0 = tok0 - NT_PRIOR * 128
                        P.dma("sp", T["qT_d"][c - CH_Q][:, q0:q0 + G], stg[:, 0:G], reads=[b_stg], disjoint=[T["b_qT"]])
                    else:
                        P.dma("sp", T["kT_d"][c - CH_AK][:, tok0:tok0 + G], stg[:, 0:G], reads=[b_stg], disjoint=[T["b_kT"]])
                else:
                    stg, b_stg = stg_r.next()
                    P.op("act", lambda e, stg=stg, pf=pf, G=G: e.activation(out=stg[:, 0:G], in_=pf[:, 0:G], func=AF.Sigmoid),
                         reads=[b_pf], writes=[b_stg])
                    q0 = tok0 - NT_PRIOR * 128
                    P.dma("sp", T["gates_d"][c - CH_G][:, q0:q0 + G], stg[:, 0:G], reads=[b_stg], disjoint=[T["b_gates"]])
            for ti in range(nt if not DEBUG_NOTM else 0):
                t = t0 + ti
                halves = [("v", CH_AV * 128)] + ([] if kind == "prior" else [("k", CH_AK * 128)])
                if DEBUG_HALF is not None:
                    halves = [h for h in halves if h[0] in DEBUG_HALF]
                for (nm, c0) in halves:
                    pm, b_pm = ps_tm.next()
                    for kc in range(8):
                        P.op("pe", lambda e, pm=pm, kc=kc, c0=c0, xnT=xnT, ti=ti: e.matmul(pm[:, :], lhsT=xnT[:, kc, ti * 128:(ti + 1) * 128],
                                                                                         rhs=win[:, kc, c0:c0 + 512], start=(kc == 0), stop=(kc == 7)),
                             reads=[b_win, b_xnT], writes=[b_pm] if kc == 0 else (), disjoint=[b_pm] if kc else (), signal=(kc == 7))
                    tmf, b_tmf = tmf_r.next()
                    P.op("act", lambda e, tmf=tmf, pm=pm: e.copy(out=tmf[:], in_=pm[:]), reads=[b_pm], writes=[b_tmf])
                    if kind != "prior":
                        q0 = (t - NT_PRIOR) * 128
                        if not DEBUG_NOOUT:
                            P.dma("sp", T[nm + "out"][q0:q0 + 128, :], tmf[:], reads=[b_tmf])
                    if nm == "v":
                        tmb, b_tmb = tmb_r.next()
                        P.op("pool", lambda e, tmb=tmb, tmf=tmf: e.tensor_copy(out=tmb[:], in_=tmf[:]), reads=[b_tmf], writes=[b_tmb])
                        P.dma("sp", T["v_d"][t * 128:(t + 1) * 128, :], tmb[:], reads=[b_tmb], disjoint=[T["b_v"]])
        P.emit("phaseA")


def phase_B1(P, nc, T):
    V, A, PO, PE = "dve", "act", "pool", "pe"
    with contextlib.ExitStack() as st:
        def mk(n, s, d):
            return Tl(st.enter_context(nc.sbuf_tensor("b1_" + n, s, d)), P.buf(n))

        def mkps(n, cols=512):
            return Tl(st.enter_context(nc.psum_tensor("b1_" + n, [128, cols], F32)), P.buf(n))

        def op(e, fn, r=(), w=(), **kw):
            return P.op(e, fn, reads=r, writes=w, **kw)

        cst = T["cst"]
        identb = mk("identb", [128, 128], BF16)
        identf = mk("identf", [128, 128], F32)
        P.dma("pool", identb.t[:], T["ident"], writes=[identb])
        P.dma("sp", identf.t[:], T["ident"], writes=[identf])
        MTc = {"p": mk("MTc_p", [128, 256], BF16), "s": mk("MTc_s", [128, 256], BF16)}
        Ms = {"p": mk("Ms_p", [128, 128], BF16), "s": mk("Ms_s", [128, 128], BF16)}
        RM = {"p": mk("RM_p", [128, 512], F32), "s": mk("RM_s", [128, 512], F32)}
        for k in "ps":
            P.dma("pool", MTc[k].t[:], T["mtc_" + k], writes=[MTc[k]])
            P.dma("pool", Ms[k].t[:], T["ms_" + k], writes=[Ms[k]])
            P.dma("sp", RM[k].t[:], T["rm_" + k], writes=[RM[k]])
        bones = mk("bones", [128, 128], BF16)
        P.dma("pool", bones.t[:], T["bones"], writes=[bones])
        sel8 = mk("sel8", [128, 4, 8], BF16)
        P.dma("pool", sel8.t[:].rearrange("p a b -> p (a b)"), T["sel8"], writes=[sel8])
        segsel = mk("segsel", [128, 16], F32)
        P.dma("sp", segsel.t[:], T["segsel"], writes=[segsel])
        LW = mk("LW", [128, 512], BF16)
        P.dma("pool", LW.t[0:64, :], T["w2"], disjoint=[LW])
        P.dma("pool", LW.t[64:128, :], T["a2"], disjoint=[LW])
        G2 = mk("G2", [128, 512], BF16)
        P.dma("pool", G2.t[:], T["g2"], writes=[G2])
        fm = {}
        for n in ["w0", "a0", "kk", "ka", "rk"]:
            fm[n] = mk("fm_" + n, [128, 4], F32)
            P.dma("sp", fm[n].t[:], T[n + "_fm"], writes=[fm[n]])
        lnw = mk("lnw", [128, 512], F32)
        lnb = mk("lnb", [128, 512], F32)
        P.dma("sp", lnw.t[:], T["ln_w"].broadcast_to([128, 512]), writes=[lnw])
        P.dma("sp", lnb.t[:], T["ln_b"].broadcast_to([128, 512]), writes=[lnb])

        def bc4(x):
            return x.t[:, :].unsqueeze(2).broadcast_to([128, 4, 128])

        PRr = Rot(P, [st.enter_context(nc.sbuf_tensor("b1_PR%d" % i, [128, 14, 128], F32)) for i in range(2)])
        LB = mk("LB", [128, 128], BF16)
        GS = mk("GS", [128, 128], BF16)
        f4 = lambda n: mk(n, [128, 4, 128], F32)
        b4 = lambda n: mk(n, [128, 4, 128], BF16)
        XW, LD, ASIG, CUM, CUMp, EW, EWi, EWp = [f4(n) for n in ["XW", "LD", "ASIG", "CUM", "CUMp", "EW", "EWi", "EWp"]]
        KK, SQ, RNr, KKN, T1, KM, TMPB, XA = [f4(n) for n in ["KK", "SQ", "RNr", "KKN", "T1", "KM", "TMPB", "XA"]]
        KK2, BT, KT, VB, PRD = [b4(n) for n in ["KK2", "BT", "KT", "VB", "PRD"]]
        ART = mk("ART", [128, 4, 2, 128], BF16)
        Atm, Btm, Ktm, Vtm = [mk(n, [128, 512], BF16) for n in ["Atm", "Btm", "Ktm", "Vtm"]]
        COEF = mk("COEF", [128, 8], F32)
        Gtm = mk("Gtm", [128, 512], F32)
        NAXr = Rot(P, [st.enter_context(nc.sbuf_tensor("b1_NAX%d" % i, [128, 256], BF16)) for i in range(6)])
        MAr = Rot(P, [st.enter_context(nc.sbuf_tensor("b1_MA%d" % i, [128, 256], BF16)) for i in range(6)])
        PXr = Rot(P, [st.enter_context(nc.sbuf_tensor("b1_PX%d" % i, [128, 256], BF16)) for i in range(12)])
        Pr_ = Rot(P, [st.enter_context(nc.sbuf_tensor("b1_Pk%d" % i, [128, 128], BF16)) for i in range(12)])
        TTr = Rot(P, [st.enter_context(nc.sbuf_tensor("b1_TT%d" % i, [128, 128], BF16)) for i in range(6)])
        MVr = Rot(P, [st.enter_context(nc.sbuf_tensor("b1_MV%d" % i, [128, 64], BF16)) for i in range(6)])
        UVr = Rot(P, [st.enter_context(nc.sbuf_tensor("b1_UV%d" % i, [128, 64], F32)) for i in range(6)])
        Usbr = Rot(P, [st.enter_context(nc.sbuf_tensor("b1_Usb%d" % i, [128, 64], BF16)) for i in range(6)])
        AhT = st.enter_context(nc.sbuf_tensor("b1_AhT", [128, 4, 128], BF16))
        b_AhT = [P.buf() for _ in range(8)]
        S0 = st.enter_context(nc.sbuf_tensor("b1_S0", [128, 4, 64], F32))
        S0b = st.enter_context(nc.sbuf_tensor("b1_S0b", [128, 4, 64], BF16))
        S0W = st.enter_context(nc.sbuf_tensor("b1_S0W", [128, 4, 64], F32))
        b_S = [P.buf() for _ in range(8)]
        b_Sb = [P.buf() for _ in range(8)]
        b_SW = P.buf()
        op(PO, lambda e: e.memset(S0[:], 0.0), w=b_S)
        op(PO, lambda e: e.memset(S0b[:], 0.0), w=b_Sb)
        Ytm = mk("Ytm", [128, 512], F32)
        O1, O2, O3 = [mk(n, [128, 512], F32) for n in ["O1", "O2", "O3"]]
        ST8 = mk("ST8", [128, 32], F32)
        OAb = mk("OAb", [128, 512], BF16)
        OAT = mk("OAT", [128, 4, 128], BF16)
        S0s = st.enter_context(nc.sbuf_tensor("b1_S0s", [128, 4, 16, 64], F32))
        S0sb = st.enter_context(nc.sbuf_tensor("b1_S0sb", [128, 4, 16, 64], BF16))
        b_S0s = [P.buf() for _ in range(4)]
        b_S0sb = [P.buf() for _ in range(4)]
        Zin = mk("Zin", [64, 16, 128], F32)
        TMPs = mk("TMPs", [128, 16, 64], F32)
        Y1 = mk("Y1", [128, 64], F32)
        U1 = mk("U1", [128, 64], F32)
        UBLK = mk("UBLK", [128, 16, 64], BF16)
        VBLK = mk("VBLK", [128, 16, 64], BF16)
        SOUT = mk("SOUT", [64, 16, 128], F32)
        psr = Rot(P, [st.enter_context(nc.psum_tensor("b1_psr%d" % i, [128, 512], F32)) for i in range(4)])
        ps2 = Rot(P, [st.enter_context(nc.psum_tensor("b1_ps2_%d" % i, [128, 1024], F32)) for i in range(1)])
        psY = mkps("psY")
        psG = mkps("psG")

        def nextps():
            t, b = psr.next()
            return Tl(t, b)

        for hp in range(4):
            for h2 in range(2):
                P.dma("sp", Zin.t[:, :, h2 * 64:(h2 + 1) * 64],
                      T["st_rwkv"][:, 2 * hp + h2, :, :].rearrange("b i j -> i b j"), writes=[Zin] if h2 == 0 else (), disjoint=[Zin] if h2 else ())
            for g in range(2):
                pt = nextps()
                for k in range(8):
                    b = g * 8 + k
                    op(PE, lambda e, pt=pt, k=k, b=b: e.transpose(out=pt.t[:, k * 64:(k + 1) * 64], in_=Zin.t[:, b, :], identity=identf.t[0:64, 0:64]),
                       r=[Zin, identf], w=[pt] if k == 0 else (), disjoint=[pt] if k else ())
                op(A, lambda e, pt=pt, hp=hp, g=g: e.copy(out=S0s[:, hp, g * 8:(g + 1) * 8, :], in_=pt.t[:, 0:512].rearrange("p (b i) -> p b i", i=64)),
                   xr=[pt], disjoint=[b_S0s[hp]])
            op(PO, lambda e, hp=hp: e.tensor_copy(out=S0sb[:, hp, :, :], in_=S0s[:, hp, :, :]), r=[b_S0s[hp]], w=[b_S0sb[hp]])

        def rwkv_tile(t):
            kind = tile_kind(t)
            mk_ = "s" if kind == "sample" else "p"
            full = kind != "prior"
            tok0 = t * 128
            q0 = tok0 - NT_PRIOR * 128
            PRt, b_PR = PRr.next()
            PR = Tl(PRt, b_PR)
            c0 = 0 if full else 4
            c1 = 14 if full else 13
            P.dma("sp", PRt[:, c0:c1, :], T["prT_d"].rearrange("c p t -> p c t")[:, c0:c1, tok0:tok0 + 128],
                  reads=[T["b_prT"]], writes=[PR])
            Rv, Kv, Vv = PRt[:, 0:4, :], PRt[:, 4:8, :], PRt[:, 8:12, :]
            op(A, lambda e: e.activation(out=LB.t[0:64, :], in_=PRt[0:64, 12, :], func=AF.Tanh), r=[PR], w=[LB])
            op(A, lambda e: e.copy(out=LB.t[64:128, :], in_=PRt[64:128, 12, :]), r=[PR], disjoint=[LB])
            pW = nextps()
            pA = nextps()
            for hp in range(4):
                op(PE, lambda e, hp=hp: e.matmul(pW.t[:, hp * 128:(hp + 1) * 128], lhsT=LW.t[0:64, hp * 128:(hp + 1) * 128], rhs=LB.t[0:64, :], start=True, stop=True),
                   r=[LW, LB], w=[pW] if hp == 0 else (), disjoint=[pW] if hp else ())
            for hp in range(4):
                op(PE, lambda e, hp=hp: e.matmul(pA.t[:, hp * 128:(hp + 1) * 128], lhsT=LW.t[64:128, hp * 128:(hp + 1) * 128], rhs=LB.t[64:128, :], start=True, stop=True),
                   r=[LW, LB], w=[pA] if hp == 0 else (), disjoint=[pA] if hp else ())
            v4 = lambda x: x.t[:, 0:512].rearrange("p (a b) -> p a b", a=4) if x.t.shape[-1] == 512 else x.t[:]
            op(V, lambda e: e.tensor_tensor(out=XW.t[:], in0=v4(pW), in1=bc4(fm["w0"]), op=ALU.add), xr=[pW], r=[fm["w0"]], w=[XW])
            op(A, lambda e: e.activation(out=XW.t[:], in_=XW.t[:], func=AF.Sigmoid), w=[XW])
            op(PO, lambda e: e.tensor_scalar(out=LD.t[:], in0=XW.t[:], scalar1=-0.6065306597126334, scalar2=None, op0=ALU.mult), r=[XW], w=[LD])
            op(V, lambda e: e.tensor_tensor(out=XA.t[:], in0=v4(pA), in1=bc4(fm["a0"]), op=ALU.add), xr=[pA], r=[fm["a0"]], w=[XA])
            op(A, lambda e: e.activation(out=ASIG.t[:], in_=XA.t[:], func=AF.Sigmoid), r=[XA], w=[ASIG])
            fl = lambda x: x.t[:].rearrange("p a b -> p (a b)")
            op(V, lambda e: e.tensor_tensor_scan(out=fl(CUM), data0=RM[mk_].t[:], data1=fl(LD), initial=0.0, op0=ALU.mult, op1=ALU.add),
               r=[RM[mk_], LD], w=[CUM])
            op(PO, lambda e: e.tensor_tensor(out=CUMp.t[:], in0=CUM.t[:], in1=LD.t[:], op=ALU.subtract), r=[CUM, LD], w=[CUMp])
            op(A, lambda e: e.activation(out=EW.t[:], in_=CUM.t[:], func=AF.Exp), r=[CUM], w=[EW])
            op(A, lambda e: e.activation(out=EWi.t[:], in_=CUM.t[:], func=AF.Exp, scale=-1.0), r=[CUM], w=[EWi])
            op(A, lambda e: e.activation(out=EWp.t[:], in_=CUMp.t[:], func=AF.Exp), r=[CUMp], w=[EWp])
            op(PO, lambda e: e.tensor_tensor(out=KK.t[:], in0=Kv, in1=bc4(fm["kk"]), op=ALU.mult), r=[PR, fm["kk"]], w=[KK])
            op(PO, lambda e: e.tensor_tensor(out=KK2.t[:], in0=KK.t[:], in1=KK.t[:], op=ALU.mult), r=[KK], w=[KK2])
            pN = nextps()
            op(PE, lambda e: e.matmul(pN.t[:, 0:512], lhsT=bones.t[:], rhs=fl(KK2), start=True, stop=True), r=[bones, KK2], w=[pN])
            op(A, lambda e: e.activation(out=SQ.t[:], in_=v4(pN), func=AF.Sqrt, bias=cst.t[:, 1:2]), xr=[pN], r=[cst], w=[SQ])
            op(V, lambda e: e.reciprocal(out=RNr.t[:], in_=SQ.t[:]), r=[SQ], w=[RNr])
            op(PO, lambda e: e.tensor_tensor(out=KKN.t[:], in0=KK.t[:], in1=RNr.t[:], op=ALU.mult), r=[KK, RNr], w=[KKN])
            op(V, lambda e: e.scalar_tensor_tensor(out=T1.t[:], in0=ASIG.t[:], scalar=-1.0, in1=bc4(fm["ka"]), op0=ALU.add, op1=ALU.mult),
               r=[ASIG, fm["ka"]], w=[T1])
            op(V, lambda e: e.scalar_tensor_tensor(out=KM.t[:], in0=T1.t[:], scalar=1.0, in1=Kv, op0=ALU.add, op1=ALU.mult), r=[T1, PR], w=[KM])
            op(V, lambda e: e.scalar_tensor_tensor(out=ART.t[:, :, 1, :], in0=KKN.t[:], scalar=-1.0, in1=EWp.t[:], op0=ALU.mult, op1=ALU.mult),
               r=[KKN, EWp], disjoint=[ART])
            if full:
                op(PO, lambda e: e.tensor_tensor(out=ART.t[:, :, 0, :], in0=Rv, in1=EW.t[:], op=ALU.mult), r=[PR, EW], disjoint=[ART])
            op(PO, lambda e: e.tensor_tensor(out=TMPB.t[:], in0=KKN.t[:], in1=ASIG.t[:], op=ALU.mult), r=[KKN, ASIG], w=[TMPB])
            op(PO, lambda e: e.tensor_tensor(out=BT.t[:], in0=TMPB.t[:], in1=EWi.t[:], op=ALU.mult), r=[TMPB, EWi], w=[BT])
            op(V, lambda e: e.tensor_tensor(out=KT.t[:], in0=KM.t[:], in1=EWi.t[:], op=ALU.mult), r=[KM, EWi], w=[KT])
            op(A, lambda e: e.copy(out=VB.t[:], in_=Vv), r=[PR], w=[VB])
            for (src_fn, dst, rd) in [(lambda hp: ART.t[:, hp, 1, :], Atm, ART), (lambda hp: BT.t[:, hp, :], Btm, BT),
                                      (lambda hp: KT.t[:, hp, :], Ktm, KT), (lambda hp: VB.t[:, hp, :], Vtm, VB)]:
                pt = nextps()
                ptb = pt.t[:].bitcast(BF16)
                for hp in range(4):
                    op(PE, lambda e, hp=hp, ptb=ptb, src_fn=src_fn: e.transpose(out=ptb[:, hp * 128:(hp + 1) * 128], in_=src_fn(hp), identity=identb.t[:]),
                       r=[rd, identb], w=[pt] if hp == 0 else (), disjoint=[pt] if hp else ())
                op(A, lambda e, ptb=ptb, dst=dst: e.copy(out=dst.t[:], in_=ptb[:, 0:512]), xr=[pt], w=[dst])
            if full:
                op(A, lambda e: e.activation(out=GS.t[:], in_=PRt[:, 13, :], func=AF.Sigmoid), r=[PR], w=[GS])
                op(PE, lambda e: e.matmul(psG.t[:, 0:512], lhsT=GS.t[:], rhs=G2.t[:], start=True, stop=True), r=[GS, G2], w=[psG])
                op(A, lambda e: e.copy(out=Gtm.t[:], in_=psG.t[:, 0:512]), xr=[psG], w=[Gtm])
                op(PO, lambda e: e.tensor_tensor(out=TMPB.t[:], in0=Rv, in1=bc4(fm["rk"]), op=ALU.mult), r=[PR, fm["rk"]], w=[TMPB])
                op(PO, lambda e: e.tensor_tensor(out=PRD.t[:], in0=TMPB.t[:], in1=KM.t[:], op=ALU.mult), r=[TMPB, KM], w=[PRD])
                pC = nextps()
                for hp in range(4):
                    op(PE, lambda e, hp=hp: e.matmul(pC.t[:, 0:8], lhsT=PRD.t[:, hp, :], rhs=sel8.t[:, hp, :], start=(hp == 0), stop=(hp == 3)),
                       r=[PRD, sel8], w=[pC] if hp == 0 else (), disjoint=[pC] if hp else (), signal=(hp == 3))
                op(A, lambda e: e.copy(out=COEF.t[:], in_=pC.t[:, 0:8]), xr=[pC], w=[COEF])
            if kind != "sample":
                op(PO, lambda e: e.tensor_tensor(out=S0W[:], in0=S0[:], in1=EW.t[:, :, 127:128].broadcast_to([128, 4, 64]), op=ALU.mult),
                   r=b_S + [EW], w=[b_SW])
            nsq = 2 if kind == "sample" else 6
            def head_gen(h):
                hp, h2 = divmod(h, 2)
                base = 64 * h2
                hc = slice(h * 64, (h + 1) * 64)
                NAXt, b_NAX = NAXr.next()
                MAt, b_MA = MAr.next()
                p1 = nextps()
                op(PE, lambda e, p1=p1, hp=hp, base=base: e.matmul(p1.t[:, 0:256], lhsT=BT.t[base:base + 64, hp, :],
                                                                  rhs=ART.t[base:base + 64, hp, :, :].rearrange("p a b -> p (a b)"), start=True, stop=True),
                   r=[BT, ART], w=[p1])
                op(V, lambda e, p1=p1, NAXt=NAXt: e.tensor_tensor(out=NAXt[:], in0=p1.t[:, 0:256], in1=MTc[mk_].t[:], op=ALU.mult),
                   xr=[p1], r=[MTc[mk_]], w=[b_NAX])
                p2 = nextps()
                op(PE, lambda e, p2=p2, hp=hp, base=base: e.matmul(p2.t[:, 0:256], lhsT=KT.t[base:base + 64, hp, :],
                                                                  rhs=ART.t[base:base + 64, hp, :, :].rearrange("p a b -> p (a b)"), start=True, stop=True),
                   r=[KT, ART], w=[p2])
                op(V, lambda e, p2=p2, MAt=MAt: e.tensor_tensor(out=MAt[:], in0=p2.t[:, 0:256], in1=MTc[mk_].t[:], op=ALU.mult),
                   xr=[p2], r=[MTc[mk_]], w=[b_MA])
                p3 = nextps()
                op(PE, lambda e, p3=p3, hp=hp, base=base: e.matmul(p3.t[:, 0:128], lhsT=ART.t[base:base + 64, hp, 1, :], rhs=BT.t[base:base + 64, hp, :], start=True, stop=True),
                   r=[ART, BT], w=[p3])
                Pk, b_Pk = Pr_.next()
                op(V, lambda e, p3=p3, Pk=Pk: e.tensor_tensor(out=Pk[:], in0=p3.t[:, 0:128], in1=Ms[mk_].t[:], op=ALU.mult), xr=[p3], r=[Ms[mk_]], w=[b_Pk])
                yield
                PX, b_PX = PXr.next()
                op(PO, lambda e, PX=PX, NAXt=NAXt: e.tensor_copy(out=PX[:, 0:128], in_=NAXt[:, 128:256]), r=[b_NAX], w=[b_PX])
                op(PO, lambda e, PX=PX, NAXt=NAXt: e.tensor_tensor(out=PX[:, 128:256], in0=NAXt[:, 128:256], in1=identb.t[:], op=ALU.add),
                   r=[b_NAX, identb], disjoint=[b_PX])
                for k in range(nsq + 1):
                    last = (k == nsq)
                    if not last:
                        PXn, b_PXn = PXr.next()
                        Pkn, b_Pkn = Pr_.next()
                        pd1 = nextps()
                        op(PE, lambda e, pd1=pd1, Pk=Pk, PX=PX: e.matmul(pd1.t[:, 0:128], lhsT=Pk[:], rhs=PX[:, 0:128], start=True, stop=True),
                           r=[b_Pk, b_PX], w=[pd1])
                        op(A, lambda e, pd1=pd1, PXn=PXn: e.copy(out=PXn[:, 0:128], in_=pd1.t[:, 0:128]), xr=[pd1], w=[b_PXn])
                        pe1 = nextps()
                        op(PE, lambda e, pe1=pe1, Pk=Pk, PX=PX: e.matmul(pe1.t[:, 0:128], lhsT=PX[:, 0:128], rhs=Pk[:], start=True, stop=True),
                           r=[b_Pk, b_PX], w=[pe1])
                        op(A, lambda e, pe1=pe1, Pkn=Pkn: e.copy(out=Pkn[:], in_=pe1.t[:, 0:128]), xr=[pe1], w=[b_Pkn])
                    if k >= 1:
                        pd2 = nextps()
                        op(PE, lambda e, pd2=pd2, Pk=Pk, PX=PX: e.matmul(pd2.t[:, 0:128], lhsT=Pk[:], rhs=PX[:, 128:256], start=True, stop=True),
                           r=[b_Pk, b_PX], w=[pd2])
                        if last:
                            TTt, b_TT = TTr.next()
                            op(V, lambda e, pd2=pd2, PX=PX, TTt=TTt: e.tensor_tensor(out=TTt[:], in0=pd2.t[:, 0:128], in1=PX[:, 128:256], op=ALU.add),
                               xr=[pd2], r=[b_PX], w=[b_TT])
                        else:
                            op(V, lambda e, pd2=pd2, PX=PX, PXn=PXn: e.tensor_tensor(out=PXn[:, 128:256], in0=pd2.t[:, 0:128], in1=PX[:, 128:256], op=ALU.add),
                               xr=[pd2], r=[b_PX], disjoint=[b_PXn])
                    elif not last:
                        op(PO, lambda e, PX=PX, PXn=PXn: e.tensor_copy(out=PXn[:, 128:256], in_=PX[:, 128:256]), r=[b_PX], disjoint=[b_PXn])
                    if not last:
                        PX, b_PX, Pk, b_Pk = PXn, b_PXn, Pkn, b_Pkn
                    yield
                MVt, b_MV = MVr.next()
                UVt, b_UV = UVr.next()
                pm = nextps()
                op(PE, lambda e, pm=pm, MAt=MAt, hc=hc: e.matmul(pm.t[:, 0:64], lhsT=MAt[:, 128:256], rhs=Vtm.t[:, hc], start=True, stop=True),
                   r=[b_MA, Vtm], w=[pm])
                op(A, lambda e, pm=pm, MVt=MVt: e.copy(out=MVt[:], in_=pm.t[:, 0:64]), xr=[pm], w=[b_MV])
                yield
                pu = nextps()
                op(PE, lambda e, pu=pu, TTt=TTt, MVt=MVt: e.matmul(pu.t[:, 0:64], lhsT=TTt[:], rhs=MVt[:], start=True, stop=True), r=[b_TT, b_MV], w=[pu])
                op(A, lambda e, pu=pu, UVt=UVt: e.copy(out=UVt[:], in_=pu.t[:, 0:64]), xr=[pu], w=[b_UV])
                pa = nextps()
                op(PE, lambda e, pa=pa, TTt=TTt, hc=hc, base=base: e.matmul(pa.t[base:base + 64, 0:128], lhsT=Atm.t[:, hc], rhs=TTt[:], start=True, stop=True),
                   r=[Atm, b_TT], w=[pa])
                op(A, lambda e, pa=pa, hp=hp, base=base: e.copy(out=AhT[base:base + 64, hp, :], in_=pa.t[base:base + 64, 0:128]), xr=[pa], w=[b_AhT[h]])
                yield
                Usb, b_Usb = Usbr.next()
                if kind != "sample":
                    pU = nextps()
                    op(PE, lambda e, pU=pU, hp=hp, base=base: e.matmul(pU.t[:, 0:64], lhsT=AhT[base:base + 64, hp, :], rhs=S0b[base:base + 64, hp, :], start=True, stop=True),
                       r=[b_AhT[h], b_Sb[h]], w=[pU])
                    op(V, lambda e, pU=pU, Usb=Usb, UVt=UVt: e.tensor_tensor(out=Usb[:], in0=pU.t[:, 0:64], in1=UVt[:], op=ALU.add), xr=[pU], r=[b_UV], w=[b_Usb])
                    yield
                    if full:
                        first = (h == 0)
                        op(PE, lambda e, hp=hp, base=base, hc=hc: e.matmul(psY.t[:, hc], lhsT=ART.t[base:base + 64, hp, 0, :], rhs=S0b[base:base + 64, hp, :], start=True, stop=False),
                           r=[ART, b_Sb[h]], w=[psY] if first else (), disjoint=() if first else [psY], signal=False)
                        op(PE, lambda e, hc=hc, NAXt=NAXt, Usb=Usb: e.matmul(psY.t[:, hc], lhsT=NAXt[:, 0:128], rhs=Usb[:], start=False, stop=False),
                           r=[b_NAX, b_Usb], disjoint=[psY], signal=False)
                        op(PE, lambda e, hc=hc, MAt=MAt: e.matmul(psY.t[:, hc], lhsT=MAt[:, 0:128], rhs=Vtm.t[:, hc], start=False, stop=True),
                           r=[b_MA, Vtm], disjoint=[psY])
                    pS = nextps()
                    op(PE, lambda e, pS=pS, hc=hc, base=base, Usb=Usb: e.matmul(pS.t[base:base + 64, 0:64], lhsT=Btm.t[:, hc], rhs=Usb[:], start=True, stop=False),
                       r=[Btm, b_Usb], w=[pS], signal=False)
                    op(PE, lambda e, pS=pS, hc=hc, base=base: e.matmul(pS.t[base:base + 64, 0:64], lhsT=Ktm.t[:, hc], rhs=Vtm.t[:, hc], start=False, stop=True),
                       r=[Ktm, Vtm], disjoint=[pS])
                    op(V, lambda e, pS=pS, hp=hp, base=base: e.scalar_tensor_tensor(out=S0[base:base + 64, hp, :], in0=pS.t[base:base + 64, 0:64],
                                                                                     scalar=EW.t[base:base + 64, hp, 127:128], in1=S0W[base:base + 64, hp, :],
                                                                                     op0=ALU.mult, op1=ALU.add),
                       xr=[pS], r=[EW, b_SW], w=[b_S[h]])
                    op(A, lambda e, hp=hp, base=base: e.copy(out=S0b[base:base + 64, hp, :], in_=S0[base:base + 64, hp, :]), r=[b_S[h]], w=[b_Sb[h]])
                else:
                    pq, b_pq = ps2.next()
                    for half in range(2):
                        op(PE, lambda e, half=half, hp=hp, base=base, pq=pq: e.matmul(pq[:, half * 512:(half + 1) * 512], lhsT=AhT[base:base + 64, hp, :],
                                                                                        rhs=S0sb[base:base + 64, hp, half * 8:(half + 1) * 8, :].rearrange("p b i -> p (b i)"),
                                                                                        start=True, stop=True),
                           r=[b_AhT[h], b_S0sb[hp]], w=[b_pq] if half == 0 else (), disjoint=[b_pq] if half else ())
                    for half in range(2):
                        op(V, lambda e, half=half, pq=pq: e.tensor_tensor(out=TMPs.t[:, half * 8:(half + 1) * 8, :],
                                                                            in0=pq[:, half * 512:(half + 1) * 512].rearrange("p (b i) -> p b i", i=64),
                                                                            in1=segsel.t[:, half * 8:(half + 1) * 8].unsqueeze(2).broadcast_to([128, 8, 64]), op=ALU.mult),
                           xr=[b_pq], r=[segsel], w=[TMPs] if half == 0 else (), disjoint=[TMPs] if half else ())
                    op(V, lambda e: e.tensor_reduce(out=U1.t[:], in_=TMPs.t[:].rearrange("p b i -> p i b"), axis=AX.X, op=ALU.add), r=[TMPs], w=[U1])
                    op(V, lambda e, Usb=Usb, UVt=UVt: e.tensor_tensor(out=Usb[:], in0=U1.t[:], in1=UVt[:], op=ALU.add), r=[U1, b_UV], w=[b_Usb])
                    pq, b_pq = ps2.next()
                    for half in range(2):
                        op(PE, lambda e, half=half, hp=hp, base=base, pq=pq: e.matmul(pq[:, half * 512:(half + 1) * 512], lhsT=ART.t[base:base + 64, hp, 0, :],
                                                                                        rhs=S0sb[base:base + 64, hp, half * 8:(half + 1) * 8, :].rearrange("p b i -> p (b i)"),
                                                                                        start=True, stop=True),
                           r=[ART, b_S0sb[hp]], w=[b_pq] if half == 0 else (), disjoint=[b_pq] if half else ())
                    for half in range(2):
                        op(V, lambda e, half=half, pq=pq: e.tensor_tensor(out=TMPs.t[:, half * 8:(half + 1) * 8, :],
                                                                            in0=pq[:, half * 512:(half + 1) * 512].rearrange("p (b i) -> p b i", i=64),
                                                                            in1=segsel.t[:, half * 8:(half + 1) * 8].unsqueeze(2).broadcast_to([128, 8, 64]), op=ALU.mult),
                           xr=[b_pq], r=[segsel], w=[TMPs] if half == 0 else (), disjoint=[TMPs] if half else ())
                    op(V, lambda e: e.tensor_reduce(out=Y1.t[:], in_=TMPs.t[:].rearrange("p b i -> p i b"), axis=AX.X, op=ALU.add), r=[TMPs], w=[Y1])
                    py = nextps()
                    op(PE, lambda e, py=py, NAXt=NAXt, Usb=Usb: e.matmul(py.t[:, 0:64], lhsT=NAXt[:, 0:128], rhs=Usb[:], start=True, stop=False),
                       r=[b_NAX, b_Usb], w=[py], signal=False)
                    op(PE, lambda e, py=py, MAt=MAt, hc=hc: e.matmul(py.t[:, 0:64], lhsT=MAt[:, 0:128], rhs=Vtm.t[:, hc], start=False, stop=True),
                       r=[b_MA, Vtm], disjoint=[py])
                    op(V, lambda e, py=py, hc=hc: e.tensor_tensor(out=Ytm.t[:, hc], in0=py.t[:, 0:64], in1=Y1.t[:], op=ALU.add), xr=[py], r=[Y1], disjoint=[Ytm])
                    op(PO, lambda e, Usb=Usb: e.tensor_tensor(out=UBLK.t[:], in0=Usb[:, :].unsqueeze(1).broadcast_to([128, 16, 64]),
                                                               in1=segsel.t[:, :].unsqueeze(2).broadcast_to([128, 16, 64]), op=ALU.mult),
                       r=[b_Usb, segsel], w=[UBLK])
                    op(PO, lambda e, hc=hc: e.tensor_tensor(out=VBLK.t[:], in0=Vtm.t[:, hc].unsqueeze(1).broadcast_to([128, 16, 64]),
                                                             in1=segsel.t[:, :].unsqueeze(2).broadcast_to([128, 16, 64]), op=ALU.mult),
                       r=[Vtm, segsel], w=[VBLK])
                    pq, b_pq = ps2.next()
                    for half in range(2):
                        op(PE, lambda e, half=half, hc=hc, base=base, pq=pq: e.matmul(pq[base:base + 64, half * 512:(half + 1) * 512], lhsT=Btm.t[:, hc],
                                                                                        rhs=UBLK.t[:, half * 8:(half + 1) * 8, :].rearrange("p b i -> p (b i)"), start=True, stop=False),
                           r=[Btm, UBLK], w=[b_pq] if half == 0 else (), disjoint=[b_pq] if half else (), signal=False)
                        op(PE, lambda e, half=half, hc=hc, base=base, pq=pq: e.matmul(pq[base:base + 64, half * 512:(half + 1) * 512], lhsT=Ktm.t[:, hc],
                                                                                        rhs=VBLK.t[:, half * 8:(half + 1) * 8, :].rearrange("p b i -> p (b i)"), start=False, stop=True),
                           r=[Ktm, VBLK], disjoint=[b_pq])
                    op(V, lambda e, hp=hp, base=base, pq=pq: e.tensor_tensor(out=S0s[base:base + 64, hp, :, :], in0=pq[base:base + 64, :].rearrange("p (b i) -> p b i", i=64),
                                                                               in1=S0s[base:base + 64, hp, :, :], op=ALU.add),
                       xr=[b_pq], w=[P.buf()], r=[b_S0s[hp]])
                    op(PO, lambda e, hp=hp, base=base: e.tensor_tensor(out=S0s[base:base + 64, hp, :, :], in0=S0s[base:base + 64, hp, :, :],
                                                                        in1=EW.t[base:base + 64, hp, :].rearrange("p (b t) -> p b t", t=8)[:, :, 7:8].broadcast_to([64, 16, 64]),
                                                                        op=ALU.mult),
                       r=[EW], w=[b_S0s[hp]] if h2 == 1 else (), disjoint=[b_S0s[hp]] if h2 == 0 else ())
            for g in range(2):
                alive = [head_gen(h) for h in range(4 * g, 4 * g + 4)]
                while alive:
                    for gn in list(alive):
                        try:
                            next(gn)
                        except StopIteration:
                            alive.remove(gn)
            if full:
                if kind != "sample":
                    op(A, lambda e: e.copy(out=Ytm.t[:], in_=psY.t[:, 0:512]), xr=[psY], w=[Ytm])
                y3 = lambda x: x.t[:, 0:512].rearrange("p (h i) -> p h i", i=64)
                b8 = lambda ap: ap.unsqueeze(2).broadcast_to([128, 8, 64])
                op(V, lambda e: e.tensor_reduce(out=ST8.t[:, 0:8], in_=y3(Ytm), axis=AX.X, op=ALU.add), r=[Ytm], w=[ST8])
                op(PO, lambda e: e.tensor_scalar(out=ST8.t[:, 8:16], in0=ST8.t[:, 0:8], scalar1=-1.0 / 64, scalar2=None, op0=ALU.mult), w=[ST8])
                op(PO, lambda e: e.tensor_tensor(out=y3(O1), in0=y3(Ytm), in1=b8(ST8.t[:, 8:16]), op=ALU.add), r=[Ytm, ST8], w=[O1])
                op(PO, lambda e: e.tensor_tensor(out=O2.t[:], in0=O1.t[:], in1=O1.t[:], op=ALU.mult), r=[O1], w=[O2])
                op(V, lambda e: e.tensor_reduce(out=ST8.t[:, 16:24], in_=y3(O2), axis=AX.X, op=ALU.add), r=[O2], w=[ST8])
                op(A, lambda e: e.activation(out=ST8.t[:, 24:32], in_=ST8.t[:, 16:24], func=AF.Sqrt, scale=1.0 / 64, bias=cst.t[:, 2:3]), r=[cst], w=[ST8])
                op(V, lambda e: e.reciprocal(out=ST8.t[:, 16:24], in_=ST8.t[:, 24:32]), w=[ST8])
                op(PO, lambda e: e.tensor_tensor(out=y3(O2), in0=y3(O1), in1=b8(ST8.t[:, 16:24]), op=ALU.mult), r=[O1, ST8], w=[O2])
                op(PO, lambda e: e.tensor_tensor(out=O1.t[:], in0=O2.t[:], in1=lnw.t[:], op=ALU.mult), r=[O2, lnw], w=[O1])
                op(PO, lambda e: e.tensor_tensor(out=O2.t[:], in0=O1.t[:], in1=lnb.t[:], op=ALU.add), r=[O1, lnb], w=[O2])
                op(V, lambda e: e.tensor_tensor(out=y3(O3), in0=Vtm.t[:, 0:512].rearrange("p (h i) -> p h i", i=64), in1=b8(COEF.t[:, 0:8]), op=ALU.mult),
                   r=[Vtm, COEF], w=[O3])
                op(V, lambda e: e.tensor_tensor(out=O1.t[:], in0=O2.t[:], in1=O3.t[:], op=ALU.add), r=[O2, O3], w=[O1])
                op(V, lambda e: e.tensor_tensor(out=OAb.t[:], in0=O1.t[:], in1=Gtm.t[:], op=ALU.mult), r=[O1, Gtm], w=[OAb])
                pt = nextps()
                ptb = pt.t[:].bitcast(BF16)
                for hp in range(4):
                    op(PE, lambda e, hp=hp, ptb=ptb: e.transpose(out=ptb[:, hp * 128:(hp + 1) * 128], in_=OAb.t[:, hp * 128:(hp + 1) * 128], identity=identb.t[:]),
                       r=[OAb, identb], w=[pt] if hp == 0 else (), disjoint=[pt] if hp else ())
                op(A, lambda e, ptb=ptb: e.copy(out=OAT.t[:].rearrange("p a b -> p (a b)"), in_=ptb[:, 0:512]), xr=[pt], w=[OAT])
                P.dma("sp", T["oaT_d"].rearrange("c p t -> p c t")[:, :, q0:q0 + 128], OAT.t[:], reads=[OAT], disjoint=[T["b_oaT"]])

        tiles = list(range(NTILE))
        if DEBUG_TILES is not None:
            tiles = DEBUG_TILES
        for t in tiles:
            rwkv_tile(t)
        for hp in range(4):
            pt = nextps()
            op(PE, lambda e, pt=pt, hp=hp: e.transpose(out=pt.t[0:64, 0:128], in_=S0[:, hp, :], identity=identf.t[:]),
               r=[b_S[2 * hp], b_S[2 * hp + 1], identf], w=[pt])
            op(A, lambda e, pt=pt, hp=hp: e.copy(out=SOUT.t[:, hp, :], in_=pt.t[0:64, 0:128]), xr=[pt], disjoint=[SOUT])
        for h2 in range(2):
            P.dma("sp", T["rwkv_p_out"].rearrange("(a h) i j -> h i a j", h=2)[h2], SOUT.t[:, 0:4, h2 * 64:(h2 + 1) * 64], reads=[SOUT])
        for hp in range(4):
            for g in range(4):
                pt = nextps()
                for k in range(4):
                    b = g * 4 + k
                    op(PE, lambda e, pt=pt, k=k, b=b, hp=hp: e.transpose(out=pt.t[0:64, k * 128:(k + 1) * 128], in_=S0s[:, hp, b, :], identity=identf.t[:]),
                       r=[b_S0s[hp], identf], w=[pt] if k == 0 else (), disjoint=[pt] if k else ())
                op(A, lambda e, pt=pt, g=g: e.copy(out=SOUT.t[:, g * 4:(g + 1) * 4, :], in_=pt.t[0:64, 0:512].rearrange("p (b c) -> p b c", c=128)),
                   xr=[pt], w=[SOUT] if g == 0 else (), disjoint=[SOUT] if g else ())
            for h2 in range(2):
                P.dma("sp", T["rwkv_s_out"][:, 2 * hp + h2, :, :].rearrange("b i j -> i b j"),
                      SOUT.t[:, :, h2 * 64:(h2 + 1) * 64], reads=[SOUT])
        P.emit("phaseB1")


def phase_B2(P, nc, T):
    V, A, PO, PE = "dve", "act", "pool", "pe"
    with contextlib.ExitStack() as st:
        def mk(n, s, d):
            return Tl(st.enter_context(nc.sbuf_tensor("b2_" + n, s, d)), P.buf(n))

        def mkps(n, cols=512):
            return Tl(st.enter_context(nc.psum_tensor("b2_" + n, [128, cols], F32)), P.buf(n))

        def op(e, fn, r=(), w=(), **kw):
            return P.op(e, fn, reads=r, writes=w, **kw)

        cst = T["cst"]
        identb = mk("identb", [128, 128], BF16)
        identf = mk("identf", [128, 128], F32)
        P.dma("pool", identb.t[:], T["ident"], writes=[identb])
        P.dma("sp", identf.t[:], T["ident"], writes=[identf])
        cm_p = mk("cm_p", [128, 128], BF16)
        cm_s = mk("cm_s", [128, 128], BF16)
        P.dma("pool", cm_p.t[:], T["mtc_p"][:, 0:128], writes=[cm_p])
        P.dma("pool", cm_s.t[:], T["mtc_s"][:, 0:128], writes=[cm_s])
        KT = mk("KT", [128, 4, NTOK], BF16)
        QT = mk("QT", [128, 4, NQ], BF16)
        Vx = mk("Vx", [128, NTILE, 4, 130], BF16)
        for h in range(4):
            P.dma("sp", KT.t[:, h, :], T["kT_d"][h], reads=[T["b_kT"]], disjoint=[KT])
            P.dma("sp", QT.t[:, h, :], T["qT_d"][h], reads=[T["b_qT"]], disjoint=[QT])
            for t0 in range(0, NTILE, 8):
                t1 = min(NTILE, t0 + 8)
                P.dma("sp", Vx.t[:, t0:t1, h, 0:128], T["v_d"].rearrange("(t p) (h e) -> h p t e", p=128, h=4)[h][:, t0:t1, :], reads=[T["b_v"]], disjoint=[Vx])
        op(PO, lambda e: e.memset(Vx.t[:, :, :, 128:130], 1.0), disjoint=[Vx])
        btab = mk("btab", [128, 4, 32], F32)
        btabp = mk("btabp", [128, 4, 32], F32)
        pmask = mk("pmask", [128, 1], F32)
        P.dma("sp", btab.t[:].rearrange("p a b -> p (a b)"), T["btab"], writes=[btab])
        P.dma("sp", pmask.t[:], T["pmask"], writes=[pmask])
        op(V, lambda e: e.tensor_scalar(out=btabp.t[:], in0=btab.t[:], scalar1=pmask.t[:, 0:1], scalar2=None, op0=ALU.add), r=[btab, pmask], w=[btabp])
        bown = mk("bown", [128, 4], F32)
        P.dma("sp", bown.t[:], T["bown"], writes=[bown])
        bcache = mk("bcache", [128, 16, 32], F32)
        P.dma("sp", bcache.t[:].rearrange("p a b -> p (a b)"), T["bcache"], writes=[bcache])
        lv = mk("lv", [128, 4, 64], F32)
        for i, n in enumerate(["lam_q1", "lam_k1", "lam_q2", "lam_k2"]):
            P.dma("sp", lv.t[:, i, :], T[n].broadcast_to([128, 64]), disjoint=[lv])
        lt = mk("lt", [128, 2, 64], F32)
        ls = mk("ls", [128, 8], F32)
        op(V, lambda e: e.tensor_tensor(out=lt.t[:, 0, :], in0=lv.t[:, 0, :], in1=lv.t[:, 1, :], op=ALU.mult), r=[lv], disjoint=[lt])
        op(V, lambda e: e.tensor_tensor(out=lt.t[:, 1, :], in0=lv.t[:, 2, :], in1=lv.t[:, 3, :], op=ALU.mult), r=[lv], disjoint=[lt])
        op(V, lambda e: e.tensor_reduce(out=ls.t[:, 0:2], in_=lt.t[:], axis=AX.X, op=ALU.add), r=[lt], w=[ls])
        op(A, lambda e: e.activation(out=ls.t[:, 2:4], in_=ls.t[:, 0:2], func=AF.Exp), w=[ls])
        op(V, lambda e: e.tensor_tensor(out=ls.t[:, 4:5], in0=ls.t[:, 2:3], in1=ls.t[:, 3:4], op=ALU.subtract), w=[ls])
        op(V, lambda e: e.tensor_scalar(out=ls.t[:, 5:6], in0=ls.t[:, 4:5], scalar1=0.2, scalar2=-1.0, op0=ALU.add, op1=ALU.mult), w=[ls])
        sg = mk("sg", [128, 128], F32)
        P.dma("sp", sg.t[:], T["subln_g"].broadcast_to([128, 128]), writes=[sg])
        op(V, lambda e: e.tensor_scalar(out=sg.t[:], in0=sg.t[:], scalar1=0.8, scalar2=None, op0=ALU.mult), w=[sg])

        psr = Rot(P, [st.enter_context(nc.psum_tensor("b2_psr%d" % i, [128, 1024], F32)) for i in range(2)])
        psO = [mkps("psO%d" % i) for i in range(3)]
        pstr = mkps("pstr")

        def nextps():
            t, b = psr.next()
            return Tl(t, b)

        ETr = Rot(P, [st.enter_context(nc.sbuf_tensor("b2_ET%d" % i, [128, 256], BF16)) for i in range(3)])
        T2 = mk("T2", [128, 128], F32)
        ATT = mk("ATT", [128, 128], F32)
        junk = mk("junk", [128, 128], BF16)
        sc = mk("sc", [128, 8], F32)
        OBb = mk("OBb", [128, 512], BF16)
        OBT = mk("OBT", [128, 4, 128], BF16)

        def combine(h, o0, o0c, o1, o1c, xr0, xr1):
            op(V, lambda e: e.reciprocal(out=sc.t[:, 0:1], in_=o0[:, 128:129]), xr=[xr0], w=[sc])
            op(V, lambda e: e.reciprocal(out=sc.t[:, 1:2], in_=o1[:, 128:129]), xr=[xr1], w=[sc])
            op(V, lambda e: e.tensor_tensor(out=sc.t[:, 2:3], in0=sc.t[:, 1:2], in1=ls.t[:, 5:6], op=ALU.mult), r=[ls], w=[sc])
            op(V, lambda e: e.tensor_scalar(out=T2.t[:], in0=o1[:, 0:128], scalar1=sc.t[:, 2:3], scalar2=None, op0=ALU.mult), xr=[xr1], r=[sc], w=[T2])
            op(V, lambda e: e.scalar_tensor_tensor(out=ATT.t[:], in0=o0[:, 0:128], scalar=sc.t[:, 0:1], in1=T2.t[:], op0=ALU.mult, op1=ALU.add),
               xr=[xr0], r=[sc, T2], w=[ATT])
            op(A, lambda e: e.activation(out=junk.t[:], in_=ATT.t[:], func=AF.Square, accum_out=sc.t[:, 3:4]), r=[ATT], w=[junk, sc])
            op(A, lambda e: e.activation(out=sc.t[:, 4:5], in_=sc.t[:, 3:4], func=AF.Sqrt, scale=1.0 / 128, bias=cst.t[:, 3:4]), r=[cst], w=[sc])
            op(V, lambda e: e.reciprocal(out=sc.t[:, 5:6], in_=sc.t[:, 4:5]), w=[sc])
            op(V, lambda e: e.scalar_tensor_tensor(out=OBb.t[:, h * 128:(h + 1) * 128], in0=ATT.t[:], scalar=sc.t[:, 5:6], in1=sg.t[:], op0=ALU.mult, op1=ALU.mult),
               r=[ATT, sc, sg], disjoint=[OBb])

        def finish_tile(qi):
            ptb = pstr.t[:].bitcast(BF16)
            for h in range(4):
                op(PE, lambda e, h=h: e.transpose(out=ptb[:, h * 128:(h + 1) * 128], in_=OBb.t[:, h * 128:(h + 1) * 128], identity=identb.t[:]),
                   r=[OBb, identb], w=[pstr] if h == 0 else (), disjoint=[pstr] if h else ())
            op(A, lambda e: e.copy(out=OBT.t[:].rearrange("p a b -> p (a b)"), in_=ptb[:, 0:512]), xr=[pstr], w=[OBT])
            P.dma("sp", T["obT_d"].rearrange("c p t -> p c t")[:, :, qi * 128:(qi + 1) * 128], OBT.t[:], reads=[OBT], disjoint=[T["b_obT"]])

        qtiles = list(range(NT_OWN)) if DEBUG_QT is None else DEBUG_QT
        if DEBUG_B2 < 1:
            qtiles = []
        for qi in qtiles:
            qg = NT_PRIOR + qi
            for h in range(4):
                o0, o1 = psO[0], psO[1]
                slope_h = 2.0 ** (-2.0 * (h + 1))
                kts = [kt for kt in range(0, qg + 1) if slope_h * (128 * (qg - kt) - 127) <= 80.0]
                for ii, kt in enumerate(kts):
                    pS = nextps()
                    for c in range(2):
                        op(PE, lambda e, c=c, kt=kt, pS=pS, h=h, qi=qi: e.matmul(pS.t[:, c * 512:c * 512 + 128], lhsT=KT.t[c * 64:(c + 1) * 64, h, kt * 128:(kt + 1) * 128],
                                                                     rhs=QT.t[c * 64:(c + 1) * 64, h, qi * 128:(qi + 1) * 128], start=True, stop=True),
                           r=[KT, QT], w=[pS] if c == 0 else (), disjoint=[pS] if c else ())
                    if DEBUG_B2 < 2:
                        continue
                    ET, b_ET = ETr.next()
                    o = qg - kt
                    bt = btabp if kt < NT_PRIOR else btab
                    op(A, lambda e, pS=pS, ET=ET, bt=bt, o=o, h=h: e.activation(out=ET[:].rearrange("p (c q) -> p c q", c=2), in_=pS.t[:, :].rearrange("p (c x) -> p c x", c=2)[:, :, 0:128],
                                                                           func=AF.Exp, scale=0.125, bias=bt.t[:, h, o:o + 1]),
                       xr=[pS], r=[bt], w=[b_ET])
                    if kt == qg:
                        op(V, lambda e, ET=ET: e.tensor_tensor(out=ET[:].rearrange("p (c q) -> p c q", c=2), in0=ET[:].rearrange("p (c q) -> p c q", c=2),
                                                                in1=cm_p.t[:, :].unsqueeze(1).broadcast_to([128, 2, 128]), op=ALU.mult), r=[cm_p], w=[b_ET])
                    first, last = (ii == 0), (ii == len(kts) - 1)
                    if DEBUG_B2 < 3:
                        continue
                    op(PE, lambda e, ET=ET, kt=kt, first=first, last=last, h=h, o0=o0: e.matmul(o0.t[:, 0:129], lhsT=ET[:, 0:128], rhs=Vx.t[:, kt, h, 0:129], start=first, stop=last),
                       r=[b_ET, Vx], w=[o0] if first else (), disjoint=() if first else [o0], signal=last)
                    op(PE, lambda e, ET=ET, kt=kt, first=first, last=last, h=h, o1=o1: e.matmul(o1.t[:, 0:129], lhsT=ET[:, 128:256], rhs=Vx.t[:, kt, h, 0:129], start=first, stop=last),
                       r=[b_ET, Vx], w=[o1] if first else (), disjoint=() if first else [o1], signal=last)
                if DEBUG_B2 >= 4:
                    combine(h, o0.t, None, o1.t, None, o0, o1)
            if DEBUG_B2 >= 5:
                finish_tile(qi)

        if DEBUG_SAMPLE_ATT:
            zb = mk("zb", [128, 512], BF16)
            op(PO, lambda e: e.memset(zb.t[:], 0.0), w=[zb])
            slot = {}
            for h in range(4):
                for c in range(2):
                    i = h * 2 + c
                    slot[(h, c)] = (i // 3, (i % 3) * 130)
            for bnk in range(3):
                op(PE, lambda e, bnk=bnk: e.matmul(psO[bnk].t[:, 0:512], lhsT=zb.t[:, 0:128], rhs=zb.t[:, 0:512], start=True, stop=False), r=[zb], w=[psO[bnk]])
            ptb_i = mk("ptab", [128, 256], I32)
            P.dma("sp", ptb_i.t[:], T["ptab"].broadcast_to([128, 256]), writes=[ptb_i])
            pcol = mk("pcol", [128, 1], I32)
            op(PO, lambda e: e.iota(pcol.t[:], pattern=[[0, 1]], base=0, channel_multiplier=1), w=[pcol])
            pcolf = mk("pcolf", [128, 1], F32)
            op(V, lambda e: e.tensor_copy(out=pcolf.t[:], in_=pcol.t[:]), r=[pcol], w=[pcolf])
            idx = mk("idx", [128, 256], I32)
            op(V, lambda e: e.tensor_scalar(out=idx.t[:], in0=ptb_i.t[:], scalar1=128.0, scalar2=pcolf.t[:, 0:1], op0=ALU.mult, op1=ALU.add), r=[ptb_i, pcolf], w=[idx])
            Kpg = Rot(P, [st.enter_context(nc.sbuf_tensor("b2_Kpg%d" % i, [128, 512], F32)) for i in range(3)])
            Vpg = Rot(P, [st.enter_context(nc.sbuf_tensor("b2_Vpg%d" % i, [128, 512], F32)) for i in range(3)])
            KcTr = Rot(P, [st.enter_context(nc.sbuf_tensor("b2_KcT%d" % i, [128, 4, 128], BF16)) for i in range(2)])
            Vpxr = Rot(P, [st.enter_context(nc.sbuf_tensor("b2_Vpx%d" % i, [128, 4, 130], BF16)) for i in range(2)])
            for (vt, vb) in Vpxr.items:
                op(PO, lambda e, vt=vt: e.memset(vt[:], 1.0), w=[vb])
            XB = mk("XB", [128, 2, 32], F32)
            ETz = [mk("ETz%d" % i, [128, 4, 2, 128], BF16) for i in range(3)]
            ETz_b = [None] * 3
            for z in ETz:
                op(PO, lambda e, z=z: e.memset(z.t[:], 0.0), w=[z])
            ez = 0
            sq0 = NT_OWN * 128
            for b in range(16):
                for pg in range(16):
                    col = b * 16 + pg
                    kp, b_kp = Kpg.next()
                    vp, b_vp = Vpg.next()
                    P.dma_ind(kp[:, :], T["cache_k"], idx.t[:, col:col + 1], reads=[idx], writes=[b_kp])
                    P.dma_ind(vp[:, :], T["cache_v"], idx.t[:, col:col + 1], reads=[idx], writes=[b_vp])
                    pT = nextps()
                    for h in range(4):
                        op(PE, lambda e, h=h, kp=kp, pT=pT: e.transpose(out=pT.t[:, h * 128:(h + 1) * 128], in_=kp[:, h * 128:(h + 1) * 128], identity=identf.t[:]),
                           r=[b_kp, identf], w=[pT] if h == 0 else (), disjoint=[pT] if h else ())
                    kc, b_kc = KcTr.next()
                    op(A, lambda e, kc=kc, pT=pT: e.copy(out=kc[:].rearrange("p a b -> p (a b)"), in_=pT.t[:, 0:512]), xr=[pT], w=[b_kc])
                    vx, b_vx = Vpxr.next()
                    op(V, lambda e, vx=vx, vp=vp: e.tensor_copy(out=vx[:, :, 0:128], in_=vp[:, :].rearrange("p (h e) -> p h e", h=4)), r=[b_vp], w=[b_vx])
                    pS = nextps()
                    for h in range(4):
                        for c in range(2):
                            first = (h == 0 and c == 0)
                            op(PE, lambda e, h=h, c=c, kc=kc, pS=pS, b=b: e.matmul(pS.t[:, c * 512 + h * 8:c * 512 + h * 8 + 8], lhsT=kc[c * 64:(c + 1) * 64, h, :],
                                                                               rhs=QT.t[c * 64:(c + 1) * 64, h, sq0 + b * 8:sq0 + b * 8 + 8], start=True, stop=True),
                               r=[b_kc, QT], w=[pS] if first else (), disjoint=() if first else [pS])
                    op(V, lambda e, pS=pS, pg=pg: e.scalar_tensor_tensor(out=XB.t[:], in0=pS.t[:, :].rearrange("p (c x) -> p c x", c=2)[:, :, 0:32], scalar=0.125,
                                                                           in1=bcache.t[:, pg, :].unsqueeze(1).broadcast_to([128, 2, 32]), op0=ALU.mult, op1=ALU.add),
                       xr=[pS], r=[bcache], w=[XB])
                    z = ETz[ez % 3]
                    zprev = ETz_b[ez % 3]
                    if zprev is not None and zprev != b:
                        op(PO, lambda e, z=z, zprev=zprev: e.memset(z.t[:, :, :, zprev * 8:zprev * 8 + 8], 0.0), w=[z])
                    ETz_b[ez % 3] = b
                    ez += 1
                    op(A, lambda e, z=z, b=b: e.activation(out=z.t[:, :, :, b * 8:b * 8 + 8], in_=XB.t[:].rearrange("p c (h q) -> p h c q", h=4), func=AF.Exp),
                       r=[XB], w=[z])
                    for h in range(4):
                        for c in range(2):
                            bnk, c0 = slot[(h, c)]
                            op(PE, lambda e, h=h, c=c, z=z, vx=vx, bnk=bnk, c0=c0: e.matmul(psO[bnk].t[:, c0:c0 + 129], lhsT=z.t[:, h, c, :], rhs=vx[:, h, 0:129], start=False, stop=False),
                               r=[z, b_vx], disjoint=[psO[bnk]], signal=(h == 3 and c == 1))
            kt = NTILE - 1
            for h in range(4):
                pS = nextps()
                for c in range(2):
                    op(PE, lambda e, c=c, h=h, pS=pS: e.matmul(pS.t[:, c * 512:c * 512 + 128], lhsT=KT.t[c * 64:(c + 1) * 64, h, kt * 128:(kt + 1) * 128],
                                                                 rhs=QT.t[c * 64:(c + 1) * 64, h, sq0:sq0 + 128], start=True, stop=True),
                       r=[KT, QT], w=[pS] if c == 0 else (), disjoint=[pS] if c else ())
                ET, b_ET = ETr.next()
                op(A, lambda e, pS=pS, ET=ET, h=h: e.activation(out=ET[:].rearrange("p (c q) -> p c q", c=2), in_=pS.t[:, :].rearrange("p (c x) -> p c x", c=2)[:, :, 0:128],
                                                                func=AF.Exp, scale=0.125, bias=bown.t[:, h:h + 1]), xr=[pS], r=[bown], w=[b_ET])
                op(V, lambda e, ET=ET: e.tensor_tensor(out=ET[:].rearrange("p (c q) -> p c q", c=2), in0=ET[:].rearrange("p (c q) -> p c q", c=2),
                                                        in1=cm_s.t[:, :].unsqueeze(1).broadcast_to([128, 2, 128]), op=ALU.mult), r=[cm_s], w=[b_ET])
                for c in range(2):
                    bnk, c0 = slot[(h, c)]
                    op(PE, lambda e, c=c, h=h, ET=ET, bnk=bnk, c0=c0: e.matmul(psO[bnk].t[:, c0:c0 + 129], lhsT=ET[:, c * 128:(c + 1) * 128], rhs=Vx.t[:, kt, h, 0:129], start=False, stop=True),
                       r=[b_ET, Vx], disjoint=[psO[bnk]])
            for h in range(4):
                b0, c0 = slot[(h, 0)]
                b1, c1 = slot[(h, 1)]
                combine(h, psO[b0].t[:, c0:c0 + 129], None, psO[b1].t[:, c1:c1 + 129], None, psO[b0], psO[b1])
            finish_tile(NT_OWN)
        P.emit("phaseB2")


def phase_C1(P, nc, T):
    V, A, PO, PE = "dve", "act", "pool", "pe"
    with contextlib.ExitStack() as st:
        def mk(n, s, d):
            return Tl(st.enter_context(nc.sbuf_tensor("c1_" + n, s, d)), P.buf(n))

        def op(e, fn, r=(), w=(), **kw):
            return P.op(e, fn, reads=r, writes=w, **kw)

        cst = T["cst"]
        identb = mk("identb", [128, 128], BF16)
        P.dma("pool", identb.t[:], T["ident"], writes=[identb])
        wa = mk("wa", [128, 4, D], BF16)
        wb = mk("wb", [128, 4, D], BF16)
        wo = mk("wo", [128, 8, D], BF16)
        for kc in range(4):
            P.dma("pool", wa.t[:, kc, :], T["w_br_a"][kc * 128:(kc + 1) * 128, :], disjoint=[wa])
            P.dma("pool", wb.t[:, kc, :], T["w_br_b"][kc * 128:(kc + 1) * 128, :], disjoint=[wb])
        for kc in range(8):
            P.dma("pool", wo.t[:, kc, :], T["w_out"][kc * 128:(kc + 1) * 128, :], disjoint=[wo])
        gffn = mk("gffn", [128, D], F32)
        P.dma("sp", gffn.t[:], T["g_ffn"].broadcast_to([128, D]), writes=[gffn])
        psr = Rot(P, [st.enter_context(nc.psum_tensor("c1_psr%d" % i, [128, 512], F32)) for i in range(8)])

        def nextps():
            t, b = psr.next()
            return Tl(t, b)

        oaT = [mk("oaT%d" % i, [128, 4, 512], BF16) for i in range(2)]
        obT = [mk("obT%d" % i, [128, 4, 512], BF16) for i in range(2)]
        gts = [mk("gts%d" % i, [128, 16, 512], BF16) for i in range(2)]
        mT = [mk("mT%d" % i, [128, 8, 512], BF16) for i in range(2)]
        t1 = [mk("t1_%d" % i, [128, 512], F32) for i in range(2)]
        t2 = [mk("t2_%d" % i, [128, 512], F32) for i in range(2)]
        xt = [mk("xt%d" % i, [128, D], F32) for i in range(2)]
        xn = [mk("xn%d" % i, [128, D], F32) for i in range(2)]
        hb = [mk("hb%d" % i, [128, D], BF16) for i in range(2)]
        hT = [mk("hT%d" % i, [128, 8, 128], BF16) for i in range(2)]
        junk = mk("junk", [128, D], BF16)
        ss = [mk("ss%d" % i, [128, 4], F32) for i in range(2)]
        groups = [(g * 4, 4) for g in range(4)] + [(16, 1)]
        tcount = 0
        for gi, (qt0, nt) in enumerate(groups):
            G = nt * 128
            q0 = qt0 * 128
            oa, ob, gt, m = oaT[gi % 2], obT[gi % 2], gts[gi % 2], mT[gi % 2]
            P.dma("sp", oa.t[:, :, 0:G], T["oaT_d"].rearrange("c p t -> p c t")[:, :, q0:q0 + G], reads=[T["b_oaT"]], writes=[oa])
            P.dma("sp", ob.t[:, :, 0:G], T["obT_d"].rearrange("c p t -> p c t")[:, :, q0:q0 + G], reads=[T["b_obT"]], writes=[ob])
            P.dma("sp", gt.t[:, :, 0:G], T["gates_d"].rearrange("c p t -> p c t")[:, :, q0:q0 + G], reads=[T["b_gates"]], writes=[gt])
            for nch in range(8):
                pa = nextps()
                pb = nextps()
                for kc in range(4):
                    op(PE, lambda e, pa=pa, kc=kc, nch=nch, oa=oa, G=G: e.matmul(pa.t[:, 0:G], lhsT=wa.t[:, kc, nch * 128:(nch + 1) * 128], rhs=oa.t[:, kc, 0:G], start=(kc == 0), stop=(kc == 3)),
                       r=[wa, oa], w=[pa] if kc == 0 else (), disjoint=[pa] if kc else (), signal=(kc == 3))
                for kc in range(4):
                    op(PE, lambda e, pb=pb, kc=kc, nch=nch, ob=ob, G=G: e.matmul(pb.t[:, 0:G], lhsT=wb.t[:, kc, nch * 128:(nch + 1) * 128], rhs=ob.t[:, kc, 0:G], start=(kc == 0), stop=(kc == 3)),
                       r=[wb, ob], w=[pb] if kc == 0 else (), disjoint=[pb] if kc else (), signal=(kc == 3))
                ta, tb = t1[nch % 2], t2[nch % 2]
                op(V, lambda e, pa=pa, ta=ta, gt=gt, nch=nch, G=G: e.tensor_tensor(out=ta.t[:, 0:G], in0=pa.t[:, 0:G], in1=gt.t[:, nch, 0:G], op=ALU.mult), xr=[pa], r=[gt], w=[ta])
                op(V, lambda e, pb=pb, tb=tb, gt=gt, nch=nch, G=G: e.tensor_tensor(out=tb.t[:, 0:G], in0=pb.t[:, 0:G], in1=gt.t[:, 8 + nch, 0:G], op=ALU.mult), xr=[pb], r=[gt], w=[tb])
                op(PO, lambda e, ta=ta, tb=tb, m=m, nch=nch, G=G: e.tensor_tensor(out=m.t[:, nch, 0:G], in0=ta.t[:, 0:G], in1=tb.t[:, 0:G], op=ALU.add), r=[ta, tb], disjoint=[m])
            for ti in range(nt):
                qt = qt0 + ti
                tg = NT_PRIOR + qt
                x, xo, h_, hTt, s_ = xt[tcount % 2], xn[tcount % 2], hb[tcount % 2], hT[tcount % 2], ss[tcount % 2]
                tcount += 1
                P.dma("sp", x.t[:], T["x_all"][tg * 128:(tg + 1) * 128, :], writes=[x])
                for half in range(2):
                    px = nextps()
                    for kc in range(8):
                        op(PE, lambda e, px=px, kc=kc, half=half, m=m, ti=ti: e.matmul(px.t[:, 0:512], lhsT=m.t[:, kc, ti * 128:(ti + 1) * 128], rhs=wo.t[:, kc, half * 512:(half + 1) * 512],
                                                                                    start=(kc == 0), stop=(kc == 7)),
                           r=[m, wo], w=[px] if kc == 0 else (), disjoint=[px] if kc else (), signal=(kc == 7))
                    op(V, lambda e, px=px, half=half, x=x, xo=xo: e.tensor_tensor(out=xo.t[:, half * 512:(half + 1) * 512], in0=px.t[:, 0:512], in1=x.t[:, half * 512:(half + 1) * 512], op=ALU.add),
                       xr=[px], r=[x], w=[xo] if half == 0 else (), disjoint=[xo] if half else ())
                P.dma("sp", T["xnew_d"][qt * 128:(qt + 1) * 128, :], xo.t[:], reads=[xo], disjoint=[T["b_xnew"]])
                op(A, lambda e, xo=xo, s_=s_: e.activation(out=junk.t[:], in_=xo.t[:], func=AF.Square, accum_out=s_.t[:, 0:1]), r=[xo], w=[junk, s_])
                op(A, lambda e, s_=s_: e.activation(out=s_.t[:, 1:2], in_=s_.t[:, 0:1], func=AF.Sqrt, scale=1.0 / D, bias=cst.t[:, 0:1]), r=[cst], w=[s_])
                op(V, lambda e, s_=s_: e.reciprocal(out=s_.t[:, 2:3], in_=s_.t[:, 1:2]), w=[s_])
                op(V, lambda e, xo=xo, s_=s_, h_=h_: e.scalar_tensor_tensor(out=h_.t[:], in0=xo.t[:], scalar=s_.t[:, 2:3], in1=gffn.t[:], op0=ALU.mult, op1=ALU.mult),
                   r=[xo, s_, gffn], w=[h_])
                pt = nextps()
                ptb = pt.t[:].bitcast(BF16)
                for kc in range(8):
                    op(PE, lambda e, kc=kc, ptb=ptb, h_=h_: e.transpose(out=ptb[:, kc * 128:(kc + 1) * 128], in_=h_.t[:, kc * 128:(kc + 1) * 128], identity=identb.t[:]),
                       r=[h_, identb], w=[pt] if kc == 0 else (), disjoint=[pt] if kc else ())
                op(A, lambda e, ptb=ptb, hTt=hTt: e.copy(out=hTt.t[:].rearrange("p a b -> p (a b)"), in_=ptb[:, 0:1024]), xr=[pt], w=[hTt])
                P.dma("sp", T["hnT_d"].rearrange("c p t -> p c t")[:, :, qt * 128:(qt + 1) * 128], hTt.t[:], reads=[hTt], disjoint=[T["b_hnT"]])
        P.emit("phaseC1")


def phase_C2(P, nc, T):
    V, A, PO, PE = "dve", "act", "pool", "pe"
    with contextlib.ExitStack() as st:
        def mk(n, s, d):
            return Tl(st.enter_context(nc.sbuf_tensor("c2_" + n, s, d)), P.buf(n))

        def op(e, fn, r=(), w=(), **kw):
            return P.op(e, fn, reads=r, writes=w, **kw)

        cst = T["cst"]
        wu = mk("wu", [128, 8, 4096], BF16)
        wd = mk("wd", [128, 32, D], BF16)
        for kc in range(8):
            for j in range(2):
                P.dma("pool", wu.t[:, kc, j * 2048:(j + 1) * 2048], T["w_up"][kc * 128:(kc + 1) * 128, j * 2048:(j + 1) * 2048], disjoint=[wu])
        for fc in range(32):
            P.dma("pool", wd.t[:, fc, :], T["w_down"][fc * 128:(fc + 1) * 128, :], disjoint=[wd])
        gfin = mk("gfin", [128, D], F32)
        P.dma("sp", gfin.t[:], T["g_final"].broadcast_to([128, D]), writes=[gfin])
        psr = Rot(P, [st.enter_context(nc.psum_tensor("c2_psr%d" % i, [128, 512], F32)) for i in range(8)])

        def nextps():
            t, b = psr.next()
            return Tl(t, b)

        hT = [mk("hT%d" % i, [128, 8, 256], BF16) for i in range(2)]
        aT = [mk("aT%d" % i, [128, 32, 256], BF16) for i in range(1)]
        rl = [mk("rl%d" % i, [128, 256], F32) for i in range(3)]
        xw = [mk("xw%d" % i, [128, D], F32) for i in range(2)]
        yo = [mk("yo%d" % i, [128, D], F32) for i in range(2)]
        yf = [mk("yf%d" % i, [128, D], F32) for i in range(2)]
        junk = mk("junk", [128, D], BF16)
        ss = [mk("ss%d" % i, [128, 4], F32) for i in range(2)]
        groups = [(g * 2, 2) for g in range(8)] + [(16, 1)]
        tcount = 0
        for gi, (qt0, nt) in enumerate(groups):
            G = nt * 128
            q0 = qt0 * 128
            h_ = hT[gi % 2]
            a_ = aT[0]
            P.dma("sp", h_.t[:, :, 0:G], T["hnT_d"].rearrange("c p t -> p c t")[:, :, q0:q0 + G], reads=[T["b_hnT"]], writes=[h_])
            for fc in range(32):
                pu = nextps()
                for kc in range(8):
                    op(PE, lambda e, pu=pu, kc=kc, fc=fc, h_=h_, G=G: e.matmul(pu.t[:, 0:G], lhsT=wu.t[:, kc, fc * 128:(fc + 1) * 128], rhs=h_.t[:, kc, 0:G], start=(kc == 0), stop=(kc == 7)),
                       r=[wu, h_], w=[pu] if kc == 0 else (), disjoint=[pu] if kc else (), signal=(kc == 7))
                r_ = rl[fc % 3]
                op(A, lambda e, pu=pu, r_=r_, G=G: e.activation(out=r_.t[:, 0:G], in_=pu.t[:, 0:G], func=AF.Relu), xr=[pu], w=[r_])
                op(PO, lambda e, r_=r_, a_=a_, fc=fc, G=G: e.tensor_tensor(out=a_.t[:, fc, 0:G], in0=r_.t[:, 0:G], in1=r_.t[:, 0:G], op=ALU.mult), r=[r_], disjoint=[a_])
            for ti in range(nt):
                qt = qt0 + ti
                x, y, yfin, s_ = xw[tcount % 2], yo[tcount % 2], yf[tcount % 2], ss[tcount % 2]
                tcount += 1
                P.dma("sp", x.t[:], T["xnew_d"][qt * 128:(qt + 1) * 128, :], reads=[T["b_xnew"]], writes=[x])
                for half in range(2):
                    pd = nextps()
                    for fc in range(32):
                        op(PE, lambda e, pd=pd, fc=fc, half=half, a_=a_, ti=ti: e.matmul(pd.t[:, 0:512], lhsT=a_.t[:, fc, ti * 128:(ti + 1) * 128], rhs=wd.t[:, fc, half * 512:(half + 1) * 512],
                                                                                      start=(fc == 0), stop=(fc == 31)),
                           r=[a_, wd], w=[pd] if fc == 0 else (), disjoint=[pd] if fc else (), signal=(fc == 31))
                    op(V, lambda e, pd=pd, half=half, x=x, y=y: e.tensor_tensor(out=y.t[:, half * 512:(half + 1) * 512], in0=pd.t[:, 0:512], in1=x.t[:, half * 512:(half + 1) * 512], op=ALU.add),
                       xr=[pd], r=[x], w=[y] if half == 0 else (), disjoint=[y] if half else ())
                op(A, lambda e, y=y, s_=s_: e.activation(out=junk.t[:], in_=y.t[:], func=AF.Square, accum_out=s_.t[:, 0:1]), r=[y], w=[junk, s_])
                op(A, lambda e, s_=s_: e.activation(out=s_.t[:, 1:2], in_=s_.t[:, 0:1], func=AF.Sqrt, scale=1.0 / D, bias=cst.t[:, 0:1]), r=[cst], w=[s_])
                op(V, lambda e, s_=s_: e.reciprocal(out=s_.t[:, 2:3], in_=s_.t[:, 1:2]), w=[s_])
                op(V, lambda e, y=y, s_=s_, yfin=yfin: e.scalar_tensor_tensor(out=yfin.t[:], in0=y.t[:], scalar=s_.t[:, 2:3], in1=gfin.t[:], op0=ALU.mult, op1=ALU.mult),
                   r=[y, s_, gfin], w=[yfin])
                P.dma("sp", T["yout"][qt * 128:(qt + 1) * 128, :], yfin.t[:], reads=[yfin])
        P.emit("phaseC2")


EPS_T = [None, None]
DEBUG_QT = None
DEBUG_DUMP = False
DEBUG_B2 = 9
DEBUG_SAMPLE_ATT = True
DEBUG_TILES = None
DEBUG_GROUPS = None
DEBUG_CHUNKS = None
DEBUG_NOTM = False
DEBUG_HALF = None
DEBUG_NOOUT = False


def build_program(stage=99):
    nc = bass.Bass("TRN2", target_bir_lowering=False)
    din = lambda n, s, d=F32: nc.dram_tensor(n, s, d, kind="ExternalInput").ap()
    dout = lambda n, s, d=F32: nc.dram_tensor(n, s, d, kind="ExternalOutput").ap()
    dscr = (lambda n, s, d: nc.dram_tensor(n, s, d, kind="ExternalOutput").ap()) if DEBUG_DUMP else (lambda n, s, d: nc.dram_tensor(n, s, d).ap())
    T = {}
    T["x_all"] = din("x_all", [NTOK, D])
    T["st_shift"] = din("st_shift", [16, D])
    T["w_in"] = din("w_in", [D, NCOL])
    T["g_mix"] = din("g_mix", [1, D])
    T["mu_fm"] = din("mu_fm", [128, 14])
    for n, shp in [("ident", [128, 128]), ("mtc_p", [128, 256]), ("mtc_s", [128, 256]), ("ms_p", [128, 128]), ("ms_s", [128, 128]),
                   ("rm_p", [128, 512]), ("rm_s", [128, 512]), ("bones", [128, 128]), ("sel8", [128, 32]), ("segsel", [128, 16]),
                   ("w2", [64, 512]), ("a2", [64, 512]), ("g2", [128, 512]), ("w0_fm", [128, 4]), ("a0_fm", [128, 4]),
                   ("kk_fm", [128, 4]), ("ka_fm", [128, 4]), ("rk_fm", [128, 4]), ("ln_w", [1, 512]), ("ln_b", [1, 512]),
                   ("st_rwkv", [16, 8, 64, 64]), ("pmask", [128, 1]), ("btab", [128, 128]), ("bown", [128, 4]), ("bcache", [128, 512]),
                   ("lam_q1", [1, 64]), ("lam_k1", [1, 64]), ("lam_q2", [1, 64]), ("lam_k2", [1, 64]), ("subln_g", [1, 128]),
                   ] + ([("cache_k", [NPOOL * 128, 512]), ("cache_v", [NPOOL * 128, 512])] if DEBUG_SAMPLE_ATT else []) + [
                   ("w_br_a", [512, D]), ("w_br_b", [512, D]), ("w_out", [D, D]), ("g_ffn", [1, D]), ("w_up", [D, 4096]), ("w_down", [4096, D]),
                   ("g_final", [1, D])]:
        T[n] = din(n, shp)
    T["ptab"] = din("ptab", [1, 256], I32)
    T["yout"] = dout("yout", [NQ, D])
    T["obT_d"] = dscr("obT_d", [4, 128, NQ], BF16)
    T["xnew_d"] = dscr("xnew_d", [NQ, D], F32)
    T["hnT_d"] = dscr("hnT_d", [8, 128, NQ], BF16)
    T["kout"] = dout("kout", [NQ, 512])
    T["vout"] = dout("vout", [NQ, 512])
    T["shift_out"] = dout("shift_out", [17, D])
    T["rwkv_p_out"] = dout("rwkv_p_out", [8, 64, 64])
    T["rwkv_s_out"] = dout("rwkv_s_out", [16, 8, 64, 64])
    T["prT_d"] = dscr("prT_d", [14, 128, NTOK], F32)
    T["qT_d"] = dscr("qT_d", [4, 128, NQ], BF16)
    T["kT_d"] = dscr("kT_d", [4, 128, NTOK], BF16)
    T["v_d"] = dscr("v_d", [NTOK, 512], BF16)
    T["gates_d"] = dscr("gates_d", [16, 128, NQ], BF16)
    T["oaT_d"] = dscr("oaT_d", [4, 128, NQ], BF16)
    with contextlib.ExitStack() as st:
        P = Prog(nc, st)
        for n in ["prT", "qT", "kT", "v", "gates", "oaT", "obT", "xnew", "hnT"]:
            T["b_" + n] = P.buf()
        cst = Tl(st.enter_context(nc.sbuf_tensor("cst", [128, 8], F32)), P.buf())
        T["cst"] = cst
        for i, val in enumerate([EPS, 1e-24, 64e-5, 1e-5]):
            P.op("pool", lambda e, i=i, val=val: e.memset(cst.t[:, i:i + 1], val), disjoint=[cst])
        phase_A(P, nc, T)
        if stage >= 2:
            phase_B1(P, nc, T)
        if stage >= 3:
            phase_B2(P, nc, T)
        if stage >= 4:
            phase_C1(P, nc, T)
            phase_C2(P, nc, T)
        P.final_wait("sp")
        P.emit("final")
    return nc


def _consts():
    s = np.arange(128)
    seg = s // 8
    le = (s[:, None] <= s[None, :]).astype(np.float32)
    lt = (s[:, None] < s[None, :]).astype(np.float32)
    same = (seg[:, None] == seg[None, :]).astype(np.float32)
    c = {}
    c["ident"] = np.eye(128, dtype=np.float32)
    c["mtc_p"] = np.concatenate([le, lt], axis=1)
    c["mtc_s"] = np.concatenate([le * same, lt * same], axis=1)
    c["ms_p"] = lt.T.copy()
    c["ms_s"] = (lt * same).T.copy()
    rm = np.ones((128, 4, 128), np.float32)
    rm[:, :, 0] = 0
    c["rm_p"] = rm.reshape(128, 512)
    rm = np.ones((128, 4, 128), np.float32)
    rm[:, :, 0::8] = 0
    c["rm_s"] = rm.reshape(128, 512)
    bo = np.zeros((128, 128), np.float32)
    bo[:64, :64] = 1
    bo[64:, 64:] = 1
    c["bones"] = bo
    sel = np.zeros((128, 4, 8), np.float32)
    for p in range(128):
        for hp in range(4):
            sel[p, hp, 2 * hp + p // 64] = 1
    c["sel8"] = sel.reshape(128, 32)
    c["segsel"] = (seg[:, None] == np.arange(16)[None, :]).astype(np.float32)
    slopes = (2.0 ** (-8.0 * np.arange(1, 5) / 4)).astype(np.float32)
    pp = np.arange(128, dtype=np.float32)
    bt = np.zeros((128, 4, 32), np.float32)
    for h in range(4):
        for o in range(32):
            bt[:, h, o] = slopes[h] * (pp - 128.0 * o)
    c["btab"] = bt.reshape(128, 128)
    c["bown"] = (slopes[None, :] * (pp[:, None] % 8)).astype(np.float32)
    bc = np.zeros((128, 16, 4, 8), np.float32)
    for pg in range(16):
        for h in range(4):
            bc[:, pg, h, :] = (slopes[h] * (pg * 128.0 + pp - 2048.0))[:, None]
    c["bcache"] = bc.reshape(128, 512)
    return c


def make_in_maps(inp):
    f = lambda k: np.asarray(inp[k], np.float32)
    xp = f("x_prompt")
    xs = f("x_sample")
    maps = []
    fm4 = lambda v: np.ascontiguousarray(np.asarray(v, np.float32).reshape(4, 128).T)
    shared = {
        "w_in": np.ascontiguousarray(f("w_in")[0]),
        "g_mix": f("g_mix")[0].reshape(1, D).copy(),
        "mu_fm": np.ascontiguousarray(f("mu_shift").reshape(14, 128).T),
        "w2": np.ascontiguousarray(f("w2")[0]), "a2": np.ascontiguousarray(f("a2")[0]), "g2": np.ascontiguousarray(f("g2")[0]),
        "w0_fm": fm4(f("w0")[0]), "a0_fm": fm4(f("a0")[0]), "kk_fm": fm4(f("k_k")[0]), "ka_fm": fm4(f("k_a")[0]),
        "rk_fm": fm4(f("r_k")[0].reshape(512)),
        "ln_w": f("ln_x_w")[0].reshape(1, 512).copy(), "ln_b": f("ln_x_b")[0].reshape(1, 512).copy(),
        "lam_q1": f("lam_q1")[0].reshape(1, 64).copy(), "lam_k1": f("lam_k1")[0].reshape(1, 64).copy(),
        "lam_q2": f("lam_q2")[0].reshape(1, 64).copy(), "lam_k2": f("lam_k2")[0].reshape(1, 64).copy(),
        "subln_g": f("subln_g")[0].reshape(1, 128).copy(),
        "cache_k": np.asarray(inp["cache_k"], np.float32).reshape(NPOOL * 128, 512),
        "cache_v": np.asarray(inp["cache_v"], np.float32).reshape(NPOOL * 128, 512),
        "w_br_a": np.ascontiguousarray(f("w_br_a")[0]), "w_br_b": np.ascontiguousarray(f("w_br_b")[0]),
        "w_out": np.ascontiguousarray(f("w_out")[0]), "g_ffn": f("g_ffn")[0].reshape(1, D).copy(),
        "w_up": np.ascontiguousarray(f("w_up")[0]), "w_down": np.ascontiguousarray(f("w_down")[0]),
        "g_final": f("g_final").reshape(1, D).copy(),
    }
    shared.update(_consts())
    for c in range(8):
        b, p = divmod(c, 2)
        xa = np.zeros((NTOK, D), np.float32)
        if p == 1:
            xa[0:2048] = xp[b, 0:2048]
        xa[2048:4096] = xp[b, 2048 * p:2048 * p + 2048]
        xa[4096:] = xs[16 * c:16 * c + 16].reshape(128, D)
        m = dict(shared)
        m["x_all"] = xa
        m["st_shift"] = np.ascontiguousarray(f("state_shift")[0, 16 * c:16 * c + 16])
        m["st_rwkv"] = np.ascontiguousarray(f("state_rwkv")[0, 16 * c:16 * c + 16])
        m["ptab"] = np.ascontiguousarray(np.asarray(inp["page_table"], np.int32)[16 * c:16 * c + 16]).reshape(1, 256)
        m["pmask"] = np.full((128, 1), 0.0 if p == 1 else -30000.0, np.float32)
        maps.append(m)
    return maps


_NC_CACHE = {}


def kernel(**inp):
    if "nc" not in _NC_CACHE:
        _NC_CACHE["nc"] = build_program()
    nc = _NC_CACHE["nc"]
    in_maps = make_in_maps(inp)
    res = run_bass_kernel_spmd(nc, in_maps, core_ids=list(range(8)))
    R = res.results
    kp = np.zeros((1, 4, 4096, 4, 128), np.float32)
    vp = np.zeros((1, 4, 4096, 4, 128), np.float32)
    ks = np.zeros((1, 128, 8, 4, 128), np.float32)
    vs = np.zeros((1, 128, 8, 4, 128), np.float32)
    shp = np.zeros((1, 4, D), np.float32)
    shs = np.zeros((1, 128, D), np.float32)
    yp = np.zeros((4, 4096, D), np.float32)
    ys = np.zeros((128, 8, D), np.float32)
    rp = np.zeros((1, 4, 8, 64, 64), np.float32)
    rs = np.zeros((1, 128, 8, 64, 64), np.float32)
    for c in range(8):
        b, p = divmod(c, 2)
        r = R[c]
        kp[0, b, 2048 * p:2048 * p + 2048] = r["kout"][0:2048].reshape(2048, 4, 128)
        vp[0, b, 2048 * p:2048 * p + 2048] = r["vout"][0:2048].reshape(2048, 4, 128)
        ks[0, 16 * c:16 * c + 16] = r["kout"][2048:].reshape(16, 8, 4, 128)
        vs[0, 16 * c:16 * c + 16] = r["vout"][2048:].reshape(16, 8, 4, 128)
        if p == 1:
            shp[0, b] = r["shift_out"][0]
            rp[0, b] = r["rwkv_p_out"]
        shs[0, 16 * c:16 * c + 16] = r["shift_out"][1:17]
        rs[0, 16 * c:16 * c + 16] = r["rwkv_s_out"]
        yp[b, 2048 * p:2048 * p + 2048] = r["yout"][0:2048]
        ys[16 * c:16 * c + 16] = r["yout"][2048:].reshape(16, 8, D)
    return (yp, ys, kp, vp, ks, vs, rp, rs, shp, shs)
```

```python
import numpy as np
import contextlib
import concourse.bass as bass
import concourse.mybir as mybir
from concourse.bass_utils import run_bass_kernel_spmd

F32 = mybir.dt.float32
BF16 = mybir.dt.bfloat16
I32 = mybir.dt.int32
AF = mybir.ActivationFunctionType
ALU = mybir.AluOpType
AX = mybir.AxisListType

NDMA = 12
SAME_ENGINE_SYNC = True

D = 1024
NCOL = 5376
NT_PRIOR = 16
NT_OWN = 16
NTILE = 33
NTOK = NTILE * 128
NQT = 17
NQ = NQT * 128
NPOOL = 2560
EPS = 1e-6


class Buf:
    __slots__ = ("name", "w", "r")

    def __init__(self, name):
        self.name = name
        self.w = {}
        self.r = {}


def _bl(xs):
    return [x.b if hasattr(x, "b") else x for x in xs]


class Tl:
    __slots__ = ("t", "b")

    def __init__(self, t, b):
        self.t = t
        self.b = b


class Prog:
    def __init__(self, nc, stack):
        self.nc = nc
        self.stack = stack
        self.eng = {"pe": nc.tensor, "act": nc.scalar, "dve": nc.vector, "pool": nc.gpsimd, "sp": nc.sync}
        self.sem = {}
        for e in self.eng:
            self.sem[e] = stack.enter_context(nc.semaphore("s_" + e))
        for i in range(NDMA):
            self.sem[("d", i)] = stack.enter_context(nc.semaphore("d%d" % i))
        self.cnt = {e: 0 for e in self.eng}
        self.dcnt = [0] * NDMA
        self.drr = 0
        self.lists = {e: [] for e in self.eng}
        self.waited = {e: {} for e in self.eng}
        self.nbuf = 0

    def buf(self, name=None):
        self.nbuf += 1
        return Buf(name or "b%d" % self.nbuf)

    def _deps(self, e, reads, writes, disjoint=(), xr=()):
        deps = {}
        reads, writes, disjoint, xr = _bl(reads), _bl(writes), _bl(disjoint), _bl(xr)

        def need(k, v):
            if deps.get(k, 0) < v:
                deps[k] = v

        for b in reads:
            for k, v in b.w.items():
                need(k, v)
        for b in writes:
            for k, v in b.w.items():
                need(k, v)
            for k, v in b.r.items():
                need(k, v)
        for b in disjoint:
            for k, v in b.r.items():
                need(k, v)
        for b in xr:
            for k, v in b.w.items():
                need(k, v)
            for k, v in b.r.items():
                need(k, v)
        waits = []
        for k, v in deps.items():
            if k == e and (e == "pe" or not SAME_ENGINE_SYNC):
                continue
            if self.waited[e].get(k, 0) >= v:
                continue
            self.waited[e][k] = v
            waits.append((k, v))
        return waits

    def op(self, e, fn, reads=(), writes=(), signal=True, disjoint=(), xr=()):
        waits = self._deps(e, reads, writes, disjoint, xr)
        reads, writes, disjoint = _bl(reads) + _bl(xr), _bl(writes), _bl(disjoint)
        if signal:
            self.cnt[e] += 1
            tok = (e, self.cnt[e])
            inc = (e, 1)
        else:
            tok = (e, self.cnt[e] + 1)
            inc = None
        for b in reads:
            b.r[tok[0]] = max(b.r.get(tok[0], 0), tok[1])
        for b in writes:
            b.w = {tok[0]: tok[1]}
            b.r = {}
        for b in disjoint:
            b.w[tok[0]] = max(b.w.get(tok[0], 0), tok[1])
        self.lists[e].append((waits, fn, inc))
        return tok

    def dma(self, q, out, in_, reads=(), writes=(), disjoint=(), **kw):
        i = self.drr
        self.drr = (i + 1) % NDMA
        key = ("d", i)
        waits = self._deps(q, reads, writes, disjoint)
        reads, writes, disjoint = _bl(reads), _bl(writes), _bl(disjoint)
        prev = self.dcnt[i]
        if prev > 0 and self.waited[q].get(key, 0) < prev:
            self.waited[q][key] = prev
            waits.append((key, prev))
        self.dcnt[i] += 16
        tok = (key, self.dcnt[i])
        for b in reads:
            b.r[key] = max(b.r.get(key, 0), tok[1])
        for b in writes:
            b.w = {tok[0]: tok[1]}
            b.r = {}
        for b in disjoint:
            b.w[tok[0]] = max(b.w.get(tok[0], 0), tok[1])
        self.lists[q].append((waits, lambda eng: eng.dma_start(out=out, in_=in_, **kw), (key, 16)))
        return tok

    def dma_ind(self, out, in_, idx_ap, reads=(), writes=()):
        q = "pool"
        i = self.drr
        self.drr = (i + 1) % NDMA
        key = ("d", i)
        waits = self._deps(q, reads, writes)
        reads, writes = _bl(reads), _bl(writes)
        prev = self.dcnt[i]
        if prev > 0 and self.waited[q].get(key, 0) < prev:
            self.waited[q][key] = prev
            waits.append((key, prev))
        self.dcnt[i] += 16
        tok = (key, self.dcnt[i])
        for b in reads:
            b.r[key] = max(b.r.get(key, 0), tok[1])
        for b in writes:
            b.w = {tok[0]: tok[1]}
            b.r = {}
        self.lists[q].append((waits, lambda eng: eng.indirect_dma_start(
            out=out, out_offset=None, in_=in_, in_offset=bass.IndirectOffsetOnAxis(ap=idx_ap, axis=0)), (key, 16)))
        return tok

    def final_wait(self, e):
        waits = []
        for i in range(NDMA):
            k, v = ("d", i), self.dcnt[i]
            if v > 0 and self.waited[e].get(k, 0) < v:
                self.waited[e][k] = v
                waits.append((k, v))
        self.lists[e].append((waits, None, None))

    def replay(self, e, eng):
        for waits, fn, inc in self.lists[e]:
            for k, v in waits:
                eng.wait_ge(self.sem[k], v)
            if fn is None:
                continue
            ins = fn(eng)
            if inc is not None:
                ins.then_inc(self.sem[inc[0]], inc[1])
        self.lists[e] = []

    def emit(self, name=None):
        with self.nc.Block(name) as block:
            @block.sync
            def _(eng):
                self.replay("sp", eng)

            @block.scalar
            def _(eng):
                self.replay("act", eng)

            @block.vector
            def _(eng):
                self.replay("dve", eng)

            @block.gpsimd
            def _(eng):
                self.replay("pool", eng)

            @block.tensor
            def _(eng):
                self.replay("pe", eng)


class Rot:
    def __init__(self, P, tiles):
        self.items = [(t, P.buf()) for t in tiles]
        self.i = 0

    def next(self):
        it = self.items[self.i % len(self.items)]
        self.i += 1
        return it


CH_R, CH_K, CH_V, CH_L1, CH_GD, CH_Q, CH_AK, CH_AV, CH_G = 0, 4, 8, 12, 13, 14, 18, 22, 26


def tile_kind(t):
    return "prior" if t < NT_PRIOR else ("own" if t < NT_PRIOR + NT_OWN else "sample")


def phase_A(P, nc, T):
    with contextlib.ExitStack() as st:
        sb = lambda n, s, d: st.enter_context(nc.sbuf_tensor(n, s, d))
        ps = lambda n: st.enter_context(nc.psum_tensor(n, [128, 512], F32))
        win = sb("win", [128, 8, NCOL], BF16)
        b_win = P.buf()
        for kc in range(8):
            for j in range(3):
                P.dma("pool", win[:, kc, j * 1792:(j + 1) * 1792],
                      T["w_in"][kc * 128:(kc + 1) * 128, j * 1792:(j + 1) * 1792], disjoint=[b_win])
        gmix = sb("gmix", [128, D], F32)
        b_gmix = P.buf()
        P.dma("sp", gmix[:], T["g_mix"].broadcast_to([128, D]), writes=[b_gmix])
        mu = sb("mu", [128, 14], F32)
        b_mu = P.buf()
        P.dma("sp", mu[:], T["mu_fm"], writes=[b_mu])
        identb = sb("identb", [128, 128], BF16)
        b_id = P.buf()
        P.op("pool", lambda e: e.memset(identb[:], 0.0), writes=[b_id])
        P.op("pool", lambda e: e.affine_select(out=identb[:], in_=identb[:], pattern=[[-1, 128]],
                                                compare_op=ALU.not_equal, fill=1.0, base=0,
                                                channel_multiplier=1), writes=[b_id])
        carry = sb("carry", [128, 14], F32)
        b_carry = P.buf()
        P.op("pool", lambda e: e.memset(carry[:], 0.0), writes=[b_carry])

        xt_r = Rot(P, [sb("xt%d" % i, [128, D], F32) for i in range(3)])
        xn_r = Rot(P, [sb("xn%d" % i, [128, D], F32) for i in range(2)])
        xnb_r = Rot(P, [sb("xnb%d" % i, [128, D], BF16) for i in range(2)])
        junk = sb("junk", [128, D], BF16)
        b_junk = P.buf()
        ss_r = Rot(P, [sb("ss%d" % i, [128, 4], F32) for i in range(4)])
        xnT_r = Rot(P, [sb("xnT%d" % i, [128, 8, 512], BF16) for i in range(2)])
        pb_r = Rot(P, [sb("pb%d" % i, [128, 520], F32) for i in range(3)])
        prs_r = Rot(P, [sb("prs%d" % i, [128, 512], F32) for i in range(3)])
        dd_r = Rot(P, [sb("dd%d" % i, [128, 512], F32) for i in range(2)])
        stg_r = Rot(P, [sb("stg%d" % i, [128, 512], BF16) for i in range(4)])
        tmf_r = Rot(P, [sb("tmf%d" % i, [128, 512], F32) for i in range(3)])
        tmb_r = Rot(P, [sb("tmb%d" % i, [128, 512], BF16) for i in range(2)])
        ps_fm = Rot(P, [ps("psfm%d" % i) for i in range(4)])
        ps_tm = Rot(P, [ps("pstm%d" % i) for i in range(2)])
        ps_tr = Rot(P, [ps("pstr%d" % i) for i in range(2)])

        ssin = sb("ssin", [16, D], F32)
        ssb = sb("ssb", [16, D], BF16)
        ssT = sb("ssT", [128, 8, 16], BF16)
        b_ssin, b_ssb, b_ssT = P.buf(), P.buf(), P.buf()
        P.dma("sp", ssin[:], T["st_shift"], writes=[b_ssin])
        P.op("dve", lambda e: e.tensor_copy(out=ssb[:], in_=ssin[:]), reads=[b_ssin], writes=[b_ssb])
        pt, b_pt = ps_tr.next()
        ptb = pt[:].bitcast(BF16)
        for kc in range(8):
            P.op("pe", lambda e, kc=kc, ptb=ptb: e.transpose(out=ptb[:, kc * 16:(kc + 1) * 16], in_=ssb[:, kc * 128:(kc + 1) * 128],
                                                    identity=identb[0:16, 0:16]),
                 reads=[b_ssb, b_id], writes=[b_pt] if kc == 0 else (), disjoint=[b_pt] if kc else ())
        P.op("act", lambda e, ptb=ptb: e.copy(out=ssT[:].rearrange("p k b -> p (k b)"), in_=ptb[:, 0:128]), reads=[b_pt], writes=[b_ssT])
        prevb = sb("prevb", [128, 128], F32)
        b_prevb = P.buf()
        pss = sb("pss", [128, 16], F32)
        b_pss = P.buf()

        groups = [(g * 4, 4) for g in range(8)] + [(32, 1)]
        if DEBUG_GROUPS is not None:
            groups = [groups[i] for i in DEBUG_GROUPS]
        for (t0, nt) in groups:
            kind = tile_kind(t0)
            G = nt * 128
            tok0 = t0 * 128
            xnT, b_xnT = xnT_r.next()
            for ti in range(nt):
                t = t0 + ti
                xt, b_xt = xt_r.next()
                P.dma("sp", xt[:], T["x_all"][t * 128:(t + 1) * 128, :], writes=[b_xt])
                ss, b_ss = ss_r.next()
                P.op("act", lambda e, xt=xt, ss=ss: e.activation(out=junk[:], in_=xt[:], func=AF.Square, accum_out=ss[:, 0:1]),
                     reads=[b_xt], writes=[b_junk, b_ss])
                P.op("act", lambda e, ss=ss: e.activation(out=ss[:, 1:2], in_=ss[:, 0:1], func=AF.Sqrt, scale=1.0 / D, bias=T["cst"].t[:, 0:1]),
                     reads=[b_ss, T["cst"]], writes=[b_ss])
                P.op("dve", lambda e, ss=ss: e.reciprocal(out=ss[:, 2:3], in_=ss[:, 1:2]), reads=[b_ss], writes=[b_ss])
                xn, b_xn = xn_r.next()
                P.op("dve", lambda e, xn=xn, xt=xt, ss=ss: e.scalar_tensor_tensor(out=xn[:], in0=xt[:], scalar=ss[:, 2:3], in1=gmix[:],
                                                                                    op0=ALU.mult, op1=ALU.mult),
                     reads=[b_xt, b_ss, b_gmix], writes=[b_xn])
                if t == NT_PRIOR + NT_OWN - 1:
                    P.dma("sp", T["shift_out"][0:1, :], xn[127:128, :], reads=[b_xn])
                if kind == "sample":
                    for b in range(16):
                        P.dma("sp", T["shift_out"][1 + b:2 + b, :], xn[b * 8 + 7:b * 8 + 8, :], reads=[b_xn])
                xnb, b_xnb = xnb_r.next()
                P.op("act", lambda e, xnb=xnb, xn=xn: e.copy(out=xnb[:], in_=xn[:]), reads=[b_xn], writes=[b_xnb])
                pt, b_pt = ps_tr.next()
                ptb = pt[:].bitcast(BF16)
                for kc in range(8):
                    P.op("pe", lambda e, kc=kc, ptb=ptb, xnb=xnb: e.transpose(out=ptb[:, kc * 128:(kc + 1) * 128],
                                                                                in_=xnb[:, kc * 128:(kc + 1) * 128], identity=identb[:]),
                         reads=[b_xnb, b_id], writes=[b_pt] if kc == 0 else (), disjoint=[b_pt] if kc else ())
                P.op("dve", lambda e, ptb=ptb, xnT=xnT, ti=ti: e.tensor_copy(out=xnT[:, :, ti * 128:(ti + 1) * 128],
                                                                               in_=ptb.rearrange("p (k t) -> p k t", k=8)),
                     reads=[b_pt], disjoint=[b_xnT], writes=())
            if kind == "prior" and t0 + nt == NT_PRIOR:
                chunks = list(range(0, 14)) + list(range(CH_AK, CH_AK + 4))
            elif kind == "prior":
                chunks = list(range(CH_K, CH_K + 4)) + list(range(CH_V, CH_V + 4)) + [CH_L1] + list(range(CH_AK, CH_AK + 4))
            else:
                chunks = list(range(0, 14)) + list(range(CH_Q, CH_Q + 8)) + list(range(CH_G, CH_G + 16))
                if DEBUG_CHUNKS is not None:
                    chunks = DEBUG_CHUNKS
            for c in chunks:
                pf, b_pf = ps_fm.next()
                for kc in range(8):
                    P.op("pe", lambda e, pf=pf, kc=kc, c=c, xnT=xnT, G=G: e.matmul(pf[:, 0:G], lhsT=win[:, kc, c * 128:(c + 1) * 128],
                                                                                  rhs=xnT[:, kc, 0:G], start=(kc == 0), stop=(kc == 7)),
                         reads=[b_win, b_xnT], writes=[b_pf] if kc == 0 else (), disjoint=[b_pf] if kc else (), signal=(kc == 7))
                if c < 14:
                    if kind != "sample":
                        pb, b_pb = pb_r.next()
                        P.op("act", lambda e, pb=pb, pf=pf, G=G: e.copy(out=pb[:, 1:1 + G], in_=pf[:, 0:G]), reads=[b_pf], writes=[b_pb])
                        P.op("pool", lambda e, pb=pb, c=c: e.tensor_copy(out=pb[:, 0:1], in_=carry[:, c:c + 1]), reads=[b_carry], disjoint=[b_pb])
                        P.op("pool", lambda e, pb=pb, c=c, G=G: e.tensor_copy(out=carry[:, c:c + 1], in_=pb[:, G:G + 1]), reads=[b_pb], disjoint=[b_carry])
                        cur = pb[:, 1:1 + G]
                        prev = pb[:, 0:G]
                        rd = [b_pb]
                    else:
                        pb, b_pb = pb_r.next()
                        P.op("act", lambda e, pb=pb, pf=pf, G=G: e.copy(out=pb[:, 0:G], in_=pf[:, 0:G]), reads=[b_pf], writes=[b_pb])
                        pf2, b_pf2 = ps_fm.next()
                        for kc in range(8):
                            P.op("pe", lambda e, pf2=pf2, kc=kc, c=c: e.matmul(pf2[:, 0:16], lhsT=win[:, kc, c * 128:(c + 1) * 128],
                                                                                rhs=ssT[:, kc, :], start=(kc == 0), stop=(kc == 7)),
                                 reads=[b_win, b_ssT], writes=[b_pf2] if kc == 0 else (), disjoint=[b_pf2] if kc else (), signal=(kc == 7))
                        P.op("dve", lambda e, pb=pb: e.tensor_copy(out=prevb[:].rearrange("p (b t) -> p b t", t=8)[:, :, 1:8],
                                                                    in_=pb[:, 0:128].rearrange("p (b t) -> p b t", t=8)[:, :, 0:7]),
                             reads=[b_pb], writes=[b_prevb])
                        P.op("dve", lambda e, pf2=pf2: e.tensor_copy(out=prevb[:].rearrange("p (b t) -> p b t", t=8)[:, :, 0:1],
                                                                      in_=pf2[:, 0:16].rearrange("p (b o) -> p b o", o=1)),
                             reads=[b_pf2], disjoint=[b_prevb])
                        cur = pb[:, 0:G]
                        prev = prevb[:, 0:G]
                        rd = [b_pb, b_prevb]
                    dd, b_dd = dd_r.next()
                    P.op("dve", lambda e, dd=dd, prev=prev, cur=cur, G=G: e.tensor_tensor(out=dd[:, 0:G], in0=prev, in1=cur, op=ALU.subtract),
                         reads=rd, writes=[b_dd])
                    prs, b_prs = prs_r.next()
                    P.op("dve", lambda e, prs=prs, dd=dd, cur=cur, c=c, G=G: e.scalar_tensor_tensor(out=prs[:, 0:G], in0=dd[:, 0:G], scalar=mu[:, c:c + 1],
                                                                                                    in1=cur, op0=ALU.mult, op1=ALU.add),
                         reads=rd + [b_dd, b_mu], writes=[b_prs])
                    P.dma("sp", T["prT_d"][c][:, tok0:tok0 + G], prs[:, 0:G], reads=[b_prs], disjoint=[T["b_prT"]])
                elif c < CH_G:
                    stg, b_stg = stg_r.next()
                    P.op("act", lambda e, stg=stg, pf=pf, G=G: e.copy(out=stg[:, 0:G], in_=pf[:, 0:G]), reads=[b_pf], writes=[b_stg])
                    if c < CH_AK:
                        q0 = tok0 - NT_PRIOR * 128
                        P.dma("sp", T["qT_d"][c - CH_Q][:, q0:q0 + G], stg[:, 0:G], reads=[b_stg], disjoint=[T["b_qT"]])
                    else:
                        P.dma("sp", T["kT_d"][c - CH_AK][:, tok0:tok0 + G], stg[:, 0:G], reads=[b_stg], disjoint=[T["b_kT"]])
                else:
                    stg, b_stg = stg_r.next()
                    P.op("act", lambda e, stg=stg, pf=pf, G=G: e.activation(out=stg[:, 0:G], in_=pf[:, 0:G], func=AF.Sigmoid),
                         reads=[b_pf], writes=[b_stg])
                    q0 = tok0 - NT_PRIOR * 128
                    P.dma("sp", T["gates_d"][c - CH_G][:, q0:q0 + G], stg[:, 0:G], reads=[b_stg], disjoint=[T["b_gates"]])
            for ti in range(nt if not DEBUG_NOTM else 0):
                t = t0 + ti
                halves = [("v", CH_AV * 128)] + ([] if kind == "prior" else [("k", CH_AK * 128)])
                if DEBUG_HALF is not None:
                    halves = [h for h in halves if h[0] in DEBUG_HALF]
                for (nm, c0) in halves:
                    pm, b_pm = ps_tm.next()
                    for kc in range(8):
                        P.op("pe", lambda e, pm=pm, kc=kc, c0=c0, xnT=xnT, ti=ti: e.matmul(pm[:, :], lhsT=xnT[:, kc, ti * 128:(ti + 1) * 128],
                                                                                         rhs=win[:, kc, c0:c0 + 512], start=(kc == 0), stop=(kc == 7)),
                             reads=[b_win, b_xnT], writes=[b_pm] if kc == 0 else (), disjoint=[b_pm] if kc else (), signal=(kc == 7))
                    tmf, b_tmf = tmf_r.next()
                    P.op("act", lambda e, tmf=tmf, pm=pm: e.copy(out=tmf[:], in_=pm[:]), reads=[b_pm], writes=[b_tmf])
                    if kind != "prior":
                        q0 = (t - NT_PRIOR) * 128
                        if not DEBUG_NOOUT:
                            P.dma("sp", T[nm + "out"][q0:q0 + 128, :], tmf[:], reads=[b_tmf])
                    if nm == "v":
                        tmb, b_tmb = tmb_r.next()
                        P.op("pool", lambda e, tmb=tmb, tmf=tmf: e.tensor_copy(out=tmb[:], in_=tmf[:]), reads=[b_tmf], writes=[b_tmb])
                        P.dma("sp", T["v_d"][t * 128:(t + 1) * 128, :], tmb[:], reads=[b_tmb], disjoint=[T["b_v"]])
        P.emit("phaseA")


def phase_B1(P, nc, T):
    V, A, PO, PE = "dve", "act", "pool", "pe"
    with contextlib.ExitStack() as st:
        def mk(n, s, d):
            return Tl(st.enter_context(nc.sbuf_tensor("b1_" + n, s, d)), P.buf(n))

        def mkps(n, cols=512):
            return Tl(st.enter_context(nc.psum_tensor("b1_" + n, [128, cols], F32)), P.buf(n))

        def op(e, fn, r=(), w=(), **kw):
            return P.op(e, fn, reads=r, writes=w, **kw)

        cst = T["cst"]
        identb = mk("identb", [128, 128], BF16)
        identf = mk("identf", [128, 128], F32)
        P.dma("pool", identb.t[:], T["ident"], writes=[identb])
        P.dma("sp", identf.t[:], T["ident"], writes=[identf])
        MTc = {"p": mk("MTc_p", [128, 256], BF16), "s": mk("MTc_s", [128, 256], BF16)}
        Ms = {"p": mk("Ms_p", [128, 128], BF16), "s": mk("Ms_s", [128, 128], BF16)}
        RM = {"p": mk("RM_p", [128, 512], F32), "s": mk("RM_s", [128, 512], F32)}
        for k in "ps":
            P.dma("pool", MTc[k].t[:], T["mtc_" + k], writes=[MTc[k]])
            P.dma("pool", Ms[k].t[:], T["ms_" + k], writes=[Ms[k]])
            P.dma("sp", RM[k].t[:], T["rm_" + k], writes=[RM[k]])
        bones = mk("bones", [128, 128], BF16)
        P.dma("pool", bones.t[:], T["bones"], writes=[bones])
        sel8 = mk("sel8", [128, 4, 8], BF16)
        P.dma("pool", sel8.t[:].rearrange("p a b -> p (a b)"), T["sel8"], writes=[sel8])
        segsel = mk("segsel", [128, 16], F32)
        P.dma("sp", segsel.t[:], T["segsel"], writes=[segsel])
        LW = mk("LW", [128, 512], BF16)
        P.dma("pool", LW.t[0:64, :], T["w2"], disjoint=[LW])
        P.dma("pool", LW.t[64:128, :], T["a2"], disjoint=[LW])
        G2 = mk("G2", [128, 512], BF16)
        P.dma("pool", G2.t[:], T["g2"], writes=[G2])
        fm = {}
        for n in ["w0", "a0", "kk", "ka", "rk"]:
            fm[n] = mk("fm_" + n, [128, 4], F32)
            P.dma("sp", fm[n].t[:], T[n + "_fm"], writes=[fm[n]])
        lnw = mk("lnw", [128, 512], F32)
        lnb = mk("lnb", [128, 512], F32)
        P.dma("sp", lnw.t[:], T["ln_w"].broadcast_to([128, 512]), writes=[lnw])
        P.dma("sp", lnb.t[:], T["ln_b"].broadcast_to([128, 512]), writes=[lnb])

        def bc4(x):
            return x.t[:, :].unsqueeze(2).broadcast_to([128, 4, 128])

        PRr = Rot(P, [st.enter_context(nc.sbuf_tensor("b1_PR%d" % i, [128, 14, 128], F32)) for i in range(2)])
        LB = mk("LB", [128, 128], BF16)
        GS = mk("GS", [128, 128], BF16)
        f4 = lambda n: mk(n, [128, 4, 128], F32)
        b4 = lambda n: mk(n, [128, 4, 128], BF16)
        XW, LD, ASIG, CUM, CUMp, EW, EWi, EWp = [f4(n) for n in ["XW", "LD", "ASIG", "CUM", "CUMp", "EW", "EWi", "EWp"]]
        KK, SQ, RNr, KKN, T1, KM, TMPB, XA = [f4(n) for n in ["KK", "SQ", "RNr", "KKN", "T1", "KM", "TMPB", "XA"]]
        KK2, BT, KT, VB, PRD = [b4(n) for n in ["KK2", "BT", "KT", "VB", "PRD"]]
        ART = mk("ART", [128, 4, 2, 128], BF16)
        Atm, Btm, Ktm, Vtm = [mk(n, [128, 512], BF16) for n in ["Atm", "Btm", "Ktm", "Vtm"]]
        COEF = mk("COEF", [128, 8], F32)
        Gtm = mk("Gtm", [128, 512], F32)
        NAXr = Rot(P, [st.enter_context(nc.sbuf_tensor("b1_NAX%d" % i, [128, 256], BF16)) for i in range(6)])
        MAr = Rot(P, [st.enter_context(nc.sbuf_tensor("b1_MA%d" % i, [128, 256], BF16)) for i in range(6)])
        PXr = Rot(P, [st.enter_context(nc.sbuf_tensor("b1_PX%d" % i, [128, 256], BF16)) for i in range(12)])
        Pr_ = Rot(P, [st.enter_context(nc.sbuf_tensor("b1_Pk%d" % i, [128, 128], BF16)) for i in range(12)])
        TTr = Rot(P, [st.enter_context(nc.sbuf_tensor("b1_TT%d" % i, [128, 128], BF16)) for i in range(6)])
        MVr = Rot(P, [st.enter_context(nc.sbuf_tensor("b1_MV%d" % i, [128, 64], BF16)) for i in range(6)])
        UVr = Rot(P, [st.enter_context(nc.sbuf_tensor("b1_UV%d" % i, [128, 64], F32)) for i in range(6)])
        Usbr = Rot(P, [st.enter_context(nc.sbuf_tensor("b1_Usb%d" % i, [128, 64], BF16)) for i in range(6)])
        AhT = st.enter_context(nc.sbuf_tensor("b1_AhT", [128, 4, 128], BF16))
        b_AhT = [P.buf() for _ in range(8)]
        S0 = st.enter_context(nc.sbuf_tensor("b1_S0", [128, 4, 64], F32))
        S0b = st.enter_context(nc.sbuf_tensor("b1_S0b", [128, 4, 64], BF16))
        S0W = st.enter_context(nc.sbuf_tensor("b1_S0W", [128, 4, 64], F32))
        b_S = [P.buf() for _ in range(8)]
        b_Sb = [P.buf() for _ in range(8)]
        b_SW = P.buf()
        op(PO, lambda e: e.memset(S0[:], 0.0), w=b_S)
        op(PO, lambda e: e.memset(S0b[:], 0.0), w=b_Sb)
        Ytm = mk("Ytm", [128, 512], F32)
        O1, O2, O3 = [mk(n, [128, 512], F32) for n in ["O1", "O2", "O3"]]
        ST8 = mk("ST8", [128, 32], F32)
        OAb = mk("OAb", [128, 512], BF16)
        OAT = mk("OAT", [128, 4, 128], BF16)
        S0s = st.enter_context(nc.sbuf_tensor("b1_S0s", [128, 4, 16, 64], F32))
        S0sb = st.enter_context(nc.sbuf_tensor("b1_S0sb", [128, 4, 16, 64], BF16))
        b_S0s = [P.buf() for _ in range(4)]
        b_S0sb = [P.buf() for _ in range(4)]
        Zin = mk("Zin", [64, 16, 128], F32)
        TMPs = mk("TMPs", [128, 16, 64], F32)
        Y1 = mk("Y1", [128, 64], F32)
        U1 = mk("U1", [128, 64], F32)
        UBLK = mk("UBLK", [128, 16, 64], BF16)
        VBLK = mk("VBLK", [128, 16, 64], BF16)
        SOUT = mk("SOUT", [64, 16, 128], F32)
        psr = Rot(P, [st.enter_context(nc.psum_tensor("b1_psr%d" % i, [128, 512], F32)) for i in range(4)])
        ps2 = Rot(P, [st.enter_context(nc.psum_tensor("b1_ps2_%d" % i, [128, 1024], F32)) for i in range(1)])
        psY = mkps("psY")
        psG = mkps("psG")

        def nextps():
            t, b = psr.next()
            return Tl(t, b)

        for hp in range(4):
            for h2 in range(2):
                P.dma("sp", Zin.t[:, :, h2 * 64:(h2 + 1) * 64],
                      T["st_rwkv"][:, 2 * hp + h2, :, :].rearrange("b i j -> i b j"), writes=[Zin] if h2 == 0 else (), disjoint=[Zin] if h2 else ())
            for g in range(2):
                pt = nextps()
                for k in range(8):
                    b = g * 8 + k
                    op(PE, lambda e, pt=pt, k=k, b=b: e.transpose(out=pt.t[:, k * 64:(k + 1) * 64], in_=Zin.t[:, b, :], identity=identf.t[0:64, 0:64]),
                       r=[Zin, identf], w=[pt] if k == 0 else (), disjoint=[pt] if k else ())
                op(A, lambda e, pt=pt, hp=hp, g=g: e.copy(out=S0s[:, hp, g * 8:(g + 1) * 8, :], in_=pt.t[:, 0:512].rearrange("p (b i) -> p b i", i=64)),
                   xr=[pt], disjoint=[b_S0s[hp]])
            op(PO, lambda e, hp=hp: e.tensor_copy(out=S0sb[:, hp, :, :], in_=S0s[:, hp, :, :]), r=[b_S0s[hp]], w=[b_S0sb[hp]])

        def rwkv_tile(t):
            kind = tile_kind(t)
            mk_ = "s" if kind == "sample" else "p"
            full = kind != "prior"
            tok0 = t * 128
            q0 = tok0 - NT_PRIOR * 128
            PRt, b_PR = PRr.next()
            PR = Tl(PRt, b_PR)
            c0 = 0 if full else 4
            c1 = 14 if full else 13
            P.dma("sp", PRt[:, c0:c1, :], T["prT_d"].rearrange("c p t -> p c t")[:, c0:c1, tok0:tok0 + 128],
                  reads=[T["b_prT"]], writes=[PR])
            Rv, Kv, Vv = PRt[:, 0:4, :], PRt[:, 4:8, :], PRt[:, 8:12, :]
            op(A, lambda e: e.activation(out=LB.t[0:64, :], in_=PRt[0:64, 12, :], func=AF.Tanh), r=[PR], w=[LB])
            op(A, lambda e: e.copy(out=LB.t[64:128, :], in_=PRt[64:128, 12, :]), r=[PR], disjoint=[LB])
            pW = nextps()
            pA = nextps()
            for hp in range(4):
                op(PE, lambda e, hp=hp: e.matmul(pW.t[:, hp * 128:(hp + 1) * 128], lhsT=LW.t[0:64, hp * 128:(hp + 1) * 128], rhs=LB.t[0:64, :], start=True, stop=True),
                   r=[LW, LB], w=[pW] if hp == 0 else (), disjoint=[pW] if hp else ())
            for hp in range(4):
                op(PE, lambda e, hp=hp: e.matmul(pA.t[:, hp * 128:(hp + 1) * 128], lhsT=LW.t[64:128, hp * 128:(hp + 1) * 128], rhs=LB.t[64:128, :], start=True, stop=True),
                   r=[LW, LB], w=[pA] if hp == 0 else (), disjoint=[pA] if hp else ())
            v4 = lambda x: x.t[:, 0:512].rearrange("p (a b) -> p a b", a=4) if x.t.shape[-1] == 512 else x.t[:]
            op(V, lambda e: e.tensor_tensor(out=XW.t[:], in0=v4(pW), in1=bc4(fm["w0"]), op=ALU.add), xr=[pW], r=[fm["w0"]], w=[XW])
            op(A, lambda e: e.activation(out=XW.t[:], in_=XW.t[:], func=AF.Sigmoid), w=[XW])
            op(PO, lambda e: e.tensor_scalar(out=LD.t[:], in0=XW.t[:], scalar1=-0.6065306597126334, scalar2=None, op0=ALU.mult), r=[XW], w=[LD])
            op(V, lambda e: e.tensor_tensor(out=XA.t[:], in0=v4(pA), in1=bc4(fm["a0"]), op=ALU.add), xr=[pA], r=[fm["a0"]], w=[XA])
            op(A, lambda e: e.activation(out=ASIG.t[:], in_=XA.t[:], func=AF.Sigmoid), r=[XA], w=[ASIG])
            fl = lambda x: x.t[:].rearrange("p a b -> p (a b)")
            op(V, lambda e: e.tensor_tensor_scan(out=fl(CUM), data0=RM[mk_].t[:], data1=fl(LD), initial=0.0, op0=ALU.mult, op1=ALU.add),
               r=[RM[mk_], LD], w=[CUM])
            op(PO, lambda e: e.tensor_tensor(out=CUMp.t[:], in0=CUM.t[:], in1=LD.t[:], op=ALU.subtract), r=[CUM, LD], w=[CUMp])
            op(A, lambda e: e.activation(out=EW.t[:], in_=CUM.t[:], func=AF.Exp), r=[CUM], w=[EW])
            op(A, lambda e: e.activation(out=EWi.t[:], in_=CUM.t[:], func=AF.Exp, scale=-1.0), r=[CUM], w=[EWi])
            op(A, lambda e: e.activation(out=EWp.t[:], in_=CUMp.t[:], func=AF.Exp), r=[CUMp], w=[EWp])
            op(PO, lambda e: e.tensor_tensor(out=KK.t[:], in0=Kv, in1=bc4(fm["kk"]), op=ALU.mult), r=[PR, fm["kk"]], w=[KK])
            op(PO, lambda e: e.tensor_tensor(out=KK2.t[:], in0=KK.t[:], in1=KK.t[:], op=ALU.mult), r=[KK], w=[KK2])
            pN = nextps()
            op(PE, lambda e: e.matmul(pN.t[:, 0:512], lhsT=bones.t[:], rhs=fl(KK2), start=True, stop=True), r=[bones, KK2], w=[pN])
            op(A, lambda e: e.activation(out=SQ.t[:], in_=v4(pN), func=AF.Sqrt, bias=cst.t[:, 1:2]), xr=[pN], r=[cst], w=[SQ])
            op(V, lambda e: e.reciprocal(out=RNr.t[:], in_=SQ.t[:]), r=[SQ], w=[RNr])
            op(PO, lambda e: e.tensor_tensor(out=KKN.t[:], in0=KK.t[:], in1=RNr.t[:], op=ALU.mult), r=[KK, RNr], w=[KKN])
            op(V, lambda e: e.scalar_tensor_tensor(out=T1.t[:], in0=ASIG.t[:], scalar=-1.0, in1=bc4(fm["ka"]), op0=ALU.add, op1=ALU.mult),
               r=[ASIG, fm["ka"]], w=[T1])
            op(V, lambda e: e.scalar_tensor_tensor(out=KM.t[:], in0=T1.t[:], scalar=1.0, in1=Kv, op0=ALU.add, op1=ALU.mult), r=[T1, PR], w=[KM])
            op(V, lambda e: e.scalar_tensor_tensor(out=ART.t[:, :, 1, :], in0=KKN.t[:], scalar=-1.0, in1=EWp.t[:], op0=ALU.mult, op1=ALU.mult),
               r=[KKN, EWp], disjoint=[ART])
            if full:
                op(PO, lambda e: e.tensor_tensor(out=ART.t[:, :, 0, :], in0=Rv, in1=EW.t[:], op=ALU.mult), r=[PR, EW], disjoint=[ART])
            op(PO, lambda e: e.tensor_tensor(out=TMPB.t[:], in0=KKN.t[:], in1=ASIG.t[:], op=ALU.mult), r=[KKN, ASIG], w=[TMPB])
            op(PO, lambda e: e.tensor_tensor(out=BT.t[:], in0=TMPB.t[:], in1=EWi.t[:], op=ALU.mult), r=[TMPB, EWi], w=[BT])
            op(V, lambda e: e.tensor_tensor(out=KT.t[:], in0=KM.t[:], in1=EWi.t[:], op=ALU.mult), r=[KM, EWi], w=[KT])
            op(A, lambda e: e.copy(out=VB.t[:], in_=Vv), r=[PR], w=[VB])
            for (src_fn, dst, rd) in [(lambda hp: ART.t[:, hp, 1, :], Atm, ART), (lambda hp: BT.t[:, hp, :], Btm, BT),
                                      (lambda hp: KT.t[:, hp, :], Ktm, KT), (lambda hp: VB.t[:, hp, :], Vtm, VB)]:
                pt = nextps()
                ptb = pt.t[:].bitcast(BF16)
                for hp in range(4):
                    op(PE, lambda e, hp=hp, ptb=ptb, src_fn=src_fn: e.transpose(out=ptb[:, hp * 128:(hp + 1) * 128], in_=src_fn(hp), identity=identb.t[:]),
                       r=[rd, identb], w=[pt] if hp == 0 else (), disjoint=[pt] if hp else ())
                op(A, lambda e, ptb=ptb, dst=dst: e.copy(out=dst.t[:], in_=ptb[:, 0:512]), xr=[pt], w=[dst])
            if full:
                op(A, lambda e: e.activation(out=GS.t[:], in_=PRt[:, 13, :], func=AF.Sigmoid), r=[PR], w=[GS])
                op(PE, lambda e: e.matmul(psG.t[:, 0:512], lhsT=GS.t[:], rhs=G2.t[:], start=True, stop=True), r=[GS, G2], w=[psG])
                op(A, lambda e: e.copy(out=Gtm.t[:], in_=psG.t[:, 0:512]), xr=[psG], w=[Gtm])
                op(PO, lambda e: e.tensor_tensor(out=TMPB.t[:], in0=Rv, in1=bc4(fm["rk"]), op=ALU.mult), r=[PR, fm["rk"]], w=[TMPB])
                op(PO, lambda e: e.tensor_tensor(out=PRD.t[:], in0=TMPB.t[:], in1=KM.t[:], op=ALU.mult), r=[TMPB, KM], w=[PRD])
                pC = nextps()
                for hp in range(4):
                    op(PE, lambda e, hp=hp: e.matmul(pC.t[:, 0:8], lhsT=PRD.t[:, hp, :], rhs=sel8.t[:, hp, :], start=(hp == 0), stop=(hp == 3)),
                       r=[PRD, sel8], w=[pC] if hp == 0 else (), disjoint=[pC] if hp else (), signal=(hp == 3))
                op(A, lambda e: e.copy(out=COEF.t[:], in_=pC.t[:, 0:8]), xr=[pC], w=[COEF])
            if kind != "sample":
                op(PO, lambda e: e.tensor_tensor(out=S0W[:], in0=S0[:], in1=EW.t[:, :, 127:128].broadcast_to([128, 4, 64]), op=ALU.mult),
                   r=b_S + [EW], w=[b_SW])
            nsq = 2 if kind == "sample" else 6
            def head_gen(h):
                hp, h2 = divmod(h, 2)
                base = 64 * h2
                hc = slice(h * 64, (h + 1) * 64)
                NAXt, b_NAX = NAXr.next()
                MAt, b_MA = MAr.next()
                p1 = nextps()
                op(PE, lambda e, p1=p1, hp=hp, base=base: e.matmul(p1.t[:, 0:256], lhsT=BT.t[base:base + 64, hp, :],
                                                                  rhs=ART.t[base:base + 64, hp, :, :].rearrange("p a b -> p (a b)"), start=True, stop=True),
                   r=[BT, ART], w=[p1])
                op(V, lambda e, p1=p1, NAXt=NAXt: e.tensor_tensor(out=NAXt[:], in0=p1.t[:, 0:256], in1=MTc[mk_].t[:], op=ALU.mult),
                   xr=[p1], r=[MTc[mk_]], w=[b_NAX])
                p2 = nextps()
                op(PE, lambda e, p2=p2, hp=hp, base=base: e.matmul(p2.t[:, 0:256], lhsT=KT.t[base:base + 64, hp, :],
                                                                  rhs=ART.t[base:base + 64, hp, :, :].rearrange("p a b -> p (a b)"), start=True, stop=True),
                   r=[KT, ART], w=[p2])
                op(V, lambda e, p2=p2, MAt=MAt: e.tensor_tensor(out=MAt[:], in0=p2.t[:, 0:256], in1=MTc[mk_].t[:], op=ALU.mult),
                   xr=[p2], r=[MTc[mk_]], w=[b_MA])
                p3 = nextps()
                op(PE, lambda e, p3=p3, hp=hp, base=base: e.matmul(p3.t[:, 0:128], lhsT=ART.t[base:base + 64, hp, 1, :], rhs=BT.t[base:base + 64, hp, :], start=True, stop=True),
                   r=[ART, BT], w=[p3])
                Pk, b_Pk = Pr_.next()
                op(V, lambda e, p3=p3, Pk=Pk: e.tensor_tensor(out=Pk[:], in0=p3.t[:, 0:128], in1=Ms[mk_].t[:], op=ALU.mult), xr=[p3], r=[Ms[mk_]], w=[b_Pk])
                yield
                PX, b_PX = PXr.next()
                op(PO, lambda e, PX=PX, NAXt=NAXt: e.tensor_copy(out=PX[:, 0:128], in_=NAXt[:, 128:256]), r=[b_NAX], w=[b_PX])
                op(PO, lambda e, PX=PX, NAXt=NAXt: e.tensor_tensor(out=PX[:, 128:256], in0=NAXt[:, 128:256], in1=identb.t[:], op=ALU.add),
                   r=[b_NAX, identb], disjoint=[b_PX])
                for k in range(nsq + 1):
                    last = (k == nsq)
                    if not last:
                        PXn, b_PXn = PXr.next()
                        Pkn, b_Pkn = Pr_.next()
                        pd1 = nextps()
                        op(PE, lambda e, pd1=pd1, Pk=Pk, PX=PX: e.matmul(pd1.t[:, 0:128], lhsT=Pk[:], rhs=PX[:, 0:128], start=True, stop=True),
                           r=[b_Pk, b_PX], w=[pd1])
                        op(A, lambda e, pd1=pd1, PXn=PXn: e.copy(out=PXn[:, 0:128], in_=pd1.t[:, 0:128]), xr=[pd1], w=[b_PXn])
                        pe1 = nextps()
                        op(PE, lambda e, pe1=pe1, Pk=Pk, PX=PX: e.matmul(pe1.t[:, 0:128], lhsT=PX[:, 0:128], rhs=Pk[:], start=True, stop=True),
                           r=[b_Pk, b_PX], w=[pe1])
                        op(A, lambda e, pe1=pe1, Pkn=Pkn: e.copy(out=Pkn[:], in_=pe1.t[:, 0:128]), xr=[pe1], w=[b_Pkn])
                    if k >= 1:
                        pd2 = nextps()
                        op(PE, lambda e, pd2=pd2, Pk=Pk, PX=PX: e.matmul(pd2.t[:, 0:128], lhsT=Pk[:], rhs=PX[:, 128:256], start=True, stop=True),
                           r=[b_Pk, b_PX], w=[pd2])
                        if last:
                            TTt, b_TT = TTr.next()
                            op(V, lambda e, pd2=pd2, PX=PX, TTt=TTt: e.tensor_tensor(out=TTt[:], in0=pd2.t[:, 0:128], in1=PX[:, 128:256], op=ALU.add),
                               xr=[pd2], r=[b_PX], w=[b_TT])
                        else:
                            op(V, lambda e, pd2=pd2, PX=PX, PXn=PXn: e.tensor_tensor(out=PXn[:, 128:256], in0=pd2.t[:, 0:128], in1=PX[:, 128:256], op=ALU.add),
                               xr=[pd2], r=[b_PX], disjoint=[b_PXn])
                    elif not last:
                        op(PO, lambda e, PX=PX, PXn=PXn: e.tensor_copy(out=PXn[:, 128:256], in_=PX[:, 128:256]), r=[b_PX], disjoint=[b_PXn])
                    if not last:
                        PX, b_PX, Pk, b_Pk = PXn, b_PXn, Pkn, b_Pkn
                    yield
                MVt, b_MV = MVr.next()
                UVt, b_UV = UVr.next()
                pm = nextps()
                op(PE, lambda e, pm=pm, MAt=MAt, hc=hc: e.matmul(pm.t[:, 0:64], lhsT=MAt[:, 128:256], rhs=Vtm.t[:, hc], start=True, stop=True),
                   r=[b_MA, Vtm], w=[pm])
                op(A, lambda e, pm=pm, MVt=MVt: e.copy(out=MVt[:], in_=pm.t[:, 0:64]), xr=[pm], w=[b_MV])
                yield
                pu = nextps()
                op(PE, lambda e, pu=pu, TTt=TTt, MVt=MVt: e.matmul(pu.t[:, 0:64], lhsT=TTt[:], rhs=MVt[:], start=True, stop=True), r=[b_TT, b_MV], w=[pu])
                op(A, lambda e, pu=pu, UVt=UVt: e.copy(out=UVt[:], in_=pu.t[:, 0:64]), xr=[pu], w=[b_UV])
                pa = nextps()
                op(PE, lambda e, pa=pa, TTt=TTt, hc=hc, base=base: e.matmul(pa.t[base:base + 64, 0:128], lhsT=Atm.t[:, hc], rhs=TTt[:], start=True, stop=True),
                   r=[Atm, b_TT], w=[pa])
                op(A, lambda e, pa=pa, hp=hp, base=base: e.copy(out=AhT[base:base + 64, hp, :], in_=pa.t[base:base + 64, 0:128]), xr=[pa], w=[b_AhT[h]])
                yield
                Usb, b_Usb = Usbr.next()
                if kind != "sample":
                    pU = nextps()
                    op(PE, lambda e, pU=pU, hp=hp, base=base: e.matmul(pU.t[:, 0:64], lhsT=AhT[base:base + 64, hp, :], rhs=S0b[base:base + 64, hp, :], start=True, stop=True),
                       r=[b_AhT[h], b_Sb[h]], w=[pU])
                    op(V, lambda e, pU=pU, Usb=Usb, UVt=UVt: e.tensor_tensor(out=Usb[:], in0=pU.t[:, 0:64], in1=UVt[:], op=ALU.add), xr=[pU], r=[b_UV], w=[b_Usb])
                    yield
                    if full:
                        first = (h == 0)
                        op(PE, lambda e, hp=hp, base=base, hc=hc: e.matmul(psY.t[:, hc], lhsT=ART.t[base:base + 64, hp, 0, :], rhs=S0b[base:base + 64, hp, :], start=True, stop=False),
                           r=[ART, b_Sb[h]], w=[psY] if first else (), disjoint=() if first else [psY], signal=False)
                        op(PE, lambda e, hc=hc, NAXt=NAXt, Usb=Usb: e.matmul(psY.t[:, hc], lhsT=NAXt[:, 0:128], rhs=Usb[:], start=False, stop=False),
                           r=[b_NAX, b_Usb], disjoint=[psY], signal=False)
                        op(PE, lambda e, hc=hc, MAt=MAt: e.matmul(psY.t[:, hc], lhsT=MAt[:, 0:128], rhs=Vtm.t[:, hc], start=False, stop=True),
                           r=[b_MA, Vtm], disjoint=[psY])
                    pS = nextps()
                    op(PE, lambda e, pS=pS, hc=hc, base=base, Usb=Usb: e.matmul(pS.t[base:base + 64, 0:64], lhsT=Btm.t[:, hc], rhs=Usb[:], start=True, stop=False),
                       r=[Btm, b_Usb], w=[pS], signal=False)
                    op(PE, lambda e, pS=pS, hc=hc, base=base: e.matmul(pS.t[base:base + 64, 0:64], lhsT=Ktm.t[:, hc], rhs=Vtm.t[:, hc], start=False, stop=True),
                       r=[Ktm, Vtm], disjoint=[pS])
                    op(V, lambda e, pS=pS, hp=hp, base=base: e.scalar_tensor_tensor(out=S0[base:base + 64, hp, :], in0=pS.t[base:base + 64, 0:64],
                                                                                     scalar=EW.t[base:base + 64, hp, 127:128], in1=S0W[base:base + 64, hp, :],
                                                                                     op0=ALU.mult, op1=ALU.add),
                       xr=[pS], r=[EW, b_SW], w=[b_S[h]])
                    op(A, lambda e, hp=hp, base=base: e.copy(out=S0b[base:base + 64, hp, :], in_=S0[base:base + 64, hp, :]), r=[b_S[h]], w=[b_Sb[h]])
                else:
                    pq, b_pq = ps2.next()
                    for half in range(2):
                        op(PE, lambda e, half=half, hp=hp, base=base, pq=pq: e.matmul(pq[:, half * 512:(half + 1) * 512], lhsT=AhT[base:base + 64, hp, :],
                                                                                        rhs=S0sb[base:base + 64, hp, half * 8:(half + 1) * 8, :].rearrange("p b i -> p (b i)"),
                                                                                        start=True, stop=True),
                           r=[b_AhT[h], b_S0sb[hp]], w=[b_pq] if half == 0 else (), disjoint=[b_pq] if half else ())
                    for half in range(2):
                        op(V, lambda e, half=half, pq=pq: e.tensor_tensor(out=TMPs.t[:, half * 8:(half + 1) * 8, :],
                                                                            in0=pq[:, half * 512:(half + 1) * 512].rearrange("p (b i) -> p b i", i=64),
                                                                            in1=segsel.t[:, half * 8:(half + 1) * 8].unsqueeze(2).broadcast_to([128, 8, 64]), op=ALU.mult),
                           xr=[b_pq], r=[segsel], w=[TMPs] if half == 0 else (), disjoint=[TMPs] if half else ())
                    op(V, lambda e: e.tensor_reduce(out=U1.t[:], in_=TMPs.t[:].rearrange("p b i -> p i b"), axis=AX.X, op=ALU.add), r=[TMPs], w=[U1])
                    op(V, lambda e, Usb=Usb, UVt=UVt: e.tensor_tensor(out=Usb[:], in0=U1.t[:], in1=UVt[:], op=ALU.add), r=[U1, b_UV], w=[b_Usb])
                    pq, b_pq = ps2.next()
                    for half in range(2):
                        op(PE, lambda e, half=half, hp=hp, base=base, pq=pq: e.matmul(pq[:, half * 512:(half + 1) * 512], lhsT=ART.t[base:base + 64, hp, 0, :],
                                                                                        rhs=S0sb[base:base + 64, hp, half * 8:(half + 1) * 8, :].rearrange("p b i -> p (b i)"),
                                                                                        start=True, stop=True),
                           r=[ART, b_S0sb[hp]], w=[b_pq] if half == 0 else (), disjoint=[b_pq] if half else ())
                    for half in range(2):
                        op(V, lambda e, half=half, pq=pq: e.tensor_tensor(out=TMPs.t[:, half * 8:(half + 1) * 8, :],
                                                                            in0=pq[:, half * 512:(half + 1) * 512].rearrange("p (b i) -> p b i", i=64),
                                                                            in1=segsel.t[:, half * 8:(half + 1) * 8].unsqueeze(2).broadcast_to([128, 8, 64]), op=ALU.mult),
                           xr=[b_pq], r=[segsel], w=[TMPs] if half == 0 else (), disjoint=[TMPs] if half else ())
                    op(V, lambda e: e.tensor_reduce(out=Y1.t[:], in_=TMPs.t[:].rearrange("p b i -> p i b"), axis=AX.X, op=ALU.add), r=[TMPs], w=[Y1])
                    py = nextps()
                    op(PE, lambda e, py=py, NAXt=NAXt, Usb=Usb: e.matmul(py.t[:, 0:64], lhsT=NAXt[:, 0:128], rhs=Usb[:], start=True, stop=False),
                       r=[b_NAX, b_Usb], w=[py], signal=False)
                    op(PE, lambda e, py=py, MAt=MAt, hc=hc: e.matmul(py.t[:, 0:64], lhsT=MAt[:, 0:128], rhs=Vtm.t[:, hc], start=False, stop=True),
                       r=[b_MA, Vtm], disjoint=[py])
                    op(V, lambda e, py=py, hc=hc: e.tensor_tensor(out=Ytm.t[:, hc], in0=py.t[:, 0:64], in1=Y1.t[:], op=ALU.add), xr=[py], r=[Y1], disjoint=[Ytm])
                    op(PO, lambda e, Usb=Usb: e.tensor_tensor(out=UBLK.t[:], in0=Usb[:, :].unsqueeze(1).broadcast_to([128, 16, 64]),
                                                               in1=segsel.t[:, :].unsqueeze(2).broadcast_to([128, 16, 64]), op=ALU.mult),
                       r=[b_Usb, segsel], w=[UBLK])
                    op(PO, lambda e, hc=hc: e.tensor_tensor(out=VBLK.t[:], in0=Vtm.t[:, hc].unsqueeze(1).broadcast_to([128, 16, 64]),
                                                             in1=segsel.t[:, :].unsqueeze(2).broadcast_to([128, 16, 64]), op=ALU.mult),
                       r=[Vtm, segsel], w=[VBLK])
                    pq, b_pq = ps2.next()
                    for half in range(2):
                        op(PE, lambda e, half=half, hc=hc, base=base, pq=pq: e.matmul(pq[base:base + 64, half * 512:(half + 1) * 512], lhsT=Btm.t[:, hc],
                                                                                        rhs=UBLK.t[:, half * 8:(half + 1) * 8, :].rearrange("p b i -> p (b i)"), start=True, stop=False),
                           r=[Btm, UBLK], w=[b_pq] if half == 0 else (), disjoint=[b_pq] if half else (), signal=False)
                        op(PE, lambda e, half=half, hc=hc, base=base, pq=pq: e.matmul(pq[base:base + 64, half * 512:(half + 1) * 512], lhsT=Ktm.t[:, hc],
                                                                                        rhs=VBLK.t[:, half * 8:(half + 1) * 8, :].rearrange("p b i -> p (b i)"), start=False, stop=True),
                           r=[Ktm, VBLK], disjoint=[b_pq])
                    op(V, lambda e, hp=hp, base=base, pq=pq: e.tensor_tensor(out=S0s[base:base + 64, hp, :, :], in0=pq[base:base + 64, :].rearrange("p (b i) -> p b i", i=64),
                                                                               in1=S0s[base:base + 64, hp, :, :], op=ALU.add),
                       xr=[b_pq], w=[P.buf()], r=[b_S0s[hp]])
                    op(PO, lambda e, hp=hp, base=base: e.tensor_tensor(out=S0s[base:base + 64, hp, :, :], in0=S0s[base:base + 64, hp, :, :],
                                                                        in1=EW.t[base:base + 64, hp, :].rearrange("p (b t) -> p b t", t=8)[:, :, 7:8].broadcast_to([64, 16, 64]),
                                                                        op=ALU.mult),
                       r=[EW], w=[b_S0s[hp]] if h2 == 1 else (), disjoint=[b_S0s[hp]] if h2 == 0 else ())
            for g in range(2):
                alive = [head_gen(h) for h in range(4 * g, 4 * g + 4)]
                while alive:
                    for gn in list(alive):
                        try:
                            next(gn)
                        except StopIteration:
                            alive.remove(gn)
            if full:
                if kind != "sample":
                    op(A, lambda e: e.copy(out=Ytm.t[:], in_=psY.t[:, 0:512]), xr=[psY], w=[Ytm])
                y3 = lambda x: x.t[:, 0:512].rearrange("p (h i) -> p h i", i=64)
                b8 = lambda ap: ap.unsqueeze(2).broadcast_to([128, 8, 64])
                op(V, lambda e: e.tensor_reduce(out=ST8.t[:, 0:8], in_=y3(Ytm), axis=AX.X, op=ALU.add), r=[Ytm], w=[ST8])
                op(PO, lambda e: e.tensor_scalar(out=ST8.t[:, 8:16], in0=ST8.t[:, 0:8], scalar1=-1.0 / 64, scalar2=None, op0=ALU.mult), w=[ST8])
                op(PO, lambda e: e.tensor_tensor(out=y3(O1), in0=y3(Ytm), in1=b8(ST8.t[:, 8:16]), op=ALU.add), r=[Ytm, ST8], w=[O1])
                op(PO, lambda e: e.tensor_tensor(out=O2.t[:], in0=O1.t[:], in1=O1.t[:], op=ALU.mult), r=[O1], w=[O2])
                op(V, lambda e: e.tensor_reduce(out=ST8.t[:, 16:24], in_=y3(O2), axis=AX.X, op=ALU.add), r=[O2], w=[ST8])
                op(A, lambda e: e.activation(out=ST8.t[:, 24:32], in_=ST8.t[:, 16:24], func=AF.Sqrt, scale=1.0 / 64, bias=cst.t[:, 2:3]), r=[cst], w=[ST8])
                op(V, lambda e: e.reciprocal(out=ST8.t[:, 16:24], in_=ST8.t[:, 24:32]), w=[ST8])
                op(PO, lambda e: e.tensor_tensor(out=y3(O2), in0=y3(O1), in1=b8(ST8.t[:, 16:24]), op=ALU.mult), r=[O1, ST8], w=[O2])
                op(PO, lambda e: e.tensor_tensor(out=O1.t[:], in0=O2.t[:], in1=lnw.t[:], op=ALU.mult), r=[O2, lnw], w=[O1])
                op(PO, lambda e: e.tensor_tensor(out=O2.t[:], in0=O1.t[:], in1=lnb.t[:], op=ALU.add), r=[O1, lnb], w=[O2])
                op(V, lambda e: e.tensor_tensor(out=y3(O3), in0=Vtm.t[:, 0:512].rearrange("p (h i) -> p h i", i=64), in1=b8(COEF.t[:, 0:8]), op=ALU.mult),
                   r=[Vtm, COEF], w=[O3])
                op(V, lambda e: e.tensor_tensor(out=O1.t[:], in0=O2.t[:], in1=O3.t[:], op=ALU.add), r=[O2, O3], w=[O1])
                op(V, lambda e: e.tensor_tensor(out=OAb.t[:], in0=O1.t[:], in1=Gtm.t[:], op=ALU.mult), r=[O1, Gtm], w=[OAb])
                pt = nextps()
                ptb = pt.t[:].bitcast(BF16)
                for hp in range(4):
                    op(PE, lambda e, hp=hp, ptb=ptb: e.transpose(out=ptb[:, hp * 128:(hp + 1) * 128], in_=OAb.t[:, hp * 128:(hp + 1) * 128], identity=identb.t[:]),
                       r=[OAb, identb], w=[pt] if hp == 0 else (), disjoint=[pt] if hp else ())
                op(A, lambda e, ptb=ptb: e.copy(out=OAT.t[:].rearrange("p a b -> p (a b)"), in_=ptb[:, 0:512]), xr=[pt], w=[OAT])
                P.dma("sp", T["oaT_d"].rearrange("c p t -> p c t")[:, :, q0:q0 + 128], OAT.t[:], reads=[OAT], disjoint=[T["b_oaT"]])

        tiles = list(range(NTILE))
        if DEBUG_TILES is not None:
            tiles = DEBUG_TILES
        for t in tiles:
            rwkv_tile(t)
        for hp in range(4):
            pt = nextps()
            op(PE, lambda e, pt=pt, hp=hp: e.transpose(out=pt.t[0:64, 0:128], in_=S0[:, hp, :], identity=identf.t[:]),
               r=[b_S[2 * hp], b_S[2 * hp + 1], identf], w=[pt])
            op(A, lambda e, pt=pt, hp=hp: e.copy(out=SOUT.t[:, hp, :], in_=pt.t[0:64, 0:128]), xr=[pt], disjoint=[SOUT])
        for h2 in range(2):
            P.dma("sp", T["rwkv_p_out"].rearrange("(a h) i j -> h i a j", h=2)[h2], SOUT.t[:, 0:4, h2 * 64:(h2 + 1) * 64], reads=[SOUT])
        for hp in range(4):
            for g in range(4):
                pt = nextps()
                for k in range(4):
                    b = g * 4 + k
                    op(PE, lambda e, pt=pt, k=k, b=b, hp=hp: e.transpose(out=pt.t[0:64, k * 128:(k + 1) * 128], in_=S0s[:, hp, b, :], identity=identf.t[:]),
                       r=[b_S0s[hp], identf], w=[pt] if k == 0 else (), disjoint=[pt] if k else ())
                op(A, lambda e, pt=pt, g=g: e.copy(out=SOUT.t[:, g * 4:(g + 1) * 4, :], in_=pt.t[0:64, 0:512].rearrange("p (b c) -> p b c", c=128)),
                   xr=[pt], w=[SOUT] if g == 0 else (), disjoint=[SOUT] if g else ())
            for h2 in range(2):
                P.dma("sp", T["rwkv_s_out"][:, 2 * hp + h2, :, :].rearrange("b i j -> i b j"),
                      SOUT.t[:, :, h2 * 64:(h2 + 1) * 64], reads=[SOUT])
        P.emit("phaseB1")


def phase_B2(P, nc, T):
    V, A, PO, PE = "dve", "act", "pool", "pe"
    with contextlib.ExitStack() as st:
        def mk(n, s, d):
            return Tl(st.enter_context(nc.sbuf_tensor("b2_" + n, s, d)), P.buf(n))

        def mkps(n, cols=512):
            return Tl(st.enter_context(nc.psum_tensor("b2_" + n, [128, cols], F32)), P.buf(n))

        def op(e, fn, r=(), w=(), **kw):
            return P.op(e, fn, reads=r, writes=w, **kw)

        cst = T["cst"]
        identb = mk("identb", [128, 128], BF16)
        identf = mk("identf", [128, 128], F32)
        P.dma("pool", identb.t[:], T["ident"], writes=[identb])
        P.dma("sp", identf.t[:], T["ident"], writes=[identf])
        cm_p = mk("cm_p", [128, 128], BF16)
        cm_s = mk("cm_s", [128, 128], BF16)
        P.dma("pool", cm_p.t[:], T["mtc_p"][:, 0:128], writes=[cm_p])
        P.dma("pool", cm_s.t[:], T["mtc_s"][:, 0:128], writes=[cm_s])
        KT = mk("KT", [128, 4, NTOK], BF16)
        QT = mk("QT", [128, 4, NQ], BF16)
        Vx = mk("Vx", [128, NTILE, 4, 130], BF16)
        for h in range(4):
            P.dma("sp", KT.t[:, h, :], T["kT_d"][h], reads=[T["b_kT"]], disjoint=[KT])
            P.dma("sp", QT.t[:, h, :], T["qT_d"][h], reads=[T["b_qT"]], disjoint=[QT])
            for t0 in range(0, NTILE, 8):
                t1 = min(NTILE, t0 + 8)
                P.dma("sp", Vx.t[:, t0:t1, h, 0:128], T["v_d"].rearrange("(t p) (h e) -> h p t e", p=128, h=4)[h][:, t0:t1, :], reads=[T["b_v"]], disjoint=[Vx])
        op(PO, lambda e: e.memset(Vx.t[:, :, :, 128:130], 1.0), disjoint=[Vx])
        btab = mk("btab", [128, 4, 32], F32)
        btabp = mk("btabp", [128, 4, 32], F32)
        pmask = mk("pmask", [128, 1], F32)
        P.dma("sp", btab.t[:].rearrange("p a b -> p (a b)"), T["btab"], writes=[btab])
        P.dma("sp", pmask.t[:], T["pmask"], writes=[pmask])
        op(V, lambda e: e.tensor_scalar(out=btabp.t[:], in0=btab.t[:], scalar1=pmask.t[:, 0:1], scalar2=None, op0=ALU.add), r=[btab, pmask], w=[btabp])
        bown = mk("bown", [128, 4], F32)
        P.dma("sp", bown.t[:], T["bown"], writes=[bown])
        bcache = mk("bcache", [128, 16, 32], F32)
        P.dma("sp", bcache.t[:].rearrange("p a b -> p (a b)"), T["bcache"], writes=[bcache])
        lv = mk("lv", [128, 4, 64], F32)
        for i, n in enumerate(["lam_q1", "lam_k1", "lam_q2", "lam_k2"]):
            P.dma("sp", lv.t[:, i, :], T[n].broadcast_to([128, 64]), disjoint=[lv])
        lt = mk("lt", [128, 2, 64], F32)
        ls = mk("ls", [128, 8], F32)
        op(V, lambda e: e.tensor_tensor(out=lt.t[:, 0, :], in0=lv.t[:, 0, :], in1=lv.t[:, 1, :], op=ALU.mult), r=[lv], disjoint=[lt])
        op(V, lambda e: e.tensor_tensor(out=lt.t[:, 1, :], in0=lv.t[:, 2, :], in1=lv.t[:, 3, :], op=ALU.mult), r=[lv], disjoint=[lt])
        op(V, lambda e: e.tensor_reduce(out=ls.t[:, 0:2], in_=lt.t[:], axis=AX.X, op=ALU.add), r=[lt], w=[ls])
        op(A, lambda e: e.activation(out=ls.t[:, 2:4], in_=ls.t[:, 0:2], func=AF.Exp), w=[ls])
        op(V, lambda e: e.tensor_tensor(out=ls.t[:, 4:5], in0=ls.t[:, 2:3], in1=ls.t[:, 3:4], op=ALU.subtract), w=[ls])
        op(V, lambda e: e.tensor_scalar(out=ls.t[:, 5:6], in0=ls.t[:, 4:5], scalar1=0.2, scalar2=-1.0, op0=ALU.add, op1=ALU.mult), w=[ls])
        sg = mk("sg", [128, 128], F32)
        P.dma("sp", sg.t[:], T["subln_g"].broadcast_to([128, 128]), writes=[sg])
        op(V, lambda e: e.tensor_scalar(out=sg.t[:], in0=sg.t[:], scalar1=0.8, scalar2=None, op0=ALU.mult), w=[sg])

        psr = Rot(P, [st.enter_context(nc.psum_tensor("b2_psr%d" % i, [128, 1024], F32)) for i in range(2)])
        psO = [mkps("psO%d" % i) for i in range(3)]
        pstr = mkps("pstr")

        def nextps():
            t, b = psr.next()
            return Tl(t, b)

        ETr = Rot(P, [st.enter_context(nc.sbuf_tensor("b2_ET%d" % i, [128, 256], BF16)) for i in range(3)])
        T2 = mk("T2", [128, 128], F32)
        ATT = mk("ATT", [128, 128], F32)
        junk = mk("junk", [128, 128], BF16)
        sc = mk("sc", [128, 8], F32)
        OBb = mk("OBb", [128, 512], BF16)
        OBT = mk("OBT", [128, 4, 128], BF16)

        def combine(h, o0, o0c, o1, o1c, xr0, xr1):
            op(V, lambda e: e.reciprocal(out=sc.t[:, 0:1], in_=o0[:, 128:129]), xr=[xr0], w=[sc])
            op(V, lambda e: e.reciprocal(out=sc.t[:, 1:2], in_=o1[:, 128:129]), xr=[xr1], w=[sc])
            op(V, lambda e: e.tensor_tensor(out=sc.t[:, 2:3], in0=sc.t[:, 1:2], in1=ls.t[:, 5:6], op=ALU.mult), r=[ls], w=[sc])
            op(V, lambda e: e.tensor_scalar(out=T2.t[:], in0=o1[:, 0:128], scalar1=sc.t[:, 2:3], scalar2=None, op0=ALU.mult), xr=[xr1], r=[sc], w=[T2])
            op(V, lambda e: e.scalar_tensor_tensor(out=ATT.t[:], in0=o0[:, 0:128], scalar=sc.t[:, 0:1], in1=T2.t[:], op0=ALU.mult, op1=ALU.add),
               xr=[xr0], r=[sc, T2], w=[ATT])
            op(A, lambda e: e.activation(out=junk.t[:], in_=ATT.t[:], func=AF.Square, accum_out=sc.t[:, 3:4]), r=[ATT], w=[junk, sc])
            op(A, lambda e: e.activation(out=sc.t[:, 4:5], in_=sc.t[:, 3:4], func=AF.Sqrt, scale=1.0 / 128, bias=cst.t[:, 3:4]), r=[cst], w=[sc])
            op(V, lambda e: e.reciprocal(out=sc.t[:, 5:6], in_=sc.t[:, 4:5]), w=[sc])
            op(V, lambda e: e.scalar_tensor_tensor(out=OBb.t[:, h * 128:(h + 1) * 128], in0=ATT.t[:], scalar=sc.t[:, 5:6], in1=sg.t[:], op0=ALU.mult, op1=ALU.mult),
               r=[ATT, sc, sg], disjoint=[OBb])

        def finish_tile(qi):
            ptb = pstr.t[:].bitcast(BF16)
            for h in range(4):
                op(PE, lambda e, h=h: e.transpose(out=ptb[:, h * 128:(h + 1) * 128], in_=OBb.t[:, h * 128:(h + 1) * 128], identity=identb.t[:]),
                   r=[OBb, identb], w=[pstr] if h == 0 else (), disjoint=[pstr] if h else ())
            op(A, lambda e: e.copy(out=OBT.t[:].rearrange("p a b -> p (a b)"), in_=ptb[:, 0:512]), xr=[pstr], w=[OBT])
            P.dma("sp", T["obT_d"].rearrange("c p t -> p c t")[:, :, qi * 128:(qi + 1) * 128], OBT.t[:], reads=[OBT], disjoint=[T["b_obT"]])

        qtiles = list(range(NT_OWN)) if DEBUG_QT is None else DEBUG_QT
        if DEBUG_B2 < 1:
            qtiles = []
        for qi in qtiles:
            qg = NT_PRIOR + qi
            for h in range(4):
                o0, o1 = psO[0], psO[1]
                slope_h = 2.0 ** (-2.0 * (h + 1))
                kts = [kt for kt in range(0, qg + 1) if slope_h * (128 * (qg - kt) - 127) <= 80.0]
                def qk_stage(kt, h=h, qi=qi):
                    pS = nextps()
                    for c in range(2):
                        op(PE, lambda e, c=c, kt=kt, pS=pS, h=h, qi=qi: e.matmul(pS.t[:, c * 512:c * 512 + 128], lhsT=KT.t[c * 64:(c + 1) * 64, h, kt * 128:(kt + 1) * 128],
                                                                     rhs=QT.t[c * 64:(c + 1) * 64, h, qi * 128:(qi + 1) * 128], start=True, stop=True),
                           r=[KT, QT], w=[pS] if c == 0 else (), disjoint=[pS] if c else ())
                    return pS

                def ex_pv_stage(ii, kt, pS, h=h, qg=qg, o0=o0, o1=o1, nk=len(kts)):
                    ET, b_ET = ETr.next()
                    o = qg - kt
                    bt = btabp if kt < NT_PRIOR else btab
                    op(A, lambda e, pS=pS, ET=ET, bt=bt, o=o, h=h: e.activation(out=ET[:].rearrange("p (c q) -> p c q", c=2), in_=pS.t[:, :].rearrange("p (c x) -> p c x", c=2)[:, :, 0:128],
                                                                           func=AF.Exp, scale=0.125, bias=bt.t[:, h, o:o + 1]),
                       xr=[pS], r=[bt], w=[b_ET])
                    if kt == qg:
                        op(V, lambda e, ET=ET: e.tensor_tensor(out=ET[:].rearrange("p (c q) -> p c q", c=2), in0=ET[:].rearrange("p (c q) -> p c q", c=2),
                                                                in1=cm_p.t[:, :].unsqueeze(1).broadcast_to([128, 2, 128]), op=ALU.mult), r=[cm_p], w=[b_ET])
                    first, last = (ii == 0), (ii == nk - 1)
                    op(PE, lambda e, ET=ET, kt=kt, first=first, last=last, h=h, o0=o0: e.matmul(o0.t[:, 0:129], lhsT=ET[:, 0:128], rhs=Vx.t[:, kt, h, 0:129], start=first, stop=last),
                       r=[b_ET, Vx], w=[o0] if first else (), disjoint=() if first else [o0], signal=last)
                    op(PE, lambda e, ET=ET, kt=kt, first=first, last=last, h=h, o1=o1: e.matmul(o1.t[:, 0:129], lhsT=ET[:, 128:256], rhs=Vx.t[:, kt, h, 0:129], start=first, stop=last),
                       r=[b_ET, Vx], w=[o1] if first else (), disjoint=() if first else [o1], signal=last)

                pend = {}
                for ii in range(len(kts) + 1):
                    if ii < len(kts):
                        pend[ii] = qk_stage(kts[ii])
                    if ii >= 1:
                        ex_pv_stage(ii - 1, kts[ii - 1], pend.pop(ii - 1))
                if DEBUG_B2 >= 4:
                    combine(h, o0.t, None, o1.t, None, o0, o1)
            if DEBUG_B2 >= 5:
                finish_tile(qi)

        if DEBUG_SAMPLE_ATT:
            zb = mk("zb", [128, 512], BF16)
            op(PO, lambda e: e.memset(zb.t[:], 0.0), w=[zb])
            slot = {}
            for h in range(4):
                for c in range(2):
                    i = h * 2 + c
                    slot[(h, c)] = (i // 3, (i % 3) * 130)
            for bnk in range(3):
                op(PE, lambda e, bnk=bnk: e.matmul(psO[bnk].t[:, 0:512], lhsT=zb.t[:, 0:128], rhs=zb.t[:, 0:512], start=True, stop=False), r=[zb], w=[psO[bnk]])
            ptb_i = mk("ptab", [128, 256], I32)
            P.dma("sp", ptb_i.t[:], T["ptab"].broadcast_to([128, 256]), writes=[ptb_i])
            pcol = mk("pcol", [128, 1], I32)
            op(PO, lambda e: e.iota(pcol.t[:], pattern=[[0, 1]], base=0, channel_multiplier=1), w=[pcol])
            pcolf = mk("pcolf", [128, 1], F32)
            op(V, lambda e: e.tensor_copy(out=pcolf.t[:], in_=pcol.t[:]), r=[pcol], w=[pcolf])
            idx = mk("idx", [128, 256], I32)
            op(V, lambda e: e.tensor_scalar(out=idx.t[:], in0=ptb_i.t[:], scalar1=128.0, scalar2=pcolf.t[:, 0:1], op0=ALU.mult, op1=ALU.add), r=[ptb_i, pcolf], w=[idx])
            Kpg = Rot(P, [st.enter_context(nc.sbuf_tensor("b2_Kpg%d" % i, [128, 512], F32)) for i in range(4)])
            Vpg = Rot(P, [st.enter_context(nc.sbuf_tensor("b2_Vpg%d" % i, [128, 512], F32)) for i in range(4)])
            KcTr = Rot(P, [st.enter_context(nc.sbuf_tensor("b2_KcT%d" % i, [128, 4, 128], BF16)) for i in range(3)])
            Vpxr = Rot(P, [st.enter_context(nc.sbuf_tensor("b2_Vpx%d" % i, [128, 4, 130], BF16)) for i in range(4)])
            for (vt, vb) in Vpxr.items:
                op(PO, lambda e, vt=vt: e.memset(vt[:], 1.0), w=[vb])
            XB = mk("XB", [128, 2, 32], F32)
            ETz = [mk("ETz%d" % i, [128, 4, 2, 128], BF16) for i in range(3)]
            ETz_b = [None] * 3
            for z in ETz:
                op(PO, lambda e, z=z: e.memset(z.t[:], 0.0), w=[z])
            ez = 0
            sq0 = NT_OWN * 128
            items = [(b, pg) for b in range(16) for pg in range(16)]
            stash = {}

            def s1(i):
                b, pg = items[i]
                col = b * 16 + pg
                kp, b_kp = Kpg.next()
                vp, b_vp = Vpg.next()
                P.dma_ind(kp[:, :], T["cache_k"], idx.t[:, col:col + 1], reads=[idx], writes=[b_kp])
                P.dma_ind(vp[:, :], T["cache_v"], idx.t[:, col:col + 1], reads=[idx], writes=[b_vp])
                pT = nextps()
                for h in range(4):
                    op(PE, lambda e, h=h, kp=kp, pT=pT: e.transpose(out=pT.t[:, h * 128:(h + 1) * 128], in_=kp[:, h * 128:(h + 1) * 128], identity=identf.t[:]),
                       r=[b_kp, identf], w=[pT] if h == 0 else (), disjoint=[pT] if h else ())
                kc, b_kc = KcTr.next()
                op(A, lambda e, kc=kc, pT=pT: e.copy(out=kc[:].rearrange("p a b -> p (a b)"), in_=pT.t[:, 0:512]), xr=[pT], w=[b_kc])
                vx, b_vx = Vpxr.next()
                op(V, lambda e, vx=vx, vp=vp: e.tensor_copy(out=vx[:, :, 0:128], in_=vp[:, :].rearrange("p (h e) -> p h e", h=4)), r=[b_vp], w=[b_vx])
                stash[i] = dict(kc=kc, b_kc=b_kc, vx=vx, b_vx=b_vx)

            def s2(i):
                b, pg = items[i]
                d = stash[i]
                kc, b_kc = d["kc"], d["b_kc"]
                pS = nextps()
                for h in range(4):
                    for c in range(2):
                        first = (h == 0 and c == 0)
                        op(PE, lambda e, h=h, c=c, kc=kc, pS=pS, b=b: e.matmul(pS.t[:, c * 512 + h * 8:c * 512 + h * 8 + 8], lhsT=kc[c * 64:(c + 1) * 64, h, :],
                                                                           rhs=QT.t[c * 64:(c + 1) * 64, h, sq0 + b * 8:sq0 + b * 8 + 8], start=True, stop=True),
                           r=[b_kc, QT], w=[pS] if first else (), disjoint=() if first else [pS])
                op(V, lambda e, pS=pS, pg=pg: e.scalar_tensor_tensor(out=XB.t[:], in0=pS.t[:, :].rearrange("p (c x) -> p c x", c=2)[:, :, 0:32], scalar=0.125,
                                                                       in1=bcache.t[:, pg, :].unsqueeze(1).broadcast_to([128, 2, 32]), op0=ALU.mult, op1=ALU.add),
                   xr=[pS], r=[bcache], w=[XB])
                zi = i % 3
                z = ETz[zi]
                zprev = ETz_b[zi]
                if zprev is not None and zprev != b:
                    op(PO, lambda e, z=z, zprev=zprev: e.memset(z.t[:, :, :, zprev * 8:zprev * 8 + 8], 0.0), w=[z])
                ETz_b[zi] = b
                op(A, lambda e, z=z, b=b: e.activation(out=z.t[:, :, :, b * 8:b * 8 + 8], in_=XB.t[:].rearrange("p c (h q) -> p h c q", h=4), func=AF.Exp),
                   r=[XB], w=[z])
                d["z"] = z

            def s3(i):
                d = stash.pop(i)
                z, vx, b_vx = d["z"], d["vx"], d["b_vx"]
                for h in range(4):
                    for c in range(2):
                        bnk, c0 = slot[(h, c)]
                        op(PE, lambda e, h=h, c=c, z=z, vx=vx, bnk=bnk, c0=c0: e.matmul(psO[bnk].t[:, c0:c0 + 129], lhsT=z.t[:, h, c, :], rhs=vx[:, h, 0:129], start=False, stop=False),
                           r=[z, b_vx], disjoint=[psO[bnk]], signal=(h == 3 and c == 1))

            NI = len(items)
            for j in range(NI + 2):
                if j < NI:
                    s1(j)
                if 1 <= j <= NI:
                    s2(j - 1)
                if j >= 2:
                    s3(j - 2)
            kt = NTILE - 1
            for h in range(4):
                pS = nextps()
                for c in range(2):
                    op(PE, lambda e, c=c, h=h, pS=pS: e.matmul(pS.t[:, c * 512:c * 512 + 128], lhsT=KT.t[c * 64:(c + 1) * 64, h, kt * 128:(kt + 1) * 128],
                                                                 rhs=QT.t[c * 64:(c + 1) * 64, h, sq0:sq0 + 128], start=True, stop=True),
                       r=[KT, QT], w=[pS] if c == 0 else (), disjoint=[pS] if c else ())
                ET, b_ET = ETr.next()
                op(A, lambda e, pS=pS, ET=ET, h=h: e.activation(out=ET[:].rearrange("p (c q) -> p c q", c=2), in_=pS.t[:, :].rearrange("p (c x) -> p c x", c=2)[:, :, 0:128],
                                                                func=AF.Exp, scale=0.125, bias=bown.t[:, h:h + 1]), xr=[pS], r=[bown], w=[b_ET])
                op(V, lambda e, ET=ET: e.tensor_tensor(out=ET[:].rearrange("p (c q) -> p c q", c=2), in0=ET[:].rearrange("p (c q) -> p c q", c=2),
                                                        in1=cm_s.t[:, :].unsqueeze(1).broadcast_to([128, 2, 128]), op=ALU.mult), r=[cm_s], w=[b_ET])
                for c in range(2):
                    bnk, c0 = slot[(h, c)]
                    op(PE, lambda e, c=c, h=h, ET=ET, bnk=bnk, c0=c0: e.matmul(psO[bnk].t[:, c0:c0 + 129], lhsT=ET[:, c * 128:(c + 1) * 128], rhs=Vx.t[:, kt, h, 0:129], start=False, stop=True),
                       r=[b_ET, Vx], disjoint=[psO[bnk]])
            for h in range(4):
                b0, c0 = slot[(h, 0)]
                b1, c1 = slot[(h, 1)]
                combine(h, psO[b0].t[:, c0:c0 + 129], None, psO[b1].t[:, c1:c1 + 129], None, psO[b0], psO[b1])
            finish_tile(NT_OWN)
        P.emit("phaseB2")


def phase_C1(P, nc, T):
    V, A, PO, PE = "dve", "act", "pool", "pe"
    with contextlib.ExitStack() as st:
        def mk(n, s, d):
            return Tl(st.enter_context(nc.sbuf_tensor("c1_" + n, s, d)), P.buf(n))

        def op(e, fn, r=(), w=(), **kw):
            return P.op(e, fn, reads=r, writes=w, **kw)

        cst = T["cst"]
        identb = mk("identb", [128, 128], BF16)
        P.dma("pool", identb.t[:], T["ident"], writes=[identb])
        wa = mk("wa", [128, 4, D], BF16)
        wb = mk("wb", [128, 4, D], BF16)
        wo = mk("wo", [128, 8, D], BF16)
        for kc in range(4):
            P.dma("pool", wa.t[:, kc, :], T["w_br_a"][kc * 128:(kc + 1) * 128, :], disjoint=[wa])
            P.dma("pool", wb.t[:, kc, :], T["w_br_b"][kc * 128:(kc + 1) * 128, :], disjoint=[wb])
        for kc in range(8):
            P.dma("pool", wo.t[:, kc, :], T["w_out"][kc * 128:(kc + 1) * 128, :], disjoint=[wo])
        gffn = mk("gffn", [128, D], F32)
        P.dma("sp", gffn.t[:], T["g_ffn"].broadcast_to([128, D]), writes=[gffn])
        psr = Rot(P, [st.enter_context(nc.psum_tensor("c1_psr%d" % i, [128, 512], F32)) for i in range(8)])

        def nextps():
            t, b = psr.next()
            return Tl(t, b)

        oaT = [mk("oaT%d" % i, [128, 4, 512], BF16) for i in range(2)]
        obT = [mk("obT%d" % i, [128, 4, 512], BF16) for i in range(2)]
        gts = [mk("gts%d" % i, [128, 16, 512], BF16) for i in range(2)]
        mT = [mk("mT%d" % i, [128, 8, 512], BF16) for i in range(2)]
        t1 = [mk("t1_%d" % i, [128, 512], F32) for i in range(2)]
        t2 = [mk("t2_%d" % i, [128, 512], F32) for i in range(2)]
        xt = [mk("xt%d" % i, [128, D], F32) for i in range(2)]
        xn = [mk("xn%d" % i, [128, D], F32) for i in range(2)]
        hb = [mk("hb%d" % i, [128, D], BF16) for i in range(2)]
        hT = [mk("hT%d" % i, [128, 8, 128], BF16) for i in range(2)]
        junk = mk("junk", [128, D], BF16)
        ss = [mk("ss%d" % i, [128, 4], F32) for i in range(2)]
        groups = [(g * 4, 4) for g in range(4)] + [(16, 1)]
        tcount = 0
        for gi, (qt0, nt) in enumerate(groups):
            G = nt * 128
            q0 = qt0 * 128
            oa, ob, gt, m = oaT[gi % 2], obT[gi % 2], gts[gi % 2], mT[gi % 2]
            P.dma("sp", oa.t[:, :, 0:G], T["oaT_d"].rearrange("c p t -> p c t")[:, :, q0:q0 + G], reads=[T["b_oaT"]], writes=[oa])
            P.dma("sp", ob.t[:, :, 0:G], T["obT_d"].rearrange("c p t -> p c t")[:, :, q0:q0 + G], reads=[T["b_obT"]], writes=[ob])
            P.dma("sp", gt.t[:, :, 0:G], T["gates_d"].rearrange("c p t -> p c t")[:, :, q0:q0 + G], reads=[T["b_gates"]], writes=[gt])
            for nch in range(8):
                pa = nextps()
                pb = nextps()
                for kc in range(4):
                    op(PE, lambda e, pa=pa, kc=kc, nch=nch, oa=oa, G=G: e.matmul(pa.t[:, 0:G], lhsT=wa.t[:, kc, nch * 128:(nch + 1) * 128], rhs=oa.t[:, kc, 0:G], start=(kc == 0), stop=(kc == 3)),
                       r=[wa, oa], w=[pa] if kc == 0 else (), disjoint=[pa] if kc else (), signal=(kc == 3))
                for kc in range(4):
                    op(PE, lambda e, pb=pb, kc=kc, nch=nch, ob=ob, G=G: e.matmul(pb.t[:, 0:G], lhsT=wb.t[:, kc, nch * 128:(nch + 1) * 128], rhs=ob.t[:, kc, 0:G], start=(kc == 0), stop=(kc == 3)),
                       r=[wb, ob], w=[pb] if kc == 0 else (), disjoint=[pb] if kc else (), signal=(kc == 3))
                ta, tb = t1[nch % 2], t2[nch % 2]
                op(V, lambda e, pa=pa, ta=ta, gt=gt, nch=nch, G=G: e.tensor_tensor(out=ta.t[:, 0:G], in0=pa.t[:, 0:G], in1=gt.t[:, nch, 0:G], op=ALU.mult), xr=[pa], r=[gt], w=[ta])
                op(V, lambda e, pb=pb, tb=tb, gt=gt, nch=nch, G=G: e.tensor_tensor(out=tb.t[:, 0:G], in0=pb.t[:, 0:G], in1=gt.t[:, 8 + nch, 0:G], op=ALU.mult), xr=[pb], r=[gt], w=[tb])
                op(PO, lambda e, ta=ta, tb=tb, m=m, nch=nch, G=G: e.tensor_tensor(out=m.t[:, nch, 0:G], in0=ta.t[:, 0:G], in1=tb.t[:, 0:G], op=ALU.add), r=[ta, tb], disjoint=[m])
            for ti in range(nt):
                qt = qt0 + ti
                tg = NT_PRIOR + qt
                x, xo, h_, hTt, s_ = xt[tcount % 2], xn[tcount % 2], hb[tcount % 2], hT[tcount % 2], ss[tcount % 2]
                tcount += 1
                P.dma("sp", x.t[:], T["x_all"][tg * 128:(tg + 1) * 128, :], writes=[x])
                for half in range(2):
                    px = nextps()
                    for kc in range(8):
                        op(PE, lambda e, px=px, kc=kc, half=half, m=m, ti=ti: e.matmul(px.t[:, 0:512], lhsT=m.t[:, kc, ti * 128:(ti + 1) * 128], rhs=wo.t[:, kc, half * 512:(half + 1) * 512],
                                                                                    start=(kc == 0), stop=(kc == 7)),
                           r=[m, wo], w=[px] if kc == 0 else (), disjoint=[px] if kc else (), signal=(kc == 7))
                    op(V, lambda e, px=px, half=half, x=x, xo=xo: e.tensor_tensor(out=xo.t[:, half * 512:(half + 1) * 512], in0=px.t[:, 0:512], in1=x.t[:, half * 512:(half + 1) * 512], op=ALU.add),
                       xr=[px], r=[x], w=[xo] if half == 0 else (), disjoint=[xo] if half else ())
                P.dma("sp", T["xnew_d"][qt * 128:(qt + 1) * 128, :], xo.t[:], reads=[xo], disjoint=[T["b_xnew"]])
                op(A, lambda e, xo=xo, s_=s_: e.activation(out=junk.t[:], in_=xo.t[:], func=AF.Square, accum_out=s_.t[:, 0:1]), r=[xo], w=[junk, s_])
                op(A, lambda e, s_=s_: e.activation(out=s_.t[:, 1:2], in_=s_.t[:, 0:1], func=AF.Sqrt, scale=1.0 / D, bias=cst.t[:, 0:1]), r=[cst], w=[s_])
                op(V, lambda e, s_=s_: e.reciprocal(out=s_.t[:, 2:3], in_=s_.t[:, 1:2]), w=[s_])
                op(V, lambda e, xo=xo, s_=s_, h_=h_: e.scalar_tensor_tensor(out=h_.t[:], in0=xo.t[:], scalar=s_.t[:, 2:3], in1=gffn.t[:], op0=ALU.mult, op1=ALU.mult),
                   r=[xo, s_, gffn], w=[h_])
                pt = nextps()
                ptb = pt.t[:].bitcast(BF16)
                for kc in range(8):
                    op(PE, lambda e, kc=kc, ptb=ptb, h_=h_: e.transpose(out=ptb[:, kc * 128:(kc + 1) * 128], in_=h_.t[:, kc * 128:(kc + 1) * 128], identity=identb.t[:]),
                       r=[h_, identb], w=[pt] if kc == 0 else (), disjoint=[pt] if kc else ())
                op(A, lambda e, ptb=ptb, hTt=hTt: e.copy(out=hTt.t[:].rearrange("p a b -> p (a b)"), in_=ptb[:, 0:1024]), xr=[pt], w=[hTt])
                P.dma("sp", T["hnT_d"].rearrange("c p t -> p c t")[:, :, qt * 128:(qt + 1) * 128], hTt.t[:], reads=[hTt], disjoint=[T["b_hnT"]])
        P.emit("phaseC1")


def phase_C2(P, nc, T):
    V, A, PO, PE = "dve", "act", "pool", "pe"
    with contextlib.ExitStack() as st:
        def mk(n, s, d):
            return Tl(st.enter_context(nc.sbuf_tensor("c2_" + n, s, d)), P.buf(n))

        def op(e, fn, r=(), w=(), **kw):
            return P.op(e, fn, reads=r, writes=w, **kw)

        cst = T["cst"]
        wu = mk("wu", [128, 8, 4096], BF16)
        wd = mk("wd", [128, 32, D], BF16)
        for kc in range(8):
            for j in range(2):
                P.dma("pool", wu.t[:, kc, j * 2048:(j + 1) * 2048], T["w_up"][kc * 128:(kc + 1) * 128, j * 2048:(j + 1) * 2048], disjoint=[wu])
        for fc in range(32):
            P.dma("pool", wd.t[:, fc, :], T["w_down"][fc * 128:(fc + 1) * 128, :], disjoint=[wd])
        gfin = mk("gfin", [128, D], F32)
        P.dma("sp", gfin.t[:], T["g_final"].broadcast_to([128, D]), writes=[gfin])
        psr = Rot(P, [st.enter_context(nc.psum_tensor("c2_psr%d" % i, [128, 512], F32)) for i in range(8)])

        def nextps():
            t, b = psr.next()
            return Tl(t, b)

        hT = [mk("hT%d" % i, [128, 8, 256], BF16) for i in range(2)]
        aT = [mk("aT%d" % i, [128, 32, 256], BF16) for i in range(1)]
        rl = [mk("rl%d" % i, [128, 256], F32) for i in range(3)]
        xw = [mk("xw%d" % i, [128, D], F32) for i in range(2)]
        yo = [mk("yo%d" % i, [128, D], F32) for i in range(2)]
        yf = [mk("yf%d" % i, [128, D], F32) for i in range(2)]
        junk = mk("junk", [128, D], BF16)
        ss = [mk("ss%d" % i, [128, 4], F32) for i in range(2)]
        groups = [(g * 2, 2) for g in range(8)] + [(16, 1)]
        tcount = 0
        for gi, (qt0, nt) in enumerate(groups):
            G = nt * 128
            q0 = qt0 * 128
            h_ = hT[gi % 2]
            a_ = aT[0]
            P.dma("sp", h_.t[:, :, 0:G], T["hnT_d"].rearrange("c p t -> p c t")[:, :, q0:q0 + G], reads=[T["b_hnT"]], writes=[h_])
            for fc in range(32):
                pu = nextps()
                for kc in range(8):
                    op(PE, lambda e, pu=pu, kc=kc, fc=fc, h_=h_, G=G: e.matmul(pu.t[:, 0:G], lhsT=wu.t[:, kc, fc * 128:(fc + 1) * 128], rhs=h_.t[:, kc, 0:G], start=(kc == 0), stop=(kc == 7)),
                       r=[wu, h_], w=[pu] if kc == 0 else (), disjoint=[pu] if kc else (), signal=(kc == 7))
                r_ = rl[fc % 3]
                op(A, lambda e, pu=pu, r_=r_, G=G: e.activation(out=r_.t[:, 0:G], in_=pu.t[:, 0:G], func=AF.Relu), xr=[pu], w=[r_])
                op(PO, lambda e, r_=r_, a_=a_, fc=fc, G=G: e.tensor_tensor(out=a_.t[:, fc, 0:G], in0=r_.t[:, 0:G], in1=r_.t[:, 0:G], op=ALU.mult), r=[r_], disjoint=[a_])
            for ti in range(nt):
                qt = qt0 + ti
                x, y, yfin, s_ = xw[tcount % 2], yo[tcount % 2], yf[tcount % 2], ss[tcount % 2]
                tcount += 1
                P.dma("sp", x.t[:], T["xnew_d"][qt * 128:(qt + 1) * 128, :], reads=[T["b_xnew"]], writes=[x])
                for half in range(2):
                    pd = nextps()
                    for fc in range(32):
                        op(PE, lambda e, pd=pd, fc=fc, half=half, a_=a_, ti=ti: e.matmul(pd.t[:, 0:512], lhsT=a_.t[:, fc, ti * 128:(ti + 1) * 128], rhs=wd.t[:, fc, half * 512:(half + 1) * 512],
                                                                                      start=(fc == 0), stop=(fc == 31)),
                           r=[a_, wd], w=[pd] if fc == 0 else (), disjoint=[pd] if fc else (), signal=(fc == 31))
                    op(V, lambda e, pd=pd, half=half, x=x, y=y: e.tensor_tensor(out=y.t[:, half * 512:(half + 1) * 512], in0=pd.t[:, 0:512], in1=x.t[:, half * 512:(half + 1) * 512], op=ALU.add),
                       xr=[pd], r=[x], w=[y] if half == 0 else (), disjoint=[y] if half else ())
                op(A, lambda e, y=y, s_=s_: e.activation(out=junk.t[:], in_=y.t[:], func=AF.Square, accum_out=s_.t[:, 0:1]), r=[y], w=[junk, s_])
                op(A, lambda e, s_=s_: e.activation(out=s_.t[:, 1:2], in_=s_.t[:, 0:1], func=AF.Sqrt, scale=1.0 / D, bias=cst.t[:, 0:1]), r=[cst], w=[s_])
                op(V, lambda e, s_=s_: e.reciprocal(out=s_.t[:, 2:3], in_=s_.t[:, 1:2]), w=[s_])
                op(V, lambda e, y=y, s_=s_, yfin=yfin: e.scalar_tensor_tensor(out=yfin.t[:], in0=y.t[:], scalar=s_.t[:, 2:3], in1=gfin.t[:], op0=ALU.mult, op1=ALU.mult),
                   r=[y, s_, gfin], w=[yfin])
                P.dma("sp", T["yout"][qt * 128:(qt + 1) * 128, :], yfin.t[:], reads=[yfin])
        P.emit("phaseC2")


EPS_T = [None, None]
DEBUG_QT = None
DEBUG_DUMP = False
DEBUG_B2 = 9
DEBUG_SAMPLE_ATT = True
DEBUG_TILES = None
DEBUG_GROUPS = None
DEBUG_CHUNKS = None
DEBUG_NOTM = False
DEBUG_HALF = None
DEBUG_NOOUT = False


def build_program(stage=99):
    nc = bass.Bass("TRN2", target_bir_lowering=False)
    din = lambda n, s, d=F32: nc.dram_tensor(n, s, d, kind="ExternalInput").ap()
    dout = lambda n, s, d=F32: nc.dram_tensor(n, s, d, kind="ExternalOutput").ap()
    dscr = (lambda n, s, d: nc.dram_tensor(n, s, d, kind="ExternalOutput").ap()) if DEBUG_DUMP else (lambda n, s, d: nc.dram_tensor(n, s, d).ap())
    T = {}
    T["x_all"] = din("x_all", [NTOK, D])
    T["st_shift"] = din("st_shift", [16, D])
    T["w_in"] = din("w_in", [D, NCOL])
    T["g_mix"] = din("g_mix", [1, D])
    T["mu_fm"] = din("mu_fm", [128, 14])
    for n, shp in [("ident", [128, 128]), ("mtc_p", [128, 256]), ("mtc_s", [128, 256]), ("ms_p", [128, 128]), ("ms_s", [128, 128]),
                   ("rm_p", [128, 512]), ("rm_s", [128, 512]), ("bones", [128, 128]), ("sel8", [128, 32]), ("segsel", [128, 16]),
                   ("w2", [64, 512]), ("a2", [64, 512]), ("g2", [128, 512]), ("w0_fm", [128, 4]), ("a0_fm", [128, 4]),
                   ("kk_fm", [128, 4]), ("ka_fm", [128, 4]), ("rk_fm", [128, 4]), ("ln_w", [1, 512]), ("ln_b", [1, 512]),
                   ("st_rwkv", [16, 8, 64, 64]), ("pmask", [128, 1]), ("btab", [128, 128]), ("bown", [128, 4]), ("bcache", [128, 512]),
                   ("lam_q1", [1, 64]), ("lam_k1", [1, 64]), ("lam_q2", [1, 64]), ("lam_k2", [1, 64]), ("subln_g", [1, 128]),
                   ] + ([("cache_k", [NPOOL * 128, 512]), ("cache_v", [NPOOL * 128, 512])] if DEBUG_SAMPLE_ATT else []) + [
                   ("w_br_a", [512, D]), ("w_br_b", [512, D]), ("w_out", [D, D]), ("g_ffn", [1, D]), ("w_up", [D, 4096]), ("w_down", [4096, D]),
                   ("g_final", [1, D])]:
        T[n] = din(n, shp)
    T["ptab"] = din("ptab", [1, 256], I32)
    T["yout"] = dout("yout", [NQ, D])
    T["obT_d"] = dscr("obT_d", [4, 128, NQ], BF16)
    T["xnew_d"] = dscr("xnew_d", [NQ, D], F32)
    T["hnT_d"] = dscr("hnT_d", [8, 128, NQ], BF16)
    T["kout"] = dout("kout", [NQ, 512])
    T["vout"] = dout("vout", [NQ, 512])
    T["shift_out"] = dout("shift_out", [17, D])
    T["rwkv_p_out"] = dout("rwkv_p_out", [8, 64, 64])
    T["rwkv_s_out"] = dout("rwkv_s_out", [16, 8, 64, 64])
    T["prT_d"] = dscr("prT_d", [14, 128, NTOK], F32)
    T["qT_d"] = dscr("qT_d", [4, 128, NQ], BF16)
    T["kT_d"] = dscr("kT_d", [4, 128, NTOK], BF16)
    T["v_d"] = dscr("v_d", [NTOK, 512], BF16)
    T["gates_d"] = dscr("gates_d", [16, 128, NQ], BF16)
    T["oaT_d"] = dscr("oaT_d", [4, 128, NQ], BF16)
    with contextlib.ExitStack() as st:
        P = Prog(nc, st)
        for n in ["prT", "qT", "kT", "v", "gates", "oaT", "obT", "xnew", "hnT"]:
            T["b_" + n] = P.buf()
        cst = Tl(st.enter_context(nc.sbuf_tensor("cst", [128, 8], F32)), P.buf())
        T["cst"] = cst
        for i, val in enumerate([EPS, 1e-24, 64e-5, 1e-5]):
            P.op("pool", lambda e, i=i, val=val: e.memset(cst.t[:, i:i + 1], val), disjoint=[cst])
        phase_A(P, nc, T)
        if stage >= 2:
            phase_B1(P, nc, T)
        if stage >= 3:
            phase_B2(P, nc, T)
        if stage >= 4:
            phase_C1(P, nc, T)
            phase_C2(P, nc, T)
        P.final_wait("sp")
        P.emit("final")
    return nc


def _consts():
    s = np.arange(128)
    seg = s // 8
    le = (s[:, None] <= s[None, :]).astype(np.float32)
    lt = (s[:, None] < s[None, :]).astype(np.float32)
    same = (seg[:, None] == seg[None, :]).astype(np.float32)
    c = {}
    c["ident"] = np.eye(128, dtype=np.float32)
    c["mtc_p"] = np.concatenate([le, lt], axis=1)
    c["mtc_s"] = np.concatenate([le * same, lt * same], axis=1)
    c["ms_p"] = lt.T.copy()
    c["ms_s"] = (lt * same).T.copy()
    rm = np.ones((128, 4, 128), np.float32)
    rm[:, :, 0] = 0
    c["rm_p"] = rm.reshape(128, 512)
    rm = np.ones((128, 4, 128), np.float32)
    rm[:, :, 0::8] = 0
    c["rm_s"] = rm.reshape(128, 512)
    bo = np.zeros((128, 128), np.float32)
    bo[:64, :64] = 1
    bo[64:, 64:] = 1
    c["bones"] = bo
    sel = np.zeros((128, 4, 8), np.float32)
    for p in range(128):
        for hp in range(4):
            sel[p, hp, 2 * hp + p // 64] = 1
    c["sel8"] = sel.reshape(128, 32)
    c["segsel"] = (seg[:, None] == np.arange(16)[None, :]).astype(np.float32)
    slopes = (2.0 ** (-8.0 * np.arange(1, 5) / 4)).astype(np.float32)
    pp = np.arange(128, dtype=np.float32)
    bt = np.zeros((128, 4, 32), np.float32)
    for h in range(4):
        for o in range(32):
            bt[:, h, o] = slopes[h] * (pp - 128.0 * o)
    c["btab"] = bt.reshape(128, 128)
    c["bown"] = (slopes[None, :] * (pp[:, None] % 8)).astype(np.float32)
    bc = np.zeros((128, 16, 4, 8), np.float32)
    for pg in range(16):
        for h in range(4):
            bc[:, pg, h, :] = (slopes[h] * (pg * 128.0 + pp - 2048.0))[:, None]
    c["bcache"] = bc.reshape(128, 512)
    return c


def make_in_maps(inp):
    f = lambda k: np.asarray(inp[k], np.float32)
    xp = f("x_prompt")
    xs = f("x_sample")
    maps = []
    fm4 = lambda v: np.ascontiguousarray(np.asarray(v, np.float32).reshape(4, 128).T)
    shared = {
        "w_in": np.ascontiguousarray(f("w_in")[0]),
        "g_mix": f("g_mix")[0].reshape(1, D).copy(),
        "mu_fm": np.ascontiguousarray(f("mu_shift").reshape(14, 128).T),
        "w2": np.ascontiguousarray(f("w2")[0]), "a2": np.ascontiguousarray(f("a2")[0]), "g2": np.ascontiguousarray(f("g2")[0]),
        "w0_fm": fm4(f("w0")[0]), "a0_fm": fm4(f("a0")[0]), "kk_fm": fm4(f("k_k")[0]), "ka_fm": fm4(f("k_a")[0]),
        "rk_fm": fm4(f("r_k")[0].reshape(512)),
        "ln_w": f("ln_x_w")[0].reshape(1, 512).copy(), "ln_b": f("ln_x_b")[0].reshape(1, 512).copy(),
        "lam_q1": f("lam_q1")[0].reshape(1, 64).copy(), "lam_k1": f("lam_k1")[0].reshape(1, 64).copy(),
        "lam_q2": f("lam_q2")[0].reshape(1, 64).copy(), "lam_k2": f("lam_k2")[0].reshape(1, 64).copy(),
        "subln_g": f("subln_g")[0].reshape(1, 128).copy(),
        "cache_k": np.asarray(inp["cache_k"], np.float32).reshape(NPOOL * 128, 512),
        "cache_v": np.asarray(inp["cache_v"], np.float32).reshape(NPOOL * 128, 512),
        "w_br_a": np.ascontiguousarray(f("w_br_a")[0]), "w_br_b": np.ascontiguousarray(f("w_br_b")[0]),
        "w_out": np.ascontiguousarray(f("w_out")[0]), "g_ffn": f("g_ffn")[0].reshape(1, D).copy(),
        "w_up": np.ascontiguousarray(f("w_up")[0]), "w_down": np.ascontiguousarray(f("w_down")[0]),
        "g_final": f("g_final").reshape(1, D).copy(),
    }
    shared.update(_consts())
    for c in range(8):
        b, p = divmod(c, 2)
        xa = np.zeros((NTOK, D), np.float32)
        if p == 1:
            xa[0:2048] = xp[b, 0:2048]
        xa[2048:4096] = xp[b, 2048 * p:2048 * p + 2048]
        xa[4096:] = xs[16 * c:16 * c + 16].reshape(128, D)
        m = dict(shared)
        m["x_all"] = xa
        m["st_shift"] = np.ascontiguousarray(f("state_shift")[0, 16 * c:16 * c + 16])
        m["st_rwkv"] = np.ascontiguousarray(f("state_rwkv")[0, 16 * c:16 * c + 16])
        m["ptab"] = np.ascontiguousarray(np.asarray(inp["page_table"], np.int32)[16 * c:16 * c + 16]).reshape(1, 256)
        m["pmask"] = np.full((128, 1), 0.0 if p == 1 else -30000.0, np.float32)
        maps.append(m)
    return maps


_NC_CACHE = {}


def kernel(**inp):
    if "nc" not in _NC_CACHE:
        _NC_CACHE["nc"] = build_program()
    nc = _NC_CACHE["nc"]
    in_maps = make_in_maps(inp)
    res = run_bass_kernel_spmd(nc, in_maps, core_ids=list(range(8)))
    R = res.results
    kp = np.zeros((1, 4, 4096, 4, 128), np.float32)
    vp = np.zeros((1, 4, 4096, 4, 128), np.float32)
    ks = np.zeros((1, 128, 8, 4, 128), np.float32)
    vs = np.zeros((1, 128, 8, 4, 128), np.float32)
    shp = np.zeros((1, 4, D), np.float32)
    shs = np.zeros((1, 128, D), np.float32)
    yp = np.zeros((4, 4096, D), np.float32)
    ys = np.zeros((128, 8, D), np.float32)
    rp = np.zeros((1, 4, 8, 64, 64), np.float32)
    rs = np.zeros((1, 128, 8, 64, 64), np.float32)
    for c in range(8):
        b, p = divmod(c, 2)
        r = R[c]
        kp[0, b, 2048 * p:2048 * p + 2048] = r["kout"][0:2048].reshape(2048, 4, 128)
        vp[0, b, 2048 * p:2048 * p + 2048] = r["vout"][0:2048].reshape(2048, 4, 128)
        ks[0, 16 * c:16 * c + 16] = r["kout"][2048:].reshape(16, 8, 4, 128)
        vs[0, 16 * c:16 * c + 16] = r["vout"][2048:].reshape(16, 8, 4, 128)
        if p == 1:
            shp[0, b] = r["shift_out"][0]
            rp[0, b] = r["rwkv_p_out"]
        shs[0, 16 * c:16 * c + 16] = r["shift_out"][1:17]
        rs[0, 16 * c:16 * c + 16] = r["rwkv_s_out"]
        yp[b, 2048 * p:2048 * p + 2048] = r["yout"][0:2048]
        ys[16 * c:16 * c + 16] = r["yout"][2048:].reshape(16, 8, D)
    return (yp, ys, kp, vp, ks, vs, rp, rs, shp, shs)
```

```python
import numpy as np
import contextlib
import concourse.bass as bass
import concourse.mybir as mybir
from concourse.bass_utils import run_bass_kernel_spmd

F32 = mybir.dt.float32
BF16 = mybir.dt.bfloat16
I32 = mybir.dt.int32
AF = mybir.ActivationFunctionType
ALU = mybir.AluOpType
AX = mybir.AxisListType

NDMA = 12
SAME_ENGINE_SYNC = True

D = 1024
NCOL = 5376
NT_PRIOR = 16
NT_OWN = 16
NTILE = 33
NTOK = NTILE * 128
NQT = 17
NQ = NQT * 128
NPOOL = 2560
EPS = 1e-6


class Buf:
    __slots__ = ("name", "w", "r")

    def __init__(self, name):
        self.name = name
        self.w = {}
        self.r = {}


def _bl(xs):
    return [x.b if hasattr(x, "b") else x for x in xs]


class Tl:
    __slots__ = ("t", "b")

    def __init__(self, t, b):
        self.t = t
        self.b = b


class Prog:
    def __init__(self, nc, stack):
        self.nc = nc
        self.stack = stack
        self.eng = {"pe": nc.tensor, "act": nc.scalar, "dve": nc.vector, "pool": nc.gpsimd, "sp": nc.sync}
        self.sem = {}
        for e in self.eng:
            self.sem[e] = stack.enter_context(nc.semaphore("s_" + e))
        for i in range(NDMA):
            self.sem[("d", i)] = stack.enter_context(nc.semaphore("d%d" % i))
        self.cnt = {e: 0 for e in self.eng}
        self.dcnt = [0] * NDMA
        self.drr = 0
        self.lists = {e: [] for e in self.eng}
        self.waited = {e: {} for e in self.eng}
        self.nbuf = 0

    def buf(self, name=None):
        self.nbuf += 1
        return Buf(name or "b%d" % self.nbuf)

    def _deps(self, e, reads, writes, disjoint=(), xr=()):
        deps = {}
        reads, writes, disjoint, xr = _bl(reads), _bl(writes), _bl(disjoint), _bl(xr)

        def need(k, v):
            if deps.get(k, 0) < v:
                deps[k] = v

        for b in reads:
            for k, v in b.w.items():
                need(k, v)
        for b in writes:
            for k, v in b.w.items():
                need(k, v)
            for k, v in b.r.items():
                need(k, v)
        for b in disjoint:
            for k, v in b.r.items():
                need(k, v)
        for b in xr:
            for k, v in b.w.items():
                need(k, v)
            for k, v in b.r.items():
                need(k, v)
        waits = []
        for k, v in deps.items():
            if k == e and (e == "pe" or not SAME_ENGINE_SYNC):
                continue
            if self.waited[e].get(k, 0) >= v:
                continue
            self.waited[e][k] = v
            waits.append((k, v))
        return waits

    def op(self, e, fn, reads=(), writes=(), signal=True, disjoint=(), xr=()):
        waits = self._deps(e, reads, writes, disjoint, xr)
        reads, writes, disjoint = _bl(reads) + _bl(xr), _bl(writes), _bl(disjoint)
        if signal:
            self.cnt[e] += 1
            tok = (e, self.cnt[e])
            inc = (e, 1)
        else:
            tok = (e, self.cnt[e] + 1)
            inc = None
        for b in reads:
            b.r[tok[0]] = max(b.r.get(tok[0], 0), tok[1])
        for b in writes:
            b.w = {tok[0]: tok[1]}
            b.r = {}
        for b in disjoint:
            b.w[tok[0]] = max(b.w.get(tok[0], 0), tok[1])
        self.lists[e].append((waits, fn, inc))
        return tok

    def dma(self, q, out, in_, reads=(), writes=(), disjoint=(), **kw):
        i = self.drr
        self.drr = (i + 1) % NDMA
        key = ("d", i)
        waits = self._deps(q, reads, writes, disjoint)
        reads, writes, disjoint = _bl(reads), _bl(writes), _bl(disjoint)
        prev = self.dcnt[i]
        if prev > 0 and self.waited[q].get(key, 0) < prev:
            self.waited[q][key] = prev
            waits.append((key, prev))
        self.dcnt[i] += 16
        tok = (key, self.dcnt[i])
        for b in reads:
            b.r[key] = max(b.r.get(key, 0), tok[1])
        for b in writes:
            b.w = {tok[0]: tok[1]}
            b.r = {}
        for b in disjoint:
            b.w[tok[0]] = max(b.w.get(tok[0], 0), tok[1])
        self.lists[q].append((waits, lambda eng: eng.dma_start(out=out, in_=in_, **kw), (key, 16)))
        return tok

    def dma_ind(self, out, in_, idx_ap, reads=(), writes=()):
        q = "pool"
        i = self.drr
        self.drr = (i + 1) % NDMA
        key = ("d", i)
        waits = self._deps(q, reads, writes)
        reads, writes = _bl(reads), _bl(writes)
        prev = self.dcnt[i]
        if prev > 0 and self.waited[q].get(key, 0) < prev:
            self.waited[q][key] = prev
            waits.append((key, prev))
        self.dcnt[i] += 16
        tok = (key, self.dcnt[i])
        for b in reads:
            b.r[key] = max(b.r.get(key, 0), tok[1])
        for b in writes:
            b.w = {tok[0]: tok[1]}
            b.r = {}
        self.lists[q].append((waits, lambda eng: eng.indirect_dma_start(
            out=out, out_offset=None, in_=in_, in_offset=bass.IndirectOffsetOnAxis(ap=idx_ap, axis=0)), (key, 16)))
        return tok

    def final_wait(self, e):
        waits = []
        for i in range(NDMA):
            k, v = ("d", i), self.dcnt[i]
            if v > 0 and self.waited[e].get(k, 0) < v:
                self.waited[e][k] = v
                waits.append((k, v))
        self.lists[e].append((waits, None, None))

    def replay(self, e, eng):
        for waits, fn, inc in self.lists[e]:
            for k, v in waits:
                eng.wait_ge(self.sem[k], v)
            if fn is None:
                continue
            ins = fn(eng)
            if inc is not None:
                ins.then_inc(self.sem[inc[0]], inc[1])
        self.lists[e] = []

    def emit(self, name=None):
        with self.nc.Block(name) as block:
            @block.sync
            def _(eng):
                self.replay("sp", eng)

            @block.scalar
            def _(eng):
                self.replay("act", eng)

            @block.vector
            def _(eng):
                self.replay("dve", eng)

            @block.gpsimd
            def _(eng):
                self.replay("pool", eng)

            @block.tensor
            def _(eng):
                self.replay("pe", eng)


class Rot:
    def __init__(self, P, tiles):
        self.items = [(t, P.buf()) for t in tiles]
        self.i = 0

    def next(self):
        it = self.items[self.i % len(self.items)]
        self.i += 1
        return it


CH_R, CH_K, CH_V, CH_L1, CH_GD, CH_Q, CH_AK, CH_AV, CH_G = 0, 4, 8, 12, 13, 14, 18, 22, 26


def tile_kind(t):
    return "prior" if t < NT_PRIOR else ("own" if t < NT_PRIOR + NT_OWN else "sample")


def phase_A(P, nc, T):
    with contextlib.ExitStack() as st:
        sb = lambda n, s, d: st.enter_context(nc.sbuf_tensor(n, s, d))
        ps = lambda n: st.enter_context(nc.psum_tensor(n, [128, 512], F32))
        win = sb("win", [128, 8, NCOL], BF16)
        b_win = P.buf()
        for kc in range(8):
            for j in range(3):
                P.dma("pool", win[:, kc, j * 1792:(j + 1) * 1792],
                      T["w_in"][kc * 128:(kc + 1) * 128, j * 1792:(j + 1) * 1792], disjoint=[b_win])
        gmix = sb("gmix", [128, D], F32)
        b_gmix = P.buf()
        P.dma("sp", gmix[:], T["g_mix"].broadcast_to([128, D]), writes=[b_gmix])
        mu = sb("mu", [128, 14], F32)
        b_mu = P.buf()
        P.dma("sp", mu[:], T["mu_fm"], writes=[b_mu])
        identb = sb("identb", [128, 128], BF16)
        b_id = P.buf()
        P.op("pool", lambda e: e.memset(identb[:], 0.0), writes=[b_id])
        P.op("pool", lambda e: e.affine_select(out=identb[:], in_=identb[:], pattern=[[-1, 128]],
                                                compare_op=ALU.not_equal, fill=1.0, base=0,
                                                channel_multiplier=1), writes=[b_id])
        carry = sb("carry", [128, 14], F32)
        b_carry = P.buf()
        P.op("pool", lambda e: e.memset(carry[:], 0.0), writes=[b_carry])

        xt_r = Rot(P, [sb("xt%d" % i, [128, D], F32) for i in range(3)])
        xn_r = Rot(P, [sb("xn%d" % i, [128, D], F32) for i in range(2)])
        xnb_r = Rot(P, [sb("xnb%d" % i, [128, D], BF16) for i in range(2)])
        junk = sb("junk", [128, D], BF16)
        b_junk = P.buf()
        ss_r = Rot(P, [sb("ss%d" % i, [128, 4], F32) for i in range(4)])
        xnT_r = Rot(P, [sb("xnT%d" % i, [128, 8, 512], BF16) for i in range(2)])
        pb_r = Rot(P, [sb("pb%d" % i, [128, 520], F32) for i in range(3)])
        prs_r = Rot(P, [sb("prs%d" % i, [128, 512], F32) for i in range(3)])
        dd_r = Rot(P, [sb("dd%d" % i, [128, 512], F32) for i in range(2)])
        stg_r = Rot(P, [sb("stg%d" % i, [128, 512], BF16) for i in range(4)])
        tmf_r = Rot(P, [sb("tmf%d" % i, [128, 512], F32) for i in range(3)])
        tmb_r = Rot(P, [sb("tmb%d" % i, [128, 512], BF16) for i in range(2)])
        ps_fm = Rot(P, [ps("psfm%d" % i) for i in range(4)])
        ps_tm = Rot(P, [ps("pstm%d" % i) for i in range(2)])
        ps_tr = Rot(P, [ps("pstr%d" % i) for i in range(2)])

        ssin = sb("ssin", [16, D], F32)
        ssb = sb("ssb", [16, D], BF16)
        ssT = sb("ssT", [128, 8, 16], BF16)
        b_ssin, b_ssb, b_ssT = P.buf(), P.buf(), P.buf()
        P.dma("sp", ssin[:], T["st_shift"], writes=[b_ssin])
        P.op("dve", lambda e: e.tensor_copy(out=ssb[:], in_=ssin[:]), reads=[b_ssin], writes=[b_ssb])
        pt, b_pt = ps_tr.next()
        ptb = pt[:].bitcast(BF16)
        for kc in range(8):
            P.op("pe", lambda e, kc=kc, ptb=ptb: e.transpose(out=ptb[:, kc * 16:(kc + 1) * 16], in_=ssb[:, kc * 128:(kc + 1) * 128],
                                                    identity=identb[0:16, 0:16]),
                 reads=[b_ssb, b_id], writes=[b_pt] if kc == 0 else (), disjoint=[b_pt] if kc else ())
        P.op("act", lambda e, ptb=ptb: e.copy(out=ssT[:].rearrange("p k b -> p (k b)"), in_=ptb[:, 0:128]), reads=[b_pt], writes=[b_ssT])
        prevb = sb("prevb", [128, 128], F32)
        b_prevb = P.buf()
        pss = sb("pss", [128, 16], F32)
        b_pss = P.buf()

        groups = [(g * 4, 4) for g in range(8)] + [(32, 1)]
        if DEBUG_GROUPS is not None:
            groups = [groups[i] for i in DEBUG_GROUPS]
        for (t0, nt) in groups:
            kind = tile_kind(t0)
            G = nt * 128
            tok0 = t0 * 128
            xnT, b_xnT = xnT_r.next()
            for ti in range(nt):
                t = t0 + ti
                xt, b_xt = xt_r.next()
                P.dma("sp", xt[:], T["x_all"][t * 128:(t + 1) * 128, :], writes=[b_xt])
                ss, b_ss = ss_r.next()
                P.op("act", lambda e, xt=xt, ss=ss: e.activation(out=junk[:], in_=xt[:], func=AF.Square, accum_out=ss[:, 0:1]),
                     reads=[b_xt], writes=[b_junk, b_ss])
                P.op("act", lambda e, ss=ss: e.activation(out=ss[:, 1:2], in_=ss[:, 0:1], func=AF.Sqrt, scale=1.0 / D, bias=T["cst"].t[:, 0:1]),
                     reads=[b_ss, T["cst"]], writes=[b_ss])
                P.op("dve", lambda e, ss=ss: e.reciprocal(out=ss[:, 2:3], in_=ss[:, 1:2]), reads=[b_ss], writes=[b_ss])
                xn, b_xn = xn_r.next()
                P.op("dve", lambda e, xn=xn, xt=xt, ss=ss: e.scalar_tensor_tensor(out=xn[:], in0=xt[:], scalar=ss[:, 2:3], in1=gmix[:],
                                                                                    op0=ALU.mult, op1=ALU.mult),
                     reads=[b_xt, b_ss, b_gmix], writes=[b_xn])
                if t == NT_PRIOR + NT_OWN - 1:
                    P.dma("sp", T["shift_out"][0:1, :], xn[127:128, :], reads=[b_xn])
                if kind == "sample":
                    for b in range(16):
                        P.dma("sp", T["shift_out"][1 + b:2 + b, :], xn[b * 8 + 7:b * 8 + 8, :], reads=[b_xn])
                xnb, b_xnb = xnb_r.next()
                P.op("act", lambda e, xnb=xnb, xn=xn: e.copy(out=xnb[:], in_=xn[:]), reads=[b_xn], writes=[b_xnb])
                pt, b_pt = ps_tr.next()
                ptb = pt[:].bitcast(BF16)
                for kc in range(8):
                    P.op("pe", lambda e, kc=kc, ptb=ptb, xnb=xnb: e.transpose(out=ptb[:, kc * 128:(kc + 1) * 128],
                                                                                in_=xnb[:, kc * 128:(kc + 1) * 128], identity=identb[:]),
                         reads=[b_xnb, b_id], writes=[b_pt] if kc == 0 else (), disjoint=[b_pt] if kc else ())
                P.op("dve", lambda e, ptb=ptb, xnT=xnT, ti=ti: e.tensor_copy(out=xnT[:, :, ti * 128:(ti + 1) * 128],
                                                                               in_=ptb.rearrange("p (k t) -> p k t", k=8)),
                     reads=[b_pt], disjoint=[b_xnT], writes=())
            if kind == "prior" and t0 + nt == NT_PRIOR:
                chunks = list(range(0, 14)) + list(range(CH_AK, CH_AK + 4))
            elif kind == "prior":
                chunks = list(range(CH_K, CH_K + 4)) + list(range(CH_V, CH_V + 4)) + [CH_L1] + list(range(CH_AK, CH_AK + 4))
            else:
                chunks = list(range(0, 14)) + list(range(CH_Q, CH_Q + 8)) + list(range(CH_G, CH_G + 16))
                if DEBUG_CHUNKS is not None:
                    chunks = DEBUG_CHUNKS
            for c in chunks:
                pf, b_pf = ps_fm.next()
                for kc in range(8):
                    P.op("pe", lambda e, pf=pf, kc=kc, c=c, xnT=xnT, G=G: e.matmul(pf[:, 0:G], lhsT=win[:, kc, c * 128:(c + 1) * 128],
                                                                                  rhs=xnT[:, kc, 0:G], start=(kc == 0), stop=(kc == 7)),
                         reads=[b_win, b_xnT], writes=[b_pf] if kc == 0 else (), disjoint=[b_pf] if kc else (), signal=(kc == 7))
                if c < 14:
                    if kind != "sample":
                        pb, b_pb = pb_r.next()
                        P.op("act", lambda e, pb=pb, pf=pf, G=G: e.copy(out=pb[:, 1:1 + G], in_=pf[:, 0:G]), reads=[b_pf], writes=[b_pb])
                        P.op("pool", lambda e, pb=pb, c=c: e.tensor_copy(out=pb[:, 0:1], in_=carry[:, c:c + 1]), reads=[b_carry], disjoint=[b_pb])
                        P.op("pool", lambda e, pb=pb, c=c, G=G: e.tensor_copy(out=carry[:, c:c + 1], in_=pb[:, G:G + 1]), reads=[b_pb], disjoint=[b_carry])
                        cur = pb[:, 1:1 + G]
                        prev = pb[:, 0:G]
                        rd = [b_pb]
                    else:
                        pb, b_pb = pb_r.next()
                        P.op("act", lambda e, pb=pb, pf=pf, G=G: e.copy(out=pb[:, 0:G], in_=pf[:, 0:G]), reads=[b_pf], writes=[b_pb])
                        pf2, b_pf2 = ps_fm.next()
                        for kc in range(8):
                            P.op("pe", lambda e, pf2=pf2, kc=kc, c=c: e.matmul(pf2[:, 0:16], lhsT=win[:, kc, c * 128:(c + 1) * 128],
                                                                                rhs=ssT[:, kc, :], start=(kc == 0), stop=(kc == 7)),
                                 reads=[b_win, b_ssT], writes=[b_pf2] if kc == 0 else (), disjoint=[b_pf2] if kc else (), signal=(kc == 7))
                        P.op("dve", lambda e, pb=pb: e.tensor_copy(out=prevb[:].rearrange("p (b t) -> p b t", t=8)[:, :, 1:8],
                                                                    in_=pb[:, 0:128].rearrange("p (b t) -> p b t", t=8)[:, :, 0:7]),
                             reads=[b_pb], writes=[b_prevb])
                        P.op("dve", lambda e, pf2=pf2: e.tensor_copy(out=prevb[:].rearrange("p (b t) -> p b t", t=8)[:, :, 0:1],
                                                                      in_=pf2[:, 0:16].rearrange("p (b o) -> p b o", o=1)),
                             reads=[b_pf2], disjoint=[b_prevb])
                        cur = pb[:, 0:G]
                        prev = prevb[:, 0:G]
                        rd = [b_pb, b_prevb]
                    dd, b_dd = dd_r.next()
                    P.op("dve", lambda e, dd=dd, prev=prev, cur=cur, G=G: e.tensor_tensor(out=dd[:, 0:G], in0=prev, in1=cur, op=ALU.subtract),
                         reads=rd, writes=[b_dd])
                    prs, b_prs = prs_r.next()
                    P.op("dve", lambda e, prs=prs, dd=dd, cur=cur, c=c, G=G: e.scalar_tensor_tensor(out=prs[:, 0:G], in0=dd[:, 0:G], scalar=mu[:, c:c + 1],
                                                                                                    in1=cur, op0=ALU.mult, op1=ALU.add),
                         reads=rd + [b_dd, b_mu], writes=[b_prs])
                    P.dma("sp", T["prT_d"][c][:, tok0:tok0 + G], prs[:, 0:G], reads=[b_prs], disjoint=[T["b_prT"]])
                elif c < CH_G:
                    stg, b_stg = stg_r.next()
                    P.op("act", lambda e, stg=stg, pf=pf, G=G: e.copy(out=stg[:, 0:G], in_=pf[:, 0:G]), reads=[b_pf], writes=[b_stg])
                    if c < CH_AK:
                        q0 = tok0 - NT_PRIOR * 128
                        P.dma("sp", T["qT_d"][c - CH_Q][:, q0:q0 + G], stg[:, 0:G], reads=[b_stg], disjoint=[T["b_qT"]])
                    else:
                        P.dma("sp", T["kT_d"][c - CH_AK][:, tok0:tok0 + G], stg[:, 0:G], reads=[b_stg], disjoint=[T["b_kT"]])
                else:
                    stg, b_stg = stg_r.next()
                    P.op("act", lambda e, stg=stg, pf=pf, G=G: e.activation(out=stg[:, 0:G], in_=pf[:, 0:G], func=AF.Sigmoid),
                         reads=[b_pf], writes=[b_stg])
                    q0 = tok0 - NT_PRIOR * 128
                    P.dma("sp", T["gates_d"][c - CH_G][:, q0:q0 + G], stg[:, 0:G], reads=[b_stg], disjoint=[T["b_gates"]])
            for ti in range(nt if not DEBUG_NOTM else 0):
                t = t0 + ti
                halves = [("v", CH_AV * 128)] + ([] if kind == "prior" else [("k", CH_AK * 128)])
                if DEBUG_HALF is not None:
                    halves = [h for h in halves if h[0] in DEBUG_HALF]
                for (nm, c0) in halves:
                    pm, b_pm = ps_tm.next()
                    for kc in range(8):
                        P.op("pe", lambda e, pm=pm, kc=kc, c0=c0, xnT=xnT, ti=ti: e.matmul(pm[:, :], lhsT=xnT[:, kc, ti * 128:(ti + 1) * 128],
                                                                                         rhs=win[:, kc, c0:c0 + 512], start=(kc == 0), stop=(kc == 7)),
                             reads=[b_win, b_xnT], writes=[b_pm] if kc == 0 else (), disjoint=[b_pm] if kc else (), signal=(kc == 7))
                    tmf, b_tmf = tmf_r.next()
                    P.op("act", lambda e, tmf=tmf, pm=pm: e.copy(out=tmf[:], in_=pm[:]), reads=[b_pm], writes=[b_tmf])
                    if kind != "prior":
                        q0 = (t - NT_PRIOR) * 128
                        if not DEBUG_NOOUT:
                            P.dma("sp", T[nm + "out"][q0:q0 + 128, :], tmf[:], reads=[b_tmf])
                    if nm == "v":
                        tmb, b_tmb = tmb_r.next()
                        P.op("pool", lambda e, tmb=tmb, tmf=tmf: e.tensor_copy(out=tmb[:], in_=tmf[:]), reads=[b_tmf], writes=[b_tmb])
                        P.dma("sp", T["v_d"][t * 128:(t + 1) * 128, :], tmb[:], reads=[b_tmb], disjoint=[T["b_v"]])
        P.emit("phaseA")


def phase_B1(P, nc, T):
    V, A, PO, PE = "dve", "act", "pool", "pe"
    with contextlib.ExitStack() as st:
        def mk(n, s, d):
            return Tl(st.enter_context(nc.sbuf_tensor("b1_" + n, s, d)), P.buf(n))

        def mkps(n, cols=512):
            return Tl(st.enter_context(nc.psum_tensor("b1_" + n, [128, cols], F32)), P.buf(n))

        def op(e, fn, r=(), w=(), **kw):
            return P.op(e, fn, reads=r, writes=w, **kw)

        cst = T["cst"]
        identb = mk("identb", [128, 128], BF16)
        identf = mk("identf", [128, 128], F32)
        P.dma("pool", identb.t[:], T["ident"], writes=[identb])
        P.dma("sp", identf.t[:], T["ident"], writes=[identf])
        MTc = {"p": mk("MTc_p", [128, 256], BF16), "s": mk("MTc_s", [128, 256], BF16)}
        Ms = {"p": mk("Ms_p", [128, 128], BF16), "s": mk("Ms_s", [128, 128], BF16)}
        RM = {"p": mk("RM_p", [128, 512], F32), "s": mk("RM_s", [128, 512], F32)}
        for k in "ps":
            P.dma("pool", MTc[k].t[:], T["mtc_" + k], writes=[MTc[k]])
            P.dma("pool", Ms[k].t[:], T["ms_" + k], writes=[Ms[k]])
            P.dma("sp", RM[k].t[:], T["rm_" + k], writes=[RM[k]])
        bones = mk("bones", [128, 128], BF16)
        P.dma("pool", bones.t[:], T["bones"], writes=[bones])
        sel8 = mk("sel8", [128, 4, 8], BF16)
        P.dma("pool", sel8.t[:].rearrange("p a b -> p (a b)"), T["sel8"], writes=[sel8])
        segsel = mk("segsel", [128, 16], F32)
        P.dma("sp", segsel.t[:], T["segsel"], writes=[segsel])
        LW = mk("LW", [128, 512], BF16)
        P.dma("pool", LW.t[0:64, :], T["w2"], disjoint=[LW])
        P.dma("pool", LW.t[64:128, :], T["a2"], disjoint=[LW])
        G2 = mk("G2", [128, 512], BF16)
        P.dma("pool", G2.t[:], T["g2"], writes=[G2])
        fm = {}
        for n in ["w0", "a0", "kk", "ka", "rk"]:
            fm[n] = mk("fm_" + n, [128, 4], F32)
            P.dma("sp", fm[n].t[:], T[n + "_fm"], writes=[fm[n]])
        lnw = mk("lnw", [128, 512], F32)
        lnb = mk("lnb", [128, 512], F32)
        P.dma("sp", lnw.t[:], T["ln_w"].broadcast_to([128, 512]), writes=[lnw])
        P.dma("sp", lnb.t[:], T["ln_b"].broadcast_to([128, 512]), writes=[lnb])

        def bc4(x):
            return x.t[:, :].unsqueeze(2).broadcast_to([128, 4, 128])

        PRr = Rot(P, [st.enter_context(nc.sbuf_tensor("b1_PR%d" % i, [128, 14, 128], F32)) for i in range(2)])
        LB = mk("LB", [128, 128], BF16)
        GS = mk("GS", [128, 128], BF16)
        f4 = lambda n: mk(n, [128, 4, 128], F32)
        b4 = lambda n: mk(n, [128, 4, 128], BF16)
        XW, LD, ASIG, CUM, CUMp, EW, EWi, EWp = [f4(n) for n in ["XW", "LD", "ASIG", "CUM", "CUMp", "EW", "EWi", "EWp"]]
        KK, SQ, RNr, KKN, T1, KM, TMPB, XA = [f4(n) for n in ["KK", "SQ", "RNr", "KKN", "T1", "KM", "TMPB", "XA"]]
        KK2, BT, KT, VB, PRD = [b4(n) for n in ["KK2", "BT", "KT", "VB", "PRD"]]
        ART = mk("ART", [128, 4, 2, 128], BF16)
        Atm, Btm, Ktm, Vtm = [mk(n, [128, 512], BF16) for n in ["Atm", "Btm", "Ktm", "Vtm"]]
        COEF = mk("COEF", [128, 8], F32)
        Gtm = mk("Gtm", [128, 512], F32)
        NAXr = Rot(P, [st.enter_context(nc.sbuf_tensor("b1_NAX%d" % i, [128, 256], BF16)) for i in range(10)])
        MAr = Rot(P, [st.enter_context(nc.sbuf_tensor("b1_MA%d" % i, [128, 256], BF16)) for i in range(10)])
        PXr = Rot(P, [st.enter_context(nc.sbuf_tensor("b1_PX%d" % i, [128, 256], BF16)) for i in range(20)])
        Pr_ = Rot(P, [st.enter_context(nc.sbuf_tensor("b1_Pk%d" % i, [128, 128], BF16)) for i in range(20)])
        TTr = Rot(P, [st.enter_context(nc.sbuf_tensor("b1_TT%d" % i, [128, 128], BF16)) for i in range(10)])
        MVr = Rot(P, [st.enter_context(nc.sbuf_tensor("b1_MV%d" % i, [128, 64], BF16)) for i in range(10)])
        UVr = Rot(P, [st.enter_context(nc.sbuf_tensor("b1_UV%d" % i, [128, 64], F32)) for i in range(10)])
        Usbr = Rot(P, [st.enter_context(nc.sbuf_tensor("b1_Usb%d" % i, [128, 64], BF16)) for i in range(10)])
        AhT = st.enter_context(nc.sbuf_tensor("b1_AhT", [128, 4, 128], BF16))
        b_AhT = [P.buf() for _ in range(8)]
        S0 = st.enter_context(nc.sbuf_tensor("b1_S0", [128, 4, 64], F32))
        S0b = st.enter_context(nc.sbuf_tensor("b1_S0b", [128, 4, 64], BF16))
        S0W = st.enter_context(nc.sbuf_tensor("b1_S0W", [128, 4, 64], F32))
        b_S = [P.buf() for _ in range(8)]
        b_Sb = [P.buf() for _ in range(8)]
        b_SW = P.buf()
        op(PO, lambda e: e.memset(S0[:], 0.0), w=b_S)
        op(PO, lambda e: e.memset(S0b[:], 0.0), w=b_Sb)
        Ytm = mk("Ytm", [128, 512], F32)
        O1, O2, O3 = [mk(n, [128, 512], F32) for n in ["O1", "O2", "O3"]]
        ST8 = mk("ST8", [128, 32], F32)
        OAb = mk("OAb", [128, 512], BF16)
        OAT = mk("OAT", [128, 4, 128], BF16)
        S0s = st.enter_context(nc.sbuf_tensor("b1_S0s", [128, 4, 16, 64], F32))
        S0sb = st.enter_context(nc.sbuf_tensor("b1_S0sb", [128, 4, 16, 64], BF16))
        b_S0s = [P.buf() for _ in range(4)]
        b_S0sb = [P.buf() for _ in range(4)]
        Zin = mk("Zin", [64, 16, 128], F32)
        TMPs = mk("TMPs", [128, 16, 64], F32)
        Y1 = mk("Y1", [128, 64], F32)
        U1 = mk("U1", [128, 64], F32)
        UBLK = mk("UBLK", [128, 16, 64], BF16)
        VBLK = mk("VBLK", [128, 16, 64], BF16)
        SOUT = mk("SOUT", [64, 16, 128], F32)
        psr = Rot(P, [st.enter_context(nc.psum_tensor("b1_psr%d" % i, [128, 512], F32)) for i in range(5)])
        ps2 = Rot(P, [st.enter_context(nc.psum_tensor("b1_ps2_%d" % i, [128, 1024], F32)) for i in range(1)])
        psY = mkps("psY")

        def nextps():
            t, b = psr.next()
            return Tl(t, b)

        for hp in range(4):
            for h2 in range(2):
                P.dma("sp", Zin.t[:, :, h2 * 64:(h2 + 1) * 64],
                      T["st_rwkv"][:, 2 * hp + h2, :, :].rearrange("b i j -> i b j"), writes=[Zin] if h2 == 0 else (), disjoint=[Zin] if h2 else ())
            for g in range(2):
                pt = nextps()
                for k in range(8):
                    b = g * 8 + k
                    op(PE, lambda e, pt=pt, k=k, b=b: e.transpose(out=pt.t[:, k * 64:(k + 1) * 64], in_=Zin.t[:, b, :], identity=identf.t[0:64, 0:64]),
                       r=[Zin, identf], w=[pt] if k == 0 else (), disjoint=[pt] if k else ())
                op(A, lambda e, pt=pt, hp=hp, g=g: e.copy(out=S0s[:, hp, g * 8:(g + 1) * 8, :], in_=pt.t[:, 0:512].rearrange("p (b i) -> p b i", i=64)),
                   xr=[pt], disjoint=[b_S0s[hp]])
            op(PO, lambda e, hp=hp: e.tensor_copy(out=S0sb[:, hp, :, :], in_=S0s[:, hp, :, :]), r=[b_S0s[hp]], w=[b_S0sb[hp]])

        def rwkv_tile(t):
            kind = tile_kind(t)
            mk_ = "s" if kind == "sample" else "p"
            full = kind != "prior"
            tok0 = t * 128
            q0 = tok0 - NT_PRIOR * 128
            PRt, b_PR = PRr.next()
            PR = Tl(PRt, b_PR)
            c0 = 0 if full else 4
            c1 = 14 if full else 13
            P.dma("sp", PRt[:, c0:c1, :], T["prT_d"].rearrange("c p t -> p c t")[:, c0:c1, tok0:tok0 + 128],
                  reads=[T["b_prT"]], writes=[PR])
            Rv, Kv, Vv = PRt[:, 0:4, :], PRt[:, 4:8, :], PRt[:, 8:12, :]
            op(A, lambda e: e.activation(out=LB.t[0:64, :], in_=PRt[0:64, 12, :], func=AF.Tanh), r=[PR], w=[LB])
            op(A, lambda e: e.copy(out=LB.t[64:128, :], in_=PRt[64:128, 12, :]), r=[PR], disjoint=[LB])
            pW = nextps()
            pA = nextps()
            for hp in range(4):
                op(PE, lambda e, hp=hp: e.matmul(pW.t[:, hp * 128:(hp + 1) * 128], lhsT=LW.t[0:64, hp * 128:(hp + 1) * 128], rhs=LB.t[0:64, :], start=True, stop=True),
                   r=[LW, LB], w=[pW] if hp == 0 else (), disjoint=[pW] if hp else ())
            for hp in range(4):
                op(PE, lambda e, hp=hp: e.matmul(pA.t[:, hp * 128:(hp + 1) * 128], lhsT=LW.t[64:128, hp * 128:(hp + 1) * 128], rhs=LB.t[64:128, :], start=True, stop=True),
                   r=[LW, LB], w=[pA] if hp == 0 else (), disjoint=[pA] if hp else ())
            v4 = lambda x: x.t[:, 0:512].rearrange("p (a b) -> p a b", a=4) if x.t.shape[-1] == 512 else x.t[:]
            op(V, lambda e: e.tensor_tensor(out=XW.t[:], in0=v4(pW), in1=bc4(fm["w0"]), op=ALU.add), xr=[pW], r=[fm["w0"]], w=[XW])
            op(A, lambda e: e.activation(out=XW.t[:], in_=XW.t[:], func=AF.Sigmoid), w=[XW])
            op(V, lambda e: e.tensor_scalar(out=LD.t[:], in0=XW.t[:], scalar1=-0.6065306597126334, scalar2=None, op0=ALU.mult), r=[XW], w=[LD])
            op(V, lambda e: e.tensor_tensor(out=XA.t[:], in0=v4(pA), in1=bc4(fm["a0"]), op=ALU.add), xr=[pA], r=[fm["a0"]], w=[XA])
            op(A, lambda e: e.activation(out=ASIG.t[:], in_=XA.t[:], func=AF.Sigmoid), r=[XA], w=[ASIG])
            fl = lambda x: x.t[:].rearrange("p a b -> p (a b)")
            op(V, lambda e: e.tensor_tensor_scan(out=fl(CUM), data0=RM[mk_].t[:], data1=fl(LD), initial=0.0, op0=ALU.mult, op1=ALU.add),
               r=[RM[mk_], LD], w=[CUM])
            op(PO, lambda e: e.tensor_tensor(out=CUMp.t[:], in0=CUM.t[:], in1=LD.t[:], op=ALU.subtract), r=[CUM, LD], w=[CUMp])
            op(A, lambda e: e.activation(out=EW.t[:], in_=CUM.t[:], func=AF.Exp), r=[CUM], w=[EW])
            op(A, lambda e: e.activation(out=EWi.t[:], in_=CUM.t[:], func=AF.Exp, scale=-1.0), r=[CUM], w=[EWi])
            op(A, lambda e: e.activation(out=EWp.t[:], in_=CUMp.t[:], func=AF.Exp), r=[CUMp], w=[EWp])
            op(PO, lambda e: e.tensor_tensor(out=KK.t[:], in0=Kv, in1=bc4(fm["kk"]), op=ALU.mult), r=[PR, fm["kk"]], w=[KK])
            op(V, lambda e: e.tensor_tensor(out=KK2.t[:], in0=KK.t[:], in1=KK.t[:], op=ALU.mult), r=[KK], w=[KK2])
            pN = nextps()
            op(PE, lambda e: e.matmul(pN.t[:, 0:512], lhsT=bones.t[:], rhs=fl(KK2), start=True, stop=True), r=[bones, KK2], w=[pN])
            op(A, lambda e: e.activation(out=SQ.t[:], in_=v4(pN), func=AF.Sqrt, bias=cst.t[:, 1:2]), xr=[pN], r=[cst], w=[SQ])
            op(V, lambda e: e.reciprocal(out=RNr.t[:], in_=SQ.t[:]), r=[SQ], w=[RNr])
            op(PO, lambda e: e.tensor_tensor(out=KKN.t[:], in0=KK.t[:], in1=RNr.t[:], op=ALU.mult), r=[KK, RNr], w=[KKN])
            op(V, lambda e: e.scalar_tensor_tensor(out=T1.t[:], in0=ASIG.t[:], scalar=-1.0, in1=bc4(fm["ka"]), op0=ALU.add, op1=ALU.mult),
               r=[ASIG, fm["ka"]], w=[T1])
            op(V, lambda e: e.scalar_tensor_tensor(out=KM.t[:], in0=T1.t[:], scalar=1.0, in1=Kv, op0=ALU.add, op1=ALU.mult), r=[T1, PR], w=[KM])
            op(V, lambda e: e.scalar_tensor_tensor(out=ART.t[:, :, 1, :], in0=KKN.t[:], scalar=-1.0, in1=EWp.t[:], op0=ALU.mult, op1=ALU.mult),
               r=[KKN, EWp], disjoint=[ART])
            if full:
                op(PO, lambda e: e.tensor_tensor(out=ART.t[:, :, 0, :], in0=Rv, in1=EW.t[:], op=ALU.mult), r=[PR, EW], disjoint=[ART])
            op(V, lambda e: e.tensor_tensor(out=TMPB.t[:], in0=KKN.t[:], in1=ASIG.t[:], op=ALU.mult), r=[KKN, ASIG], w=[TMPB])
            op(PO, lambda e: e.tensor_tensor(out=BT.t[:], in0=TMPB.t[:], in1=EWi.t[:], op=ALU.mult), r=[TMPB, EWi], w=[BT])
            op(V, lambda e: e.tensor_tensor(out=KT.t[:], in0=KM.t[:], in1=EWi.t[:], op=ALU.mult), r=[KM, EWi], w=[KT])
            op(A, lambda e: e.copy(out=VB.t[:], in_=Vv), r=[PR], w=[VB])
            for (src_fn, dst, rd) in [(lambda hp: ART.t[:, hp, 1, :], Atm, ART), (lambda hp: BT.t[:, hp, :], Btm, BT),
                                      (lambda hp: KT.t[:, hp, :], Ktm, KT), (lambda hp: VB.t[:, hp, :], Vtm, VB)]:
                pt = nextps()
                ptb = pt.t[:].bitcast(BF16)
                for hp in range(4):
                    op(PE, lambda e, hp=hp, ptb=ptb, src_fn=src_fn: e.transpose(out=ptb[:, hp * 128:(hp + 1) * 128], in_=src_fn(hp), identity=identb.t[:]),
                       r=[rd, identb], w=[pt] if hp == 0 else (), disjoint=[pt] if hp else ())
                op(A, lambda e, ptb=ptb, dst=dst: e.copy(out=dst.t[:], in_=ptb[:, 0:512]), xr=[pt], w=[dst])
            if full:
                op(A, lambda e: e.activation(out=GS.t[:], in_=PRt[:, 13, :], func=AF.Sigmoid), r=[PR], w=[GS])
                psG = nextps()
                op(PE, lambda e: e.matmul(psG.t[:, 0:512], lhsT=GS.t[:], rhs=G2.t[:], start=True, stop=True), r=[GS, G2], w=[psG])
                op(A, lambda e: e.copy(out=Gtm.t[:], in_=psG.t[:, 0:512]), xr=[psG], w=[Gtm])
                op(PO, lambda e: e.tensor_tensor(out=TMPB.t[:], in0=Rv, in1=bc4(fm["rk"]), op=ALU.mult), r=[PR, fm["rk"]], w=[TMPB])
                op(PO, lambda e: e.tensor_tensor(out=PRD.t[:], in0=TMPB.t[:], in1=KM.t[:], op=ALU.mult), r=[TMPB, KM], w=[PRD])
                pC = nextps()
                for hp in range(4):
                    op(PE, lambda e, hp=hp: e.matmul(pC.t[:, 0:8], lhsT=PRD.t[:, hp, :], rhs=sel8.t[:, hp, :], start=(hp == 0), stop=(hp == 3)),
                       r=[PRD, sel8], w=[pC] if hp == 0 else (), disjoint=[pC] if hp else (), signal=(hp == 3))
                op(A, lambda e: e.copy(out=COEF.t[:], in_=pC.t[:, 0:8]), xr=[pC], w=[COEF])
            if kind != "sample":
                op(PO, lambda e: e.tensor_tensor(out=S0W[:], in0=S0[:], in1=EW.t[:, :, 127:128].broadcast_to([128, 4, 64]), op=ALU.mult),
                   r=b_S + [EW], w=[b_SW])
            nsq = 2 if kind == "sample" else 6
            def head_gen(h):
                hp, h2 = divmod(h, 2)
                base = 64 * h2
                hc = slice(h * 64, (h + 1) * 64)
                NAXt, b_NAX = NAXr.next()
                MAt, b_MA = MAr.next()
                p1 = nextps()
                op(PE, lambda e, p1=p1, hp=hp, base=base: e.matmul(p1.t[:, 0:256], lhsT=BT.t[base:base + 64, hp, :],
                                                                  rhs=ART.t[base:base + 64, hp, :, :].rearrange("p a b -> p (a b)"), start=True, stop=True),
                   r=[BT, ART], w=[p1])
                op(V, lambda e, p1=p1, NAXt=NAXt: e.tensor_tensor(out=NAXt[:], in0=p1.t[:, 0:256], in1=MTc[mk_].t[:], op=ALU.mult),
                   xr=[p1], r=[MTc[mk_]], w=[b_NAX])
                p2 = nextps()
                op(PE, lambda e, p2=p2, hp=hp, base=base: e.matmul(p2.t[:, 0:256], lhsT=KT.t[base:base + 64, hp, :],
                                                                  rhs=ART.t[base:base + 64, hp, :, :].rearrange("p a b -> p (a b)"), start=True, stop=True),
                   r=[KT, ART], w=[p2])
                op(V, lambda e, p2=p2, MAt=MAt: e.tensor_tensor(out=MAt[:], in0=p2.t[:, 0:256], in1=MTc[mk_].t[:], op=ALU.mult),
                   xr=[p2], r=[MTc[mk_]], w=[b_MA])
                p3 = nextps()
                op(PE, lambda e, p3=p3, hp=hp, base=base: e.matmul(p3.t[:, 0:128], lhsT=ART.t[base:base + 64, hp, 1, :], rhs=BT.t[base:base + 64, hp, :], start=True, stop=True),
                   r=[ART, BT], w=[p3])
                Pk, b_Pk = Pr_.next()
                op(V, lambda e, p3=p3, Pk=Pk: e.tensor_tensor(out=Pk[:], in0=p3.t[:, 0:128], in1=Ms[mk_].t[:], op=ALU.mult), xr=[p3], r=[Ms[mk_]], w=[b_Pk])
                yield
                PX, b_PX = PXr.next()
                op(PO, lambda e, PX=PX, NAXt=NAXt: e.tensor_copy(out=PX[:, 0:128], in_=NAXt[:, 128:256]), r=[b_NAX], w=[b_PX])
                op(PO, lambda e, PX=PX, NAXt=NAXt: e.tensor_tensor(out=PX[:, 128:256], in0=NAXt[:, 128:256], in1=identb.t[:], op=ALU.add),
                   r=[b_NAX, identb], disjoint=[b_PX])
                for k in range(nsq + 1):
                    last = (k == nsq)
                    if not last:
                        PXn, b_PXn = PXr.next()
                        Pkn, b_Pkn = Pr_.next()
                        pd1 = nextps()
                        op(PE, lambda e, pd1=pd1, Pk=Pk, PX=PX: e.matmul(pd1.t[:, 0:128], lhsT=Pk[:], rhs=PX[:, 0:128], start=True, stop=True),
                           r=[b_Pk, b_PX], w=[pd1])
                        op(A, lambda e, pd1=pd1, PXn=PXn: e.copy(out=PXn[:, 0:128], in_=pd1.t[:, 0:128]), xr=[pd1], w=[b_PXn])
                        pe1 = nextps()
                        op(PE, lambda e, pe1=pe1, Pk=Pk, PX=PX: e.matmul(pe1.t[:, 0:128], lhsT=PX[:, 0:128], rhs=Pk[:], start=True, stop=True),
                           r=[b_Pk, b_PX], w=[pe1])
                        op(A, lambda e, pe1=pe1, Pkn=Pkn: e.copy(out=Pkn[:], in_=pe1.t[:, 0:128]), xr=[pe1], w=[b_Pkn])
                    if k >= 1:
                        pd2 = nextps()
                        op(PE, lambda e, pd2=pd2, Pk=Pk, PX=PX: e.matmul(pd2.t[:, 0:128], lhsT=Pk[:], rhs=PX[:, 128:256], start=True, stop=True),
                           r=[b_Pk, b_PX], w=[pd2])
                        if last:
                            TTt, b_TT = TTr.next()
                            op(V, lambda e, pd2=pd2, PX=PX, TTt=TTt: e.tensor_tensor(out=TTt[:], in0=pd2.t[:, 0:128], in1=PX[:, 128:256], op=ALU.add),
                               xr=[pd2], r=[b_PX], w=[b_TT])
                        else:
                            op(V, lambda e, pd2=pd2, PX=PX, PXn=PXn: e.tensor_tensor(out=PXn[:, 128:256], in0=pd2.t[:, 0:128], in1=PX[:, 128:256], op=ALU.add),
                               xr=[pd2], r=[b_PX], disjoint=[b_PXn])
                    elif not last:
                        op(PO, lambda e, PX=PX, PXn=PXn: e.tensor_copy(out=PXn[:, 128:256], in_=PX[:, 128:256]), r=[b_PX], disjoint=[b_PXn])
                    if not last:
                        PX, b_PX, Pk, b_Pk = PXn, b_PXn, Pkn, b_Pkn
                    yield
                MVt, b_MV = MVr.next()
                UVt, b_UV = UVr.next()
                pm = nextps()
                op(PE, lambda e, pm=pm, MAt=MAt, hc=hc: e.matmul(pm.t[:, 0:64], lhsT=MAt[:, 128:256], rhs=Vtm.t[:, hc], start=True, stop=True),
                   r=[b_MA, Vtm], w=[pm])
                op(A, lambda e, pm=pm, MVt=MVt: e.copy(out=MVt[:], in_=pm.t[:, 0:64]), xr=[pm], w=[b_MV])
                yield
                pu = nextps()
                op(PE, lambda e, pu=pu, TTt=TTt, MVt=MVt: e.matmul(pu.t[:, 0:64], lhsT=TTt[:], rhs=MVt[:], start=True, stop=True), r=[b_TT, b_MV], w=[pu])
                op(A, lambda e, pu=pu, UVt=UVt: e.copy(out=UVt[:], in_=pu.t[:, 0:64]), xr=[pu], w=[b_UV])
                pa = nextps()
                op(PE, lambda e, pa=pa, TTt=TTt, hc=hc, base=base: e.matmul(pa.t[base:base + 64, 0:128], lhsT=Atm.t[:, hc], rhs=TTt[:], start=True, stop=True),
                   r=[Atm, b_TT], w=[pa])
                op(A, lambda e, pa=pa, hp=hp, base=base: e.copy(out=AhT[base:base + 64, hp, :], in_=pa.t[base:base + 64, 0:128]), xr=[pa], w=[b_AhT[h]])
                yield
                Usb, b_Usb = Usbr.next()
                if kind != "sample":
                    pU = nextps()
                    op(PE, lambda e, pU=pU, hp=hp, base=base: e.matmul(pU.t[:, 0:64], lhsT=AhT[base:base + 64, hp, :], rhs=S0b[base:base + 64, hp, :], start=True, stop=True),
                       r=[b_AhT[h], b_Sb[h]], w=[pU])
                    op(V, lambda e, pU=pU, Usb=Usb, UVt=UVt: e.tensor_tensor(out=Usb[:], in0=pU.t[:, 0:64], in1=UVt[:], op=ALU.add), xr=[pU], r=[b_UV], w=[b_Usb])
                    yield
                    if full:
                        first = (h == 0)
                        op(PE, lambda e, hp=hp, base=base, hc=hc: e.matmul(psY.t[:, hc], lhsT=ART.t[base:base + 64, hp, 0, :], rhs=S0b[base:base + 64, hp, :], start=True, stop=False),
                           r=[ART, b_Sb[h]], w=[psY] if first else (), disjoint=() if first else [psY], signal=False)
                        op(PE, lambda e, hc=hc, NAXt=NAXt, Usb=Usb: e.matmul(psY.t[:, hc], lhsT=NAXt[:, 0:128], rhs=Usb[:], start=False, stop=False),
                           r=[b_NAX, b_Usb], disjoint=[psY], signal=False)
                        op(PE, lambda e, hc=hc, MAt=MAt: e.matmul(psY.t[:, hc], lhsT=MAt[:, 0:128], rhs=Vtm.t[:, hc], start=False, stop=True),
                           r=[b_MA, Vtm], disjoint=[psY])
                    pS = nextps()
                    op(PE, lambda e, pS=pS, hc=hc, base=base, Usb=Usb: e.matmul(pS.t[base:base + 64, 0:64], lhsT=Btm.t[:, hc], rhs=Usb[:], start=True, stop=False),
                       r=[Btm, b_Usb], w=[pS], signal=False)
                    op(PE, lambda e, pS=pS, hc=hc, base=base: e.matmul(pS.t[base:base + 64, 0:64], lhsT=Ktm.t[:, hc], rhs=Vtm.t[:, hc], start=False, stop=True),
                       r=[Ktm, Vtm], disjoint=[pS])
                    op(V, lambda e, pS=pS, hp=hp, base=base: e.scalar_tensor_tensor(out=S0[base:base + 64, hp, :], in0=pS.t[base:base + 64, 0:64],
                                                                                     scalar=EW.t[base:base + 64, hp, 127:128], in1=S0W[base:base + 64, hp, :],
                                                                                     op0=ALU.mult, op1=ALU.add),
                       xr=[pS], r=[EW, b_SW], w=[b_S[h]])
                    op(A, lambda e, hp=hp, base=base: e.copy(out=S0b[base:base + 64, hp, :], in_=S0[base:base + 64, hp, :]), r=[b_S[h]], w=[b_Sb[h]])
                else:
                    pq, b_pq = ps2.next()
                    for half in range(2):
                        op(PE, lambda e, half=half, hp=hp, base=base, pq=pq: e.matmul(pq[:, half * 512:(half + 1) * 512], lhsT=AhT[base:base + 64, hp, :],
                                                                                        rhs=S0sb[base:base + 64, hp, half * 8:(half + 1) * 8, :].rearrange("p b i -> p (b i)"),
                                                                                        start=True, stop=True),
                           r=[b_AhT[h], b_S0sb[hp]], w=[b_pq] if half == 0 else (), disjoint=[b_pq] if half else ())
                    for half in range(2):
                        op(V, lambda e, half=half, pq=pq: e.tensor_tensor(out=TMPs.t[:, half * 8:(half + 1) * 8, :],
                                                                            in0=pq[:, half * 512:(half + 1) * 512].rearrange("p (b i) -> p b i", i=64),
                                                                            in1=segsel.t[:, half * 8:(half + 1) * 8].unsqueeze(2).broadcast_to([128, 8, 64]), op=ALU.mult),
                           xr=[b_pq], r=[segsel], w=[TMPs] if half == 0 else (), disjoint=[TMPs] if half else ())
                    op(V, lambda e: e.tensor_reduce(out=U1.t[:], in_=TMPs.t[:].rearrange("p b i -> p i b"), axis=AX.X, op=ALU.add), r=[TMPs], w=[U1])
                    op(V, lambda e, Usb=Usb, UVt=UVt: e.tensor_tensor(out=Usb[:], in0=U1.t[:], in1=UVt[:], op=ALU.add), r=[U1, b_UV], w=[b_Usb])
                    pq, b_pq = ps2.next()
                    for half in range(2):
                        op(PE, lambda e, half=half, hp=hp, base=base, pq=pq: e.matmul(pq[:, half * 512:(half + 1) * 512], lhsT=ART.t[base:base + 64, hp, 0, :],
                                                                                        rhs=S0sb[base:base + 64, hp, half * 8:(half + 1) * 8, :].rearrange("p b i -> p (b i)"),
                                                                                        start=True, stop=True),
                           r=[ART, b_S0sb[hp]], w=[b_pq] if half == 0 else (), disjoint=[b_pq] if half else ())
                    for half in range(2):
                        op(V, lambda e, half=half, pq=pq: e.tensor_tensor(out=TMPs.t[:, half * 8:(half + 1) * 8, :],
                                                                            in0=pq[:, half * 512:(half + 1) * 512].rearrange("p (b i) -> p b i", i=64),
                                                                            in1=segsel.t[:, half * 8:(half + 1) * 8].unsqueeze(2).broadcast_to([128, 8, 64]), op=ALU.mult),
                           xr=[b_pq], r=[segsel], w=[TMPs] if half == 0 else (), disjoint=[TMPs] if half else ())
                    op(V, lambda e: e.tensor_reduce(out=Y1.t[:], in_=TMPs.t[:].rearrange("p b i -> p i b"), axis=AX.X, op=ALU.add), r=[TMPs], w=[Y1])
                    py = nextps()
                    op(PE, lambda e, py=py, NAXt=NAXt, Usb=Usb: e.matmul(py.t[:, 0:64], lhsT=NAXt[:, 0:128], rhs=Usb[:], start=True, stop=False),
                       r=[b_NAX, b_Usb], w=[py], signal=False)
                    op(PE, lambda e, py=py, MAt=MAt, hc=hc: e.matmul(py.t[:, 0:64], lhsT=MAt[:, 0:128], rhs=Vtm.t[:, hc], start=False, stop=True),
                       r=[b_MA, Vtm], disjoint=[py])
                    op(V, lambda e, py=py, hc=hc: e.tensor_tensor(out=Ytm.t[:, hc], in0=py.t[:, 0:64], in1=Y1.t[:], op=ALU.add), xr=[py], r=[Y1], disjoint=[Ytm])
                    op(PO, lambda e, Usb=Usb: e.tensor_tensor(out=UBLK.t[:], in0=Usb[:, :].unsqueeze(1).broadcast_to([128, 16, 64]),
                                                               in1=segsel.t[:, :].unsqueeze(2).broadcast_to([128, 16, 64]), op=ALU.mult),
                       r=[b_Usb, segsel], w=[UBLK])
                    op(PO, lambda e, hc=hc: e.tensor_tensor(out=VBLK.t[:], in0=Vtm.t[:, hc].unsqueeze(1).broadcast_to([128, 16, 64]),
                                                             in1=segsel.t[:, :].unsqueeze(2).broadcast_to([128, 16, 64]), op=ALU.mult),
                       r=[Vtm, segsel], w=[VBLK])
                    pq, b_pq = ps2.next()
                    for half in range(2):
                        op(PE, lambda e, half=half, hc=hc, base=base, pq=pq: e.matmul(pq[base:base + 64, half * 512:(half + 1) * 512], lhsT=Btm.t[:, hc],
                                                                                        rhs=UBLK.t[:, half * 8:(half + 1) * 8, :].rearrange("p b i -> p (b i)"), start=True, stop=False),
                           r=[Btm, UBLK], w=[b_pq] if half == 0 else (), disjoint=[b_pq] if half else (), signal=False)
                        op(PE, lambda e, half=half, hc=hc, base=base, pq=pq: e.matmul(pq[base:base + 64, half * 512:(half + 1) * 512], lhsT=Ktm.t[:, hc],
                                                                                        rhs=VBLK.t[:, half * 8:(half + 1) * 8, :].rearrange("p b i -> p (b i)"), start=False, stop=True),
                           r=[Ktm, VBLK], disjoint=[b_pq])
                    op(V, lambda e, hp=hp, base=base, pq=pq: e.tensor_tensor(out=S0s[base:base + 64, hp, :, :], in0=pq[base:base + 64, :].rearrange("p (b i) -> p b i", i=64),
                                                                               in1=S0s[base:base + 64, hp, :, :], op=ALU.add),
                       xr=[b_pq], w=[P.buf()], r=[b_S0s[hp]])
                    op(PO, lambda e, hp=hp, base=base: e.tensor_tensor(out=S0s[base:base + 64, hp, :, :], in0=S0s[base:base + 64, hp, :, :],
                                                                        in1=EW.t[base:base + 64, hp, :].rearrange("p (b t) -> p b t", t=8)[:, :, 7:8].broadcast_to([64, 16, 64]),
                                                                        op=ALU.mult),
                       r=[EW], w=[b_S0s[hp]] if h2 == 1 else (), disjoint=[b_S0s[hp]] if h2 == 0 else ())
            for g in range(1):
                alive = [head_gen(h) for h in range(8)]
                while alive:
                    for gn in list(alive):
                        try:
                            next(gn)
                        except StopIteration:
                            alive.remove(gn)
            if full:
                if kind != "sample":
                    op(A, lambda e: e.copy(out=Ytm.t[:], in_=psY.t[:, 0:512]), xr=[psY], w=[Ytm])
                y3 = lambda x: x.t[:, 0:512].rearrange("p (h i) -> p h i", i=64)
                b8 = lambda ap: ap.unsqueeze(2).broadcast_to([128, 8, 64])
                op(V, lambda e: e.tensor_reduce(out=ST8.t[:, 0:8], in_=y3(Ytm), axis=AX.X, op=ALU.add), r=[Ytm], w=[ST8])
                op(PO, lambda e: e.tensor_scalar(out=ST8.t[:, 8:16], in0=ST8.t[:, 0:8], scalar1=-1.0 / 64, scalar2=None, op0=ALU.mult), w=[ST8])
                op(PO, lambda e: e.tensor_tensor(out=y3(O1), in0=y3(Ytm), in1=b8(ST8.t[:, 8:16]), op=ALU.add), r=[Ytm, ST8], w=[O1])
                op(PO, lambda e: e.tensor_tensor(out=O2.t[:], in0=O1.t[:], in1=O1.t[:], op=ALU.mult), r=[O1], w=[O2])
                op(V, lambda e: e.tensor_reduce(out=ST8.t[:, 16:24], in_=y3(O2), axis=AX.X, op=ALU.add), r=[O2], w=[ST8])
                op(A, lambda e: e.activation(out=ST8.t[:, 24:32], in_=ST8.t[:, 16:24], func=AF.Sqrt, scale=1.0 / 64, bias=cst.t[:, 2:3]), r=[cst], w=[ST8])
                op(V, lambda e: e.reciprocal(out=ST8.t[:, 16:24], in_=ST8.t[:, 24:32]), w=[ST8])
                op(PO, lambda e: e.tensor_tensor(out=y3(O2), in0=y3(O1), in1=b8(ST8.t[:, 16:24]), op=ALU.mult), r=[O1, ST8], w=[O2])
                op(PO, lambda e: e.tensor_tensor(out=O1.t[:], in0=O2.t[:], in1=lnw.t[:], op=ALU.mult), r=[O2, lnw], w=[O1])
                op(PO, lambda e: e.tensor_tensor(out=O2.t[:], in0=O1.t[:], in1=lnb.t[:], op=ALU.add), r=[O1, lnb], w=[O2])
                op(V, lambda e: e.tensor_tensor(out=y3(O3), in0=Vtm.t[:, 0:512].rearrange("p (h i) -> p h i", i=64), in1=b8(COEF.t[:, 0:8]), op=ALU.mult),
                   r=[Vtm, COEF], w=[O3])
                op(V, lambda e: e.tensor_tensor(out=O1.t[:], in0=O2.t[:], in1=O3.t[:], op=ALU.add), r=[O2, O3], w=[O1])
                op(V, lambda e: e.tensor_tensor(out=OAb.t[:], in0=O1.t[:], in1=Gtm.t[:], op=ALU.mult), r=[O1, Gtm], w=[OAb])
                pt = nextps()
                ptb = pt.t[:].bitcast(BF16)
                for hp in range(4):
                    op(PE, lambda e, hp=hp, ptb=ptb: e.transpose(out=ptb[:, hp * 128:(hp + 1) * 128], in_=OAb.t[:, hp * 128:(hp + 1) * 128], identity=identb.t[:]),
                       r=[OAb, identb], w=[pt] if hp == 0 else (), disjoint=[pt] if hp else ())
                op(A, lambda e, ptb=ptb: e.copy(out=OAT.t[:].rearrange("p a b -> p (a b)"), in_=ptb[:, 0:512]), xr=[pt], w=[OAT])
                P.dma("sp", T["oaT_d"].rearrange("c p t -> p c t")[:, :, q0:q0 + 128], OAT.t[:], reads=[OAT], disjoint=[T["b_oaT"]])

        tiles = list(range(NTILE))
        if DEBUG_TILES is not None:
            tiles = DEBUG_TILES
        for t in tiles:
            rwkv_tile(t)
        for hp in range(4):
            pt = nextps()
            op(PE, lambda e, pt=pt, hp=hp: e.transpose(out=pt.t[0:64, 0:128], in_=S0[:, hp, :], identity=identf.t[:]),
               r=[b_S[2 * hp], b_S[2 * hp + 1], identf], w=[pt])
            op(A, lambda e, pt=pt, hp=hp: e.copy(out=SOUT.t[:, hp, :], in_=pt.t[0:64, 0:128]), xr=[pt], disjoint=[SOUT])
        for h2 in range(2):
            P.dma("sp", T["rwkv_p_out"].rearrange("(a h) i j -> h i a j", h=2)[h2], SOUT.t[:, 0:4, h2 * 64:(h2 + 1) * 64], reads=[SOUT])
        for hp in range(4):
            for g in range(4):
                pt = nextps()
                for k in range(4):
                    b = g * 4 + k
                    op(PE, lambda e, pt=pt, k=k, b=b, hp=hp: e.transpose(out=pt.t[0:64, k * 128:(k + 1) * 128], in_=S0s[:, hp, b, :], identity=identf.t[:]),
                       r=[b_S0s[hp], identf], w=[pt] if k == 0 else (), disjoint=[pt] if k else ())
                op(A, lambda e, pt=pt, g=g: e.copy(out=SOUT.t[:, g * 4:(g + 1) * 4, :], in_=pt.t[0:64, 0:512].rearrange("p (b c) -> p b c", c=128)),
                   xr=[pt], w=[SOUT] if g == 0 else (), disjoint=[SOUT] if g else ())
            for h2 in range(2):
                P.dma("sp", T["rwkv_s_out"][:, 2 * hp + h2, :, :].rearrange("b i j -> i b j"),
                      SOUT.t[:, :, h2 * 64:(h2 + 1) * 64], reads=[SOUT])
        P.emit("phaseB1")


def phase_B2(P, nc, T):
    V, A, PO, PE = "dve", "act", "pool", "pe"
    with contextlib.ExitStack() as st:
        def mk(n, s, d):
            return Tl(st.enter_context(nc.sbuf_tensor("b2_" + n, s, d)), P.buf(n))

        def mkps(n, cols=512):
            return Tl(st.enter_context(nc.psum_tensor("b2_" + n, [128, cols], F32)), P.buf(n))

        def op(e, fn, r=(), w=(), **kw):
            return P.op(e, fn, reads=r, writes=w, **kw)

        cst = T["cst"]
        identb = mk("identb", [128, 128], BF16)
        identf = mk("identf", [128, 128], F32)
        P.dma("pool", identb.t[:], T["ident"], writes=[identb])
        P.dma("sp", identf.t[:], T["ident"], writes=[identf])
        cm_p = mk("cm_p", [128, 128], BF16)
        cm_s = mk("cm_s", [128, 128], BF16)
        P.dma("pool", cm_p.t[:], T["mtc_p"][:, 0:128], writes=[cm_p])
        P.dma("pool", cm_s.t[:], T["mtc_s"][:, 0:128], writes=[cm_s])
        KT = mk("KT", [128, 4, NTOK], BF16)
        QT = mk("QT", [128, 4, NQ], BF16)
        Vx = mk("Vx", [128, NTILE, 4, 130], BF16)
        for h in range(4):
            P.dma("sp", KT.t[:, h, :], T["kT_d"][h], reads=[T["b_kT"]], disjoint=[KT])
            P.dma("sp", QT.t[:, h, :], T["qT_d"][h], reads=[T["b_qT"]], disjoint=[QT])
            for t0 in range(0, NTILE, 8):
                t1 = min(NTILE, t0 + 8)
                P.dma("sp", Vx.t[:, t0:t1, h, 0:128], T["v_d"].rearrange("(t p) (h e) -> h p t e", p=128, h=4)[h][:, t0:t1, :], reads=[T["b_v"]], disjoint=[Vx])
        op(PO, lambda e: e.memset(Vx.t[:, :, :, 128:130], 1.0), disjoint=[Vx])
        btab = mk("btab", [128, 4, 32], F32)
        btabp = mk("btabp", [128, 4, 32], F32)
        pmask = mk("pmask", [128, 1], F32)
        P.dma("sp", btab.t[:].rearrange("p a b -> p (a b)"), T["btab"], writes=[btab])
        P.dma("sp", pmask.t[:], T["pmask"], writes=[pmask])
        op(V, lambda e: e.tensor_scalar(out=btabp.t[:], in0=btab.t[:], scalar1=pmask.t[:, 0:1], scalar2=None, op0=ALU.add), r=[btab, pmask], w=[btabp])
        bown = mk("bown", [128, 4], F32)
        P.dma("sp", bown.t[:], T["bown"], writes=[bown])
        bcache = mk("bcache", [128, 16, 32], F32)
        P.dma("sp", bcache.t[:].rearrange("p a b -> p (a b)"), T["bcache"], writes=[bcache])
        lv = mk("lv", [128, 4, 64], F32)
        for i, n in enumerate(["lam_q1", "lam_k1", "lam_q2", "lam_k2"]):
            P.dma("sp", lv.t[:, i, :], T[n].broadcast_to([128, 64]), disjoint=[lv])
        lt = mk("lt", [128, 2, 64], F32)
        ls = mk("ls", [128, 8], F32)
        op(V, lambda e: e.tensor_tensor(out=lt.t[:, 0, :], in0=lv.t[:, 0, :], in1=lv.t[:, 1, :], op=ALU.mult), r=[lv], disjoint=[lt])
        op(V, lambda e: e.tensor_tensor(out=lt.t[:, 1, :], in0=lv.t[:, 2, :], in1=lv.t[:, 3, :], op=ALU.mult), r=[lv], disjoint=[lt])
        op(V, lambda e: e.tensor_reduce(out=ls.t[:, 0:2], in_=lt.t[:], axis=AX.X, op=ALU.add), r=[lt], w=[ls])
        op(A, lambda e: e.activation(out=ls.t[:, 2:4], in_=ls.t[:, 0:2], func=AF.Exp), w=[ls])
        op(V, lambda e: e.tensor_tensor(out=ls.t[:, 4:5], in0=ls.t[:, 2:3], in1=ls.t[:, 3:4], op=ALU.subtract), w=[ls])
        op(V, lambda e: e.tensor_scalar(out=ls.t[:, 5:6], in0=ls.t[:, 4:5], scalar1=0.2, scalar2=-1.0, op0=ALU.add, op1=ALU.mult), w=[ls])
        sg = mk("sg", [128, 128], F32)
        P.dma("sp", sg.t[:], T["subln_g"].broadcast_to([128, 128]), writes=[sg])
        op(V, lambda e: e.tensor_scalar(out=sg.t[:], in0=sg.t[:], scalar1=0.8, scalar2=None, op0=ALU.mult), w=[sg])

        psr = Rot(P, [st.enter_context(nc.psum_tensor("b2_psr%d" % i, [128, 1024], F32)) for i in range(2)])
        psO = [mkps("psO%d" % i) for i in range(3)]
        pstr = mkps("pstr")

        def nextps():
            t, b = psr.next()
            return Tl(t, b)

        ETr = Rot(P, [st.enter_context(nc.sbuf_tensor("b2_ET%d" % i, [128, 256], BF16)) for i in range(3)])
        T2 = mk("T2", [128, 128], F32)
        ATT = mk("ATT", [128, 128], F32)
        junk = mk("junk", [128, 128], BF16)
        sc = mk("sc", [128, 8], F32)
        OBb = mk("OBb", [128, 512], BF16)
        OBT = mk("OBT", [128, 4, 128], BF16)

        def combine(h, o0, o0c, o1, o1c, xr0, xr1):
            op(V, lambda e: e.reciprocal(out=sc.t[:, 0:1], in_=o0[:, 128:129]), xr=[xr0], w=[sc])
            op(V, lambda e: e.reciprocal(out=sc.t[:, 1:2], in_=o1[:, 128:129]), xr=[xr1], w=[sc])
            op(V, lambda e: e.tensor_tensor(out=sc.t[:, 2:3], in0=sc.t[:, 1:2], in1=ls.t[:, 5:6], op=ALU.mult), r=[ls], w=[sc])
            op(V, lambda e: e.tensor_scalar(out=T2.t[:], in0=o1[:, 0:128], scalar1=sc.t[:, 2:3], scalar2=None, op0=ALU.mult), xr=[xr1], r=[sc], w=[T2])
            op(V, lambda e: e.scalar_tensor_tensor(out=ATT.t[:], in0=o0[:, 0:128], scalar=sc.t[:, 0:1], in1=T2.t[:], op0=ALU.mult, op1=ALU.add),
               xr=[xr0], r=[sc, T2], w=[ATT])
            op(A, lambda e: e.activation(out=junk.t[:], in_=ATT.t[:], func=AF.Square, accum_out=sc.t[:, 3:4]), r=[ATT], w=[junk, sc])
            op(A, lambda e: e.activation(out=sc.t[:, 4:5], in_=sc.t[:, 3:4], func=AF.Sqrt, scale=1.0 / 128, bias=cst.t[:, 3:4]), r=[cst], w=[sc])
            op(V, lambda e: e.reciprocal(out=sc.t[:, 5:6], in_=sc.t[:, 4:5]), w=[sc])
            op(V, lambda e: e.scalar_tensor_tensor(out=OBb.t[:, h * 128:(h + 1) * 128], in0=ATT.t[:], scalar=sc.t[:, 5:6], in1=sg.t[:], op0=ALU.mult, op1=ALU.mult),
               r=[ATT, sc, sg], disjoint=[OBb])

        def finish_tile(qi):
            ptb = pstr.t[:].bitcast(BF16)
            for h in range(4):
                op(PE, lambda e, h=h: e.transpose(out=ptb[:, h * 128:(h + 1) * 128], in_=OBb.t[:, h * 128:(h + 1) * 128], identity=identb.t[:]),
                   r=[OBb, identb], w=[pstr] if h == 0 else (), disjoint=[pstr] if h else ())
            op(A, lambda e: e.copy(out=OBT.t[:].rearrange("p a b -> p (a b)"), in_=ptb[:, 0:512]), xr=[pstr], w=[OBT])
            P.dma("sp", T["obT_d"].rearrange("c p t -> p c t")[:, :, qi * 128:(qi + 1) * 128], OBT.t[:], reads=[OBT], disjoint=[T["b_obT"]])

        qtiles = list(range(NT_OWN)) if DEBUG_QT is None else DEBUG_QT
        if DEBUG_B2 < 1:
            qtiles = []
        for qi in qtiles:
            qg = NT_PRIOR + qi
            for h in range(4):
                o0, o1 = psO[0], psO[1]
                slope_h = 2.0 ** (-2.0 * (h + 1))
                kts = [kt for kt in range(0, qg + 1) if slope_h * (128 * (qg - kt) - 127) <= 80.0]
                def qk_stage(kt, h=h, qi=qi):
                    pS = nextps()
                    for c in range(2):
                        op(PE, lambda e, c=c, kt=kt, pS=pS, h=h, qi=qi: e.matmul(pS.t[:, c * 512:c * 512 + 128], lhsT=KT.t[c * 64:(c + 1) * 64, h, kt * 128:(kt + 1) * 128],
                                                                     rhs=QT.t[c * 64:(c + 1) * 64, h, qi * 128:(qi + 1) * 128], start=True, stop=True),
                           r=[KT, QT], w=[pS] if c == 0 else (), disjoint=[pS] if c else ())
                    return pS

                def ex_pv_stage(ii, kt, pS, h=h, qg=qg, o0=o0, o1=o1, nk=len(kts)):
                    ET, b_ET = ETr.next()
                    o = qg - kt
                    bt = btabp if kt < NT_PRIOR else btab
                    op(A, lambda e, pS=pS, ET=ET, bt=bt, o=o, h=h: e.activation(out=ET[:].rearrange("p (c q) -> p c q", c=2), in_=pS.t[:, :].rearrange("p (c x) -> p c x", c=2)[:, :, 0:128],
                                                                           func=AF.Exp, scale=0.125, bias=bt.t[:, h, o:o + 1]),
                       xr=[pS], r=[bt], w=[b_ET])
                    if kt == qg:
                        op(V, lambda e, ET=ET: e.tensor_tensor(out=ET[:].rearrange("p (c q) -> p c q", c=2), in0=ET[:].rearrange("p (c q) -> p c q", c=2),
                                                                in1=cm_p.t[:, :].unsqueeze(1).broadcast_to([128, 2, 128]), op=ALU.mult), r=[cm_p], w=[b_ET])
                    first, last = (ii == 0), (ii == nk - 1)
                    op(PE, lambda e, ET=ET, kt=kt, first=first, last=last, h=h, o0=o0: e.matmul(o0.t[:, 0:129], lhsT=ET[:, 0:128], rhs=Vx.t[:, kt, h, 0:129], start=first, stop=last),
                       r=[b_ET, Vx], w=[o0] if first else (), disjoint=() if first else [o0], signal=last)
                    op(PE, lambda e, ET=ET, kt=kt, first=first, last=last, h=h, o1=o1: e.matmul(o1.t[:, 0:129], lhsT=ET[:, 128:256], rhs=Vx.t[:, kt, h, 0:129], start=first, stop=last),
                       r=[b_ET, Vx], w=[o1] if first else (), disjoint=() if first else [o1], signal=last)

                pend = {}
                for ii in range(len(kts) + 1):
                    if ii < len(kts):
                        pend[ii] = qk_stage(kts[ii])
                    if ii >= 1:
                        ex_pv_stage(ii - 1, kts[ii - 1], pend.pop(ii - 1))
                if DEBUG_B2 >= 4:
                    combine(h, o0.t, None, o1.t, None, o0, o1)
            if DEBUG_B2 >= 5:
                finish_tile(qi)

        if DEBUG_SAMPLE_ATT:
            zb = mk("zb", [128, 512], BF16)
            op(PO, lambda e: e.memset(zb.t[:], 0.0), w=[zb])
            slot = {}
            for h in range(4):
                for c in range(2):
                    i = h * 2 + c
                    slot[(h, c)] = (i // 3, (i % 3) * 130)
            for bnk in range(3):
                op(PE, lambda e, bnk=bnk: e.matmul(psO[bnk].t[:, 0:512], lhsT=zb.t[:, 0:128], rhs=zb.t[:, 0:512], start=True, stop=False), r=[zb], w=[psO[bnk]])
            ptb_i = mk("ptab", [128, 256], I32)
            P.dma("sp", ptb_i.t[:], T["ptab"].broadcast_to([128, 256]), writes=[ptb_i])
            pcol = mk("pcol", [128, 1], I32)
            op(PO, lambda e: e.iota(pcol.t[:], pattern=[[0, 1]], base=0, channel_multiplier=1), w=[pcol])
            pcolf = mk("pcolf", [128, 1], F32)
            op(V, lambda e: e.tensor_copy(out=pcolf.t[:], in_=pcol.t[:]), r=[pcol], w=[pcolf])
            idx = mk("idx", [128, 256], I32)
            op(V, lambda e: e.tensor_scalar(out=idx.t[:], in0=ptb_i.t[:], scalar1=128.0, scalar2=pcolf.t[:, 0:1], op0=ALU.mult, op1=ALU.add), r=[ptb_i, pcolf], w=[idx])
            Kpg = Rot(P, [st.enter_context(nc.sbuf_tensor("b2_Kpg%d" % i, [128, 512], F32)) for i in range(4)])
            Vpg = Rot(P, [st.enter_context(nc.sbuf_tensor("b2_Vpg%d" % i, [128, 512], F32)) for i in range(4)])
            KcTr = Rot(P, [st.enter_context(nc.sbuf_tensor("b2_KcT%d" % i, [128, 4, 128], BF16)) for i in range(3)])
            Vpxr = Rot(P, [st.enter_context(nc.sbuf_tensor("b2_Vpx%d" % i, [128, 4, 130], BF16)) for i in range(4)])
            for (vt, vb) in Vpxr.items:
                op(PO, lambda e, vt=vt: e.memset(vt[:], 1.0), w=[vb])
            XB = mk("XB", [128, 2, 32], F32)
            ETz = [mk("ETz%d" % i, [128, 4, 2, 128], BF16) for i in range(3)]
            ETz_b = [None] * 3
            for z in ETz:
                op(PO, lambda e, z=z: e.memset(z.t[:], 0.0), w=[z])
            ez = 0
            sq0 = NT_OWN * 128
            items = [(b, pg) for b in range(16) for pg in range(16)]
            stash = {}

            def s1(i):
                b, pg = items[i]
                col = b * 16 + pg
                kp, b_kp = Kpg.next()
                vp, b_vp = Vpg.next()
                P.dma_ind(kp[:, :], T["cache_k"], idx.t[:, col:col + 1], reads=[idx], writes=[b_kp])
                P.dma_ind(vp[:, :], T["cache_v"], idx.t[:, col:col + 1], reads=[idx], writes=[b_vp])
                pT = nextps()
                for h in range(4):
                    op(PE, lambda e, h=h, kp=kp, pT=pT: e.transpose(out=pT.t[:, h * 128:(h + 1) * 128], in_=kp[:, h * 128:(h + 1) * 128], identity=identf.t[:]),
                       r=[b_kp, identf], w=[pT] if h == 0 else (), disjoint=[pT] if h else ())
                kc, b_kc = KcTr.next()
                op(A, lambda e, kc=kc, pT=pT: e.copy(out=kc[:].rearrange("p a b -> p (a b)"), in_=pT.t[:, 0:512]), xr=[pT], w=[b_kc])
                vx, b_vx = Vpxr.next()
                op(V, lambda e, vx=vx, vp=vp: e.tensor_copy(out=vx[:, :, 0:128], in_=vp[:, :].rearrange("p (h e) -> p h e", h=4)), r=[b_vp], w=[b_vx])
                stash[i] = dict(kc=kc, b_kc=b_kc, vx=vx, b_vx=b_vx)

            def s2(i):
                b, pg = items[i]
                d = stash[i]
                kc, b_kc = d["kc"], d["b_kc"]
                pS = nextps()
                for h in range(4):
                    for c in range(2):
                        first = (h == 0 and c == 0)
                        op(PE, lambda e, h=h, c=c, kc=kc, pS=pS, b=b: e.matmul(pS.t[:, c * 512 + h * 8:c * 512 + h * 8 + 8], lhsT=kc[c * 64:(c + 1) * 64, h, :],
                                                                           rhs=QT.t[c * 64:(c + 1) * 64, h, sq0 + b * 8:sq0 + b * 8 + 8], start=True, stop=True),
                           r=[b_kc, QT], w=[pS] if first else (), disjoint=() if first else [pS])
                op(V, lambda e, pS=pS, pg=pg: e.scalar_tensor_tensor(out=XB.t[:], in0=pS.t[:, :].rearrange("p (c x) -> p c x", c=2)[:, :, 0:32], scalar=0.125,
                                                                       in1=bcache.t[:, pg, :].unsqueeze(1).broadcast_to([128, 2, 32]), op0=ALU.mult, op1=ALU.add),
                   xr=[pS], r=[bcache], w=[XB])
                zi = i % 3
                z = ETz[zi]
                zprev = ETz_b[zi]
                if zprev is not None and zprev != b:
                    op(PO, lambda e, z=z, zprev=zprev: e.memset(z.t[:, :, :, zprev * 8:zprev * 8 + 8], 0.0), w=[z])
                ETz_b[zi] = b
                op(A, lambda e, z=z, b=b: e.activation(out=z.t[:, :, :, b * 8:b * 8 + 8], in_=XB.t[:].rearrange("p c (h q) -> p h c q", h=4), func=AF.Exp),
                   r=[XB], w=[z])
                d["z"] = z

            def s3(i):
                d = stash.pop(i)
                z, vx, b_vx = d["z"], d["vx"], d["b_vx"]
                for h in range(4):
                    for c in range(2):
                        bnk, c0 = slot[(h, c)]
                        op(PE, lambda e, h=h, c=c, z=z, vx=vx, bnk=bnk, c0=c0: e.matmul(psO[bnk].t[:, c0:c0 + 129], lhsT=z.t[:, h, c, :], rhs=vx[:, h, 0:129], start=False, stop=False),
                           r=[z, b_vx], disjoint=[psO[bnk]], signal=(h == 3 and c == 1))

            NI = len(items)
            for j in range(NI + 2):
                if j < NI:
                    s1(j)
                if 1 <= j <= NI:
                    s2(j - 1)
                if j >= 2:
                    s3(j - 2)
            kt = NTILE - 1
            for h in range(4):
                pS = nextps()
                for c in range(2):
                    op(PE, lambda e, c=c, h=h, pS=pS: e.matmul(pS.t[:, c * 512:c * 512 + 128], lhsT=KT.t[c * 64:(c + 1) * 64, h, kt * 128:(kt + 1) * 128],
                                                                 rhs=QT.t[c * 64:(c + 1) * 64, h, sq0:sq0 + 128], start=True, stop=True),
                       r=[KT, QT], w=[pS] if c == 0 else (), disjoint=[pS] if c else ())
                ET, b_ET = ETr.next()
                op(A, lambda e, pS=pS, ET=ET, h=h: e.activation(out=ET[:].rearrange("p (c q) -> p c q", c=2), in_=pS.t[:, :].rearrange("p (c x) -> p c x", c=2)[:, :, 0:128],
                                                                func=AF.Exp, scale=0.125, bias=bown.t[:, h:h + 1]), xr=[pS], r=[bown], w=[b_ET])
                op(V, lambda e, ET=ET: e.tensor_tensor(out=ET[:].rearrange("p (c q) -> p c q", c=2), in0=ET[:].rearrange("p (c q) -> p c q", c=2),
                                                        in1=cm_s.t[:, :].unsqueeze(1).broadcast_to([128, 2, 128]), op=ALU.mult), r=[cm_s], w=[b_ET])
                for c in range(2):
                    bnk, c0 = slot[(h, c)]
                    op(PE, lambda e, c=c, h=h, ET=ET, bnk=bnk, c0=c0: e.matmul(psO[bnk].t[:, c0:c0 + 129], lhsT=ET[:, c * 128:(c + 1) * 128], rhs=Vx.t[:, kt, h, 0:129], start=False, stop=True),
                       r=[b_ET, Vx], disjoint=[psO[bnk]])
            for h in range(4):
                b0, c0 = slot[(h, 0)]
                b1, c1 = slot[(h, 1)]
                combine(h, psO[b0].t[:, c0:c0 + 129], None, psO[b1].t[:, c1:c1 + 129], None, psO[b0], psO[b1])
            finish_tile(NT_OWN)
        P.emit("phaseB2")


def phase_C1(P, nc, T):
    V, A, PO, PE = "dve", "act", "pool", "pe"
    with contextlib.ExitStack() as st:
        def mk(n, s, d):
            return Tl(st.enter_context(nc.sbuf_tensor("c1_" + n, s, d)), P.buf(n))

        def op(e, fn, r=(), w=(), **kw):
            return P.op(e, fn, reads=r, writes=w, **kw)

        cst = T["cst"]
        identb = mk("identb", [128, 128], BF16)
        P.dma("pool", identb.t[:], T["ident"], writes=[identb])
        wa = mk("wa", [128, 4, D], BF16)
        wb = mk("wb", [128, 4, D], BF16)
        wo = mk("wo", [128, 8, D], BF16)
        for kc in range(4):
            P.dma("pool", wa.t[:, kc, :], T["w_br_a"][kc * 128:(kc + 1) * 128, :], disjoint=[wa])
            P.dma("pool", wb.t[:, kc, :], T["w_br_b"][kc * 128:(kc + 1) * 128, :], disjoint=[wb])
        for kc in range(8):
            P.dma("pool", wo.t[:, kc, :], T["w_out"][kc * 128:(kc + 1) * 128, :], disjoint=[wo])
        gffn = mk("gffn", [128, D], F32)
        P.dma("sp", gffn.t[:], T["g_ffn"].broadcast_to([128, D]), writes=[gffn])
        psr = Rot(P, [st.enter_context(nc.psum_tensor("c1_psr%d" % i, [128, 512], F32)) for i in range(8)])

        def nextps():
            t, b = psr.next()
            return Tl(t, b)

        oaT = [mk("oaT%d" % i, [128, 4, 512], BF16) for i in range(2)]
        obT = [mk("obT%d" % i, [128, 4, 512], BF16) for i in range(2)]
        gts = [mk("gts%d" % i, [128, 16, 512], BF16) for i in range(2)]
        mT = [mk("mT%d" % i, [128, 8, 512], BF16) for i in range(2)]
        t1 = [mk("t1_%d" % i, [128, 512], F32) for i in range(2)]
        t2 = [mk("t2_%d" % i, [128, 512], F32) for i in range(2)]
        xt = [mk("xt%d" % i, [128, D], F32) for i in range(2)]
        xn = [mk("xn%d" % i, [128, D], F32) for i in range(2)]
        hb = [mk("hb%d" % i, [128, D], BF16) for i in range(2)]
        hT = [mk("hT%d" % i, [128, 8, 128], BF16) for i in range(2)]
        junk = mk("junk", [128, D], BF16)
        ss = [mk("ss%d" % i, [128, 4], F32) for i in range(2)]
        groups = [(g * 4, 4) for g in range(4)] + [(16, 1)]
        tcount = 0
        for gi, (qt0, nt) in enumerate(groups):
            G = nt * 128
            q0 = qt0 * 128
            oa, ob, gt, m = oaT[gi % 2], obT[gi % 2], gts[gi % 2], mT[gi % 2]
            P.dma("sp", oa.t[:, :, 0:G], T["oaT_d"].rearrange("c p t -> p c t")[:, :, q0:q0 + G], reads=[T["b_oaT"]], writes=[oa])
            P.dma("sp", ob.t[:, :, 0:G], T["obT_d"].rearrange("c p t -> p c t")[:, :, q0:q0 + G], reads=[T["b_obT"]], writes=[ob])
            P.dma("sp", gt.t[:, :, 0:G], T["gates_d"].rearrange("c p t -> p c t")[:, :, q0:q0 + G], reads=[T["b_gates"]], writes=[gt])
            for nch in range(8):
                pa = nextps()
                pb = nextps()
                for kc in range(4):
                    op(PE, lambda e, pa=pa, kc=kc, nch=nch, oa=oa, G=G: e.matmul(pa.t[:, 0:G], lhsT=wa.t[:, kc, nch * 128:(nch + 1) * 128], rhs=oa.t[:, kc, 0:G], start=(kc == 0), stop=(kc == 3)),
                       r=[wa, oa], w=[pa] if kc == 0 else (), disjoint=[pa] if kc else (), signal=(kc == 3))
                for kc in range(4):
                    op(PE, lambda e, pb=pb, kc=kc, nch=nch, ob=ob, G=G: e.matmul(pb.t[:, 0:G], lhsT=wb.t[:, kc, nch * 128:(nch + 1) * 128], rhs=ob.t[:, kc, 0:G], start=(kc == 0), stop=(kc == 3)),
                       r=[wb, ob], w=[pb] if kc == 0 else (), disjoint=[pb] if kc else (), signal=(kc == 3))
                ta, tb = t1[nch % 2], t2[nch % 2]
                op(V, lambda e, pa=pa, ta=ta, gt=gt, nch=nch, G=G: e.tensor_tensor(out=ta.t[:, 0:G], in0=pa.t[:, 0:G], in1=gt.t[:, nch, 0:G], op=ALU.mult), xr=[pa], r=[gt], w=[ta])
                op(V, lambda e, pb=pb, tb=tb, gt=gt, nch=nch, G=G: e.tensor_tensor(out=tb.t[:, 0:G], in0=pb.t[:, 0:G], in1=gt.t[:, 8 + nch, 0:G], op=ALU.mult), xr=[pb], r=[gt], w=[tb])
                op(PO, lambda e, ta=ta, tb=tb, m=m, nch=nch, G=G: e.tensor_tensor(out=m.t[:, nch, 0:G], in0=ta.t[:, 0:G], in1=tb.t[:, 0:G], op=ALU.add), r=[ta, tb], disjoint=[m])
            for ti in range(nt):
                qt = qt0 + ti
                tg = NT_PRIOR + qt
                x, xo, h_, hTt, s_ = xt[tcount % 2], xn[tcount % 2], hb[tcount % 2], hT[tcount % 2], ss[tcount % 2]
                tcount += 1
                P.dma("sp", x.t[:], T["x_all"][tg * 128:(tg + 1) * 128, :], writes=[x])
                for half in range(2):
                    px = nextps()
                    for kc in range(8):
                        op(PE, lambda e, px=px, kc=kc, half=half, m=m, ti=ti: e.matmul(px.t[:, 0:512], lhsT=m.t[:, kc, ti * 128:(ti + 1) * 128], rhs=wo.t[:, kc, half * 512:(half + 1) * 512],
                                                                                    start=(kc == 0), stop=(kc == 7)),
                           r=[m, wo], w=[px] if kc == 0 else (), disjoint=[px] if kc else (), signal=(kc == 7))
                    op(V, lambda e, px=px, half=half, x=x, xo=xo: e.tensor_tensor(out=xo.t[:, half * 512:(half + 1) * 512], in0=px.t[:, 0:512], in1=x.t[:, half * 512:(half + 1) * 512], op=ALU.add),
                       xr=[px], r=[x], w=[xo] if half == 0 else (), disjoint=[xo] if half else ())
                P.dma("sp", T["xnew_d"][qt * 128:(qt + 1) * 128, :], xo.t[:], reads=[xo], disjoint=[T["b_xnew"]])
                op(A, lambda e, xo=xo, s_=s_: e.activation(out=junk.t[:], in_=xo.t[:], func=AF.Square, accum_out=s_.t[:, 0:1]), r=[xo], w=[junk, s_])
                op(A, lambda e, s_=s_: e.activation(out=s_.t[:, 1:2], in_=s_.t[:, 0:1], func=AF.Sqrt, scale=1.0 / D, bias=cst.t[:, 0:1]), r=[cst], w=[s_])
                op(V, lambda e, s_=s_: e.reciprocal(out=s_.t[:, 2:3], in_=s_.t[:, 1:2]), w=[s_])
                op(V, lambda e, xo=xo, s_=s_, h_=h_: e.scalar_tensor_tensor(out=h_.t[:], in0=xo.t[:], scalar=s_.t[:, 2:3], in1=gffn.t[:], op0=ALU.mult, op1=ALU.mult),
                   r=[xo, s_, gffn], w=[h_])
                pt = nextps()
                ptb = pt.t[:].bitcast(BF16)
                for kc in range(8):
                    op(PE, lambda e, kc=kc, ptb=ptb, h_=h_: e.transpose(out=ptb[:, kc * 128:(kc + 1) * 128], in_=h_.t[:, kc * 128:(kc + 1) * 128], identity=identb.t[:]),
                       r=[h_, identb], w=[pt] if kc == 0 else (), disjoint=[pt] if kc else ())
                op(A, lambda e, ptb=ptb, hTt=hTt: e.copy(out=hTt.t[:].rearrange("p a b -> p (a b)"), in_=ptb[:, 0:1024]), xr=[pt], w=[hTt])
                P.dma("sp", T["hnT_d"].rearrange("c p t -> p c t")[:, :, qt * 128:(qt + 1) * 128], hTt.t[:], reads=[hTt], disjoint=[T["b_hnT"]])
        P.emit("phaseC1")


def phase_C2(P, nc, T):
    V, A, PO, PE = "dve", "act", "pool", "pe"
    with contextlib.ExitStack() as st:
        def mk(n, s, d):
            return Tl(st.enter_context(nc.sbuf_tensor("c2_" + n, s, d)), P.buf(n))

        def op(e, fn, r=(), w=(), **kw):
            return P.op(e, fn, reads=r, writes=w, **kw)

        cst = T["cst"]
        wu = mk("wu", [128, 8, 4096], BF16)
        wd = mk("wd", [128, 32, D], BF16)
        for kc in range(8):
            for j in range(2):
                P.dma("pool", wu.t[:, kc, j * 2048:(j + 1) * 2048], T["w_up"][kc * 128:(kc + 1) * 128, j * 2048:(j + 1) * 2048], disjoint=[wu])
        for fc in range(32):
            P.dma("pool", wd.t[:, fc, :], T["w_down"][fc * 128:(fc + 1) * 128, :], disjoint=[wd])
        gfin = mk("gfin", [128, D], F32)
        P.dma("sp", gfin.t[:], T["g_final"].broadcast_to([128, D]), writes=[gfin])
        psr = Rot(P, [st.enter_context(nc.psum_tensor("c2_psr%d" % i, [128, 512], F32)) for i in range(8)])

        def nextps():
            t, b = psr.next()
            return Tl(t, b)

        hT = [mk("hT%d" % i, [128, 8, 256], BF16) for i in range(2)]
        aT = [mk("aT%d" % i, [128, 32, 256], BF16) for i in range(1)]
        rl = [mk("rl%d" % i, [128, 256], F32) for i in range(3)]
        xw = [mk("xw%d" % i, [128, D], F32) for i in range(2)]
        yo = [mk("yo%d" % i, [128, D], F32) for i in range(2)]
        yf = [mk("yf%d" % i, [128, D], F32) for i in range(2)]
        junk = mk("junk", [128, D], BF16)
        ss = [mk("ss%d" % i, [128, 4], F32) for i in range(2)]
        groups = [(g * 2, 2) for g in range(8)] + [(16, 1)]
        tcount = 0
        for gi, (qt0, nt) in enumerate(groups):
            G = nt * 128
            q0 = qt0 * 128
            h_ = hT[gi % 2]
            a_ = aT[0]
            P.dma("sp", h_.t[:, :, 0:G], T["hnT_d"].rearrange("c p t -> p c t")[:, :, q0:q0 + G], reads=[T["b_hnT"]], writes=[h_])
            for fc in range(32):
                pu = nextps()
                for kc in range(8):
                    op(PE, lambda e, pu=pu, kc=kc, fc=fc, h_=h_, G=G: e.matmul(pu.t[:, 0:G], lhsT=wu.t[:, kc, fc * 128:(fc + 1) * 128], rhs=h_.t[:, kc, 0:G], start=(kc == 0), stop=(kc == 7)),
                       r=[wu, h_], w=[pu] if kc == 0 else (), disjoint=[pu] if kc else (), signal=(kc == 7))
                r_ = rl[fc % 3]
                op(A, lambda e, pu=pu, r_=r_, G=G: e.activation(out=r_.t[:, 0:G], in_=pu.t[:, 0:G], func=AF.Relu), xr=[pu], w=[r_])
                op(PO, lambda e, r_=r_, a_=a_, fc=fc, G=G: e.tensor_tensor(out=a_.t[:, fc, 0:G], in0=r_.t[:, 0:G], in1=r_.t[:, 0:G], op=ALU.mult), r=[r_], disjoint=[a_])
            for ti in range(nt):
                qt = qt0 + ti
                x, y, yfin, s_ = xw[tcount % 2], yo[tcount % 2], yf[tcount % 2], ss[tcount % 2]
                tcount += 1
                P.dma("sp", x.t[:], T["xnew_d"][qt * 128:(qt + 1) * 128, :], reads=[T["b_xnew"]], writes=[x])
                for half in range(2):
                    pd = nextps()
                    for fc in range(32):
                        op(PE, lambda e, pd=pd, fc=fc, half=half, a_=a_, ti=ti: e.matmul(pd.t[:, 0:512], lhsT=a_.t[:, fc, ti * 128:(ti + 1) * 128], rhs=wd.t[:, fc, half * 512:(half + 1) * 512],
                                                                                      start=(fc == 0), stop=(fc == 31)),
                           r=[a_, wd], w=[pd] if fc == 0 else (), disjoint=[pd] if fc else (), signal=(fc == 31))
                    op(V, lambda e, pd=pd, half=half, x=x, y=y: e.tensor_tensor(out=y.t[:, half * 512:(half + 1) * 512], in0=pd.t[:, 0:512], in1=x.t[:, half * 512:(half + 1) * 512], op=ALU.add),
                       xr=[pd], r=[x], w=[y] if half == 0 else (), disjoint=[y] if half else ())
                op(A, lambda e, y=y, s_=s_: e.activation(out=junk.t[:], in_=y.t[:], func=AF.Square, accum_out=s_.t[:, 0:1]), r=[y], w=[junk, s_])
                op(A, lambda e, s_=s_: e.activation(out=s_.t[:, 1:2], in_=s_.t[:, 0:1], func=AF.Sqrt, scale=1.0 / D, bias=cst.t[:, 0:1]), r=[cst], w=[s_])
                op(V, lambda e, s_=s_: e.reciprocal(out=s_.t[:, 2:3], in_=s_.t[:, 1:2]), w=[s_])
                op(V, lambda e, y=y, s_=s_, yfin=yfin: e.scalar_tensor_tensor(out=yfin.t[:], in0=y.t[:], scalar=s_.t[:, 2:3], in1=gfin.t[:], op0=ALU.mult, op1=ALU.mult),
                   r=[y, s_, gfin], w=[yfin])
                P.dma("sp", T["yout"][qt * 128:(qt + 1) * 128, :], yfin.t[:], reads=[yfin])
        P.emit("phaseC2")


EPS_T = [None, None]
DEBUG_QT = None
DEBUG_DUMP = False
DEBUG_B2 = 9
DEBUG_SAMPLE_ATT = True
DEBUG_TILES = None
DEBUG_GROUPS = None
DEBUG_CHUNKS = None
DEBUG_NOTM = False
DEBUG_HALF = None
DEBUG_NOOUT = False


def build_program(stage=99):
    nc = bass.Bass("TRN2", target_bir_lowering=False)
    din = lambda n, s, d=F32: nc.dram_tensor(n, s, d, kind="ExternalInput").ap()
    dout = lambda n, s, d=F32: nc.dram_tensor(n, s, d, kind="ExternalOutput").ap()
    dscr = (lambda n, s, d: nc.dram_tensor(n, s, d, kind="ExternalOutput").ap()) if DEBUG_DUMP else (lambda n, s, d: nc.dram_tensor(n, s, d).ap())
    T = {}
    T["x_all"] = din("x_all", [NTOK, D])
    T["st_shift"] = din("st_shift", [16, D])
    T["w_in"] = din("w_in", [D, NCOL])
    T["g_mix"] = din("g_mix", [1, D])
    T["mu_fm"] = din("mu_fm", [128, 14])
    for n, shp in [("ident", [128, 128]), ("mtc_p", [128, 256]), ("mtc_s", [128, 256]), ("ms_p", [128, 128]), ("ms_s", [128, 128]),
                   ("rm_p", [128, 512]), ("rm_s", [128, 512]), ("bones", [128, 128]), ("sel8", [128, 32]), ("segsel", [128, 16]),
                   ("w2", [64, 512]), ("a2", [64, 512]), ("g2", [128, 512]), ("w0_fm", [128, 4]), ("a0_fm", [128, 4]),
                   ("kk_fm", [128, 4]), ("ka_fm", [128, 4]), ("rk_fm", [128, 4]), ("ln_w", [1, 512]), ("ln_b", [1, 512]),
                   ("st_rwkv", [16, 8, 64, 64]), ("pmask", [128, 1]), ("btab", [128, 128]), ("bown", [128, 4]), ("bcache", [128, 512]),
                   ("lam_q1", [1, 64]), ("lam_k1", [1, 64]), ("lam_q2", [1, 64]), ("lam_k2", [1, 64]), ("subln_g", [1, 128]),
                   ] + ([("cache_k", [NPOOL * 128, 512]), ("cache_v", [NPOOL * 128, 512])] if DEBUG_SAMPLE_ATT else []) + [
                   ("w_br_a", [512, D]), ("w_br_b", [512, D]), ("w_out", [D, D]), ("g_ffn", [1, D]), ("w_up", [D, 4096]), ("w_down", [4096, D]),
                   ("g_final", [1, D])]:
        T[n] = din(n, shp)
    T["ptab"] = din("ptab", [1, 256], I32)
    T["yout"] = dout("yout", [NQ, D])
    T["obT_d"] = dscr("obT_d", [4, 128, NQ], BF16)
    T["xnew_d"] = dscr("xnew_d", [NQ, D], F32)
    T["hnT_d"] = dscr("hnT_d", [8, 128, NQ], BF16)
    T["kout"] = dout("kout", [NQ, 512])
    T["vout"] = dout("vout", [NQ, 512])
    T["shift_out"] = dout("shift_out", [17, D])
    T["rwkv_p_out"] = dout("rwkv_p_out", [8, 64, 64])
    T["rwkv_s_out"] = dout("rwkv_s_out", [16, 8, 64, 64])
    T["prT_d"] = dscr("prT_d", [14, 128, NTOK], F32)
    T["qT_d"] = dscr("qT_d", [4, 128, NQ], BF16)
    T["kT_d"] = dscr("kT_d", [4, 128, NTOK], BF16)
    T["v_d"] = dscr("v_d", [NTOK, 512], BF16)
    T["gates_d"] = dscr("gates_d", [16, 128, NQ], BF16)
    T["oaT_d"] = dscr("oaT_d", [4, 128, NQ], BF16)
    with contextlib.ExitStack() as st:
        P = Prog(nc, st)
        for n in ["prT", "qT", "kT", "v", "gates", "oaT", "obT", "xnew", "hnT"]:
            T["b_" + n] = P.buf()
        cst = Tl(st.enter_context(nc.sbuf_tensor("cst", [128, 8], F32)), P.buf())
        T["cst"] = cst
        for i, val in enumerate([EPS, 1e-24, 64e-5, 1e-5]):
            P.op("pool", lambda e, i=i, val=val: e.memset(cst.t[:, i:i + 1], val), disjoint=[cst])
        phase_A(P, nc, T)
        if stage >= 2:
            phase_B1(P, nc, T)
        if stage >= 3:
            phase_B2(P, nc, T)
        if stage >= 4:
            phase_C1(P, nc, T)
            phase_C2(P, nc, T)
        P.final_wait("sp")
        P.emit("final")
    return nc


def _consts():
    s = np.arange(128)
    seg = s // 8
    le = (s[:, None] <= s[None, :]).astype(np.float32)
    lt = (s[:, None] < s[None, :]).astype(np.float32)
    same = (seg[:, None] == seg[None, :]).astype(np.float32)
    c = {}
    c["ident"] = np.eye(128, dtype=np.float32)
    c["mtc_p"] = np.concatenate([le, lt], axis=1)
    c["mtc_s"] = np.concatenate([le * same, lt * same], axis=1)
    c["ms_p"] = lt.T.copy()
    c["ms_s"] = (lt * same).T.copy()
    rm = np.ones((128, 4, 128), np.float32)
    rm[:, :, 0] = 0
    c["rm_p"] = rm.reshape(128, 512)
    rm = np.ones((128, 4, 128), np.float32)
    rm[:, :, 0::8] = 0
    c["rm_s"] = rm.reshape(128, 512)
    bo = np.zeros((128, 128), np.float32)
    bo[:64, :64] = 1
    bo[64:, 64:] = 1
    c["bones"] = bo
    sel = np.zeros((128, 4, 8), np.float32)
    for p in range(128):
        for hp in range(4):
            sel[p, hp, 2 * hp + p // 64] = 1
    c["sel8"] = sel.reshape(128, 32)
    c["segsel"] = (seg[:, None] == np.arange(16)[None, :]).astype(np.float32)
    slopes = (2.0 ** (-8.0 * np.arange(1, 5) / 4)).astype(np.float32)
    pp = np.arange(128, dtype=np.float32)
    bt = np.zeros((128, 4, 32), np.float32)
    for h in range(4):
        for o in range(32):
            bt[:, h, o] = slopes[h] * (pp - 128.0 * o)
    c["btab"] = bt.reshape(128, 128)
    c["bown"] = (slopes[None, :] * (pp[:, None] % 8)).astype(np.float32)
    bc = np.zeros((128, 16, 4, 8), np.float32)
    for pg in range(16):
        for h in range(4):
            bc[:, pg, h, :] = (slopes[h] * (pg * 128.0 + pp - 2048.0))[:, None]
    c["bcache"] = bc.reshape(128, 512)
    return c


def make_in_maps(inp):
    f = lambda k: np.asarray(inp[k], np.float32)
    xp = f("x_prompt")
    xs = f("x_sample")
    maps = []
    fm4 = lambda v: np.ascontiguousarray(np.asarray(v, np.float32).reshape(4, 128).T)
    shared = {
        "w_in": np.ascontiguousarray(f("w_in")[0]),
        "g_mix": f("g_mix")[0].reshape(1, D).copy(),
        "mu_fm": np.ascontiguousarray(f("mu_shift").reshape(14, 128).T),
        "w2": np.ascontiguousarray(f("w2")[0]), "a2": np.ascontiguousarray(f("a2")[0]), "g2": np.ascontiguousarray(f("g2")[0]),
        "w0_fm": fm4(f("w0")[0]), "a0_fm": fm4(f("a0")[0]), "kk_fm": fm4(f("k_k")[0]), "ka_fm": fm4(f("k_a")[0]),
        "rk_fm": fm4(f("r_k")[0].reshape(512)),
        "ln_w": f("ln_x_w")[0].reshape(1, 512).copy(), "ln_b": f("ln_x_b")[0].reshape(1, 512).copy(),
        "lam_q1": f("lam_q1")[0].reshape(1, 64).copy(), "lam_k1": f("lam_k1")[0].reshape(1, 64).copy(),
        "lam_q2": f("lam_q2")[0].reshape(1, 64).copy(), "lam_k2": f("lam_k2")[0].reshape(1, 64).copy(),
        "subln_g": f("subln_g")[0].reshape(1, 128).copy(),
        "cache_k": np.asarray(inp["cache_k"], np.float32).reshape(NPOOL * 128, 512),
        "cache_v": np.asarray(inp["cache_v"], np.float32).reshape(NPOOL * 128, 512),
        "w_br_a": np.ascontiguousarray(f("w_br_a")[0]), "w_br_b": np.ascontiguousarray(f("w_br_b")[0]),
        "w_out": np.ascontiguousarray(f("w_out")[0]), "g_ffn": f("g_ffn")[0].reshape(1, D).copy(),
        "w_up": np.ascontiguousarray(f("w_up")[0]), "w_down": np.ascontiguousarray(f("w_down")[0]),
        "g_final": f("g_final").reshape(1, D).copy(),
    }
    shared.update(_consts())
    for c in range(8):
        b, p = divmod(c, 2)
        xa = np.zeros((NTOK, D), np.float32)
        if p == 1:
            xa[0:2048] = xp[b, 0:2048]
        xa[2048:4096] = xp[b, 2048 * p:2048 * p + 2048]
        xa[4096:] = xs[16 * c:16 * c + 16].reshape(128, D)
        m = dict(shared)
        m["x_all"] = xa
        m["st_shift"] = np.ascontiguousarray(f("state_shift")[0, 16 * c:16 * c + 16])
        m["st_rwkv"] = np.ascontiguousarray(f("state_rwkv")[0, 16 * c:16 * c + 16])
        m["ptab"] = np.ascontiguousarray(np.asarray(inp["page_table"], np.int32)[16 * c:16 * c + 16]).reshape(1, 256)
        m["pmask"] = np.full((128, 1), 0.0 if p == 1 else -30000.0, np.float32)
        maps.append(m)
    return maps


_NC_CACHE = {}


def kernel(**inp):
    if "nc" not in _NC_CACHE:
        _NC_CACHE["nc"] = build_program()
    nc = _NC_CACHE["nc"]
    in_maps = make_in_maps(inp)
    res = run_bass_kernel_spmd(nc, in_maps, core_ids=list(range(8)))
    R = res.results
    kp = np.zeros((1, 4, 4096, 4, 128), np.float32)
    vp = np.zeros((1, 4, 4096, 4, 128), np.float32)
    ks = np.zeros((1, 128, 8, 4, 128), np.float32)
    vs = np.zeros((1, 128, 8, 4, 128), np.float32)
    shp = np.zeros((1, 4, D), np.float32)
    shs = np.zeros((1, 128, D), np.float32)
    yp = np.zeros((4, 4096, D), np.float32)
    ys = np.zeros((128, 8, D), np.float32)
    rp = np.zeros((1, 4, 8, 64, 64), np.float32)
    rs = np.zeros((1, 128, 8, 64, 64), np.float32)
    for c in range(8):
        b, p = divmod(c, 2)
        r = R[c]
        kp[0, b, 2048 * p:2048 * p + 2048] = r["kout"][0:2048].reshape(2048, 4, 128)
        vp[0, b, 2048 * p:2048 * p + 2048] = r["vout"][0:2048].reshape(2048, 4, 128)
        ks[0, 16 * c:16 * c + 16] = r["kout"][2048:].reshape(16, 8, 4, 128)
        vs[0, 16 * c:16 * c + 16] = r["vout"][2048:].reshape(16, 8, 4, 128)
        if p == 1:
            shp[0, b] = r["shift_out"][0]
            rp[0, b] = r["rwkv_p_out"]
        shs[0, 16 * c:16 * c + 16] = r["shift_out"][1:17]
        rs[0, 16 * c:16 * c + 16] = r["rwkv_s_out"]
        yp[b, 2048 * p:2048 * p + 2048] = r["yout"][0:2048]
        ys[16 * c:16 * c + 16] = r["yout"][2048:].reshape(16, 8, D)
    return (yp, ys, kp, vp, ks, vs, rp, rs, shp, shs)
```

```python
import numpy as np
import contextlib
import concourse.bass as bass
import concourse.mybir as mybir
from concourse.bass_utils import run_bass_kernel_spmd

F32 = mybir.dt.float32
BF16 = mybir.dt.bfloat16
I32 = mybir.dt.int32
AF = mybir.ActivationFunctionType
ALU = mybir.AluOpType
AX = mybir.AxisListType

NDMA = 12
SAME_ENGINE_SYNC = True

D = 1024
NCOL = 5376
NT_PRIOR = 16
NT_OWN = 16
NTILE = 33
NTOK = NTILE * 128
NQT = 17
NQ = NQT * 128
NPOOL = 2560
EPS = 1e-6


class Buf:
    __slots__ = ("name", "w", "r")

    def __init__(self, name):
        self.name = name
        self.w = {}
        self.r = {}


def _bl(xs):
    return [x.b if hasattr(x, "b") else x for x in xs]


class Tl:
    __slots__ = ("t", "b")

    def __init__(self, t, b):
        self.t = t
        self.b = b


class Prog:
    def __init__(self, nc, stack):
        self.nc = nc
        self.stack = stack
        self.eng = {"pe": nc.tensor, "act": nc.scalar, "dve": nc.vector, "pool": nc.gpsimd, "sp": nc.sync}
        self.sem = {}
        for e in self.eng:
            self.sem[e] = stack.enter_context(nc.semaphore("s_" + e))
        for i in range(NDMA):
            self.sem[("d", i)] = stack.enter_context(nc.semaphore("d%d" % i))
        self.cnt = {e: 0 for e in self.eng}
        self.dcnt = [0] * NDMA
        self.drr = 0
        self.lists = {e: [] for e in self.eng}
        self.waited = {e: {} for e in self.eng}
        self.nbuf = 0

    def buf(self, name=None):
        self.nbuf += 1
        return Buf(name or "b%d" % self.nbuf)

    def _deps(self, e, reads, writes, disjoint=(), xr=()):
        deps = {}
        reads, writes, disjoint, xr = _bl(reads), _bl(writes), _bl(disjoint), _bl(xr)

        def need(k, v):
            if deps.get(k, 0) < v:
                deps[k] = v

        for b in reads:
            for k, v in b.w.items():
                need(k, v)
        for b in writes:
            for k, v in b.w.items():
                need(k, v)
            for k, v in b.r.items():
                need(k, v)
        for b in disjoint:
            for k, v in b.r.items():
                need(k, v)
        for b in xr:
            for k, v in b.w.items():
                need(k, v)
            for k, v in b.r.items():
                need(k, v)
        waits = []
        for k, v in deps.items():
            if k == e and (e == "pe" or not SAME_ENGINE_SYNC):
                continue
            if self.waited[e].get(k, 0) >= v:
                continue
            self.waited[e][k] = v
            waits.append((k, v))
        return waits

    def op(self, e, fn, reads=(), writes=(), signal=True, disjoint=(), xr=()):
        waits = self._deps(e, reads, writes, disjoint, xr)
        reads, writes, disjoint = _bl(reads) + _bl(xr), _bl(writes), _bl(disjoint)
        if signal:
            self.cnt[e] += 1
            tok = (e, self.cnt[e])
            inc = (e, 1)
        else:
            tok = (e, self.cnt[e] + 1)
            inc = None
        for b in reads:
            b.r[tok[0]] = max(b.r.get(tok[0], 0), tok[1])
        for b in writes:
            b.w = {tok[0]: tok[1]}
            b.r = {}
        for b in disjoint:
            b.w[tok[0]] = max(b.w.get(tok[0], 0), tok[1])
        self.lists[e].append((waits, fn, inc))
        return tok

    def dma(self, q, out, in_, reads=(), writes=(), disjoint=(), **kw):
        i = self.drr
        self.drr = (i + 1) % NDMA
        key = ("d", i)
        waits = self._deps(q, reads, writes, disjoint)
        reads, writes, disjoint = _bl(reads), _bl(writes), _bl(disjoint)
        prev = self.dcnt[i]
        if prev > 0 and self.waited[q].get(key, 0) < prev:
            self.waited[q][key] = prev
            waits.append((key, prev))
        self.dcnt[i] += 16
        tok = (key, self.dcnt[i])
        for b in reads:
            b.r[key] = max(b.r.get(key, 0), tok[1])
        for b in writes:
            b.w = {tok[0]: tok[1]}
            b.r = {}
        for b in disjoint:
            b.w[tok[0]] = max(b.w.get(tok[0], 0), tok[1])
        self.lists[q].append((waits, lambda eng: eng.dma_start(out=out, in_=in_, **kw), (key, 16)))
        return tok

    def dma_ind(self, out, in_, idx_ap, reads=(), writes=()):
        q = "pool"
        i = self.drr
        self.drr = (i + 1) % NDMA
        key = ("d", i)
        waits = self._deps(q, reads, writes)
        reads, writes = _bl(reads), _bl(writes)
        prev = self.dcnt[i]
        if prev > 0 and self.waited[q].get(key, 0) < prev:
            self.waited[q][key] = prev
            waits.append((key, prev))
        self.dcnt[i] += 16
        tok = (key, self.dcnt[i])
        for b in reads:
            b.r[key] = max(b.r.get(key, 0), tok[1])
        for b in writes:
            b.w = {tok[0]: tok[1]}
            b.r = {}
        self.lists[q].append((waits, lambda eng: eng.indirect_dma_start(
            out=out, out_offset=None, in_=in_, in_offset=bass.IndirectOffsetOnAxis(ap=idx_ap, axis=0)), (key, 16)))
        return tok

    def final_wait(self, e):
        waits = []
        for i in range(NDMA):
            k, v = ("d", i), self.dcnt[i]
            if v > 0 and self.waited[e].get(k, 0) < v:
                self.waited[e][k] = v
                waits.append((k, v))
        self.lists[e].append((waits, None, None))

    def replay(self, e, eng):
        for waits, fn, inc in self.lists[e]:
            for k, v in waits:
                eng.wait_ge(self.sem[k], v)
            if fn is None:
                continue
            ins = fn(eng)
            if inc is not None:
                ins.then_inc(self.sem[inc[0]], inc[1])
        self.lists[e] = []

    def emit(self, name=None):
        with self.nc.Block(name) as block:
            @block.sync
            def _(eng):
                self.replay("sp", eng)

            @block.scalar
            def _(eng):
                self.replay("act", eng)

            @block.vector
            def _(eng):
                self.replay("dve", eng)

            @block.gpsimd
            def _(eng):
                self.replay("pool", eng)

            @block.tensor
            def _(eng):
                self.replay("pe", eng)


class Rot:
    def __init__(self, P, tiles):
        self.items = [(t, P.buf()) for t in tiles]
        self.i = 0

    def next(self):
        it = self.items[self.i % len(self.items)]
        self.i += 1
        return it


CH_R, CH_K, CH_V, CH_L1, CH_GD, CH_Q, CH_AK, CH_AV, CH_G = 0, 4, 8, 12, 13, 14, 18, 22, 26


def tile_kind(t):
    return "prior" if t < NT_PRIOR else ("own" if t < NT_PRIOR + NT_OWN else "sample")


def phase_A(P, nc, T):
    with contextlib.ExitStack() as st:
        sb = lambda n, s, d: st.enter_context(nc.sbuf_tensor(n, s, d))
        ps = lambda n: st.enter_context(nc.psum_tensor(n, [128, 512], F32))
        win = sb("win", [128, 8, NCOL], BF16)
        b_win = P.buf()
        for kc in range(8):
            for j in range(3):
                P.dma("pool", win[:, kc, j * 1792:(j + 1) * 1792],
                      T["w_in"][kc * 128:(kc + 1) * 128, j * 1792:(j + 1) * 1792], disjoint=[b_win])
        gmix = sb("gmix", [128, D], F32)
        b_gmix = P.buf()
        P.dma("sp", gmix[:], T["g_mix"].broadcast_to([128, D]), writes=[b_gmix])
        mu = sb("mu", [128, 14], F32)
        b_mu = P.buf()
        P.dma("sp", mu[:], T["mu_fm"], writes=[b_mu])
        identb = sb("identb", [128, 128], BF16)
        b_id = P.buf()
        P.op("pool", lambda e: e.memset(identb[:], 0.0), writes=[b_id])
        P.op("pool", lambda e: e.affine_select(out=identb[:], in_=identb[:], pattern=[[-1, 128]],
                                                compare_op=ALU.not_equal, fill=1.0, base=0,
                                                channel_multiplier=1), writes=[b_id])
        carry = sb("carry", [128, 14], F32)
        b_carry = P.buf()
        P.op("pool", lambda e: e.memset(carry[:], 0.0), writes=[b_carry])

        xt_r = Rot(P, [sb("xt%d" % i, [128, D], F32) for i in range(3)])
        xn_r = Rot(P, [sb("xn%d" % i, [128, D], F32) for i in range(2)])
        xnb_r = Rot(P, [sb("xnb%d" % i, [128, D], BF16) for i in range(2)])
        junk = sb("junk", [128, D], BF16)
        b_junk = P.buf()
        ss_r = Rot(P, [sb("ss%d" % i, [128, 4], F32) for i in range(4)])
        xnT_r = Rot(P, [sb("xnT%d" % i, [128, 8, 512], BF16) for i in range(2)])
        pb_r = Rot(P, [sb("pb%d" % i, [128, 520], F32) for i in range(3)])
        prs_r = Rot(P, [sb("prs%d" % i, [128, 512], F32) for i in range(3)])
        dd_r = Rot(P, [sb("dd%d" % i, [128, 512], F32) for i in range(2)])
        stg_r = Rot(P, [sb("stg%d" % i, [128, 512], BF16) for i in range(4)])
        tmf_r = Rot(P, [sb("tmf%d" % i, [128, 512], F32) for i in range(3)])
        tmb_r = Rot(P, [sb("tmb%d" % i, [128, 512], BF16) for i in range(2)])
        ps_fm = Rot(P, [ps("psfm%d" % i) for i in range(4)])
        ps_tm = Rot(P, [ps("pstm%d" % i) for i in range(2)])
        ps_tr = Rot(P, [ps("pstr%d" % i) for i in range(2)])

        ssin = sb("ssin", [16, D], F32)
        ssb = sb("ssb", [16, D], BF16)
        ssT = sb("ssT", [128, 8, 16], BF16)
        b_ssin, b_ssb, b_ssT = P.buf(), P.buf(), P.buf()
        P.dma("sp", ssin[:], T["st_shift"], writes=[b_ssin])
        P.op("dve", lambda e: e.tensor_copy(out=ssb[:], in_=ssin[:]), reads=[b_ssin], writes=[b_ssb])
        pt, b_pt = ps_tr.next()
        ptb = pt[:].bitcast(BF16)
        for kc in range(8):
            P.op("pe", lambda e, kc=kc, ptb=ptb: e.transpose(out=ptb[:, kc * 16:(kc + 1) * 16], in_=ssb[:, kc * 128:(kc + 1) * 128],
                                                    identity=identb[0:16, 0:16]),
                 reads=[b_ssb, b_id], writes=[b_pt] if kc == 0 else (), disjoint=[b_pt] if kc else ())
        P.op("act", lambda e, ptb=ptb: e.copy(out=ssT[:].rearrange("p k b -> p (k b)"), in_=ptb[:, 0:128]), reads=[b_pt], writes=[b_ssT])
        prevb = sb("prevb", [128, 128], F32)
        b_prevb = P.buf()
        pss = sb("pss", [128, 16], F32)
        b_pss = P.buf()

        groups = [(g * 4, 4) for g in range(8)] + [(32, 1)]
        if DEBUG_GROUPS is not None:
            groups = [groups[i] for i in DEBUG_GROUPS]
        for (t0, nt) in groups:
            kind = tile_kind(t0)
            G = nt * 128
            tok0 = t0 * 128
            xnT, b_xnT = xnT_r.next()
            for ti in range(nt):
                t = t0 + ti
                xt, b_xt = xt_r.next()
                P.dma("sp", xt[:], T["x_all"][t * 128:(t + 1) * 128, :], writes=[b_xt])
                ss, b_ss = ss_r.next()
                P.op("act", lambda e, xt=xt, ss=ss: e.activation(out=junk[:], in_=xt[:], func=AF.Square, accum_out=ss[:, 0:1]),
                     reads=[b_xt], writes=[b_junk, b_ss])
                P.op("act", lambda e, ss=ss: e.activation(out=ss[:, 1:2], in_=ss[:, 0:1], func=AF.Sqrt, scale=1.0 / D, bias=T["cst"].t[:, 0:1]),
                     reads=[b_ss, T["cst"]], writes=[b_ss])
                P.op("dve", lambda e, ss=ss: e.reciprocal(out=ss[:, 2:3], in_=ss[:, 1:2]), reads=[b_ss], writes=[b_ss])
                xn, b_xn = xn_r.next()
                P.op("dve", lambda e, xn=xn, xt=xt, ss=ss: e.scalar_tensor_tensor(out=xn[:], in0=xt[:], scalar=ss[:, 2:3], in1=gmix[:],
                                                                                    op0=ALU.mult, op1=ALU.mult),
                     reads=[b_xt, b_ss, b_gmix], writes=[b_xn])
                if t == NT_PRIOR + NT_OWN - 1:
                    P.dma("sp", T["shift_out"][0:1, :], xn[127:128, :], reads=[b_xn])
                if kind == "sample":
                    for b in range(16):
                        P.dma("sp", T["shift_out"][1 + b:2 + b, :], xn[b * 8 + 7:b * 8 + 8, :], reads=[b_xn])
                xnb, b_xnb = xnb_r.next()
                P.op("act", lambda e, xnb=xnb, xn=xn: e.copy(out=xnb[:], in_=xn[:]), reads=[b_xn], writes=[b_xnb])
                pt, b_pt = ps_tr.next()
                ptb = pt[:].bitcast(BF16)
                for kc in range(8):
                    P.op("pe", lambda e, kc=kc, ptb=ptb, xnb=xnb: e.transpose(out=ptb[:, kc * 128:(kc + 1) * 128],
                                                                                in_=xnb[:, kc * 128:(kc + 1) * 128], identity=identb[:]),
                         reads=[b_xnb, b_id], writes=[b_pt] if kc == 0 else (), disjoint=[b_pt] if kc else ())
                P.op("dve", lambda e, ptb=ptb, xnT=xnT, ti=ti: e.tensor_copy(out=xnT[:, :, ti * 128:(ti + 1) * 128],
                                                                               in_=ptb.rearrange("p (k t) -> p k t", k=8)),
                     reads=[b_pt], disjoint=[b_xnT], writes=())
            if kind == "prior" and t0 + nt == NT_PRIOR:
                chunks = list(range(0, 14)) + list(range(CH_AK, CH_AK + 4))
            elif kind == "prior":
                chunks = list(range(CH_K, CH_K + 4)) + list(range(CH_V, CH_V + 4)) + [CH_L1] + list(range(CH_AK, CH_AK + 4))
            else:
                chunks = list(range(0, 14)) + list(range(CH_Q, CH_Q + 8)) + list(range(CH_G, CH_G + 16))
                if DEBUG_CHUNKS is not None:
                    chunks = DEBUG_CHUNKS
            for c in chunks:
                pf, b_pf = ps_fm.next()
                for kc in range(8):
                    P.op("pe", lambda e, pf=pf, kc=kc, c=c, xnT=xnT, G=G: e.matmul(pf[:, 0:G], lhsT=win[:, kc, c * 128:(c + 1) * 128],
                                                                                  rhs=xnT[:, kc, 0:G], start=(kc == 0), stop=(kc == 7)),
                         reads=[b_win, b_xnT], writes=[b_pf] if kc == 0 else (), disjoint=[b_pf] if kc else (), signal=(kc == 7))
                if c < 14:
                    if kind != "sample":
                        pb, b_pb = pb_r.next()
                        P.op("act", lambda e, pb=pb, pf=pf, G=G: e.copy(out=pb[:, 1:1 + G], in_=pf[:, 0:G]), reads=[b_pf], writes=[b_pb])
                        P.op("pool", lambda e, pb=pb, c=c: e.tensor_copy(out=pb[:, 0:1], in_=carry[:, c:c + 1]), reads=[b_carry], disjoint=[b_pb])
                        P.op("pool", lambda e, pb=pb, c=c, G=G: e.tensor_copy(out=carry[:, c:c + 1], in_=pb[:, G:G + 1]), reads=[b_pb], disjoint=[b_carry])
                        cur = pb[:, 1:1 + G]
                        prev = pb[:, 0:G]
                        rd = [b_pb]
                    else:
                        pb, b_pb = pb_r.next()
                        P.op("act", lambda e, pb=pb, pf=pf, G=G: e.copy(out=pb[:, 0:G], in_=pf[:, 0:G]), reads=[b_pf], writes=[b_pb])
                        pf2, b_pf2 = ps_fm.next()
                        for kc in range(8):
                            P.op("pe", lambda e, pf2=pf2, kc=kc, c=c: e.matmul(pf2[:, 0:16], lhsT=win[:, kc, c * 128:(c + 1) * 128],
                                                                                rhs=ssT[:, kc, :], start=(kc == 0), stop=(kc == 7)),
                                 reads=[b_win, b_ssT], writes=[b_pf2] if kc == 0 else (), disjoint=[b_pf2] if kc else (), signal=(kc == 7))
                        P.op("dve", lambda e, pb=pb: e.tensor_copy(out=prevb[:].rearrange("p (b t) -> p b t", t=8)[:, :, 1:8],
                                                                    in_=pb[:, 0:128].rearrange("p (b t) -> p b t", t=8)[:, :, 0:7]),
                             reads=[b_pb], writes=[b_prevb])
                        P.op("dve", lambda e, pf2=pf2: e.tensor_copy(out=prevb[:].rearrange("p (b t) -> p b t", t=8)[:, :, 0:1],
                                                                      in_=pf2[:, 0:16].rearrange("p (b o) -> p b o", o=1)),
                             reads=[b_pf2], disjoint=[b_prevb])
                        cur = pb[:, 0:G]
                        prev = prevb[:, 0:G]
                        rd = [b_pb, b_prevb]
                    dd, b_dd = dd_r.next()
                    P.op("dve", lambda e, dd=dd, prev=prev, cur=cur, G=G: e.tensor_tensor(out=dd[:, 0:G], in0=prev, in1=cur, op=ALU.subtract),
                         reads=rd, writes=[b_dd])
                    prs, b_prs = prs_r.next()
                    P.op("dve", lambda e, prs=prs, dd=dd, cur=cur, c=c, G=G: e.scalar_tensor_tensor(out=prs[:, 0:G], in0=dd[:, 0:G], scalar=mu[:, c:c + 1],
                                                                                                    in1=cur, op0=ALU.mult, op1=ALU.add),
                         reads=rd + [b_dd, b_mu], writes=[b_prs])
                    P.dma("sp", T["prT_d"][c][:, tok0:tok0 + G], prs[:, 0:G], reads=[b_prs], disjoint=[T["b_prT"]])
                elif c < CH_G:
                    stg, b_stg = stg_r.next()
                    P.op("act", lambda e, stg=stg, pf=pf, G=G: e.copy(out=stg[:, 0:G], in_=pf[:, 0:G]), reads=[b_pf], writes=[b_stg])
                    if c < CH_AK:
                        q0 = tok0 - NT_PRIOR * 128
                        P.dma("sp", T["qT_d"][c - CH_Q][:, q0:q0 + G], stg[:, 0:G], reads=[b_stg], disjoint=[T["b_qT"]])
                    else:
                        P.dma("sp", T["kT_d"][c - CH_AK][:, tok0:tok0 + G], stg[:, 0:G], reads=[b_stg], disjoint=[T["b_kT"]])
                else:
                    stg, b_stg = stg_r.next()
                    P.op("act", lambda e, stg=stg, pf=pf, G=G: e.activation(out=stg[:, 0:G], in_=pf[:, 0:G], func=AF.Sigmoid),
                         reads=[b_pf], writes=[b_stg])
                    q0 = tok0 - NT_PRIOR * 128
                    P.dma("sp", T["gates_d"][c - CH_G][:, q0:q0 + G], stg[:, 0:G], reads=[b_stg], disjoint=[T["b_gates"]])
            for ti in range(nt if not DEBUG_NOTM else 0):
                t = t0 + ti
                halves = [("v", CH_AV * 128)] + ([] if kind == "prior" else [("k", CH_AK * 128)])
                if DEBUG_HALF is not None:
                    halves = [h for h in halves if h[0] in DEBUG_HALF]
                for (nm, c0) in halves:
                    pm, b_pm = ps_tm.next()
                    for kc in range(8):
                        P.op("pe", lambda e, pm=pm, kc=kc, c0=c0, xnT=xnT, ti=ti: e.matmul(pm[:, :], lhsT=xnT[:, kc, ti * 128:(ti + 1) * 128],
                                                                                         rhs=win[:, kc, c0:c0 + 512], start=(kc == 0), stop=(kc == 7)),
                             reads=[b_win, b_xnT], writes=[b_pm] if kc == 0 else (), disjoint=[b_pm] if kc else (), signal=(kc == 7))
                    tmf, b_tmf = tmf_r.next()
                    P.op("act", lambda e, tmf=tmf, pm=pm: e.copy(out=tmf[:], in_=pm[:]), reads=[b_pm], writes=[b_tmf])
                    if kind != "prior":
                        q0 = (t - NT_PRIOR) * 128
                        if not DEBUG_NOOUT:
                            P.dma("sp", T[nm + "out"][q0:q0 + 128, :], tmf[:], reads=[b_tmf])
                    if nm == "v":
                        tmb, b_tmb = tmb_r.next()
                        P.op("pool", lambda e, tmb=tmb, tmf=tmf: e.tensor_copy(out=tmb[:], in_=tmf[:]), reads=[b_tmf], writes=[b_tmb])
                        P.dma("sp", T["v_d"][t * 128:(t + 1) * 128, :], tmb[:], reads=[b_tmb], disjoint=[T["b_v"]])
        P.emit("phaseA")


def phase_B1(P, nc, T):
    V, A, PO, PE = "dve", "act", "pool", "pe"
    with contextlib.ExitStack() as st:
        def mk(n, s, d):
            return Tl(st.enter_context(nc.sbuf_tensor("b1_" + n, s, d)), P.buf(n))

        def mkps(n, cols=512):
            return Tl(st.enter_context(nc.psum_tensor("b1_" + n, [128, cols], F32)), P.buf(n))

        def op(e, fn, r=(), w=(), **kw):
            return P.op(e, fn, reads=r, writes=w, **kw)

        cst = T["cst"]
        identb = mk("identb", [128, 128], BF16)
        identf = mk("identf", [128, 128], F32)
        P.dma("pool", identb.t[:], T["ident"], writes=[identb])
        P.dma("sp", identf.t[:], T["ident"], writes=[identf])
        MTc = {"p": mk("MTc_p", [128, 256], BF16), "s": mk("MTc_s", [128, 256], BF16)}
        Ms = {"p": mk("Ms_p", [128, 128], BF16), "s": mk("Ms_s", [128, 128], BF16)}
        RM = {"p": mk("RM_p", [128, 512], F32), "s": mk("RM_s", [128, 512], F32)}
        for k in "ps":
            P.dma("pool", MTc[k].t[:], T["mtc_" + k], writes=[MTc[k]])
            P.dma("pool", Ms[k].t[:], T["ms_" + k], writes=[Ms[k]])
            P.dma("sp", RM[k].t[:], T["rm_" + k], writes=[RM[k]])
        bones = mk("bones", [128, 128], BF16)
        P.dma("pool", bones.t[:], T["bones"], writes=[bones])
        sel8 = mk("sel8", [128, 4, 8], BF16)
        P.dma("pool", sel8.t[:].rearrange("p a b -> p (a b)"), T["sel8"], writes=[sel8])
        segsel = mk("segsel", [128, 16], F32)
        P.dma("sp", segsel.t[:], T["segsel"], writes=[segsel])
        LW = mk("LW", [128, 512], BF16)
        P.dma("pool", LW.t[0:64, :], T["w2"], disjoint=[LW])
        P.dma("pool", LW.t[64:128, :], T["a2"], disjoint=[LW])
        G2 = mk("G2", [128, 512], BF16)
        P.dma("pool", G2.t[:], T["g2"], writes=[G2])
        fm = {}
        for n in ["w0", "a0", "kk", "ka", "rk"]:
            fm[n] = mk("fm_" + n, [128, 4], F32)
            P.dma("sp", fm[n].t[:], T[n + "_fm"], writes=[fm[n]])
        lnw = mk("lnw", [128, 512], F32)
        lnb = mk("lnb", [128, 512], F32)
        P.dma("sp", lnw.t[:], T["ln_w"].broadcast_to([128, 512]), writes=[lnw])
        P.dma("sp", lnb.t[:], T["ln_b"].broadcast_to([128, 512]), writes=[lnb])

        def bc4(x):
            return x.t[:, :].unsqueeze(2).broadcast_to([128, 4, 128])

        PRr = Rot(P, [st.enter_context(nc.sbuf_tensor("b1_PR%d" % i, [128, 14, 128], F32)) for i in range(2)])
        LB = mk("LB", [128, 128], BF16)
        GS = mk("GS", [128, 128], BF16)
        f4 = lambda n: mk(n, [128, 4, 128], F32)
        b4 = lambda n: mk(n, [128, 4, 128], BF16)
        XW, LD, ASIG, CUM, CUMp, EW, EWi, EWp = [f4(n) for n in ["XW", "LD", "ASIG", "CUM", "CUMp", "EW", "EWi", "EWp"]]
        KK, SQ, RNr, KKN, T1, KM, TMPB, XA = [f4(n) for n in ["KK", "SQ", "RNr", "KKN", "T1", "KM", "TMPB", "XA"]]
        KK2, BT, KT, VB, PRD = [b4(n) for n in ["KK2", "BT", "KT", "VB", "PRD"]]
        ART = mk("ART", [128, 4, 2, 128], BF16)
        Atm, Btm, Ktm, Vtm = [mk(n, [128, 512], BF16) for n in ["Atm", "Btm", "Ktm", "Vtm"]]
        COEF = mk("COEF", [128, 8], F32)
        Gtm = mk("Gtm", [128, 512], F32)
        NAXr = Rot(P, [st.enter_context(nc.sbuf_tensor("b1_NAX%d" % i, [128, 256], BF16)) for i in range(10)])
        MAr = Rot(P, [st.enter_context(nc.sbuf_tensor("b1_MA%d" % i, [128, 256], BF16)) for i in range(10)])
        PXr = Rot(P, [st.enter_context(nc.sbuf_tensor("b1_PX%d" % i, [128, 256], BF16)) for i in range(20)])
        Pr_ = Rot(P, [st.enter_context(nc.sbuf_tensor("b1_Pk%d" % i, [128, 128], BF16)) for i in range(20)])
        TTr = Rot(P, [st.enter_context(nc.sbuf_tensor("b1_TT%d" % i, [128, 128], BF16)) for i in range(10)])
        MVr = Rot(P, [st.enter_context(nc.sbuf_tensor("b1_MV%d" % i, [128, 64], BF16)) for i in range(10)])
        UVr = Rot(P, [st.enter_context(nc.sbuf_tensor("b1_UV%d" % i, [128, 64], F32)) for i in range(10)])
        Usbr = Rot(P, [st.enter_context(nc.sbuf_tensor("b1_Usb%d" % i, [128, 64], BF16)) for i in range(10)])
        AhT = st.enter_context(nc.sbuf_tensor("b1_AhT", [128, 4, 128], BF16))
        b_AhT = [P.buf() for _ in range(8)]
        S0 = st.enter_context(nc.sbuf_tensor("b1_S0", [128, 4, 64], F32))
        S0b = st.enter_context(nc.sbuf_tensor("b1_S0b", [128, 4, 64], BF16))
        S0W = st.enter_context(nc.sbuf_tensor("b1_S0W", [128, 4, 64], F32))
        b_S = [P.buf() for _ in range(8)]
        b_Sb = [P.buf() for _ in range(8)]
        b_SW = P.buf()
        op(PO, lambda e: e.memset(S0[:], 0.0), w=b_S)
        op(PO, lambda e: e.memset(S0b[:], 0.0), w=b_Sb)
        Ytm = mk("Ytm", [128, 512], F32)
        O1, O2, O3 = [mk(n, [128, 512], F32) for n in ["O1", "O2", "O3"]]
        ST8 = mk("ST8", [128, 32], F32)
        OAb = mk("OAb", [128, 512], BF16)
        OAT = mk("OAT", [128, 4, 128], BF16)
        S0s = st.enter_context(nc.sbuf_tensor("b1_S0s", [128, 4, 16, 64], F32))
        S0sb = st.enter_context(nc.sbuf_tensor("b1_S0sb", [128, 4, 16, 64], BF16))
        b_S0s = [P.buf() for _ in range(4)]
        b_S0sb = [P.buf() for _ in range(4)]
        Zin = mk("Zin", [64, 16, 128], F32)
        TMPs = mk("TMPs", [128, 16, 64], F32)
        Y1 = mk("Y1", [128, 64], F32)
        U1 = mk("U1", [128, 64], F32)
        UBLK = mk("UBLK", [128, 16, 64], BF16)
        VBLK = mk("VBLK", [128, 16, 64], BF16)
        SOUT = mk("SOUT", [64, 16, 128], F32)
        psr = Rot(P, [st.enter_context(nc.psum_tensor("b1_psr%d" % i, [128, 512], F32)) for i in range(5)])
        ps2 = Rot(P, [st.enter_context(nc.psum_tensor("b1_ps2_%d" % i, [128, 1024], F32)) for i in range(1)])
        psY = mkps("psY")

        def nextps():
            t, b = psr.next()
            return Tl(t, b)

        for hp in range(4):
            for h2 in range(2):
                P.dma("sp", Zin.t[:, :, h2 * 64:(h2 + 1) * 64],
                      T["st_rwkv"][:, 2 * hp + h2, :, :].rearrange("b i j -> i b j"), writes=[Zin] if h2 == 0 else (), disjoint=[Zin] if h2 else ())
            for g in range(2):
                pt = nextps()
                for k in range(8):
                    b = g * 8 + k
                    op(PE, lambda e, pt=pt, k=k, b=b: e.transpose(out=pt.t[:, k * 64:(k + 1) * 64], in_=Zin.t[:, b, :], identity=identf.t[0:64, 0:64]),
                       r=[Zin, identf], w=[pt] if k == 0 else (), disjoint=[pt] if k else ())
                op(A, lambda e, pt=pt, hp=hp, g=g: e.copy(out=S0s[:, hp, g * 8:(g + 1) * 8, :], in_=pt.t[:, 0:512].rearrange("p (b i) -> p b i", i=64)),
                   xr=[pt], disjoint=[b_S0s[hp]])
            op(PO, lambda e, hp=hp: e.tensor_copy(out=S0sb[:, hp, :, :], in_=S0s[:, hp, :, :]), r=[b_S0s[hp]], w=[b_S0sb[hp]])

        def rwkv_tile(t):
            kind = tile_kind(t)
            mk_ = "s" if kind == "sample" else "p"
            full = kind != "prior"
            tok0 = t * 128
            q0 = tok0 - NT_PRIOR * 128
            PRt, b_PR = PRr.next()
            PR = Tl(PRt, b_PR)
            c0 = 0 if full else 4
            c1 = 14 if full else 13
            P.dma("sp", PRt[:, c0:c1, :], T["prT_d"].rearrange("c p t -> p c t")[:, c0:c1, tok0:tok0 + 128],
                  reads=[T["b_prT"]], writes=[PR])
            Rv, Kv, Vv = PRt[:, 0:4, :], PRt[:, 4:8, :], PRt[:, 8:12, :]
            op(A, lambda e: e.activation(out=LB.t[0:64, :], in_=PRt[0:64, 12, :], func=AF.Tanh), r=[PR], w=[LB])
            op(A, lambda e: e.copy(out=LB.t[64:128, :], in_=PRt[64:128, 12, :]), r=[PR], disjoint=[LB])
            pW = nextps()
            pA = nextps()
            for hp in range(4):
                op(PE, lambda e, hp=hp: e.matmul(pW.t[:, hp * 128:(hp + 1) * 128], lhsT=LW.t[0:64, hp * 128:(hp + 1) * 128], rhs=LB.t[0:64, :], start=True, stop=True),
                   r=[LW, LB], w=[pW] if hp == 0 else (), disjoint=[pW] if hp else ())
            for hp in range(4):
                op(PE, lambda e, hp=hp: e.matmul(pA.t[:, hp * 128:(hp + 1) * 128], lhsT=LW.t[64:128, hp * 128:(hp + 1) * 128], rhs=LB.t[64:128, :], start=True, stop=True),
                   r=[LW, LB], w=[pA] if hp == 0 else (), disjoint=[pA] if hp else ())
            v4 = lambda x: x.t[:, 0:512].rearrange("p (a b) -> p a b", a=4) if x.t.shape[-1] == 512 else x.t[:]
            op(V, lambda e: e.tensor_tensor(out=XW.t[:], in0=v4(pW), in1=bc4(fm["w0"]), op=ALU.add), xr=[pW], r=[fm["w0"]], w=[XW])
            op(A, lambda e: e.activation(out=XW.t[:], in_=XW.t[:], func=AF.Sigmoid), w=[XW])
            op(V, lambda e: e.tensor_scalar(out=LD.t[:], in0=XW.t[:], scalar1=-0.6065306597126334, scalar2=None, op0=ALU.mult), r=[XW], w=[LD])
            op(V, lambda e: e.tensor_tensor(out=XA.t[:], in0=v4(pA), in1=bc4(fm["a0"]), op=ALU.add), xr=[pA], r=[fm["a0"]], w=[XA])
            op(A, lambda e: e.activation(out=ASIG.t[:], in_=XA.t[:], func=AF.Sigmoid), r=[XA], w=[ASIG])
            fl = lambda x: x.t[:].rearrange("p a b -> p (a b)")
            op(V, lambda e: e.tensor_tensor_scan(out=fl(CUM), data0=RM[mk_].t[:], data1=fl(LD), initial=0.0, op0=ALU.mult, op1=ALU.add),
               r=[RM[mk_], LD], w=[CUM])
            op(V, lambda e: e.tensor_tensor(out=CUMp.t[:], in0=CUM.t[:], in1=LD.t[:], op=ALU.subtract), r=[CUM, LD], w=[CUMp])
            op(A, lambda e: e.activation(out=EW.t[:], in_=CUM.t[:], func=AF.Exp), r=[CUM], w=[EW])
            op(A, lambda e: e.activation(out=EWi.t[:], in_=CUM.t[:], func=AF.Exp, scale=-1.0), r=[CUM], w=[EWi])
            op(A, lambda e: e.activation(out=EWp.t[:], in_=CUMp.t[:], func=AF.Exp), r=[CUMp], w=[EWp])
            op(V, lambda e: e.tensor_tensor(out=KK.t[:], in0=Kv, in1=bc4(fm["kk"]), op=ALU.mult), r=[PR, fm["kk"]], w=[KK])
            op(V, lambda e: e.tensor_tensor(out=KK2.t[:], in0=KK.t[:], in1=KK.t[:], op=ALU.mult), r=[KK], w=[KK2])
            pN = nextps()
            op(PE, lambda e: e.matmul(pN.t[:, 0:512], lhsT=bones.t[:], rhs=fl(KK2), start=True, stop=True), r=[bones, KK2], w=[pN])
            op(A, lambda e: e.activation(out=SQ.t[:], in_=v4(pN), func=AF.Sqrt, bias=cst.t[:, 1:2]), xr=[pN], r=[cst], w=[SQ])
            op(V, lambda e: e.reciprocal(out=RNr.t[:], in_=SQ.t[:]), r=[SQ], w=[RNr])
            op(V, lambda e: e.tensor_tensor(out=KKN.t[:], in0=KK.t[:], in1=RNr.t[:], op=ALU.mult), r=[KK, RNr], w=[KKN])
            op(V, lambda e: e.scalar_tensor_tensor(out=T1.t[:], in0=ASIG.t[:], scalar=-1.0, in1=bc4(fm["ka"]), op0=ALU.add, op1=ALU.mult),
               r=[ASIG, fm["ka"]], w=[T1])
            op(V, lambda e: e.scalar_tensor_tensor(out=KM.t[:], in0=T1.t[:], scalar=1.0, in1=Kv, op0=ALU.add, op1=ALU.mult), r=[T1, PR], w=[KM])
            op(V, lambda e: e.scalar_tensor_tensor(out=ART.t[:, :, 1, :], in0=KKN.t[:], scalar=-1.0, in1=EWp.t[:], op0=ALU.mult, op1=ALU.mult),
               r=[KKN, EWp], disjoint=[ART])
            if full:
                op(V, lambda e: e.tensor_tensor(out=ART.t[:, :, 0, :], in0=Rv, in1=EW.t[:], op=ALU.mult), r=[PR, EW], disjoint=[ART])
            op(V, lambda e: e.tensor_tensor(out=TMPB.t[:], in0=KKN.t[:], in1=ASIG.t[:], op=ALU.mult), r=[KKN, ASIG], w=[TMPB])
            op(V, lambda e: e.tensor_tensor(out=BT.t[:], in0=TMPB.t[:], in1=EWi.t[:], op=ALU.mult), r=[TMPB, EWi], w=[BT])
            op(V, lambda e: e.tensor_tensor(out=KT.t[:], in0=KM.t[:], in1=EWi.t[:], op=ALU.mult), r=[KM, EWi], w=[KT])
            op(A, lambda e: e.copy(out=VB.t[:], in_=Vv), r=[PR], w=[VB])
            for (src_fn, dst, rd) in [(lambda hp: ART.t[:, hp, 1, :], Atm, ART), (lambda hp: BT.t[:, hp, :], Btm, BT),
                                      (lambda hp: KT.t[:, hp, :], Ktm, KT), (lambda hp: VB.t[:, hp, :], Vtm, VB)]:
                pt = nextps()
                ptb = pt.t[:].bitcast(BF16)
                for hp in range(4):
                    op(PE, lambda e, hp=hp, ptb=ptb, src_fn=src_fn: e.transpose(out=ptb[:, hp * 128:(hp + 1) * 128], in_=src_fn(hp), identity=identb.t[:]),
                       r=[rd, identb], w=[pt] if hp == 0 else (), disjoint=[pt] if hp else ())
                op(A, lambda e, ptb=ptb, dst=dst: e.copy(out=dst.t[:], in_=ptb[:, 0:512]), xr=[pt], w=[dst])
            if full:
                op(A, lambda e: e.activation(out=GS.t[:], in_=PRt[:, 13, :], func=AF.Sigmoid), r=[PR], w=[GS])
                psG = nextps()
                op(PE, lambda e: e.matmul(psG.t[:, 0:512], lhsT=GS.t[:], rhs=G2.t[:], start=True, stop=True), r=[GS, G2], w=[psG])
                op(A, lambda e: e.copy(out=Gtm.t[:], in_=psG.t[:, 0:512]), xr=[psG], w=[Gtm])
                op(V, lambda e: e.tensor_tensor(out=TMPB.t[:], in0=Rv, in1=bc4(fm["rk"]), op=ALU.mult), r=[PR, fm["rk"]], w=[TMPB])
                op(V, lambda e: e.tensor_tensor(out=PRD.t[:], in0=TMPB.t[:], in1=KM.t[:], op=ALU.mult), r=[TMPB, KM], w=[PRD])
                pC = nextps()
                for hp in range(4):
                    op(PE, lambda e, hp=hp: e.matmul(pC.t[:, 0:8], lhsT=PRD.t[:, hp, :], rhs=sel8.t[:, hp, :], start=(hp == 0), stop=(hp == 3)),
                       r=[PRD, sel8], w=[pC] if hp == 0 else (), disjoint=[pC] if hp else (), signal=(hp == 3))
                op(A, lambda e: e.copy(out=COEF.t[:], in_=pC.t[:, 0:8]), xr=[pC], w=[COEF])
            if kind != "sample":
                op(V, lambda e: e.tensor_tensor(out=S0W[:], in0=S0[:], in1=EW.t[:, :, 127:128].broadcast_to([128, 4, 64]), op=ALU.mult),
                   r=b_S + [EW], w=[b_SW])
            nsq = 2 if kind == "sample" else 6
            def head_gen(h):
                hp, h2 = divmod(h, 2)
                base = 64 * h2
                hc = slice(h * 64, (h + 1) * 64)
                NAXt, b_NAX = NAXr.next()
                MAt, b_MA = MAr.next()
                p1 = nextps()
                op(PE, lambda e, p1=p1, hp=hp, base=base: e.matmul(p1.t[:, 0:256], lhsT=BT.t[base:base + 64, hp, :],
                                                                  rhs=ART.t[base:base + 64, hp, :, :].rearrange("p a b -> p (a b)"), start=True, stop=True),
                   r=[BT, ART], w=[p1])
                op(V, lambda e, p1=p1, NAXt=NAXt: e.tensor_tensor(out=NAXt[:], in0=p1.t[:, 0:256], in1=MTc[mk_].t[:], op=ALU.mult),
                   xr=[p1], r=[MTc[mk_]], w=[b_NAX])
                p2 = nextps()
                op(PE, lambda e, p2=p2, hp=hp, base=base: e.matmul(p2.t[:, 0:256], lhsT=KT.t[base:base + 64, hp, :],
                                                                  rhs=ART.t[base:base + 64, hp, :, :].rearrange("p a b -> p (a b)"), start=True, stop=True),
                   r=[KT, ART], w=[p2])
                op(V, lambda e, p2=p2, MAt=MAt: e.tensor_tensor(out=MAt[:], in0=p2.t[:, 0:256], in1=MTc[mk_].t[:], op=ALU.mult),
                   xr=[p2], r=[MTc[mk_]], w=[b_MA])
                p3 = nextps()
                op(PE, lambda e, p3=p3, hp=hp, base=base: e.matmul(p3.t[:, 0:128], lhsT=ART.t[base:base + 64, hp, 1, :], rhs=BT.t[base:base + 64, hp, :], start=True, stop=True),
                   r=[ART, BT], w=[p3])
                Pk, b_Pk = Pr_.next()
                op(V, lambda e, p3=p3, Pk=Pk: e.tensor_tensor(out=Pk[:], in0=p3.t[:, 0:128], in1=Ms[mk_].t[:], op=ALU.mult), xr=[p3], r=[Ms[mk_]], w=[b_Pk])
                yield
                PX, b_PX = PXr.next()
                op(PO, lambda e, PX=PX, NAXt=NAXt: e.tensor_copy(out=PX[:, 0:128], in_=NAXt[:, 128:256]), r=[b_NAX], w=[b_PX])
                op(PO, lambda e, PX=PX, NAXt=NAXt: e.tensor_tensor(out=PX[:, 128:256], in0=NAXt[:, 128:256], in1=identb.t[:], op=ALU.add),
                   r=[b_NAX, identb], disjoint=[b_PX])
                for k in range(nsq + 1):
                    last = (k == nsq)
                    if not last:
                        PXn, b_PXn = PXr.next()
                        Pkn, b_Pkn = Pr_.next()
                        pd1 = nextps()
                        op(PE, lambda e, pd1=pd1, Pk=Pk, PX=PX: e.matmul(pd1.t[:, 0:128], lhsT=Pk[:], rhs=PX[:, 0:128], start=True, stop=True),
                           r=[b_Pk, b_PX], w=[pd1])
                        op(A, lambda e, pd1=pd1, PXn=PXn: e.copy(out=PXn[:, 0:128], in_=pd1.t[:, 0:128]), xr=[pd1], w=[b_PXn])
                        pe1 = nextps()
                        op(PE, lambda e, pe1=pe1, Pk=Pk, PX=PX: e.matmul(pe1.t[:, 0:128], lhsT=PX[:, 0:128], rhs=Pk[:], start=True, stop=True),
                           r=[b_Pk, b_PX], w=[pe1])
                        op(A, lambda e, pe1=pe1, Pkn=Pkn: e.copy(out=Pkn[:], in_=pe1.t[:, 0:128]), xr=[pe1], w=[b_Pkn])
                    if k >= 1:
                        pd2 = nextps()
                        op(PE, lambda e, pd2=pd2, Pk=Pk, PX=PX: e.matmul(pd2.t[:, 0:128], lhsT=Pk[:], rhs=PX[:, 128:256], start=True, stop=True),
                           r=[b_Pk, b_PX], w=[pd2])
                        if last:
                            TTt, b_TT = TTr.next()
                            op(V, lambda e, pd2=pd2, PX=PX, TTt=TTt: e.tensor_tensor(out=TTt[:], in0=pd2.t[:, 0:128], in1=PX[:, 128:256], op=ALU.add),
                               xr=[pd2], r=[b_PX], w=[b_TT])
                        else:
                            op(V, lambda e, pd2=pd2, PX=PX, PXn=PXn: e.tensor_tensor(out=PXn[:, 128:256], in0=pd2.t[:, 0:128], in1=PX[:, 128:256], op=ALU.add),
                               xr=[pd2], r=[b_PX], disjoint=[b_PXn])
                    elif not last:
                        op(PO, lambda e, PX=PX, PXn=PXn: e.tensor_copy(out=PXn[:, 128:256], in_=PX[:, 128:256]), r=[b_PX], disjoint=[b_PXn])
                    if not last:
                        PX, b_PX, Pk, b_Pk = PXn, b_PXn, Pkn, b_Pkn
                    yield
                MVt, b_MV = MVr.next()
                UVt, b_UV = UVr.next()
                pm = nextps()
                op(PE, lambda e, pm=pm, MAt=MAt, hc=hc: e.matmul(pm.t[:, 0:64], lhsT=MAt[:, 128:256], rhs=Vtm.t[:, hc], start=True, stop=True),
                   r=[b_MA, Vtm], w=[pm])
                op(A, lambda e, pm=pm, MVt=MVt: e.copy(out=MVt[:], in_=pm.t[:, 0:64]), xr=[pm], w=[b_MV])
                yield
                pu = nextps()
                op(PE, lambda e, pu=pu, TTt=TTt, MVt=MVt: e.matmul(pu.t[:, 0:64], lhsT=TTt[:], rhs=MVt[:], start=True, stop=True), r=[b_TT, b_MV], w=[pu])
                op(A, lambda e, pu=pu, UVt=UVt: e.copy(out=UVt[:], in_=pu.t[:, 0:64]), xr=[pu], w=[b_UV])
                pa = nextps()
                op(PE, lambda e, pa=pa, TTt=TTt, hc=hc, base=base: e.matmul(pa.t[base:base + 64, 0:128], lhsT=Atm.t[:, hc], rhs=TTt[:], start=True, stop=True),
                   r=[Atm, b_TT], w=[pa])
                op(A, lambda e, pa=pa, hp=hp, base=base: e.copy(out=AhT[base:base + 64, hp, :], in_=pa.t[base:base + 64, 0:128]), xr=[pa], w=[b_AhT[h]])
                yield
                Usb, b_Usb = Usbr.next()
                if kind != "sample":
                    pU = nextps()
                    op(PE, lambda e, pU=pU, hp=hp, base=base: e.matmul(pU.t[:, 0:64], lhsT=AhT[base:base + 64, hp, :], rhs=S0b[base:base + 64, hp, :], start=True, stop=True),
                       r=[b_AhT[h], b_Sb[h]], w=[pU])
                    op(V, lambda e, pU=pU, Usb=Usb, UVt=UVt: e.tensor_tensor(out=Usb[:], in0=pU.t[:, 0:64], in1=UVt[:], op=ALU.add), xr=[pU], r=[b_UV], w=[b_Usb])
                    yield
                    if full:
                        first = (h == 0)
                        op(PE, lambda e, hp=hp, base=base, hc=hc: e.matmul(psY.t[:, hc], lhsT=ART.t[base:base + 64, hp, 0, :], rhs=S0b[base:base + 64, hp, :], start=True, stop=False),
                           r=[ART, b_Sb[h]], w=[psY] if first else (), disjoint=() if first else [psY], signal=False)
                        op(PE, lambda e, hc=hc, NAXt=NAXt, Usb=Usb: e.matmul(psY.t[:, hc], lhsT=NAXt[:, 0:128], rhs=Usb[:], start=False, stop=False),
                           r=[b_NAX, b_Usb], disjoint=[psY], signal=False)
                        op(PE, lambda e, hc=hc, MAt=MAt: e.matmul(psY.t[:, hc], lhsT=MAt[:, 0:128], rhs=Vtm.t[:, hc], start=False, stop=True),
                           r=[b_MA, Vtm], disjoint=[psY])
                    pS = nextps()
                    op(PE, lambda e, pS=pS, hc=hc, base=base, Usb=Usb: e.matmul(pS.t[base:base + 64, 0:64], lhsT=Btm.t[:, hc], rhs=Usb[:], start=True, stop=False),
                       r=[Btm, b_Usb], w=[pS], signal=False)
                    op(PE, lambda e, pS=pS, hc=hc, base=base: e.matmul(pS.t[base:base + 64, 0:64], lhsT=Ktm.t[:, hc], rhs=Vtm.t[:, hc], start=False, stop=True),
                       r=[Ktm, Vtm], disjoint=[pS])
                    op(V, lambda e, pS=pS, hp=hp, base=base: e.scalar_tensor_tensor(out=S0[base:base + 64, hp, :], in0=pS.t[base:base + 64, 0:64],
                                                                                     scalar=EW.t[base:base + 64, hp, 127:128], in1=S0W[base:base + 64, hp, :],
                                                                                     op0=ALU.mult, op1=ALU.add),
                       xr=[pS], r=[EW, b_SW], w=[b_S[h]])
                    op(A, lambda e, hp=hp, base=base: e.copy(out=S0b[base:base + 64, hp, :], in_=S0[base:base + 64, hp, :]), r=[b_S[h]], w=[b_Sb[h]])
                else:
                    pq, b_pq = ps2.next()
                    for half in range(2):
                        op(PE, lambda e, half=half, hp=hp, base=base, pq=pq: e.matmul(pq[:, half * 512:(half + 1) * 512], lhsT=AhT[base:base + 64, hp, :],
                                                                                        rhs=S0sb[base:base + 64, hp, half * 8:(half + 1) * 8, :].rearrange("p b i -> p (b i)"),
                                                                                        start=True, stop=True),
                           r=[b_AhT[h], b_S0sb[hp]], w=[b_pq] if half == 0 else (), disjoint=[b_pq] if half else ())
                    for half in range(2):
                        op(V, lambda e, half=half, pq=pq: e.tensor_tensor(out=TMPs.t[:, half * 8:(half + 1) * 8, :],
                                                                            in0=pq[:, half * 512:(half + 1) * 512].rearrange("p (b i) -> p b i", i=64),
                                                                            in1=segsel.t[:, half * 8:(half + 1) * 8].unsqueeze(2).broadcast_to([128, 8, 64]), op=ALU.mult),
                           xr=[b_pq], r=[segsel], w=[TMPs] if half == 0 else (), disjoint=[TMPs] if half else ())
                    op(V, lambda e: e.tensor_reduce(out=U1.t[:], in_=TMPs.t[:].rearrange("p b i -> p i b"), axis=AX.X, op=ALU.add), r=[TMPs], w=[U1])
                    op(V, lambda e, Usb=Usb, UVt=UVt: e.tensor_tensor(out=Usb[:], in0=U1.t[:], in1=UVt[:], op=ALU.add), r=[U1, b_UV], w=[b_Usb])
                    pq, b_pq = ps2.next()
                    for half in range(2):
                        op(PE, lambda e, half=half, hp=hp, base=base, pq=pq: e.matmul(pq[:, half * 512:(half + 1) * 512], lhsT=ART.t[base:base + 64, hp, 0, :],
                                                                                        rhs=S0sb[base:base + 64, hp, half * 8:(half + 1) * 8, :].rearrange("p b i -> p (b i)"),
                                                                                        start=True, stop=True),
                           r=[ART, b_S0sb[hp]], w=[b_pq] if half == 0 else (), disjoint=[b_pq] if half else ())
                    for half in range(2):
                        op(V, lambda e, half=half, pq=pq: e.tensor_tensor(out=TMPs.t[:, half * 8:(half + 1) * 8, :],
                                                                            in0=pq[:, half * 512:(half + 1) * 512].rearrange("p (b i) -> p b i", i=64),
                                                                            in1=segsel.t[:, half * 8:(half + 1) * 8].unsqueeze(2).broadcast_to([128, 8, 64]), op=ALU.mult),
                           xr=[b_pq], r=[segsel], w=[TMPs] if half == 0 else (), disjoint=[TMPs] if half else ())
                    op(V, lambda e: e.tensor_reduce(out=Y1.t[:], in_=TMPs.t[:].rearrange("p b i -> p i b"), axis=AX.X, op=ALU.add), r=[TMPs], w=[Y1])
                    py = nextps()
                    op(PE, lambda e, py=py, NAXt=NAXt, Usb=Usb: e.matmul(py.t[:, 0:64], lhsT=NAXt[:, 0:128], rhs=Usb[:], start=True, stop=False),
                       r=[b_NAX, b_Usb], w=[py], signal=False)
                    op(PE, lambda e, py=py, MAt=MAt, hc=hc: e.matmul(py.t[:, 0:64], lhsT=MAt[:, 0:128], rhs=Vtm.t[:, hc], start=False, stop=True),
                       r=[b_MA, Vtm], disjoint=[py])
                    op(V, lambda e, py=py, hc=hc: e.tensor_tensor(out=Ytm.t[:, hc], in0=py.t[:, 0:64], in1=Y1.t[:], op=ALU.add), xr=[py], r=[Y1], disjoint=[Ytm])
                    op(PO, lambda e, Usb=Usb: e.tensor_tensor(out=UBLK.t[:], in0=Usb[:, :].unsqueeze(1).broadcast_to([128, 16, 64]),
                                                               in1=segsel.t[:, :].unsqueeze(2).broadcast_to([128, 16, 64]), op=ALU.mult),
                       r=[b_Usb, segsel], w=[UBLK])
                    op(PO, lambda e, hc=hc: e.tensor_tensor(out=VBLK.t[:], in0=Vtm.t[:, hc].unsqueeze(1).broadcast_to([128, 16, 64]),
                                                             in1=segsel.t[:, :].unsqueeze(2).broadcast_to([128, 16, 64]), op=ALU.mult),
                       r=[Vtm, segsel], w=[VBLK])
                    pq, b_pq = ps2.next()
                    for half in range(2):
                        op(PE, lambda e, half=half, hc=hc, base=base, pq=pq: e.matmul(pq[base:base + 64, half * 512:(half + 1) * 512], lhsT=Btm.t[:, hc],
                                                                                        rhs=UBLK.t[:, half * 8:(half + 1) * 8, :].rearrange("p b i -> p (b i)"), start=True, stop=False),
                           r=[Btm, UBLK], w=[b_pq] if half == 0 else (), disjoint=[b_pq] if half else (), signal=False)
                        op(PE, lambda e, half=half, hc=hc, base=base, pq=pq: e.matmul(pq[base:base + 64, half * 512:(half + 1) * 512], lhsT=Ktm.t[:, hc],
                                                                                        rhs=VBLK.t[:, half * 8:(half + 1) * 8, :].rearrange("p b i -> p (b i)"), start=False, stop=True),
                           r=[Ktm, VBLK], disjoint=[b_pq])
                    op(V, lambda e, hp=hp, base=base, pq=pq: e.tensor_tensor(out=S0s[base:base + 64, hp, :, :], in0=pq[base:base + 64, :].rearrange("p (b i) -> p b i", i=64),
                                                                               in1=S0s[base:base + 64, hp, :, :], op=ALU.add),
                       xr=[b_pq], w=[P.buf()], r=[b_S0s[hp]])
                    op(PO, lambda e, hp=hp, base=base: e.tensor_tensor(out=S0s[base:base + 64, hp, :, :], in0=S0s[base:base + 64, hp, :, :],
                                                                        in1=EW.t[base:base + 64, hp, :].rearrange("p (b t) -> p b t", t=8)[:, :, 7:8].broadcast_to([64, 16, 64]),
                                                                        op=ALU.mult),
                       r=[EW], w=[b_S0s[hp]] if h2 == 1 else (), disjoint=[b_S0s[hp]] if h2 == 0 else ())
            for g in range(1):
                alive = [head_gen(h) for h in range(8)]
                while alive:
                    for gn in list(alive):
                        try:
                            next(gn)
                        except StopIteration:
                            alive.remove(gn)
            if full:
                if kind != "sample":
                    op(A, lambda e: e.copy(out=Ytm.t[:], in_=psY.t[:, 0:512]), xr=[psY], w=[Ytm])
                y3 = lambda x: x.t[:, 0:512].rearrange("p (h i) -> p h i", i=64)
                b8 = lambda ap: ap.unsqueeze(2).broadcast_to([128, 8, 64])
                op(V, lambda e: e.tensor_reduce(out=ST8.t[:, 0:8], in_=y3(Ytm), axis=AX.X, op=ALU.add), r=[Ytm], w=[ST8])
                op(V, lambda e: e.tensor_scalar(out=ST8.t[:, 8:16], in0=ST8.t[:, 0:8], scalar1=-1.0 / 64, scalar2=None, op0=ALU.mult), w=[ST8])
                op(V, lambda e: e.tensor_tensor(out=y3(O1), in0=y3(Ytm), in1=b8(ST8.t[:, 8:16]), op=ALU.add), r=[Ytm, ST8], w=[O1])
                op(V, lambda e: e.tensor_tensor(out=O2.t[:], in0=O1.t[:], in1=O1.t[:], op=ALU.mult), r=[O1], w=[O2])
                op(V, lambda e: e.tensor_reduce(out=ST8.t[:, 16:24], in_=y3(O2), axis=AX.X, op=ALU.add), r=[O2], w=[ST8])
                op(A, lambda e: e.activation(out=ST8.t[:, 24:32], in_=ST8.t[:, 16:24], func=AF.Sqrt, scale=1.0 / 64, bias=cst.t[:, 2:3]), r=[cst], w=[ST8])
                op(V, lambda e: e.reciprocal(out=ST8.t[:, 16:24], in_=ST8.t[:, 24:32]), w=[ST8])
                op(V, lambda e: e.tensor_tensor(out=y3(O2), in0=y3(O1), in1=b8(ST8.t[:, 16:24]), op=ALU.mult), r=[O1, ST8], w=[O2])
                op(V, lambda e: e.tensor_tensor(out=O1.t[:], in0=O2.t[:], in1=lnw.t[:], op=ALU.mult), r=[O2, lnw], w=[O1])
                op(V, lambda e: e.tensor_tensor(out=O2.t[:], in0=O1.t[:], in1=lnb.t[:], op=ALU.add), r=[O1, lnb], w=[O2])
                op(V, lambda e: e.tensor_tensor(out=y3(O3), in0=Vtm.t[:, 0:512].rearrange("p (h i) -> p h i", i=64), in1=b8(COEF.t[:, 0:8]), op=ALU.mult),
                   r=[Vtm, COEF], w=[O3])
                op(V, lambda e: e.tensor_tensor(out=O1.t[:], in0=O2.t[:], in1=O3.t[:], op=ALU.add), r=[O2, O3], w=[O1])
                op(V, lambda e: e.tensor_tensor(out=OAb.t[:], in0=O1.t[:], in1=Gtm.t[:], op=ALU.mult), r=[O1, Gtm], w=[OAb])
                pt = nextps()
                ptb = pt.t[:].bitcast(BF16)
                for hp in range(4):
                    op(PE, lambda e, hp=hp, ptb=ptb: e.transpose(out=ptb[:, hp * 128:(hp + 1) * 128], in_=OAb.t[:, hp * 128:(hp + 1) * 128], identity=identb.t[:]),
                       r=[OAb, identb], w=[pt] if hp == 0 else (), disjoint=[pt] if hp else ())
                op(A, lambda e, ptb=ptb: e.copy(out=OAT.t[:].rearrange("p a b -> p (a b)"), in_=ptb[:, 0:512]), xr=[pt], w=[OAT])
                P.dma("sp", T["oaT_d"].rearrange("c p t -> p c t")[:, :, q0:q0 + 128], OAT.t[:], reads=[OAT], disjoint=[T["b_oaT"]])

        tiles = list(range(NTILE))
        if DEBUG_TILES is not None:
            tiles = DEBUG_TILES
        for t in tiles:
            rwkv_tile(t)
        for hp in range(4):
            pt = nextps()
            op(PE, lambda e, pt=pt, hp=hp: e.transpose(out=pt.t[0:64, 0:128], in_=S0[:, hp, :], identity=identf.t[:]),
               r=[b_S[2 * hp], b_S[2 * hp + 1], identf], w=[pt])
            op(A, lambda e, pt=pt, hp=hp: e.copy(out=SOUT.t[:, hp, :], in_=pt.t[0:64, 0:128]), xr=[pt], disjoint=[SOUT])
        for h2 in range(2):
            P.dma("sp", T["rwkv_p_out"].rearrange("(a h) i j -> h i a j", h=2)[h2], SOUT.t[:, 0:4, h2 * 64:(h2 + 1) * 64], reads=[SOUT])
        for hp in range(4):
            for g in range(4):
                pt = nextps()
                for k in range(4):
                    b = g * 4 + k
                    op(PE, lambda e, pt=pt, k=k, b=b, hp=hp: e.transpose(out=pt.t[0:64, k * 128:(k + 1) * 128], in_=S0s[:, hp, b, :], identity=identf.t[:]),
                       r=[b_S0s[hp], identf], w=[pt] if k == 0 else (), disjoint=[pt] if k else ())
                op(A, lambda e, pt=pt, g=g: e.copy(out=SOUT.t[:, g * 4:(g + 1) * 4, :], in_=pt.t[0:64, 0:512].rearrange("p (b c) -> p b c", c=128)),
                   xr=[pt], w=[SOUT] if g == 0 else (), disjoint=[SOUT] if g else ())
            for h2 in range(2):
                P.dma("sp", T["rwkv_s_out"][:, 2 * hp + h2, :, :].rearrange("b i j -> i b j"),
                      SOUT.t[:, :, h2 * 64:(h2 + 1) * 64], reads=[SOUT])
        P.emit("phaseB1")


def phase_B2(P, nc, T):
    V, A, PO, PE = "dve", "act", "pool", "pe"
    with contextlib.ExitStack() as st:
        def mk(n, s, d):
            return Tl(st.enter_context(nc.sbuf_tensor("b2_" + n, s, d)), P.buf(n))

        def mkps(n, cols=512):
            return Tl(st.enter_context(nc.psum_tensor("b2_" + n, [128, cols], F32)), P.buf(n))

        def op(e, fn, r=(), w=(), **kw):
            return P.op(e, fn, reads=r, writes=w, **kw)

        cst = T["cst"]
        identb = mk("identb", [128, 128], BF16)
        identf = mk("identf", [128, 128], F32)
        P.dma("pool", identb.t[:], T["ident"], writes=[identb])
        P.dma("sp", identf.t[:], T["ident"], writes=[identf])
        cm_p = mk("cm_p", [128, 128], BF16)
        cm_s = mk("cm_s", [128, 128], BF16)
        P.dma("pool", cm_p.t[:], T["mtc_p"][:, 0:128], writes=[cm_p])
        P.dma("pool", cm_s.t[:], T["mtc_s"][:, 0:128], writes=[cm_s])
        KT = mk("KT", [128, 4, NTOK], BF16)
        QT = mk("QT", [128, 4, NQ], BF16)
        Vx = mk("Vx", [128, NTILE, 4, 130], BF16)
        for h in range(4):
            P.dma("sp", KT.t[:, h, :], T["kT_d"][h], reads=[T["b_kT"]], disjoint=[KT])
            P.dma("sp", QT.t[:, h, :], T["qT_d"][h], reads=[T["b_qT"]], disjoint=[QT])
            for t0 in range(0, NTILE, 8):
                t1 = min(NTILE, t0 + 8)
                P.dma("sp", Vx.t[:, t0:t1, h, 0:128], T["v_d"].rearrange("(t p) (h e) -> h p t e", p=128, h=4)[h][:, t0:t1, :], reads=[T["b_v"]], disjoint=[Vx])
        op(PO, lambda e: e.memset(Vx.t[:, :, :, 128:130], 1.0), disjoint=[Vx])
        btab = mk("btab", [128, 4, 32], F32)
        btabp = mk("btabp", [128, 4, 32], F32)
        pmask = mk("pmask", [128, 1], F32)
        P.dma("sp", btab.t[:].rearrange("p a b -> p (a b)"), T["btab"], writes=[btab])
        P.dma("sp", pmask.t[:], T["pmask"], writes=[pmask])
        op(V, lambda e: e.tensor_scalar(out=btabp.t[:], in0=btab.t[:], scalar1=pmask.t[:, 0:1], scalar2=None, op0=ALU.add), r=[btab, pmask], w=[btabp])
        bown = mk("bown", [128, 4], F32)
        P.dma("sp", bown.t[:], T["bown"], writes=[bown])
        bcache = mk("bcache", [128, 16, 32], F32)
        P.dma("sp", bcache.t[:].rearrange("p a b -> p (a b)"), T["bcache"], writes=[bcache])
        lv = mk("lv", [128, 4, 64], F32)
        for i, n in enumerate(["lam_q1", "lam_k1", "lam_q2", "lam_k2"]):
            P.dma("sp", lv.t[:, i, :], T[n].broadcast_to([128, 64]), disjoint=[lv])
        lt = mk("lt", [128, 2, 64], F32)
        ls = mk("ls", [128, 8], F32)
        op(V, lambda e: e.tensor_tensor(out=lt.t[:, 0, :], in0=lv.t[:, 0, :], in1=lv.t[:, 1, :], op=ALU.mult), r=[lv], disjoint=[lt])
        op(V, lambda e: e.tensor_tensor(out=lt.t[:, 1, :], in0=lv.t[:, 2, :], in1=lv.t[:, 3, :], op=ALU.mult), r=[lv], disjoint=[lt])
        op(V, lambda e: e.tensor_reduce(out=ls.t[:, 0:2], in_=lt.t[:], axis=AX.X, op=ALU.add), r=[lt], w=[ls])
        op(A, lambda e: e.activation(out=ls.t[:, 2:4], in_=ls.t[:, 0:2], func=AF.Exp), w=[ls])
        op(V, lambda e: e.tensor_tensor(out=ls.t[:, 4:5], in0=ls.t[:, 2:3], in1=ls.t[:, 3:4], op=ALU.subtract), w=[ls])
        op(V, lambda e: e.tensor_scalar(out=ls.t[:, 5:6], in0=ls.t[:, 4:5], scalar1=0.2, scalar2=-1.0, op0=ALU.add, op1=ALU.mult), w=[ls])
        sg = mk("sg", [128, 128], F32)
        P.dma("sp", sg.t[:], T["subln_g"].broadcast_to([128, 128]), writes=[sg])
        op(V, lambda e: e.tensor_scalar(out=sg.t[:], in0=sg.t[:], scalar1=0.8, scalar2=None, op0=ALU.mult), w=[sg])

        psr = Rot(P, [st.enter_context(nc.psum_tensor("b2_psr%d" % i, [128, 1024], F32)) for i in range(2)])
        psO = [mkps("psO%d" % i) for i in range(3)]
        pstr = mkps("pstr")

        def nextps():
            t, b = psr.next()
            return Tl(t, b)

        ETr = Rot(P, [st.enter_context(nc.sbuf_tensor("b2_ET%d" % i, [128, 256], BF16)) for i in range(3)])
        T2 = mk("T2", [128, 128], F32)
        ATT = mk("ATT", [128, 128], F32)
        junk = mk("junk", [128, 128], BF16)
        sc = mk("sc", [128, 8], F32)
        OBb = mk("OBb", [128, 512], BF16)
        OBT = mk("OBT", [128, 4, 128], BF16)

        def combine(h, o0, o0c, o1, o1c, xr0, xr1):
            op(V, lambda e: e.reciprocal(out=sc.t[:, 0:1], in_=o0[:, 128:129]), xr=[xr0], w=[sc])
            op(V, lambda e: e.reciprocal(out=sc.t[:, 1:2], in_=o1[:, 128:129]), xr=[xr1], w=[sc])
            op(V, lambda e: e.tensor_tensor(out=sc.t[:, 2:3], in0=sc.t[:, 1:2], in1=ls.t[:, 5:6], op=ALU.mult), r=[ls], w=[sc])
            op(V, lambda e: e.tensor_scalar(out=T2.t[:], in0=o1[:, 0:128], scalar1=sc.t[:, 2:3], scalar2=None, op0=ALU.mult), xr=[xr1], r=[sc], w=[T2])
            op(V, lambda e: e.scalar_tensor_tensor(out=ATT.t[:], in0=o0[:, 0:128], scalar=sc.t[:, 0:1], in1=T2.t[:], op0=ALU.mult, op1=ALU.add),
               xr=[xr0], r=[sc, T2], w=[ATT])
            op(A, lambda e: e.activation(out=junk.t[:], in_=ATT.t[:], func=AF.Square, accum_out=sc.t[:, 3:4]), r=[ATT], w=[junk, sc])
            op(A, lambda e: e.activation(out=sc.t[:, 4:5], in_=sc.t[:, 3:4], func=AF.Sqrt, scale=1.0 / 128, bias=cst.t[:, 3:4]), r=[cst], w=[sc])
            op(V, lambda e: e.reciprocal(out=sc.t[:, 5:6], in_=sc.t[:, 4:5]), w=[sc])
            op(V, lambda e: e.scalar_tensor_tensor(out=OBb.t[:, h * 128:(h + 1) * 128], in0=ATT.t[:], scalar=sc.t[:, 5:6], in1=sg.t[:], op0=ALU.mult, op1=ALU.mult),
               r=[ATT, sc, sg], disjoint=[OBb])

        def finish_tile(qi):
            ptb = pstr.t[:].bitcast(BF16)
            for h in range(4):
                op(PE, lambda e, h=h: e.transpose(out=ptb[:, h * 128:(h + 1) * 128], in_=OBb.t[:, h * 128:(h + 1) * 128], identity=identb.t[:]),
                   r=[OBb, identb], w=[pstr] if h == 0 else (), disjoint=[pstr] if h else ())
            op(A, lambda e: e.copy(out=OBT.t[:].rearrange("p a b -> p (a b)"), in_=ptb[:, 0:512]), xr=[pstr], w=[OBT])
            P.dma("sp", T["obT_d"].rearrange("c p t -> p c t")[:, :, qi * 128:(qi + 1) * 128], OBT.t[:], reads=[OBT], disjoint=[T["b_obT"]])

        qtiles = list(range(NT_OWN)) if DEBUG_QT is None else DEBUG_QT
        if DEBUG_B2 < 1:
            qtiles = []
        for qi in qtiles:
            qg = NT_PRIOR + qi
            for h in range(4):
                o0, o1 = psO[0], psO[1]
                slope_h = 2.0 ** (-2.0 * (h + 1))
                kts = [kt for kt in range(0, qg + 1) if slope_h * (128 * (qg - kt) - 127) <= 80.0]
                def qk_stage(kt, h=h, qi=qi):
                    pS = nextps()
                    for c in range(2):
                        op(PE, lambda e, c=c, kt=kt, pS=pS, h=h, qi=qi: e.matmul(pS.t[:, c * 512:c * 512 + 128], lhsT=KT.t[c * 64:(c + 1) * 64, h, kt * 128:(kt + 1) * 128],
                                                                     rhs=QT.t[c * 64:(c + 1) * 64, h, qi * 128:(qi + 1) * 128], start=True, stop=True),
                           r=[KT, QT], w=[pS] if c == 0 else (), disjoint=[pS] if c else ())
                    return pS

                def ex_pv_stage(ii, kt, pS, h=h, qg=qg, o0=o0, o1=o1, nk=len(kts)):
                    ET, b_ET = ETr.next()
                    o = qg - kt
                    bt = btabp if kt < NT_PRIOR else btab
                    op(A, lambda e, pS=pS, ET=ET, bt=bt, o=o, h=h: e.activation(out=ET[:].rearrange("p (c q) -> p c q", c=2), in_=pS.t[:, :].rearrange("p (c x) -> p c x", c=2)[:, :, 0:128],
                                                                           func=AF.Exp, scale=0.125, bias=bt.t[:, h, o:o + 1]),
                       xr=[pS], r=[bt], w=[b_ET])
                    if kt == qg:
                        op(V, lambda e, ET=ET: e.tensor_tensor(out=ET[:].rearrange("p (c q) -> p c q", c=2), in0=ET[:].rearrange("p (c q) -> p c q", c=2),
                                                                in1=cm_p.t[:, :].unsqueeze(1).broadcast_to([128, 2, 128]), op=ALU.mult), r=[cm_p], w=[b_ET])
                    first, last = (ii == 0), (ii == nk - 1)
                    op(PE, lambda e, ET=ET, kt=kt, first=first, last=last, h=h, o0=o0: e.matmul(o0.t[:, 0:129], lhsT=ET[:, 0:128], rhs=Vx.t[:, kt, h, 0:129], start=first, stop=last),
                       r=[b_ET, Vx], w=[o0] if first else (), disjoint=() if first else [o0], signal=last)
                    op(PE, lambda e, ET=ET, kt=kt, first=first, last=last, h=h, o1=o1: e.matmul(o1.t[:, 0:129], lhsT=ET[:, 128:256], rhs=Vx.t[:, kt, h, 0:129], start=first, stop=last),
                       r=[b_ET, Vx], w=[o1] if first else (), disjoint=() if first else [o1], signal=last)

                pend = {}
                for ii in range(len(kts) + 1):
                    if ii < len(kts):
                        pend[ii] = qk_stage(kts[ii])
                    if ii >= 1:
                        ex_pv_stage(ii - 1, kts[ii - 1], pend.pop(ii - 1))
                if DEBUG_B2 >= 4:
                    combine(h, o0.t, None, o1.t, None, o0, o1)
            if DEBUG_B2 >= 5:
                finish_tile(qi)

        if DEBUG_SAMPLE_ATT:
            zb = mk("zb", [128, 512], BF16)
            op(PO, lambda e: e.memset(zb.t[:], 0.0), w=[zb])
            slot = {}
            for h in range(4):
                for c in range(2):
                    i = h * 2 + c
                    slot[(h, c)] = (i // 3, (i % 3) * 130)
            for bnk in range(3):
                op(PE, lambda e, bnk=bnk: e.matmul(psO[bnk].t[:, 0:512], lhsT=zb.t[:, 0:128], rhs=zb.t[:, 0:512], start=True, stop=False), r=[zb], w=[psO[bnk]])
            ptb_i = mk("ptab", [128, 256], I32)
            P.dma("sp", ptb_i.t[:], T["ptab"].broadcast_to([128, 256]), writes=[ptb_i])
            pcol = mk("pcol", [128, 1], I32)
            op(PO, lambda e: e.iota(pcol.t[:], pattern=[[0, 1]], base=0, channel_multiplier=1), w=[pcol])
            pcolf = mk("pcolf", [128, 1], F32)
            op(V, lambda e: e.tensor_copy(out=pcolf.t[:], in_=pcol.t[:]), r=[pcol], w=[pcolf])
            idx = mk("idx", [128, 256], I32)
            op(V, lambda e: e.tensor_scalar(out=idx.t[:], in0=ptb_i.t[:], scalar1=128.0, scalar2=pcolf.t[:, 0:1], op0=ALU.mult, op1=ALU.add), r=[ptb_i, pcolf], w=[idx])
            Kpg = Rot(P, [st.enter_context(nc.sbuf_tensor("b2_Kpg%d" % i, [128, 512], F32)) for i in range(4)])
            Vpg = Rot(P, [st.enter_context(nc.sbuf_tensor("b2_Vpg%d" % i, [128, 512], F32)) for i in range(4)])
            KcTr = Rot(P, [st.enter_context(nc.sbuf_tensor("b2_KcT%d" % i, [128, 4, 128], BF16)) for i in range(3)])
            Vpxr = Rot(P, [st.enter_context(nc.sbuf_tensor("b2_Vpx%d" % i, [128, 4, 130], BF16)) for i in range(4)])
            for (vt, vb) in Vpxr.items:
                op(PO, lambda e, vt=vt: e.memset(vt[:], 1.0), w=[vb])
            XB = mk("XB", [128, 2, 32], F32)
            ETz = [mk("ETz%d" % i, [128, 4, 2, 128], BF16) for i in range(3)]
            ETz_b = [None] * 3
            for z in ETz:
                op(PO, lambda e, z=z: e.memset(z.t[:], 0.0), w=[z])
            ez = 0
            sq0 = NT_OWN * 128
            items = [(b, pg) for b in range(16) for pg in range(16)]
            stash = {}

            def s1(i):
                b, pg = items[i]
                col = b * 16 + pg
                kp, b_kp = Kpg.next()
                vp, b_vp = Vpg.next()
                P.dma_ind(kp[:, :], T["cache_k"], idx.t[:, col:col + 1], reads=[idx], writes=[b_kp])
                P.dma_ind(vp[:, :], T["cache_v"], idx.t[:, col:col + 1], reads=[idx], writes=[b_vp])
                pT = nextps()
                for h in range(4):
                    op(PE, lambda e, h=h, kp=kp, pT=pT: e.transpose(out=pT.t[:, h * 128:(h + 1) * 128], in_=kp[:, h * 128:(h + 1) * 128], identity=identf.t[:]),
                       r=[b_kp, identf], w=[pT] if h == 0 else (), disjoint=[pT] if h else ())
                kc, b_kc = KcTr.next()
                op(A, lambda e, kc=kc, pT=pT: e.copy(out=kc[:].rearrange("p a b -> p (a b)"), in_=pT.t[:, 0:512]), xr=[pT], w=[b_kc])
                vx, b_vx = Vpxr.next()
                op(V, lambda e, vx=vx, vp=vp: e.tensor_copy(out=vx[:, :, 0:128], in_=vp[:, :].rearrange("p (h e) -> p h e", h=4)), r=[b_vp], w=[b_vx])
                stash[i] = dict(kc=kc, b_kc=b_kc, vx=vx, b_vx=b_vx)

            def s2(i):
                b, pg = items[i]
                d = stash[i]
                kc, b_kc = d["kc"], d["b_kc"]
                pS = nextps()
                for h in range(4):
                    for c in range(2):
                        first = (h == 0 and c == 0)
                        op(PE, lambda e, h=h, c=c, kc=kc, pS=pS, b=b: e.matmul(pS.t[:, c * 512 + h * 8:c * 512 + h * 8 + 8], lhsT=kc[c * 64:(c + 1) * 64, h, :],
                                                                           rhs=QT.t[c * 64:(c + 1) * 64, h, sq0 + b * 8:sq0 + b * 8 + 8], start=True, stop=True),
                           r=[b_kc, QT], w=[pS] if first else (), disjoint=() if first else [pS])
                op(V, lambda e, pS=pS, pg=pg: e.scalar_tensor_tensor(out=XB.t[:], in0=pS.t[:, :].rearrange("p (c x) -> p c x", c=2)[:, :, 0:32], scalar=0.125,
                                                                       in1=bcache.t[:, pg, :].unsqueeze(1).broadcast_to([128, 2, 32]), op0=ALU.mult, op1=ALU.add),
                   xr=[pS], r=[bcache], w=[XB])
                zi = i % 3
                z = ETz[zi]
                zprev = ETz_b[zi]
                if zprev is not None and zprev != b:
                    op(PO, lambda e, z=z, zprev=zprev: e.memset(z.t[:, :, :, zprev * 8:zprev * 8 + 8], 0.0), w=[z])
                ETz_b[zi] = b
                op(A, lambda e, z=z, b=b: e.activation(out=z.t[:, :, :, b * 8:b * 8 + 8], in_=XB.t[:].rearrange("p c (h q) -> p h c q", h=4), func=AF.Exp),
                   r=[XB], w=[z])
                d["z"] = z

            def s3(i):
                d = stash.pop(i)
                z, vx, b_vx = d["z"], d["vx"], d["b_vx"]
                for h in range(4):
                    for c in range(2):
                        bnk, c0 = slot[(h, c)]
                        op(PE, lambda e, h=h, c=c, z=z, vx=vx, bnk=bnk, c0=c0: e.matmul(psO[bnk].t[:, c0:c0 + 129], lhsT=z.t[:, h, c, :], rhs=vx[:, h, 0:129], start=False, stop=False),
                           r=[z, b_vx], disjoint=[psO[bnk]], signal=(h == 3 and c == 1))

            NI = len(items)
            for j in range(NI + 2):
                if j < NI:
                    s1(j)
                if 1 <= j <= NI:
                    s2(j - 1)
                if j >= 2:
                    s3(j - 2)
            kt = NTILE - 1
            for h in range(4):
                pS = nextps()
                for c in range(2):
                    op(PE, lambda e, c=c, h=h, pS=pS: e.matmul(pS.t[:, c * 512:c * 512 + 128], lhsT=KT.t[c * 64:(c + 1) * 64, h, kt * 128:(kt + 1) * 128],
                                                                 rhs=QT.t[c * 64:(c + 1) * 64, h, sq0:sq0 + 128], start=True, stop=True),
                       r=[KT, QT], w=[pS] if c == 0 else (), disjoint=[pS] if c else ())
                ET, b_ET = ETr.next()
                op(A, lambda e, pS=pS, ET=ET, h=h: e.activation(out=ET[:].rearrange("p (c q) -> p c q", c=2), in_=pS.t[:, :].rearrange("p (c x) -> p c x", c=2)[:, :, 0:128],
                                                                func=AF.Exp, scale=0.125, bias=bown.t[:, h:h + 1]), xr=[pS], r=[bown], w=[b_ET])
                op(V, lambda e, ET=ET: e.tensor_tensor(out=ET[:].rearrange("p (c q) -> p c q", c=2), in0=ET[:].rearrange("p (c q) -> p c q", c=2),
                                                        in1=cm_s.t[:, :].unsqueeze(1).broadcast_to([128, 2, 128]), op=ALU.mult), r=[cm_s], w=[b_ET])
                for c in range(2):
                    bnk, c0 = slot[(h, c)]
                    op(PE, lambda e, c=c, h=h, ET=ET, bnk=bnk, c0=c0: e.matmul(psO[bnk].t[:, c0:c0 + 129], lhsT=ET[:, c * 128:(c + 1) * 128], rhs=Vx.t[:, kt, h, 0:129], start=False, stop=True),
                       r=[b_ET, Vx], disjoint=[psO[bnk]])
            for h in range(4):
                b0, c0 = slot[(h, 0)]
                b1, c1 = slot[(h, 1)]
                combine(h, psO[b0].t[:, c0:c0 + 129], None, psO[b1].t[:, c1:c1 + 129], None, psO[b0], psO[b1])
            finish_tile(NT_OWN)
        P.emit("phaseB2")


def phase_C1(P, nc, T):
    V, A, PO, PE = "dve", "act", "pool", "pe"
    with contextlib.ExitStack() as st:
        def mk(n, s, d):
            return Tl(st.enter_context(nc.sbuf_tensor("c1_" + n, s, d)), P.buf(n))

        def op(e, fn, r=(), w=(), **kw):
            return P.op(e, fn, reads=r, writes=w, **kw)

        cst = T["cst"]
        identb = mk("identb", [128, 128], BF16)
        P.dma("pool", identb.t[:], T["ident"], writes=[identb])
        wa = mk("wa", [128, 4, D], BF16)
        wb = mk("wb", [128, 4, D], BF16)
        wo = mk("wo", [128, 8, D], BF16)
        for kc in range(4):
            P.dma("pool", wa.t[:, kc, :], T["w_br_a"][kc * 128:(kc + 1) * 128, :], disjoint=[wa])
            P.dma("pool", wb.t[:, kc, :], T["w_br_b"][kc * 128:(kc + 1) * 128, :], disjoint=[wb])
        for kc in range(8):
            P.dma("pool", wo.t[:, kc, :], T["w_out"][kc * 128:(kc + 1) * 128, :], disjoint=[wo])
        gffn = mk("gffn", [128, D], F32)
        P.dma("sp", gffn.t[:], T["g_ffn"].broadcast_to([128, D]), writes=[gffn])
        psr = Rot(P, [st.enter_context(nc.psum_tensor("c1_psr%d" % i, [128, 512], F32)) for i in range(8)])

        def nextps():
            t, b = psr.next()
            return Tl(t, b)

        oaT = [mk("oaT%d" % i, [128, 4, 512], BF16) for i in range(2)]
        obT = [mk("obT%d" % i, [128, 4, 512], BF16) for i in range(2)]
        gts = [mk("gts%d" % i, [128, 16, 512], BF16) for i in range(2)]
        mT = [mk("mT%d" % i, [128, 8, 512], BF16) for i in range(2)]
        t1 = [mk("t1_%d" % i, [128, 512], F32) for i in range(2)]
        t2 = [mk("t2_%d" % i, [128, 512], F32) for i in range(2)]
        xt = [mk("xt%d" % i, [128, D], F32) for i in range(2)]
        xn = [mk("xn%d" % i, [128, D], F32) for i in range(2)]
        hb = [mk("hb%d" % i, [128, D], BF16) for i in range(2)]
        hT = [mk("hT%d" % i, [128, 8, 128], BF16) for i in range(2)]
        junk = mk("junk", [128, D], BF16)
        ss = [mk("ss%d" % i, [128, 4], F32) for i in range(2)]
        groups = [(g * 4, 4) for g in range(4)] + [(16, 1)]
        tcount = 0
        for gi, (qt0, nt) in enumerate(groups):
            G = nt * 128
            q0 = qt0 * 128
            oa, ob, gt, m = oaT[gi % 2], obT[gi % 2], gts[gi % 2], mT[gi % 2]
            P.dma("sp", oa.t[:, :, 0:G], T["oaT_d"].rearrange("c p t -> p c t")[:, :, q0:q0 + G], reads=[T["b_oaT"]], writes=[oa])
            P.dma("sp", ob.t[:, :, 0:G], T["obT_d"].rearrange("c p t -> p c t")[:, :, q0:q0 + G], reads=[T["b_obT"]], writes=[ob])
            P.dma("sp", gt.t[:, :, 0:G], T["gates_d"].rearrange("c p t -> p c t")[:, :, q0:q0 + G], reads=[T["b_gates"]], writes=[gt])
            for nch in range(8):
                pa = nextps()
                pb = nextps()
                for kc in range(4):
                    op(PE, lambda e, pa=pa, kc=kc, nch=nch, oa=oa, G=G: e.matmul(pa.t[:, 0:G], lhsT=wa.t[:, kc, nch * 128:(nch + 1) * 128], rhs=oa.t[:, kc, 0:G], start=(kc == 0), stop=(kc == 3)),
                       r=[wa, oa], w=[pa] if kc == 0 else (), disjoint=[pa] if kc else (), signal=(kc == 3))
                for kc in range(4):
                    op(PE, lambda e, pb=pb, kc=kc, nch=nch, ob=ob, G=G: e.matmul(pb.t[:, 0:G], lhsT=wb.t[:, kc, nch * 128:(nch + 1) * 128], rhs=ob.t[:, kc, 0:G], start=(kc == 0), stop=(kc == 3)),
                       r=[wb, ob], w=[pb] if kc == 0 else (), disjoint=[pb] if kc else (), signal=(kc == 3))
                ta, tb = t1[nch % 2], t2[nch % 2]
                op(V, lambda e, pa=pa, ta=ta, gt=gt, nch=nch, G=G: e.tensor_tensor(out=ta.t[:, 0:G], in0=pa.t[:, 0:G], in1=gt.t[:, nch, 0:G], op=ALU.mult), xr=[pa], r=[gt], w=[ta])
                op(V, lambda e, pb=pb, tb=tb, gt=gt, nch=nch, G=G: e.tensor_tensor(out=tb.t[:, 0:G], in0=pb.t[:, 0:G], in1=gt.t[:, 8 + nch, 0:G], op=ALU.mult), xr=[pb], r=[gt], w=[tb])
                op(PO, lambda e, ta=ta, tb=tb, m=m, nch=nch, G=G: e.tensor_tensor(out=m.t[:, nch, 0:G], in0=ta.t[:, 0:G], in1=tb.t[:, 0:G], op=ALU.add), r=[ta, tb], disjoint=[m])
            for ti in range(nt):
                qt = qt0 + ti
                tg = NT_PRIOR + qt
                x, xo, h_, hTt, s_ = xt[tcount % 2], xn[tcount % 2], hb[tcount % 2], hT[tcount % 2], ss[tcount % 2]
                tcount += 1
                P.dma("sp", x.t[:], T["x_all"][tg * 128:(tg + 1) * 128, :], writes=[x])
                for half in range(2):
                    px = nextps()
                    for kc in range(8):
                        op(PE, lambda e, px=px, kc=kc, half=half, m=m, ti=ti: e.matmul(px.t[:, 0:512], lhsT=m.t[:, kc, ti * 128:(ti + 1) * 128], rhs=wo.t[:, kc, half * 512:(half + 1) * 512],
                                                                                    start=(kc == 0), stop=(kc == 7)),
                           r=[m, wo], w=[px] if kc == 0 else (), disjoint=[px] if kc else (), signal=(kc == 7))
                    op(V, lambda e, px=px, half=half, x=x, xo=xo: e.tensor_tensor(out=xo.t[:, half * 512:(half + 1) * 512], in0=px.t[:, 0:512], in1=x.t[:, half * 512:(half + 1) * 512], op=ALU.add),
                       xr=[px], r=[x], w=[xo] if half == 0 else (), disjoint=[xo] if half else ())
                P.dma("sp", T["xnew_d"][qt * 128:(qt + 1) * 128, :], xo.t[:], reads=[xo], disjoint=[T["b_xnew"]])
                op(A, lambda e, xo=xo, s_=s_: e.activation(out=junk.t[:], in_=xo.t[:], func=AF.Square, accum_out=s_.t[:, 0:1]), r=[xo], w=[junk, s_])
                op(A, lambda e, s_=s_: e.activation(out=s_.t[:, 1:2], in_=s_.t[:, 0:1], func=AF.Sqrt, scale=1.0 / D, bias=cst.t[:, 0:1]), r=[cst], w=[s_])
                op(V, lambda e, s_=s_: e.reciprocal(out=s_.t[:, 2:3], in_=s_.t[:, 1:2]), w=[s_])
                op(V, lambda e, xo=xo, s_=s_, h_=h_: e.scalar_tensor_tensor(out=h_.t[:], in0=xo.t[:], scalar=s_.t[:, 2:3], in1=gffn.t[:], op0=ALU.mult, op1=ALU.mult),
                   r=[xo, s_, gffn], w=[h_])
                pt = nextps()
                ptb = pt.t[:].bitcast(BF16)
                for kc in range(8):
                    op(PE, lambda e, kc=kc, ptb=ptb, h_=h_: e.transpose(out=ptb[:, kc * 128:(kc + 1) * 128], in_=h_.t[:, kc * 128:(kc + 1) * 128], identity=identb.t[:]),
                       r=[h_, identb], w=[pt] if kc == 0 else (), disjoint=[pt] if kc else ())
                op(A, lambda e, ptb=ptb, hTt=hTt: e.copy(out=hTt.t[:].rearrange("p a b -> p (a b)"), in_=ptb[:, 0:1024]), xr=[pt], w=[hTt])
                P.dma("sp", T["hnT_d"].rearrange("c p t -> p c t")[:, :, qt * 128:(qt + 1) * 128], hTt.t[:], reads=[hTt], disjoint=[T["b_hnT"]])
        P.emit("phaseC1")


def phase_C2(P, nc, T):
    V, A, PO, PE = "dve", "act", "pool", "pe"
    with contextlib.ExitStack() as st:
        def mk(n, s, d):
            return Tl(st.enter_context(nc.sbuf_tensor("c2_" + n, s, d)), P.buf(n))

        def op(e, fn, r=(), w=(), **kw):
            return P.op(e, fn, reads=r, writes=w, **kw)

        cst = T["cst"]
        wu = mk("wu", [128, 8, 4096], BF16)
        wd = mk("wd", [128, 32, D], BF16)
        for kc in range(8):
            for j in range(2):
                P.dma("pool", wu.t[:, kc, j * 2048:(j + 1) * 2048], T["w_up"][kc * 128:(kc + 1) * 128, j * 2048:(j + 1) * 2048], disjoint=[wu])
        for fc in range(32):
            P.dma("pool", wd.t[:, fc, :], T["w_down"][fc * 128:(fc + 1) * 128, :], disjoint=[wd])
        gfin = mk("gfin", [128, D], F32)
        P.dma("sp", gfin.t[:], T["g_final"].broadcast_to([128, D]), writes=[gfin])
        psr = Rot(P, [st.enter_context(nc.psum_tensor("c2_psr%d" % i, [128, 512], F32)) for i in range(8)])

        def nextps():
            t, b = psr.next()
            return Tl(t, b)

        hT = [mk("hT%d" % i, [128, 8, 256], BF16) for i in range(2)]
        aT = [mk("aT%d" % i, [128, 32, 256], BF16) for i in range(1)]
        rl = [mk("rl%d" % i, [128, 256], F32) for i in range(3)]
        xw = [mk("xw%d" % i, [128, D], F32) for i in range(2)]
        yo = [mk("yo%d" % i, [128, D], F32) for i in range(2)]
        yf = [mk("yf%d" % i, [128, D], F32) for i in range(2)]
        junk = mk("junk", [128, D], BF16)
        ss = [mk("ss%d" % i, [128, 4], F32) for i in range(2)]
        groups = [(g * 2, 2) for g in range(8)] + [(16, 1)]
        tcount = 0
        for gi, (qt0, nt) in enumerate(groups):
            G = nt * 128
            q0 = qt0 * 128
            h_ = hT[gi % 2]
            a_ = aT[0]
            P.dma("sp", h_.t[:, :, 0:G], T["hnT_d"].rearrange("c p t -> p c t")[:, :, q0:q0 + G], reads=[T["b_hnT"]], writes=[h_])
            for fc in range(32):
                pu = nextps()
                for kc in range(8):
                    op(PE, lambda e, pu=pu, kc=kc, fc=fc, h_=h_, G=G: e.matmul(pu.t[:, 0:G], lhsT=wu.t[:, kc, fc * 128:(fc + 1) * 128], rhs=h_.t[:, kc, 0:G], start=(kc == 0), stop=(kc == 7)),
                       r=[wu, h_], w=[pu] if kc == 0 else (), disjoint=[pu] if kc else (), signal=(kc == 7))
                r_ = rl[fc % 3]
                op(A, lambda e, pu=pu, r_=r_, G=G: e.activation(out=r_.t[:, 0:G], in_=pu.t[:, 0:G], func=AF.Relu), xr=[pu], w=[r_])
                op(PO, lambda e, r_=r_, a_=a_, fc=fc, G=G: e.tensor_tensor(out=a_.t[:, fc, 0:G], in0=r_.t[:, 0:G], in1=r_.t[:, 0:G], op=ALU.mult), r=[r_], disjoint=[a_])
            for ti in range(nt):
                qt = qt0 + ti
                x, y, yfin, s_ = xw[tcount % 2], yo[tcount % 2], yf[tcount % 2], ss[tcount % 2]
                tcount += 1
                P.dma("sp", x.t[:], T["xnew_d"][qt * 128:(qt + 1) * 128, :], reads=[T["b_xnew"]], writes=[x])
                for half in range(2):
                    pd = nextps()
                    for fc in range(32):
                        op(PE, lambda e, pd=pd, fc=fc, half=half, a_=a_, ti=ti: e.matmul(pd.t[:, 0:512], lhsT=a_.t[:, fc, ti * 128:(ti + 1) * 128], rhs=wd.t[:, fc, half * 512:(half + 1) * 512],
                                                                                      start=(fc == 0), stop=(fc == 31)),
                           r=[a_, wd], w=[pd] if fc == 0 else (), disjoint=[pd] if fc else (), signal=(fc == 31))
                    op(V, lambda e, pd=pd, half=half, x=x, y=y: e.tensor_tensor(out=y.t[:, half * 512:(half + 1) * 512], in0=pd.t[:, 0:512], in1=x.t[:, half * 512:(half + 1) * 512], op=ALU.add),
                       xr=[pd], r=[x], w=[y] if half == 0 else (), disjoint=[y] if half else ())
                op(A, lambda e, y=y, s_=s_: e.activation(out=junk.t[:], in_=y.t[:], func=AF.Square, accum_out=s_.t[:, 0:1]), r=[y], w=[junk, s_])
                op(A, lambda e, s_=s_: e.activation(out=s_.t[:, 1:2], in_=s_.t[:, 0:1], func=AF.Sqrt, scale=1.0 / D, bias=cst.t[:, 0:1]), r=[cst], w=[s_])
                op(V, lambda e, s_=s_: e.reciprocal(out=s_.t[:, 2:3], in_=s_.t[:, 1:2]), w=[s_])
                op(V, lambda e, y=y, s_=s_, yfin=yfin: e.scalar_tensor_tensor(out=yfin.t[:], in0=y.t[:], scalar=s_.t[:, 2:3], in1=gfin.t[:], op0=ALU.mult, op1=ALU.mult),
                   r=[y, s_, gfin], w=[yfin])
                P.dma("sp", T["yout"][qt * 128:(qt + 1) * 128, :], yfin.t[:], reads=[yfin])
        P.emit("phaseC2")


EPS_T = [None, None]
DEBUG_QT = None
DEBUG_DUMP = False
DEBUG_B2 = 9
DEBUG_SAMPLE_ATT = True
DEBUG_TILES = None
DEBUG_GROUPS = None
DEBUG_CHUNKS = None
DEBUG_NOTM = False
DEBUG_HALF = None
DEBUG_NOOUT = False


def build_program(stage=99):
    nc = bass.Bass("TRN2", target_bir_lowering=False)
    din = lambda n, s, d=F32: nc.dram_tensor(n, s, d, kind="ExternalInput").ap()
    dout = lambda n, s, d=F32: nc.dram_tensor(n, s, d, kind="ExternalOutput").ap()
    dscr = (lambda n, s, d: nc.dram_tensor(n, s, d, kind="ExternalOutput").ap()) if DEBUG_DUMP else (lambda n, s, d: nc.dram_tensor(n, s, d).ap())
    T = {}
    T["x_all"] = din("x_all", [NTOK, D])
    T["st_shift"] = din("st_shift", [16, D])
    T["w_in"] = din("w_in", [D, NCOL])
    T["g_mix"] = din("g_mix", [1, D])
    T["mu_fm"] = din("mu_fm", [128, 14])
    for n, shp in [("ident", [128, 128]), ("mtc_p", [128, 256]), ("mtc_s", [128, 256]), ("ms_p", [128, 128]), ("ms_s", [128, 128]),
                   ("rm_p", [128, 512]), ("rm_s", [128, 512]), ("bones", [128, 128]), ("sel8", [128, 32]), ("segsel", [128, 16]),
                   ("w2", [64, 512]), ("a2", [64, 512]), ("g2", [128, 512]), ("w0_fm", [128, 4]), ("a0_fm", [128, 4]),
                   ("kk_fm", [128, 4]), ("ka_fm", [128, 4]), ("rk_fm", [128, 4]), ("ln_w", [1, 512]), ("ln_b", [1, 512]),
                   ("st_rwkv", [16, 8, 64, 64]), ("pmask", [128, 1]), ("btab", [128, 128]), ("bown", [128, 4]), ("bcache", [128, 512]),
                   ("lam_q1", [1, 64]), ("lam_k1", [1, 64]), ("lam_q2", [1, 64]), ("lam_k2", [1, 64]), ("subln_g", [1, 128]),
                   ] + ([("cache_k", [NPOOL * 128, 512]), ("cache_v", [NPOOL * 128, 512])] if DEBUG_SAMPLE_ATT else []) + [
                   ("w_br_a", [512, D]), ("w_br_b", [512, D]), ("w_out", [D, D]), ("g_ffn", [1, D]), ("w_up", [D, 4096]), ("w_down", [4096, D]),
                   ("g_final", [1, D])]:
        T[n] = din(n, shp)
    T["ptab"] = din("ptab", [1, 256], I32)
    T["yout"] = dout("yout", [NQ, D])
    T["obT_d"] = dscr("obT_d", [4, 128, NQ], BF16)
    T["xnew_d"] = dscr("xnew_d", [NQ, D], F32)
    T["hnT_d"] = dscr("hnT_d", [8, 128, NQ], BF16)
    T["kout"] = dout("kout", [NQ, 512])
    T["vout"] = dout("vout", [NQ, 512])
    T["shift_out"] = dout("shift_out", [17, D])
    T["rwkv_p_out"] = dout("rwkv_p_out", [8, 64, 64])
    T["rwkv_s_out"] = dout("rwkv_s_out", [16, 8, 64, 64])
    T["prT_d"] = dscr("prT_d", [14, 128, NTOK], F32)
    T["qT_d"] = dscr("qT_d", [4, 128, NQ], BF16)
    T["kT_d"] = dscr("kT_d", [4, 128, NTOK], BF16)
    T["v_d"] = dscr("v_d", [NTOK, 512], BF16)
    T["gates_d"] = dscr("gates_d", [16, 128, NQ], BF16)
    T["oaT_d"] = dscr("oaT_d", [4, 128, NQ], BF16)
    with contextlib.ExitStack() as st:
        P = Prog(nc, st)
        for n in ["prT", "qT", "kT", "v", "gates", "oaT", "obT", "xnew", "hnT"]:
            T["b_" + n] = P.buf()
        cst = Tl(st.enter_context(nc.sbuf_tensor("cst", [128, 8], F32)), P.buf())
        T["cst"] = cst
        for i, val in enumerate([EPS, 1e-24, 64e-5, 1e-5]):
            P.op("pool", lambda e, i=i, val=val: e.memset(cst.t[:, i:i + 1], val), disjoint=[cst])
        phase_A(P, nc, T)
        if stage >= 2:
            phase_B1(P, nc, T)
        if stage >= 3:
            phase_B2(P, nc, T)
        if stage >= 4:
            phase_C1(P, nc, T)
            phase_C2(P, nc, T)
        P.final_wait("sp")
        P.emit("final")
    return nc


def _consts():
    s = np.arange(128)
    seg = s // 8
    le = (s[:, None] <= s[None, :]).astype(np.float32)
    lt = (s[:, None] < s[None, :]).astype(np.float32)
    same = (seg[:, None] == seg[None, :]).astype(np.float32)
    c = {}
    c["ident"] = np.eye(128, dtype=np.float32)
    c["mtc_p"] = np.concatenate([le, lt], axis=1)
    c["mtc_s"] = np.concatenate([le * same, lt * same], axis=1)
    c["ms_p"] = lt.T.copy()
    c["ms_s"] = (lt * same).T.copy()
    rm = np.ones((128, 4, 128), np.float32)
    rm[:, :, 0] = 0
    c["rm_p"] = rm.reshape(128, 512)
    rm = np.ones((128, 4, 128), np.float32)
    rm[:, :, 0::8] = 0
    c["rm_s"] = rm.reshape(128, 512)
    bo = np.zeros((128, 128), np.float32)
    bo[:64, :64] = 1
    bo[64:, 64:] = 1
    c["bones"] = bo
    sel = np.zeros((128, 4, 8), np.float32)
    for p in range(128):
        for hp in range(4):
            sel[p, hp, 2 * hp + p // 64] = 1
    c["sel8"] = sel.reshape(128, 32)
    c["segsel"] = (seg[:, None] == np.arange(16)[None, :]).astype(np.float32)
    slopes = (2.0 ** (-8.0 * np.arange(1, 5) / 4)).astype(np.float32)
    pp = np.arange(128, dtype=np.float32)
    bt = np.zeros((128, 4, 32), np.float32)
    for h in range(4):
        for o in range(32):
            bt[:, h, o] = slopes[h] * (pp - 128.0 * o)
    c["btab"] = bt.reshape(128, 128)
    c["bown"] = (slopes[None, :] * (pp[:, None] % 8)).astype(np.float32)
    bc = np.zeros((128, 16, 4, 8), np.float32)
    for pg in range(16):
        for h in range(4):
            bc[:, pg, h, :] = (slopes[h] * (pg * 128.0 + pp - 2048.0))[:, None]
    c["bcache"] = bc.reshape(128, 512)
    return c


def make_in_maps(inp):
    f = lambda k: np.asarray(inp[k], np.float32)
    xp = f("x_prompt")
    xs = f("x_sample")
    maps = []
    fm4 = lambda v: np.ascontiguousarray(np.asarray(v, np.float32).reshape(4, 128).T)
    shared = {
        "w_in": np.ascontiguousarray(f("w_in")[0]),
        "g_mix": f("g_mix")[0].reshape(1, D).copy(),
        "mu_fm": np.ascontiguousarray(f("mu_shift").reshape(14, 128).T),
        "w2": np.ascontiguousarray(f("w2")[0]), "a2": np.ascontiguousarray(f("a2")[0]), "g2": np.ascontiguousarray(f("g2")[0]),
        "w0_fm": fm4(f("w0")[0]), "a0_fm": fm4(f("a0")[0]), "kk_fm": fm4(f("k_k")[0]), "ka_fm": fm4(f("k_a")[0]),
        "rk_fm": fm4(f("r_k")[0].reshape(512)),
        "ln_w": f("ln_x_w")[0].reshape(1, 512).copy(), "ln_b": f("ln_x_b")[0].reshape(1, 512).copy(),
        "lam_q1": f("lam_q1")[0].reshape(1, 64).copy(), "lam_k1": f("lam_k1")[0].reshape(1, 64).copy(),
        "lam_q2": f("lam_q2")[0].reshape(1, 64).copy(), "lam_k2": f("lam_k2")[0].reshape(1, 64).copy(),
        "subln_g": f("subln_g")[0].reshape(1, 128).copy(),
        "cache_k": np.asarray(inp["cache_k"], np.float32).reshape(NPOOL * 128, 512),
        "cache_v": np.asarray(inp["cache_v"], np.float32).reshape(NPOOL * 128, 512),
        "w_br_a": np.ascontiguousarray(f("w_br_a")[0]), "w_br_b": np.ascontiguousarray(f("w_br_b")[0]),
        "w_out": np.ascontiguousarray(f("w_out")[0]), "g_ffn": f("g_ffn")[0].reshape(1, D).copy(),
        "w_up": np.ascontiguousarray(f("w_up")[0]), "w_down": np.ascontiguousarray(f("w_down")[0]),
        "g_final": f("g_final").reshape(1, D).copy(),
    }
    shared.update(_consts())
    for c in range(8):
        b, p = divmod(c, 2)
        xa = np.zeros((NTOK, D), np.float32)
        if p == 1:
            xa[0:2048] = xp[b, 0:2048]
        xa[2048:4096] = xp[b, 2048 * p:2048 * p + 2048]
        xa[4096:] = xs[16 * c:16 * c + 16].reshape(128, D)
        m = dict(shared)
        m["x_all"] = xa
        m["st_shift"] = np.ascontiguousarray(f("state_shift")[0, 16 * c:16 * c + 16])
        m["st_rwkv"] = np.ascontiguousarray(f("state_rwkv")[0, 16 * c:16 * c + 16])
        m["ptab"] = np.ascontiguousarray(np.asarray(inp["page_table"], np.int32)[16 * c:16 * c + 16]).reshape(1, 256)
        m["pmask"] = np.full((128, 1), 0.0 if p == 1 else -30000.0, np.float32)
        maps.append(m)
    return maps


_NC_CACHE = {}


def kernel(**inp):
    if "nc" not in _NC_CACHE:
        _NC_CACHE["nc"] = build_program()
    nc = _NC_CACHE["nc"]
    in_maps = make_in_maps(inp)
    res = run_bass_kernel_spmd(nc, in_maps, core_ids=list(range(8)))
    R = res.results
    kp = np.zeros((1, 4, 4096, 4, 128), np.float32)
    vp = np.zeros((1, 4, 4096, 4, 128), np.float32)
    ks = np.zeros((1, 128, 8, 4, 128), np.float32)
    vs = np.zeros((1, 128, 8, 4, 128), np.float32)
    shp = np.zeros((1, 4, D), np.float32)
    shs = np.zeros((1, 128, D), np.float32)
    yp = np.zeros((4, 4096, D), np.float32)
    ys = np.zeros((128, 8, D), np.float32)
    rp = np.zeros((1, 4, 8, 64, 64), np.float32)
    rs = np.zeros((1, 128, 8, 64, 64), np.float32)
    for c in range(8):
        b, p = divmod(c, 2)
        r = R[c]
        kp[0, b, 2048 * p:2048 * p + 2048] = r["kout"][0:2048].reshape(2048, 4, 128)
        vp[0, b, 2048 * p:2048 * p + 2048] = r["vout"][0:2048].reshape(2048, 4, 128)
        ks[0, 16 * c:16 * c + 16] = r["kout"][2048:].reshape(16, 8, 4, 128)
        vs[0, 16 * c:16 * c + 16] = r["vout"][2048:].reshape(16, 8, 4, 128)
        if p == 1:
            shp[0, b] = r["shift_out"][0]
            rp[0, b] = r["rwkv_p_out"]
        shs[0, 16 * c:16 * c + 16] = r["shift_out"][1:17]
        rs[0, 16 * c:16 * c + 16] = r["rwkv_s_out"]
        yp[b, 2048 * p:2048 * p + 2048] = r["yout"][0:2048]
        ys[16 * c:16 * c + 16] = r["yout"][2048:].reshape(16, 8, D)
    return (yp, ys, kp, vp, ks, vs, rp, rs, shp, shs)
```
